# Optimizing a Trainium2 kernel written in Bass

```python
import jax
import jax.numpy as jnp
from jax import lax
import numpy as np

D_MODEL = 2048
BATCH = 2
SEQ = 8192
DEPTH = 4

GRID_W = 64
CTX_LEN = 256
HEAD_DIM = 64
D_RWKV = D_MODEL // 2
N_RWKV_HEADS = D_RWKV // HEAD_DIM
D_CONV = D_MODEL // 4
D_FOURIER = D_MODEL // 4
N_FOURIER_GROUPS = D_FOURIER // HEAD_DIM
D_MIX = D_RWKV + D_CONV + D_FOURIER
DECAY_LORA = 96
ICLR_LORA = 96
GATE_LORA = 256
D_FF = ((8 * D_MODEL + 3 * 256 - 1) // (3 * 256)) * 256
N_MOD = 6
RMS_EPS = 1e-6
GN_EPS = 64e-5
KK_EPS = 1e-12

R0 = 0
K0 = R0 + D_RWKV
V0 = K0 + D_RWKV
WD0 = V0 + D_RWKV
AD0 = WD0 + DECAY_LORA
GD0 = AD0 + ICLR_LORA
RW_COLS = GD0 + GATE_LORA
CG0 = RW_COLS
CX0 = CG0 + D_CONV
CB0 = CX0 + D_CONV
FT0 = CB0 + D_CONV
D_IN = FT0 + D_FOURIER

kernel_name = "hybrid_rwkv7_shortconv_fnet_dit"


def rms_norm(x, g):
    xf = x.astype(jnp.float32)
    y = xf * lax.rsqrt(jnp.mean(xf * xf, axis=-1, keepdims=True) + RMS_EPS)
    return (y * g.astype(jnp.float32)).astype(x.dtype)


def adaln(cvec, w, b):
    m = jax.nn.silu(cvec) @ w + b
    return jnp.split(m[:, None, :], N_MOD, axis=-1)


def modulate(h, shift, scale):
    return h * (1.0 + scale) + shift


def conv3(u, w):
    pad = [(0, 0)] * (u.ndim - 2) + [(1, 1), (0, 0)]
    up = jnp.pad(u, pad)
    return up[..., :-2, :] * w[0] + up[..., 1:-1, :] * w[1] + up[..., 2:, :] * w[2]


def grid_conv3(u, w):
    b, l, ch = u.shape
    rows = l // GRID_W
    return conv3(u.reshape(b, rows, GRID_W, ch), w).reshape(b, l, ch)


def rwkv_streams(rw, dec_w0, dec_up, iclr_a0, iclr_up, k_k, k_a):
    rw = rw.astype(jnp.float32)
    b, l = rw.shape[:2]

    def heads(t):
        return t.reshape(b, l, N_RWKV_HEADS, HEAD_DIM)

    r = rw[..., R0:K0]
    k = rw[..., K0:V0]
    v = rw[..., V0:WD0]
    wd = jnp.tanh(rw[..., WD0:AD0])
    ad = rw[..., AD0:GD0]
    gd = rw[..., GD0:RW_COLS]
    kk = heads(k * k_k)
    kk = kk / jnp.maximum(jnp.sqrt(jnp.sum(kk * kk, axis=-1, keepdims=True)), KK_EPS)
    per_dir = []
    for d in range(2):
        w_log = -jax.nn.softplus(-(dec_w0[d] + wd @ dec_up[d])) - 0.5
        a = jax.nn.sigmoid(iclr_a0[d] + ad @ iclr_up[d])
        kd = k * (1.0 + (a - 1.0) * k_a)
        per_dir.append((heads(jnp.exp(-jnp.exp(w_log))), heads(kd), heads(a)))
    return heads(r), heads(v), kk, per_dir, gd


def wkv_scan(r, w, k, v, kk, a, s0, reverse):
    def step(s, inp):
        r_t, w_t, k_t, v_t, kk_t, a_t = inp
        sa = jnp.einsum('bhij,bhj->bhi', s, -kk_t)
        s = (s * w_t[:, :, None, :] + sa[..., None] * (kk_t * a_t)[:, :, None, :]
             + v_t[..., None] * k_t[:, :, None, :])
        return s, jnp.einsum('bhij,bhj->bhi', s, r_t)

    xs = tuple(jnp.moveaxis(t, 1, 0) for t in (r, w, k, v, kk, a))
    s_fin, ys = lax.scan(step, s0, xs, reverse=reverse)
    return jnp.moveaxis(ys, 0, 1), s_fin


def rwkv_mix(streams, s0, ln_w, ln_b, r_k, g_up):
    r, v, kk, per_dir, gd = streams
    (wf, kf, af), (wb, kb, ab) = per_dir
    yf, sf = wkv_scan(r, wf, kf, v, kk, af, s0[0], False)
    yb, sb = wkv_scan(r, wb, kb, v, kk, ab, s0[1], True)
    y = yf + yb
    mu = jnp.mean(y, axis=-1, keepdims=True)
    var = jnp.mean(jnp.square(y - mu), axis=-1, keepdims=True)
    y = ((y - mu) * lax.rsqrt(var + GN_EPS) * ln_w.reshape(N_RWKV_HEADS, HEAD_DIM)
         + ln_b.reshape(N_RWKV_HEADS, HEAD_DIM))
    y = y + jnp.sum(r * (kf + kb) * r_k, axis=-1, keepdims=True) * v
    b, l = y.shape[:2]
    g = jax.nn.sigmoid(gd) @ g_up
    return y.reshape(b, l, D_RWKV) * g, (sf, sb)


def short_conv_mix(p, conv_w, conv_fn):
    return p[..., CB0:FT0] * conv_fn(p[..., CG0:CX0] * p[..., CX0:CB0], conv_w)


def fourier_mix(p):
    u = p[..., FT0:D_IN].astype(jnp.float32)
    b, l = u.shape[:2]
    u = u.reshape(b, l, N_FOURIER_GROUPS, HEAD_DIM)
    y = jnp.fft.fftn(u, axes=(1, 3), norm='ortho').real
    return y.reshape(b, l, D_FOURIER).astype(p.dtype)


def ffn(h, wg, wu, wd):
    return (jax.nn.silu(h @ wg) * (h @ wu)) @ wd


def setup_inputs(seed: int = 0) -> dict:
    key = jax.random.key(seed)
    ks = jax.random.split(key, 32)
    f32 = jnp.float32

    def nrm(k, shape, s):
        return jax.random.normal(k, shape, f32) * s

    L = DEPTH
    left = jax.random.uniform(ks[8], (L, 1, RW_COLS), f32, 0.0, 0.4)
    right = jax.random.uniform(ks[9], (L, 1, RW_COLS), f32, 0.0, 0.4)
    rw_shift = jnp.concatenate([left, 1.0 - 0.5 * (left + right), right], axis=1)
    return {
        'x': nrm(ks[0], (BATCH, SEQ, D_MODEL), 1.0),
        'c': nrm(ks[1], (BATCH, D_MODEL), 1.0),
        'ctx': nrm(ks[2], (BATCH, CTX_LEN, D_MODEL), 1.0),
        'c_ctx': nrm(ks[3], (D_MODEL,), 1.0),
        'w_mod': nrm(ks[4], (L, D_MODEL, N_MOD * D_MODEL), 0.5 * D_MODEL ** -0.5),
        'b_mod': nrm(ks[5], (L, N_MOD * D_MODEL), 0.02),
        'norm_mix': 1.0 + nrm(ks[6], (L, D_MODEL), 0.05),
        'w_in': nrm(ks[7], (L, D_MODEL, D_IN), D_MODEL ** -0.5),
        'rw_shift': rw_shift,
        'dec_w0': jax.random.uniform(ks[10], (L, 2, D_RWKV), f32, -6.0, 1.0),
        'dec_up': nrm(ks[11], (L, 2, DECAY_LORA, D_RWKV), 0.5 * DECAY_LORA ** -0.5),
        'iclr_a0': nrm(ks[12], (L, 2, D_RWKV), 0.1),
        'iclr_up': nrm(ks[13], (L, 2, ICLR_LORA, D_RWKV), 0.5 * ICLR_LORA ** -0.5),
        'k_k': 0.85 + nrm(ks[14], (L, D_RWKV), 0.05),
        'k_a': 1.0 + nrm(ks[15], (L, D_RWKV), 0.05),
        'r_k': nrm(ks[16], (L, N_RWKV_HEADS, HEAD_DIM), 0.1),
        'ln_w': 1.0 + nrm(ks[17], (L, D_RWKV), 0.05),
        'ln_b': nrm(ks[18], (L, D_RWKV), 0.02),
        'g_up': nrm(ks[19], (L, GATE_LORA, D_RWKV), GATE_LORA ** -0.5),
        'conv_w': nrm(ks[20], (L, 3, D_CONV), 0.5),
        'w_out': nrm(ks[21], (L, D_MIX, D_MODEL), D_MIX ** -0.5),
        'norm_ffn': 1.0 + nrm(ks[22], (L, D_MODEL), 0.05),
        'w_gate': nrm(ks[23], (L, D_MODEL, D_FF), D_MODEL ** -0.5),
        'w_up': nrm(ks[24], (L, D_MODEL, D_FF), D_MODEL ** -0.5),
        'w_down': nrm(ks[25], (L, D_FF, D_MODEL), D_FF ** -0.5),
        'norm_final': 1.0 + nrm(ks[26], (D_MODEL,), 0.05),
    }


def reference(x, c, ctx, c_ctx, w_mod, b_mod, norm_mix, w_in, rw_shift, dec_w0, dec_up,
              iclr_a0, iclr_up, k_k, k_a, r_k, ln_w, ln_b, g_up, conv_w, w_out,
              norm_ffn, w_gate, w_up, w_down, norm_final):
    xc = ctx
    s_zero = jnp.zeros((ctx.shape[0], N_RWKV_HEADS, HEAD_DIM, HEAD_DIM), jnp.float32)
    for i in range(DEPTH):
        sh1, sc1, ga1, sh2, sc2, ga2 = adaln(c, w_mod[i], b_mod[i])
        csh1, csc1, cga1, csh2, csc2, cga2 = adaln(c_ctx[None, :], w_mod[i], b_mod[i])
        px = modulate(rms_norm(x, norm_mix[i]), sh1, sc1) @ w_in[i]
        pc = modulate(rms_norm(xc, norm_mix[i]), csh1, csc1) @ w_in[i]
        rw_args = (dec_w0[i], dec_up[i], iclr_a0[i], iclr_up[i], k_k[i], k_a[i])
        out_args = (ln_w[i], ln_b[i], r_k[i], g_up[i])
        yc_rw, s_ctx = rwkv_mix(rwkv_streams(conv3(pc[..., :RW_COLS], rw_shift[i]), *rw_args),
                                (s_zero, s_zero), *out_args)
        yx_rw, _ = rwkv_mix(rwkv_streams(conv3(px[..., :RW_COLS], rw_shift[i]), *rw_args),
                            s_ctx, *out_args)
        yx = jnp.concatenate([yx_rw.astype(px.dtype),
                              short_conv_mix(px, conv_w[i], grid_conv3),
                              fourier_mix(px)], axis=-1) @ w_out[i]
        x = x + ga1 * yx
        x = x + ga2 * ffn(modulate(rms_norm(x, norm_ffn[i]), sh2, sc2),
                          w_gate[i], w_up[i], w_down[i])
        if i < DEPTH - 1:
            yc = jnp.concatenate([yc_rw.astype(pc.dtype),
                                  short_conv_mix(pc, conv_w[i], conv3),
                                  fourier_mix(pc)], axis=-1) @ w_out[i]
            xc = xc + cga1 * yc
            xc = xc + cga2 * ffn(modulate(rms_norm(xc, norm_ffn[i]), csh2, csc2),
                                 w_gate[i], w_up[i], w_down[i])
    return rms_norm(x, norm_final)
```

```python
import numpy as np
import ml_dtypes
from contextlib import ExitStack
import concourse.bass as bass
import concourse.mybir as mybir
from concourse.bass_utils import run_bass_kernel_spmd

F32 = mybir.dt.float32
BF16 = mybir.dt.bfloat16
AF = mybir.ActivationFunctionType
ALU = mybir.AluOpType

D = 2048
DC = 16
HD = 64
D_RWKV = 1024
D_FF = 5632
FC = 44
R0, K0, V0, WD0, AD0, GD0, RW_COLS = 0, 1024, 2048, 3072, 3168, 3264, 3520
D_IN = 5568
PCG, PCX, PCB, PFT, PEND = 3584, 4096, 4608, 5120, 5632
RMS_EPS = 1e-6
GN_EPS = 64e-5
CH = 128
NT = 512


class Tr:
    __slots__ = ("w", "r", "sem", "dcnt", "const")

    def __init__(self, const=False):
        self.w = None
        self.r = {}
        self.sem = None
        self.dcnt = 0
        self.const = const


class Buf:
    def __init__(self, tile):
        self.t = tile
        self.tr = Tr()

    def __getitem__(self, k):
        return self.t[k]


class FW:
    ENG = ("pe", "act", "dve", "pool", "sp")

    def __init__(self, nc):
        self.nc = nc
        self.es = ExitStack()
        self.eng = {"pe": nc.tensor, "act": nc.scalar, "dve": nc.vector, "pool": nc.gpsimd, "sp": nc.sync}
        self.sem = {e: self.es.enter_context(nc.semaphore("sem_" + e)) for e in self.ENG}
        self.cnt = {e: 0 for e in self.ENG}
        self.seen = {e: {} for e in self.ENG}
        self.nsem = 0
        self.n_inst = 0
        self.dma_trs = []
        self.sem_pool = [[], []]
        self.uid = 0

    def sb(self, shape, dt, es=None, name=None):
        self.uid += 1
        return Buf((es or self.es).enter_context(self.nc.sbuf_tensor("%s_%d" % (name or "sb", self.uid), shape, dt)))

    def ps(self, shape, dt=F32, es=None, name=None):
        self.uid += 1
        return Buf((es or self.es).enter_context(self.nc.psum_tensor("%s_%d" % (name or "ps", self.uid), shape, dt)))

    def dram(self, name, shape, dt, kind="Internal"):
        return self.nc.dram_tensor(name, shape, dt, kind=kind).ap()

    def _wait(self, e, tok):
        if tok is None:
            return
        kind, key, val = tok
        if kind == "e":
            if self.seen[e].get(key, 0) >= val:
                return
            self.eng[e].wait_ge(self.sem[key], val)
            self.seen[e][key] = val
        else:
            k = id(key)
            if self.seen[e].get(k, 0) >= val:
                return
            self.eng[e].wait_ge(key, val)
            self.seen[e][k] = val

    def _deps(self, e, reads, writes):
        for t in reads:
            self._wait(e, t.w)
        for t in writes:
            self._wait(e, t.w)
            for tok in t.r.values():
                self._wait(e, tok)

    def _commit(self, tok, reads, writes):
        key = tok[1] if tok[0] == "e" else id(tok[1])
        for t in reads:
            if not t.const:
                t.r[key] = tok
        for t in writes:
            t.w = tok
            t.r = {}

    @staticmethod
    def _trs(bufs):
        return [b.tr if isinstance(b, Buf) else b for b in bufs]

    def op(self, e, fn, reads=(), writes=()):
        reads = self._trs(reads)
        writes = self._trs(writes)
        self._deps(e, reads, writes)
        ins = fn(self.eng[e])
        self.cnt[e] += 1
        ins.then_inc(self.sem[e], 1)
        self._commit(("e", e, self.cnt[e]), reads, writes)
        self.n_inst += 1

    def ops(self, e, fns, reads=(), writes=()):
        reads = self._trs(reads)
        writes = self._trs(writes)
        self._deps(e, reads, writes)
        ins = None
        for fn in fns:
            ins = fn(self.eng[e])
            self.n_inst += 1
        self.cnt[e] += 1
        ins.then_inc(self.sem[e], 1)
        self._commit(("e", e, self.cnt[e]), reads, writes)

    def dma(self, q, out, in_, own, reads=(), writes=()):
        own = own.tr if isinstance(own, Buf) else own
        reads = self._trs(reads)
        writes = self._trs(writes)
        sw = 1 if q == "pool" else 0
        if own.sem is None:
            own.sem = [None, None]
            own.dcnt = [0, 0]
            self.dma_trs.append(own)
        if own.sem[sw] is None:
            if self.sem_pool[sw]:
                own.sem[sw], own.dcnt[sw] = self.sem_pool[sw].pop()
            else:
                self.nsem += 1
                own.sem[sw] = self.es.enter_context(self.nc.semaphore("dsem%d" % self.nsem))
                own.dcnt[sw] = 0
        for i in (0, 1):
            if own.sem[i] is not None and own.dcnt[i] > 0:
                self._wait(q, ("d", own.sem[i], own.dcnt[i]))
        self._deps(q, reads, writes)
        ins = self.eng[q].dma_start(out=out, in_=in_)
        own.dcnt[sw] += 16
        ins.then_inc(own.sem[sw], 16)
        self._commit(("d", own.sem[sw], own.dcnt[sw]), reads, writes)
        self.n_inst += 1

    def load(self, q, buf, dst, src):
        self.dma(q, dst, src, buf, writes=[buf])

    def store(self, q, buf, dst, src):
        self.dma(q, dst, src, buf, reads=[buf])

    def barrier(self):
        for e in self.ENG:
            for f in self.ENG:
                if f != e and self.cnt[f] > 0:
                    self._wait(e, ("e", f, self.cnt[f]))
            for t in self.dma_trs:
                for i in (0, 1):
                    if t.sem[i] is not None and t.dcnt[i] > 0:
                        self._wait(e, ("d", t.sem[i], t.dcnt[i]))
        for t in self.dma_trs:
            for i in (0, 1):
                if t.sem[i] is not None:
                    self.sem_pool[i].append((t.sem[i], t.dcnt[i]))
            t.sem = None
            t.dcnt = 0
        self.dma_trs = []


class Ring:
    def __init__(self, bufs):
        self.bufs = bufs
        self.i = 0

    def next(self):
        b = self.bufs[self.i % len(self.bufs)]
        self.i += 1
        return b


def build(T, CTX, DEPTH, dbg=()):
    nc = bass.Bass("TRN2", target_bir_lowering=False)
    fw = FW(nc)
    TT = CTX + T
    L = DEPTH

    def din(name, shape, dt=F32):
        return nc.dram_tensor(name, shape, dt, kind="ExternalInput").ap()

    x_in = din("x", [T, D])
    ctx_in = din("ctx", [CTX, D])
    cvec_in = din("cvec", [2, D])
    w_mod = din("w_mod", [L, D, 6 * D])
    b_mod = din("b_mod", [L, 6 * D])
    norm_mix = din("norm_mix", [L, D])
    w_in = din("w_in", [L, D, D_IN])
    rw_shift = din("rw_shift", [L, 3, RW_COLS])
    dec_w0 = din("dec_w0", [L, 2, D_RWKV])
    dec_up = din("dec_up", [L, 2, 96, D_RWKV])
    iclr_a0 = din("iclr_a0", [L, 2, D_RWKV])
    iclr_up = din("iclr_up", [L, 2, 96, D_RWKV])
    k_k = din("k_k", [L, D_RWKV])
    k_a = din("k_a", [L, D_RWKV])
    r_k = din("r_k", [L, D_RWKV])
    ln_w = din("ln_w", [L, D_RWKV])
    ln_b = din("ln_b", [L, D_RWKV])
    g_up = din("g_up", [L, 256, D_RWKV])
    conv_w = din("conv_w", [L, 3, 512])
    w_out = din("w_out", [L, D, D])
    norm_ffn = din("norm_ffn", [L, D])
    w_gate = din("w_gate", [L, D, D_FF])
    w_up = din("w_up", [L, D, D_FF])
    w_down = din("w_down", [L, D_FF, D])
    norm_final = din("norm_final", [1, D])
    ftab_x = din("ftab_x", [2, T, T], BF16)
    ftab_c = din("ftab_c", [2, CTX, CTX], BF16)
    c64_in = din("c64", [128, 256], BF16)
    out_ap = nc.dram_tensor("out", [T, D], F32, kind="ExternalOutput").ap()

    def scratch(name, shape, dt):
        kind = "ExternalOutput" if name in dbg else "Internal"
        return nc.dram_tensor(name, shape, dt, kind=kind).ap()

    XT = scratch("XT", [D, TT], F32)
    PXT = scratch("PXT", [PEND, TT], F32)
    YF = scratch("YF", [D_RWKV, TT], F32)
    YT = scratch("YT", [D, TT], BF16)
    MODS = scratch("MODS", [128, 96 * 2], F32)
    WIN_b = scratch("WIN_b", [11, 128, 16, 512], BF16)
    WOUT_b = scratch("WOUT_b", [8, 128, 16, 256], BF16)
    WG_b = scratch("WG_b", [22, 128, 16, 256], BF16)
    WU_b = scratch("WU_b", [22, 128, 16, 256], BF16)
    WD_b = scratch("WD_b", [16, 128, FC, 128], BF16)

    XTv = XT.rearrange("(c p) t -> p c t", p=128)

    tiles = []
    for c0 in range(0, CTX, NT):
        n = min(NT, CTX - c0)
        tiles.append((c0, n, True, c0 == 0, c0 + n == CTX))
    for c0 in range(0, T, NT):
        n = min(NT, T - c0)
        tiles.append((CTX + c0, n, False, c0 == 0, c0 + n == T))

    with fw.es:
        ident = fw.sb([128, 128], F32, name="ident")
        identb = fw.sb([128, 128], BF16, name="identb")
        onesb = fw.sb([128, 128], BF16, name="onesb")
        blk = fw.sb([128, 128], F32, name="blk")
        iot = fw.sb([128, 128], F32, name="iot")
        m_f = fw.sb([128, 256], F32, name="m_f")
        m_b = fw.sb([128, 256], F32, name="m_b")
        mt_f = fw.sb([128, 128], F32, name="mt_f")
        mt_b = fw.sb([128, 128], F32, name="mt_b")
        cmask = fw.sb([128, NT], F32, name="cmask")
        c64 = fw.sb([128, 256], BF16, name="c64")
        sc_fm = fw.sb([128, 16, 2], F32, name="sc_fm")
        nf_fm = fw.sb([128, 16], F32, name="nf_fm")
        for b in (ident, identb, onesb, blk, iot, m_f, m_b, mt_f, mt_b, cmask, c64):
            b.tr.const = True

        P = fw.eng["pool"]
        fw.op("pool", lambda e: e.iota(iot[:], pattern=[[1, 128]], base=0, channel_multiplier=-1,
                                       allow_small_or_imprecise_dtypes=True), writes=[iot])
        fw.op("dve", lambda e: e.tensor_single_scalar(out=ident[:], in_=iot[:], scalar=0.0, op=ALU.is_equal),
              reads=[iot], writes=[ident])
        fw.op("dve", lambda e: e.tensor_copy(out=identb[:], in_=ident[:]), reads=[ident], writes=[identb])
        fw.op("dve", lambda e: e.memset(onesb[:], 1.0), writes=[onesb])
        fw.op("dve", lambda e: e.memset(blk[:], 0.0), writes=[blk])
        fw.op("dve", lambda e: e.memset(blk[0:64, 0:64], 1.0), writes=[blk])
        fw.op("dve", lambda e: e.memset(blk[64:128, 64:128], 1.0), writes=[blk])
        fw.op("dve", lambda e: e.tensor_single_scalar(out=m_f[:, 0:128], in_=iot[:], scalar=0.0, op=ALU.is_gt), reads=[iot], writes=[m_f])
        fw.op("dve", lambda e: e.tensor_single_scalar(out=m_f[:, 128:256], in_=iot[:], scalar=0.0, op=ALU.is_ge), reads=[iot], writes=[m_f])
        fw.op("dve", lambda e: e.tensor_single_scalar(out=m_b[:, 0:128], in_=iot[:], scalar=0.0, op=ALU.is_lt), reads=[iot], writes=[m_b])
        fw.op("dve", lambda e: e.tensor_single_scalar(out=m_b[:, 128:256], in_=iot[:], scalar=0.0, op=ALU.is_le), reads=[iot], writes=[m_b])
        fw.op("dve", lambda e: e.tensor_single_scalar(out=mt_f[:], in_=iot[:], scalar=0.0, op=ALU.is_lt), reads=[iot], writes=[mt_f])
        fw.op("dve", lambda e: e.tensor_single_scalar(out=mt_b[:], in_=iot[:], scalar=0.0, op=ALU.is_gt), reads=[iot], writes=[mt_b])
        fw.op("dve", lambda e: e.memset(cmask[:], 1.0), writes=[cmask])
        for c in range(NT // CH):
            fw.op("dve", lambda e, c=c: e.memset(cmask[:, c * CH:c * CH + 1], 0.0), writes=[cmask])
        fw.load("sp", c64, c64[:], c64_in[:, :])

        def rows_to_fm(es, rows, R, nq, dst, dst_q0=0):
            psr = fw.ps([128, 16, R], F32, es=es, name="psr")
            for q0 in range(0, nq, 16):
                qn = min(16, nq - q0)
                fw.ops("pe", [lambda e, q=q: e.matmul(psr[:, q - q0, :], lhsT=rows[0:R, q * 128:(q + 1) * 128],
                                                      rhs=ident[0:R, 0:R], start=True, stop=True)
                              for q in range(q0, q0 + qn)], reads=[rows, ident], writes=[psr])
                fw.op("dve", lambda e: e.tensor_copy(out=dst[:, dst_q0 + q0:dst_q0 + q0 + qn, 0:R], in_=psr[:, 0:qn, :]),
                      reads=[psr], writes=[dst])

        with ExitStack() as es:
            crow = fw.sb([2, D], F32, es=es)
            fw.load("sp", crow, crow[:], cvec_in[:, :])
            fw.op("act", lambda e: e.activation(out=crow[:], in_=crow[:], func=AF.Silu), reads=[crow], writes=[crow])
            rows_to_fm(es, crow, 2, 16, sc_fm)
            nrow = fw.sb([1, D], F32, es=es)
            fw.load("sp", nrow, nrow[:], norm_final[:, :])
            nf3 = fw.sb([128, 16, 1], F32, es=es)
            rows_to_fm(es, nrow, 1, 16, nf3)
            fw.op("dve", lambda e: e.tensor_copy(out=nf_fm[:], in_=nf3[:, :, 0]), reads=[nf3], writes=[nf_fm])

            inb = [fw.sb([128, D], F32, es=es) for _ in range(2)]
            stg = [fw.sb([128, 16, 128], F32, es=es) for _ in range(2)]
            pst = [fw.ps([128, 4, 128], F32, es=es) for _ in range(4)]
            k = 0
            for (src, n0, c0) in ((ctx_in, CTX, 0), (x_in, T, CTX)):
                for tb in range(n0 // 128):
                    ib = inb[k % 2]
                    sg = stg[k % 2]
                    fw.load("sp", ib, ib[:], src[tb * 128:(tb + 1) * 128, :])
                    for g4 in range(4):
                        pt = pst[g4 % 4]
                        fw.ops("pe", [lambda e, j=j: e.transpose(pt[:, j, :], ib[:, (g4 * 4 + j) * 128:(g4 * 4 + j + 1) * 128], ident[:])
                                      for j in range(4)], reads=[ib, ident], writes=[pt])
                        fw.op("dve" if g4 % 2 == 0 else "act",
                              (lambda e: e.tensor_copy(out=sg[:, g4 * 4:(g4 + 1) * 4, :], in_=pt[:])) if g4 % 2 == 0 else
                              (lambda e: e.activation(out=sg[:, g4 * 4:(g4 + 1) * 4, :], in_=pt[:], func=AF.Copy)),
                              reads=[pt], writes=[sg])
                    fw.store("pool", sg, XTv[:, :, c0 + tb * 128:c0 + (tb + 1) * 128], sg[:])
                    k += 1
        fw.barrier()

        for li in range(L):
            last_layer = (li == L - 1)
            with ExitStack() as esl:
              with ExitStack() as es:
                st16 = Ring([fw.sb([128, 16, 512], BF16, es=es) for _ in range(3)])
                for g in range(11):
                    b = st16.next()
                    pc0 = g * 512
                    if g < 6:
                        fw.load("pool", b, b[:], w_in[li, :, pc0:pc0 + 512].rearrange("(c p) n -> p c n", p=128))
                    elif g == 6:
                        fw.load("pool", b, b[:, :, 0:448], w_in[li, :, pc0:pc0 + 448].rearrange("(c p) n -> p c n", p=128))
                    else:
                        fw.load("pool", b, b[:], w_in[li, :, pc0 - 64:pc0 - 64 + 512].rearrange("(c p) n -> p c n", p=128))
                    fw.store("sp", b, WIN_b[g], b[:])
                for (wsrc, wdst, ng) in ((w_out, WOUT_b, 4), (w_gate, WG_b, 11), (w_up, WU_b, 11)):
                    for g in range(ng):
                        b = st16.next()
                        fw.load("pool", b, b[:], wsrc[li, :, g * 512:(g + 1) * 512].rearrange("(c p) n -> p c n", p=128))
                        fw.store("sp", b, wdst[2 * g], b[:, :, 0:256])
                        fw.store("sp", b, wdst[2 * g + 1], b[:, :, 256:512])
                st44 = Ring([fw.sb([128, FC, 256], BF16, es=es) for _ in range(2)])
                for g in range(8):
                    b = st44.next()
                    fw.load("pool", b, b[:], w_down[li, :, g * 256:(g + 1) * 256].rearrange("(c p) n -> p c n", p=128))
                    fw.store("sp", b, WD_b[2 * g], b[:, :, 0:128])
                    fw.store("sp", b, WD_b[2 * g + 1], b[:, :, 128:256])
                fw.barrier()
              vfm = fw.sb([128, 96, 24], F32, name="vfm", es=esl)
              cw_fm = fw.sb([128, 4, 3], F32, name="cw_fm", es=esl)
              mods = fw.sb([128, 96, 2], F32, name="mods", es=esl)
              A1 = fw.sb([128, 16, 2], F32, name="A1", es=esl)
              A2 = fw.sb([128, 16, 2], F32, name="A2", es=esl)
              omk = fw.sb([128, 8], F32, name="omk", es=esl)
              omk2 = fw.sb([128, 8], F32, name="omk2", es=esl)
              with ExitStack() as es:
                NR = 24
                rowb = fw.sb([NR, 6 * D], F32, es=es)
                fw.op("pool", lambda e: e.memset(rowb[:], 0.0), writes=[rowb])
                rowspec = [(0, b_mod[li:li + 1, :], 6 * D), (1, norm_mix[li:li + 1, :], D), (2, norm_ffn[li:li + 1, :], D),
                           (3, rw_shift[li, :, 0:3072], 3072), (6, rw_shift[li, :, WD0:WD0 + 96], 96),
                           (9, rw_shift[li, :, AD0:AD0 + 96], 96), (12, rw_shift[li, :, GD0:GD0 + 256], 256),
                           (15, dec_w0[li], D_RWKV), (17, iclr_a0[li], D_RWKV), (19, k_k[li:li + 1, :], D_RWKV),
                           (20, k_a[li:li + 1, :], D_RWKV), (21, r_k[li:li + 1, :], D_RWKV), (22, ln_w[li:li + 1, :], D_RWKV),
                           (23, ln_b[li:li + 1, :], D_RWKV)]
                for (r0, src, ln) in rowspec:
                    nr = src.shape[0]
                    fw.dma("sp", rowb[r0:r0 + nr, 0:ln], src, rowb, writes=[rowb])
                crow3 = fw.sb([3, 512], F32, es=es)
                fw.load("sp", crow3, crow3[:], conv_w[li])
                rows_to_fm(es, rowb, NR, 96, vfm)
                rows_to_fm(es, crow3, 3, 4, cw_fm)

                wm = Ring([fw.sb([128, 16, 512], F32, es=es) for _ in range(2)])
                psm = Ring([fw.ps([128, 4, 2], F32, es=es) for _ in range(2)])
                for g in range(24):
                    b = wm.next()
                    fw.load("sp" if g % 2 == 0 else "act", b, b[:], w_mod[li, :, g * 512:(g + 1) * 512].rearrange("(c p) n -> p c n", p=128))
                    pm = psm.next()
                    fns = []
                    for j in range(4):
                        for kc in range(16):
                            fns.append(lambda e, j=j, kc=kc: e.matmul(pm[:, j, :], lhsT=b[:, kc, j * 128:(j + 1) * 128], rhs=sc_fm[:, kc, :],
                                                                      start=(kc == 0), stop=(kc == 15)))
                    fw.ops("pe", fns, reads=[b, sc_fm], writes=[pm])
                    for r in range(2):
                        fw.op("dve", lambda e, r=r: e.tensor_tensor(out=mods[:, g * 4:(g + 1) * 4, r], in0=pm[:, :, r],
                                                                    in1=vfm[:, g * 4:(g + 1) * 4, 0], op=ALU.add),
                              reads=[pm, vfm], writes=[mods])
                for r in range(2):
                    fw.op("dve", lambda e, r=r: e.scalar_tensor_tensor(out=A1[:, :, r], in0=mods[:, 16:32, r], scalar=1.0, in1=vfm[:, 0:16, 1],
                                                                       op0=ALU.add, op1=ALU.mult), reads=[mods, vfm], writes=[A1])
                    fw.op("dve", lambda e, r=r: e.scalar_tensor_tensor(out=A2[:, :, r], in0=mods[:, 64:80, r], scalar=1.0, in1=vfm[:, 0:16, 2],
                                                                       op0=ALU.add, op1=ALU.mult), reads=[mods, vfm], writes=[A2])
                if "MODS" in dbg:
                    fw.store("sp", mods, MODS, mods[:].rearrange("p q r -> p (q r)"))
                fw.op("dve", lambda e: e.tensor_scalar(out=omk[:], in0=vfm[:, 0:8, 20], scalar1=-1.0, scalar2=1.0, op0=ALU.mult, op1=ALU.add),
                      reads=[vfm], writes=[omk])
                fw.op("dve", lambda e: e.tensor_scalar(out=omk2[:], in0=omk[:], scalar1=2.0, scalar2=None, op0=ALU.mult), reads=[omk], writes=[omk2])
                fw.barrier()
              if True:
                with ExitStack() as es1:
                    xs = fw.sb([128, 16, NT], F32, es=es1)
                    sq = fw.sb([128, 16, NT], BF16, es=es1)
                    rstd = fw.sb([128, NT], F32, es=es1)
                    tmp = Ring([fw.sb([128, NT], F32, es=es1) for _ in range(2)])
                    xn = fw.sb([128, 16, 2 * NT], BF16, es=es1)
                    wr = Ring([fw.sb([128, 16, 512], BF16, es=es1) for _ in range(2)])
                    pss = fw.ps([128, NT], F32, es=es1)
                    pso = Ring([fw.ps([128, 2 * NT], F32, es=es1) for _ in range(3)])
                    ost = Ring([fw.sb([128, 2 * NT], F32, es=es1) for _ in range(3)])
                    supers = []
                    for c0 in range(0, CTX, 2 * NT):
                        supers.append((c0, min(2 * NT, CTX - c0), 1))
                    for c0 in range(0, T, 2 * NT):
                        supers.append((CTX + c0, min(2 * NT, T - c0), 0))

                    def norm_mod(col0, n, r, Amod, sh_q0, dst, dst0, src_buf=None):
                        xb = src_buf
                        if xb is None:
                            xb = xs
                            fw.load("sp", xs, xs[:, :, 0:n], XTv[:, :, col0:col0 + n])
                        fw.op("act", lambda e: e.activation(out=sq[:, :, 0:n], in_=xb[:, :, 0:n], func=AF.Square), reads=[xb], writes=[sq])
                        fw.ops("pe", [lambda e, dc=dc: e.matmul(pss[:, 0:n], lhsT=onesb[:], rhs=sq[:, dc, 0:n], start=(dc == 0), stop=(dc == 15))
                                      for dc in range(16)], reads=[sq, onesb], writes=[pss])
                        fw.op("act", lambda e: e.activation(out=rstd[:, 0:n], in_=pss[:, 0:n], func=AF.Sqrt, bias=RMS_EPS, scale=1.0 / D),
                              reads=[pss], writes=[rstd])
                        fw.op("dve", lambda e: e.reciprocal(out=rstd[:, 0:n], in_=rstd[:, 0:n]), reads=[rstd], writes=[rstd])
                        for dc in range(16):
                            tb = tmp.next()
                            fw.op("dve", lambda e, dc=dc: e.scalar_tensor_tensor(out=tb[:, 0:n], in0=xb[:, dc, 0:n], scalar=Amod[:, dc, r:r + 1],
                                                                                 in1=rstd[:, 0:n], op0=ALU.mult, op1=ALU.mult),
                                  reads=[xb, Amod, rstd], writes=[tb])
                            fw.op("act", lambda e, dc=dc: e.activation(out=dst[:, dc, dst0:dst0 + n], in_=tb[:, 0:n], func=AF.Identity,
                                                                       bias=mods[:, sh_q0 + dc, r:r + 1], scale=1.0),
                                  reads=[tb, mods], writes=[dst])

                    ev = 0
                    for (s0, sn, r) in supers:
                        for o in range(0, sn, NT):
                            norm_mod(s0 + o, min(NT, sn - o), r, A1, 0, xn, o)
                        for g in range(11):
                            wb = wr.next()
                            fw.load("act", wb, wb[:], WIN_b[g])
                            for j in range(4):
                                pq = g * 4 + j
                                if pq == 27:
                                    pass
                                po = pso.next()
                                fns = []
                                for h0 in range(0, sn, NT):
                                    hn = min(NT, sn - h0)
                                    for kc in range(16):
                                        fns.append(lambda e, h0=h0, hn=hn, kc=kc, j=j: e.matmul(po[:, h0:h0 + hn], lhsT=wb[:, kc, j * 128:(j + 1) * 128],
                                                                                               rhs=xn[:, kc, h0:h0 + hn], start=(kc == 0), stop=(kc == 15)))
                                fw.ops("pe", fns, reads=[wb, xn], writes=[po])
                                ob = ost.next()
                                if ev % 2 == 0:
                                    fw.op("dve", lambda e: e.tensor_copy(out=ob[:, 0:sn], in_=po[:, 0:sn]), reads=[po], writes=[ob])
                                else:
                                    fw.op("act", lambda e: e.activation(out=ob[:, 0:sn], in_=po[:, 0:sn], func=AF.Copy), reads=[po], writes=[ob])
                                ev += 1
                                fw.store("sp", ob, PXT[pq * 128:(pq + 1) * 128, s0:s0 + sn], ob[:, 0:sn])
                fw.barrier()
                if "stop_p1" in dbg:
                    break

                with ExitStack() as es2:
                    HP = 8
                    SL = 9
                    for d_ in dbg:
                        if d_.startswith('sl='):
                            SL = int(d_[3:])
                    SDT = F32 if 'scan32' in dbg else BF16
                    identS = ident if 'scan32' in dbg else identb
                    IDT = BF16 if 'inv16' in dbg else F32
                    decw = fw.sb([96, 2, D_RWKV], F32, es=es2)
                    iclw = fw.sb([96, 2, D_RWKV], F32, es=es2)
                    gupw = fw.sb([128, 2, D_RWKV], BF16, es=es2)
                    for d in range(2):
                        fw.dma("sp", decw[:, d, :], dec_up[li, d], decw, writes=[decw])
                        fw.dma("sp", iclw[:, d, :], iclr_up[li, d], iclw, writes=[iclw])
                    fw.load("pool", gupw, gupw[:], g_up[li].rearrange("(c p) n -> p c n", p=128))
                    for b in (decw, iclw, gupw):
                        pass
                    S2 = [fw.sb([128, 128], F32, es=es2) for _ in range(HP)]
                    S2b = [fw.sb([128, 128], SDT, es=es2) for _ in range(HP)]
                    wdh = fw.sb([96, NT + 2], F32, es=es2)
                    adh = fw.sb([96, NT + 2], F32, es=es2)
                    gdh = fw.sb([128, 2, NT + 2], F32, es=es2)
                    wdt = fw.sb([96, NT], F32, es=es2)
                    ads = fw.sb([96, NT], F32, es=es2)
                    gds = fw.sb([128, 2, NT], BF16, es=es2)
                    gtmp = fw.sb([128, 2, NT], F32, es=es2)
                    rkv_h = Ring([[fw.sb([128, NT + 2], F32, es=es2) for _ in range(3)] for _ in range(2)])
                    r_s = [fw.sb([128, NT], F32, es=es2) for _ in range(4)]
                    ks_s = [fw.sb([128, NT], F32, es=es2) for _ in range(4)]
                    v_s = [fw.sb([128, NT], F32, es=es2) for _ in range(4)]
                    vb_s = [fw.sb([128, NT], SDT, es=es2) for _ in range(4)]
                    strm = [fw.sb([128, NT // CH, 4, CH], SDT, es=es2) for _ in range(4)]
                    gC = [fw.sb([128, NT // CH], F32, es=es2) for _ in range(4)]
                    ybuf = [fw.sb([128, NT], F32, es=es2) for _ in range(4)]
                    tw = Ring([fw.sb([128, NT], F32, es=es2) for _ in range(12)])
                    LCt = fw.sb([128, NT // CH], F32, es=es2)
                    vT = [fw.sb([128, 128], SDT, es=es2) for _ in range(2)]
                    bT = [fw.sb([128, 128], SDT, es=es2) for _ in range(2)]
                    kT = [fw.sb([128, 128], SDT, es=es2) for _ in range(2)]
                    Gb = [[fw.sb([128, 256], SDT, es=es2) for _ in range(2)] for _ in range(2)]
                    Gk = [[fw.sb([128, 256], SDT, es=es2) for _ in range(2)] for _ in range(2)]
                    Pk = [[[fw.sb([128, 128], IDT, es=es2) for _ in range(2)] for _ in range(2)] for _ in range(2)]
                    PkT = [[[fw.sb([128, 128], IDT, es=es2) for _ in range(2)] for _ in range(2)] for _ in range(2)]
                    Zb = [[[fw.sb([128, 64], IDT, es=es2) for _ in range(2)] for _ in range(2)] for _ in range(2)]
                    Up = [fw.sb([128, 128], SDT, es=es2) for _ in range(2)]
                    A0 = [[fw.sb([128, 128], IDT, es=es2) for _ in range(2)] for _ in range(2)]
                    stmp = Ring([fw.sb([128, 64], F32, es=es2) for _ in range(4)])
                    psA = Ring([fw.ps([128, 512], F32, es=es2) for _ in range(4)])
                    psB = Ring([fw.ps([128, 512], F32, es=es2) for _ in range(2)])
                    psZ = Ring([fw.ps([128, 512], F32, es=es2) for _ in range(2)])

                    yio = Ring([fw.sb([128, NT], F32, es=es2) for _ in range(2)])
                    yob = Ring([fw.sb([128, NT], BF16, es=es2) for _ in range(2)])

                    for hp in range(HP):
                        fw.op("pool", lambda e, hp=hp: e.memset(S2[hp][:], 0.0), writes=[S2[hp]])
                        fw.op("pool", lambda e, hp=hp: e.memset(S2b[hp][:], 0.0), writes=[S2b[hp]])

                    ptmp = Ring([fw.sb([128, NT], F32, es=es2) for _ in range(2)])

                    def shift3(eng, dst, src, n, wq, rows, row0, dv=None, sv_=None):
                        dv = dv or (lambda: dst[0:rows, 0:n])
                        sv_ = sv_ or (lambda a, b_: src[0:rows, a:b_])
                        w = lambda tap: vfm[0:rows, wq, row0 + tap:row0 + tap + 1]
                        fw.op(eng, lambda e: e.tensor_scalar(out=dv(), in0=sv_(1, n + 1), scalar1=w(1), scalar2=None, op0=ALU.mult), reads=[src, vfm], writes=[dst])
                        for tap, a in ((0, 0), (2, 2)):
                            if eng == "dve":
                                fw.op(eng, lambda e: e.scalar_tensor_tensor(out=dv(), in0=sv_(a, a + n), scalar=w(tap), in1=dv(), op0=ALU.mult, op1=ALU.add),
                                      reads=[src, vfm, dst], writes=[dst])
                            else:
                                pt_ = ptmp.next()
                                fw.op(eng, lambda e: e.tensor_scalar(out=pt_[0:rows, 0:n], in0=sv_(a, a + n), scalar1=w(tap), scalar2=None, op0=ALU.mult),
                                      reads=[src, vfm], writes=[pt_])
                                fw.op(eng, lambda e: e.tensor_tensor(out=dv(), in0=dv(), in1=pt_[0:rows, 0:n], op=ALU.add), reads=[dst, pt_], writes=[dst])

                    def load_halo(q, buf, dst_rows, row_a, row_b, col0, n, first, last):
                        lo = 1 if first else 0
                        hi = n + 1 if last else n + 2
                        if first:
                            fw.op("pool", lambda e: e.memset(dst_rows[:, 0:1], 0.0), writes=[buf])
                        if last:
                            fw.op("pool", lambda e: e.memset(dst_rows[:, n + 1:n + 2], 0.0), writes=[buf])
                        fw.dma(q, dst_rows[:, lo:hi], PXT[row_a:row_b, col0 - 1 + lo:col0 - 1 + hi], buf, writes=[buf])

                    for pas in (0, 1):
                        if pas == 0:
                            order = list(tiles)
                        else:
                            ctx_t = [t for t in tiles if t[2]]
                            x_t = [t for t in tiles if not t[2]]
                            order = ctx_t[::-1] + x_t[::-1]
                        msk = m_f if pas == 0 else m_b
                        mskT = mt_f if pas == 0 else mt_b
                        for (col0, n, is_ctx, first, last) in order:
                            ncn = n // CH
                            skip_fin = is_ctx and last_layer
                            load_halo("sp", wdh, wdh[0:96, 0:n + 2], WD0, WD0 + 96, col0, n, first, last)
                            load_halo("sp", adh, adh[0:96, 0:n + 2], AD0, AD0 + 96, col0, n, first, last)
                            shift3("pool", wdt, wdh, n, 0, 96, 6)
                            fw.op("act", lambda e: e.activation(out=wdt[:, 0:n], in_=wdt[:, 0:n], func=AF.Tanh), reads=[wdt], writes=[wdt])
                            shift3("pool", ads, adh, n, 0, 96, 9)
                            if pas == 1 and not skip_fin:
                                for c2 in range(2):
                                    load_halo("sp", gdh, gdh[:, c2, 0:n + 2], GD0 + c2 * 128, GD0 + (c2 + 1) * 128, col0, n, first, last)
                                for c2 in range(2):
                                    shift3("pool", gtmp, gdh, n, c2, 128, 12, dv=lambda c2=c2: gtmp[:, c2, 0:n], sv_=lambda a, b_, c2=c2: gdh[:, c2, a:b_])
                                fw.op("act", lambda e: e.activation(out=gds[:, :, 0:n], in_=gtmp[:, :, 0:n], func=AF.Sigmoid), reads=[gtmp], writes=[gds])
                            for hg in range(2):
                              hps = list(range(hg * 4, hg * 4 + 4))
                              if True:
                                for hp in hps:
                                    hb = rkv_h.next()
                                    for i3, base in enumerate((R0, K0, V0)):
                                        load_halo("sp" if i3 != 1 else "act", hb[i3], hb[i3][:, 0:n + 2], base + hp * 128, base + (hp + 1) * 128, col0, n, first, last)
                                    rr, vv = r_s[hp % 4], v_s[hp % 4]
                                    kk_ = tw.next()
                                    shift3("dve", rr, hb[0], n, R0 // 128 + hp, 128, 3)
                                    shift3("pool", kk_, hb[1], n, K0 // 128 + hp, 128, 3)
                                    shift3("dve", vv, hb[2], n, V0 // 128 + hp, 128, 3)
                                    fw.op("act", lambda e: e.activation(out=vb_s[hp % 4][:, 0:n], in_=vv[:, 0:n], func=AF.Copy), reads=[vv], writes=[vb_s[hp % 4]])
                                    t1 = tw.next()
                                    fw.op("dve", lambda e: e.tensor_scalar(out=t1[:, 0:n], in0=kk_[:, 0:n], scalar1=vfm[:, hp, 19:20], scalar2=None, op0=ALU.mult),
                                          reads=[kk_, vfm], writes=[t1])
                                    t2 = tw.next()
                                    fw.op("act", lambda e: e.activation(out=t2[:, 0:n], in_=t1[:, 0:n], func=AF.Square), reads=[t1], writes=[t2])
                                    pa = psA.next()
                                    fw.op("pe", lambda e: e.matmul(pa[:, 0:n], lhsT=blk[:], rhs=t2[:, 0:n], start=True, stop=True), reads=[blk, t2], writes=[pa])
                                    fw.op("act", lambda e: e.activation(out=t2[:, 0:n], in_=pa[:, 0:n], func=AF.Sqrt), reads=[pa], writes=[t2])
                                    fw.op("dve", lambda e: e.tensor_scalar(out=t2[:, 0:n], in0=t2[:, 0:n], scalar1=1e-12, scalar2=None, op0=ALU.max), reads=[t2], writes=[t2])
                                    fw.op("dve", lambda e: e.reciprocal(out=t2[:, 0:n], in_=t2[:, 0:n]), reads=[t2], writes=[t2])
                                    kkn = t1
                                    fw.op("dve", lambda e: e.tensor_tensor(out=kkn[:, 0:n], in0=t1[:, 0:n], in1=t2[:, 0:n], op=ALU.mult), reads=[t1, t2], writes=[kkn])
                                    d = pas
                                    pw = psA.next()
                                    fw.op("pe", lambda e: e.matmul(pw[:, 0:n], lhsT=decw[:, d, hp * 128:(hp + 1) * 128], rhs=wdt[:, 0:n], start=True, stop=True),
                                          reads=[decw, wdt], writes=[pw])
                                    lw = tw.next()
                                    fw.op("act", lambda e: e.activation(out=lw[:, 0:n], in_=pw[:, 0:n], func=AF.Sigmoid, bias=vfm[:, hp, 15 + d:16 + d], scale=1.0),
                                          reads=[pw, vfm], writes=[lw])
                                    fw.op("dve", lambda e: e.tensor_scalar(out=lw[:, 0:n], in0=lw[:, 0:n], scalar1=-0.6065306597126334, scalar2=None, op0=ALU.mult),
                                          reads=[lw], writes=[lw])
                                    pa2 = psA.next()
                                    fw.op("pe", lambda e: e.matmul(pa2[:, 0:n], lhsT=iclw[:, d, hp * 128:(hp + 1) * 128], rhs=ads[:, 0:n], start=True, stop=True),
                                          reads=[iclw, ads], writes=[pa2])
                                    aa = tw.next()
                                    fw.op("act", lambda e: e.activation(out=aa[:, 0:n], in_=pa2[:, 0:n], func=AF.Sigmoid, bias=vfm[:, hp, 17 + d:18 + d], scale=1.0),
                                          reads=[pa2, vfm], writes=[aa])
                                    kd = tw.next()
                                    fw.op("dve", lambda e: e.tensor_scalar(out=kd[:, 0:n], in0=aa[:, 0:n], scalar1=vfm[:, hp, 20:21], scalar2=omk[:, hp:hp + 1],
                                                                           op0=ALU.mult, op1=ALU.add), reads=[aa, vfm, omk], writes=[kd])
                                    fw.op("dve", lambda e: e.tensor_tensor(out=kd[:, 0:n], in0=kd[:, 0:n], in1=kk_[:, 0:n], op=ALU.mult), reads=[kd, kk_], writes=[kd])
                                    if pas == 1 and not skip_fin:
                                        pa3 = psA.next()
                                        fw.op("pe", lambda e: e.matmul(pa3[:, 0:n], lhsT=iclw[:, 0, hp * 128:(hp + 1) * 128], rhs=ads[:, 0:n], start=True, stop=True),
                                              reads=[iclw, ads], writes=[pa3])
                                        af = tw.next()
                                        fw.op("act", lambda e: e.activation(out=af[:, 0:n], in_=pa3[:, 0:n], func=AF.Sigmoid, bias=vfm[:, hp, 17:18], scale=1.0),
                                              reads=[pa3, vfm], writes=[af])
                                        fw.op("dve", lambda e: e.tensor_tensor(out=af[:, 0:n], in0=af[:, 0:n], in1=aa[:, 0:n], op=ALU.add), reads=[af, aa], writes=[af])
                                        fw.op("dve", lambda e: e.tensor_scalar(out=af[:, 0:n], in0=af[:, 0:n], scalar1=vfm[:, hp, 20:21], scalar2=omk2[:, hp:hp + 1],
                                                                               op0=ALU.mult, op1=ALU.add), reads=[af, vfm, omk2], writes=[af])
                                        fw.op("dve", lambda e: e.tensor_tensor(out=af[:, 0:n], in0=af[:, 0:n], in1=kk_[:, 0:n], op=ALU.mult), reads=[af, kk_], writes=[af])
                                        fw.op("dve", lambda e: e.scalar_tensor_tensor(out=ks_s[hp % 4][:, 0:n], in0=af[:, 0:n], scalar=vfm[:, hp, 21:22], in1=rr[:, 0:n],
                                                                                      op0=ALU.mult, op1=ALU.mult), reads=[af, vfm, rr], writes=[ks_s[hp % 4]])
                                    Lc = tw.next()
                                    fw.op("dve", lambda e: e.tensor_tensor_scan(out=Lc[:, 0:n], data0=cmask[:, 0:n], data1=lw[:, 0:n], initial=0.0,
                                                                                op0=ALU.mult, op1=ALU.add), reads=[cmask, lw], writes=[Lc])
                                    Ginc, Gexc, Ginv = tw.next(), tw.next(), tw.next()
                                    st = strm[hp % 4]
                                    if pas == 0:
                                        fw.op("act", lambda e: e.activation(out=Ginc[:, 0:n], in_=Lc[:, 0:n], func=AF.Exp), reads=[Lc], writes=[Ginc])
                                        fw.op("act", lambda e: e.activation(out=Ginv[:, 0:n], in_=Lc[:, 0:n], func=AF.Exp, scale=-1.0), reads=[Lc], writes=[Ginv])
                                        fw.op("dve", lambda e: e.tensor_tensor(out=Gexc[:, 0:n], in0=Lc[:, 0:n], in1=lw[:, 0:n], op=ALU.subtract), reads=[Lc, lw], writes=[Gexc])
                                        fw.op("act", lambda e: e.activation(out=Gexc[:, 0:n], in_=Gexc[:, 0:n], func=AF.Exp), reads=[Gexc], writes=[Gexc])
                                        for c in range(ncn):
                                            fw.op("pool", lambda e, c=c: e.tensor_copy(out=gC[hp % 4][:, c:c + 1], in_=Ginc[:, c * CH + CH - 1:c * CH + CH]),
                                                  reads=[Ginc], writes=[gC[hp % 4]])
                                    else:
                                        for c in range(ncn):
                                            fw.op("dve", lambda e, c=c: e.tensor_scalar(out=Gexc[:, c * CH:(c + 1) * CH], in0=Lc[:, c * CH:(c + 1) * CH],
                                                                                        scalar1=Lc[:, c * CH + CH - 1:c * CH + CH], scalar2=None, op0=ALU.subtract),
                                                  reads=[Lc], writes=[Gexc])
                                            fw.op("act", lambda e, c=c: e.activation(out=gC[hp % 4][:, c:c + 1], in_=Lc[:, c * CH + CH - 1:c * CH + CH], func=AF.Exp),
                                                  reads=[Lc], writes=[gC[hp % 4]])
                                        fw.op("dve", lambda e: e.tensor_tensor(out=Ginc[:, 0:n], in0=lw[:, 0:n], in1=Gexc[:, 0:n], op=ALU.subtract), reads=[lw, Gexc], writes=[Ginc])
                                        fw.op("act", lambda e: e.activation(out=Ginv[:, 0:n], in_=Ginc[:, 0:n], func=AF.Exp, scale=-1.0), reads=[Ginc], writes=[Ginv])
                                        fw.op("act", lambda e: e.activation(out=Ginc[:, 0:n], in_=Ginc[:, 0:n], func=AF.Exp), reads=[Ginc], writes=[Ginc])
                                        fw.op("act", lambda e: e.activation(out=Gexc[:, 0:n], in_=Gexc[:, 0:n], func=AF.Exp, scale=-1.0), reads=[Gexc], writes=[Gexc])
                                    sv = lambda i4: st[:, 0:ncn, i4, :]
                                    v3 = lambda b_: b_[:, 0:n].rearrange("p (c t) -> p c t", t=CH)
                                    fw.op("dve", lambda e: e.scalar_tensor_tensor(out=sv(0), in0=v3(kkn), scalar=-1.0, in1=v3(Gexc), op0=ALU.mult, op1=ALU.mult),
                                          reads=[kkn, Gexc], writes=[st])
                                    fw.op("pool", lambda e: e.tensor_tensor(out=sv(1), in0=v3(rr), in1=v3(Ginc), op=ALU.mult), reads=[rr, Ginc], writes=[st])
                                    fw.op("dve", lambda e: e.tensor_tensor(out=aa[:, 0:n], in0=aa[:, 0:n], in1=kkn[:, 0:n], op=ALU.mult), reads=[aa, kkn], writes=[aa])
                                    fw.op("dve", lambda e: e.tensor_tensor(out=sv(2), in0=v3(aa), in1=v3(Ginv), op=ALU.mult), reads=[aa, Ginv], writes=[st])
                                    fw.op("pool", lambda e: e.tensor_tensor(out=sv(3), in0=v3(kd), in1=v3(Ginv), op=ALU.mult), reads=[kd, Ginv], writes=[st])

                                if 'no_scan' in dbg:
                                    continue
                                crange = list(range(ncn)) if pas == 0 else list(range(ncn))[::-1]
                                ui = 0
                                for c in crange:
                                    for hp in hps:
                                        par = ui % 2
                                        ui += 1
                                        st = strm[hp % 4]
                                        cs = slice(c * CH, (c + 1) * CH)
                                        if SL <= 0:
                                            continue
                                        pb = psB.next()
                                        fw.ops("pe", [lambda e: e.matmul(pb[:, 0:128], lhsT=vb_s[hp % 4][:, cs], rhs=identS[:], start=True, stop=True),
                                                      lambda e: e.matmul(pb[:, 128:256], lhsT=st[:, c, 2, :], rhs=identS[:], start=True, stop=True),
                                                      lambda e: e.matmul(pb[:, 256:384], lhsT=st[:, c, 3, :], rhs=identS[:], start=True, stop=True)],
                                               reads=[vb_s[hp % 4], st, identS], writes=[pb])
                                        fw.op("act", lambda e: e.activation(out=vT[par][:], in_=pb[:, 0:128], func=AF.Copy), reads=[pb], writes=[vT[par]])
                                        fw.op("act", lambda e: e.activation(out=bT[par][:], in_=pb[:, 128:256], func=AF.Copy), reads=[pb], writes=[bT[par]])
                                        fw.op("act", lambda e: e.activation(out=kT[par][:], in_=pb[:, 256:384], func=AF.Copy), reads=[pb], writes=[kT[par]])
                                        if SL <= 1:
                                            continue
                                        pX = psZ.next()
                                        for h in range(2):
                                            hs = slice(h * 64, (h + 1) * 64)
                                            pg = psA.next()
                                            fw.ops("pe", [lambda e: e.matmul(pg[:, 0:256], lhsT=st[hs, c, 2, :], rhs=st[hs, c, 0:2, :], start=True, stop=True),
                                                          lambda e: e.matmul(pg[:, 256:512], lhsT=st[hs, c, 3, :], rhs=st[hs, c, 0:2, :], start=True, stop=True)],
                                                   reads=[st], writes=[pg])
                                            pt = psA.next()
                                            fw.op("pe", lambda e: e.matmul(pt[:, 0:128], lhsT=st[hs, c, 0, :], rhs=st[hs, c, 2, :], start=True, stop=True),
                                                  reads=[st], writes=[pt])
                                            fw.op("dve", lambda e: e.tensor_tensor(out=Gb[par][h][:], in0=pg[:, 0:256], in1=msk[:], op=ALU.mult),
                                                  reads=[pg, msk], writes=[Gb[par][h]])
                                            fw.op("dve", lambda e: e.tensor_tensor(out=A0[par][h][:], in0=pg[:, 0:128], in1=msk[:, 0:128], op=ALU.mult),
                                                  reads=[pg, msk], writes=[A0[par][h]])
                                            fw.op("dve", lambda e: e.tensor_tensor(out=Gk[par][h][:], in0=pg[:, 256:512], in1=msk[:], op=ALU.mult),
                                                  reads=[pg, msk], writes=[Gk[par][h]])
                                            fw.op("dve", lambda e: e.tensor_tensor(out=PkT[par][h][0][:], in0=pt[:, 0:128], in1=mskT[:], op=ALU.mult),
                                                  reads=[pt, mskT], writes=[PkT[par][h][0]])
                                        if SL <= 2:
                                            continue
                                        fw.ops("pe", [lambda e: e.matmul(pX[:, 0:64], lhsT=st[:, c, 0, :], rhs=S2b[hp][:, 0:64], start=True, stop=False),
                                                      lambda e: e.matmul(pX[:, 0:64], lhsT=Gk[par][0][:, 0:128], rhs=vT[par][:, 0:64], start=False, stop=True),
                                                      lambda e: e.matmul(pX[:, 64:128], lhsT=st[:, c, 0, :], rhs=S2b[hp][:, 64:128], start=True, stop=False),
                                                      lambda e: e.matmul(pX[:, 64:128], lhsT=Gk[par][1][:, 0:128], rhs=vT[par][:, 64:128], start=False, stop=True)],
                                               reads=[st, S2b[hp], Gk[par][0], Gk[par][1], vT[par]], writes=[pX])
                                        for h in range(2):
                                            fw.op("act", lambda e, h=h: e.activation(out=Zb[par][h][0][:], in_=pX[:, h * 64:(h + 1) * 64], func=AF.Copy),
                                                  reads=[pX], writes=[Zb[par][h][0]])
                                        if SL <= 3:
                                            continue
                                        NLEV = 1 if 'no_inv' in dbg else 7
                                        for h in range(2):
                                            Pc = Gb[par][h]
                                            cur = None
                                            for lv in range(NLEV):
                                                A_ap = (A0[par][h][:] if lv == 0 else Pk[par][h][lv % 2][:])
                                                A_buf = (A0[par][h] if lv == 0 else Pk[par][h][lv % 2])
                                                AT_buf = PkT[par][h][lv % 2]
                                                zi, zo = Zb[par][h][lv % 2], Zb[par][h][(lv + 1) % 2]
                                                pz = psZ.next()
                                                fw.op("pe", lambda e: e.matmul(pz[:, 0:64], lhsT=A_ap, rhs=zi[:], start=True, stop=True), reads=[A_buf, zi], writes=[pz])
                                                if lv == NLEV - 1:
                                                    fw.op("dve", lambda e: e.tensor_tensor(out=Up[par][:, h * 64:(h + 1) * 64], in0=pz[:, 0:64], in1=zi[:], op=ALU.add),
                                                          reads=[pz, zi], writes=[Up[par]])
                                                else:
                                                    fw.op("dve", lambda e: e.tensor_tensor(out=zo[:], in0=pz[:, 0:64], in1=zi[:], op=ALU.add), reads=[pz, zi], writes=[zo])
                                                    nA, nAT = Pk[par][h][(lv + 1) % 2], PkT[par][h][(lv + 1) % 2]
                                                    pq = psA.next()
                                                    if lv < NLEV - 2:
                                                        fw.ops("pe", [lambda e: e.matmul(pq[:, 0:128], lhsT=AT_buf[:], rhs=A_ap, start=True, stop=True),
                                                                      lambda e: e.matmul(pq[:, 128:256], lhsT=A_ap, rhs=AT_buf[:], start=True, stop=True)],
                                                               reads=[A_buf, AT_buf], writes=[pq])
                                                        if lv % 2 == 0:
                                                            fw.op("act", lambda e: e.activation(out=nA[:], in_=pq[:, 0:128], func=AF.Copy), reads=[pq], writes=[nA])
                                                            fw.op("act", lambda e: e.activation(out=nAT[:], in_=pq[:, 128:256], func=AF.Copy), reads=[pq], writes=[nAT])
                                                        else:
                                                            fw.op("dve", lambda e: e.tensor_copy(out=nA[:], in_=pq[:, 0:128]), reads=[pq], writes=[nA])
                                                            fw.op("dve", lambda e: e.tensor_copy(out=nAT[:], in_=pq[:, 128:256]), reads=[pq], writes=[nAT])
                                                    else:
                                                        fw.op("pe", lambda e: e.matmul(pq[:, 0:128], lhsT=AT_buf[:], rhs=A_ap, start=True, stop=True),
                                                              reads=[A_buf, AT_buf], writes=[pq])
                                                        fw.op("act", lambda e: e.activation(out=nA[:], in_=pq[:, 0:128], func=AF.Copy), reads=[pq], writes=[nA])
                                        if SL <= 4:
                                            continue
                                        pY = psA.next()
                                        fw.ops("pe", [lambda e: e.matmul(pY[0:64, 0:128], lhsT=S2b[hp][:, 0:64], rhs=st[:, c, 1, :], start=True, stop=False),
                                                      lambda e: e.matmul(pY[0:64, 0:128], lhsT=Up[par][:, 0:64], rhs=Gb[par][0][:, 128:256], start=False, stop=False),
                                                      lambda e: e.matmul(pY[0:64, 0:128], lhsT=vT[par][:, 0:64], rhs=Gk[par][0][:, 128:256], start=False, stop=True),
                                                      lambda e: e.matmul(pY[64:128, 0:128], lhsT=S2b[hp][:, 64:128], rhs=st[:, c, 1, :], start=True, stop=False),
                                                      lambda e: e.matmul(pY[64:128, 0:128], lhsT=Up[par][:, 64:128], rhs=Gb[par][1][:, 128:256], start=False, stop=False),
                                                      lambda e: e.matmul(pY[64:128, 0:128], lhsT=vT[par][:, 64:128], rhs=Gk[par][1][:, 128:256], start=False, stop=True)],
                                               reads=[S2b[hp], st, Up[par], vT[par], Gb[par][0], Gb[par][1], Gk[par][0], Gk[par][1]], writes=[pY])
                                        fw.op("act", lambda e: e.activation(out=ybuf[hp % 4][:, cs], in_=pY[:, 0:128], func=AF.Copy), reads=[pY], writes=[ybuf[hp % 4]])
                                        if SL <= 5:
                                            continue
                                        pS = psA.next()
                                        fw.ops("pe", [lambda e: e.matmul(pS[:, 0:128], lhsT=bT[par][:], rhs=Up[par][:], start=True, stop=False),
                                                      lambda e: e.matmul(pS[:, 0:128], lhsT=kT[par][:], rhs=vT[par][:], start=False, stop=True)],
                                               reads=[bT[par], kT[par], Up[par], vT[par]], writes=[pS])
                                        for h in range(2):
                                            hs = slice(h * 64, (h + 1) * 64)
                                            s_t = stmp.next()
                                            fw.op("dve", lambda e: e.tensor_tensor(out=s_t[hs, :], in0=pS[hs, h * 64:(h + 1) * 64], in1=S2[hp][hs, hs], op=ALU.add),
                                                  reads=[pS, S2[hp]], writes=[s_t])
                                            fw.op("dve", lambda e: e.tensor_scalar(out=S2[hp][hs, hs], in0=s_t[hs, :], scalar1=gC[hp % 4][hs, c:c + 1], scalar2=None, op0=ALU.mult),
                                                  reads=[s_t, gC[hp % 4]], writes=[S2[hp]])
                                            fw.op("act", lambda e: e.activation(out=S2b[hp][hs, hs], in_=s_t[hs, :], func=AF.Copy, scale=gC[hp % 4][hs, c:c + 1]),
                                                  reads=[s_t, gC[hp % 4]], writes=[S2b[hp]])

                                for hp in hps:
                                    rows = slice(hp * 128, (hp + 1) * 128)
                                    if pas == 0:
                                        fw.store("sp", ybuf[hp % 4], YF[rows, col0:col0 + n], ybuf[hp % 4][:, 0:n])
                                        continue
                                    if skip_fin or 'no_fin' in dbg:
                                        continue
                                    yf = yio.next()
                                    fw.load("sp", yf, yf[:, 0:n], YF[rows, col0:col0 + n])
                                    y = yf
                                    fw.op("dve", lambda e: e.tensor_tensor(out=y[:, 0:n], in0=yf[:, 0:n], in1=ybuf[hp % 4][:, 0:n], op=ALU.add), reads=[yf, ybuf[hp % 4]], writes=[y])
                                    pm_ = psA.next()
                                    fw.op("pe", lambda e: e.matmul(pm_[:, 0:n], lhsT=blk[:], rhs=y[:, 0:n], start=True, stop=True), reads=[blk, y], writes=[pm_])
                                    dd = tw.next()
                                    fw.op("dve", lambda e: e.scalar_tensor_tensor(out=dd[:, 0:n], in0=pm_[:, 0:n], scalar=-1.0 / 64, in1=y[:, 0:n], op0=ALU.mult, op1=ALU.add),
                                          reads=[pm_, y], writes=[dd])
                                    d2 = tw.next()
                                    fw.op("act", lambda e: e.activation(out=d2[:, 0:n], in_=dd[:, 0:n], func=AF.Square), reads=[dd], writes=[d2])
                                    pv_ = psA.next()
                                    fw.op("pe", lambda e: e.matmul(pv_[:, 0:n], lhsT=blk[:], rhs=d2[:, 0:n], start=True, stop=True), reads=[blk, d2], writes=[pv_])
                                    fw.op("act", lambda e: e.activation(out=d2[:, 0:n], in_=pv_[:, 0:n], func=AF.Sqrt, bias=GN_EPS, scale=1.0 / 64), reads=[pv_], writes=[d2])
                                    fw.op("dve", lambda e: e.reciprocal(out=d2[:, 0:n], in_=d2[:, 0:n]), reads=[d2], writes=[d2])
                                    fw.op("dve", lambda e: e.tensor_tensor(out=dd[:, 0:n], in0=dd[:, 0:n], in1=d2[:, 0:n], op=ALU.mult), reads=[dd, d2], writes=[dd])
                                    fw.op("dve", lambda e: e.tensor_scalar(out=dd[:, 0:n], in0=dd[:, 0:n], scalar1=vfm[:, hp, 22:23], scalar2=vfm[:, hp, 23:24],
                                                                           op0=ALU.mult, op1=ALU.add), reads=[dd, vfm], writes=[dd])
                                    pb_ = psA.next()
                                    fw.op("pe", lambda e: e.matmul(pb_[:, 0:n], lhsT=blk[:], rhs=ks_s[hp % 4][:, 0:n], start=True, stop=True), reads=[blk, ks_s[hp % 4]], writes=[pb_])
                                    fw.op("dve", lambda e: e.tensor_tensor(out=d2[:, 0:n], in0=pb_[:, 0:n], in1=v_s[hp % 4][:, 0:n], op=ALU.mult), reads=[pb_, v_s[hp % 4]], writes=[d2])
                                    fw.op("dve", lambda e: e.tensor_tensor(out=dd[:, 0:n], in0=dd[:, 0:n], in1=d2[:, 0:n], op=ALU.add), reads=[dd, d2], writes=[dd])
                                    pg_ = psA.next()
                                    fw.ops("pe", [lambda e, c2=c2: e.matmul(pg_[:, 0:n], lhsT=gupw[:, c2, hp * 128:(hp + 1) * 128], rhs=gds[:, c2, 0:n],
                                                                            start=(c2 == 0), stop=(c2 == 1)) for c2 in range(2)],
                                           reads=[gupw, gds], writes=[pg_])
                                    yo = yob.next()
                                    fw.op("dve", lambda e: e.tensor_tensor(out=yo[:, 0:n], in0=pg_[:, 0:n], in1=dd[:, 0:n], op=ALU.mult), reads=[pg_, dd], writes=[yo])
                                    fw.store("sp", yo, YT[rows, col0:col0 + n], yo[:, 0:n])
                        fw.barrier()
                        if pas == 0:
                            for hp in range(HP):
                                fw.op("pool", lambda e, hp=hp: e.memset(S2[hp][:], 0.0), writes=[S2[hp]])
                                fw.op("pool", lambda e, hp=hp: e.memset(S2b[hp][:], 0.0), writes=[S2b[hp]])
                fw.barrier()
                if "stop_p2" in dbg:
                    break

                with ExitStack() as es3:
                    cin = Ring([[fw.sb([128, NT], F32, es=es3) for _ in range(3)] for _ in range(2)])
                    cu = Ring([fw.sb([128, NT], F32, es=es3) for _ in range(2)])
                    co = Ring([fw.sb([128, NT], F32, es=es3) for _ in range(2)])
                    cob = Ring([fw.sb([128, NT], BF16, es=es3) for _ in range(2)])
                    for (col0, n, is_ctx, first, last) in tiles:
                        if is_ctx and last_layer:
                            continue
                        for q in range(4):
                            ib = cin.next()
                            for i3, base in enumerate((PCG, PCX, PCB)):
                                fw.load("sp", ib[i3], ib[i3][:, 0:n], PXT[base + q * 128:base + (q + 1) * 128, col0:col0 + n])
                            u = cu.next()
                            o = co.next()
                            fw.op("pool", lambda e: e.tensor_tensor(out=u[:, 0:n], in0=ib[0][:, 0:n], in1=ib[1][:, 0:n], op=ALU.mult), reads=[ib[0], ib[1]], writes=[u])
                            fw.op("dve", lambda e: e.tensor_scalar(out=o[:, 0:n], in0=u[:, 0:n], scalar1=cw_fm[:, q, 1:2], scalar2=None, op0=ALU.mult),
                                  reads=[u, cw_fm], writes=[o])
                            if is_ctx:
                                assert first and last
                                W_ = n
                            else:
                                W_ = 64
                            u3 = u[:, 0:n].rearrange("p (r w) -> p r w", w=W_)
                            o3 = o[:, 0:n].rearrange("p (r w) -> p r w", w=W_)
                            fw.op("dve", lambda e: e.scalar_tensor_tensor(out=o3[:, :, 1:W_], in0=u3[:, :, 0:W_ - 1], scalar=cw_fm[:, q, 0:1], in1=o3[:, :, 1:W_],
                                                                          op0=ALU.mult, op1=ALU.add), reads=[u, cw_fm, o], writes=[o])
                            fw.op("dve", lambda e: e.scalar_tensor_tensor(out=o3[:, :, 0:W_ - 1], in0=u3[:, :, 1:W_], scalar=cw_fm[:, q, 2:3], in1=o3[:, :, 0:W_ - 1],
                                                                          op0=ALU.mult, op1=ALU.add), reads=[u, cw_fm, o], writes=[o])
                            ob = cob.next()
                            fw.op("pool", lambda e: e.tensor_tensor(out=ob[:, 0:n], in0=o[:, 0:n], in1=ib[2][:, 0:n], op=ALU.mult), reads=[o, ib[2]], writes=[ob])
                            fw.store("sp", ob, YT[1024 + q * 128:1024 + (q + 1) * 128, col0:col0 + n], ob[:, 0:n])
                fw.barrier()

                with ExitStack() as es4:
                    LCmax = T // 128
                    UC = fw.sb([128, LCmax, 512], BF16, es=es4)
                    US = fw.sb([128, LCmax, 512], BF16, es=es4)
                    uT = Ring([fw.sb([128, NT], BF16, es=es4) for _ in range(3)])
                    pu = Ring([fw.ps([128, 512], F32, es=es4) for _ in range(2)])
                    py = [fw.ps([128, 512], F32, es=es4) for _ in range(4)]
                    LG = 8
                    tabr = Ring([[fw.sb([128, LG, 512], BF16, es=es4) for _ in range(2)] for _ in range(2)])
                    fo = Ring([fw.sb([128, 512], BF16, es=es4) for _ in range(3)])
                    for (seq0, sn, tab, is_ctx) in ((0, CTX, ftab_c, True), (CTX, T, ftab_x, False)):
                        if is_ctx and last_layer:
                            continue
                        nlc = sn // 128
                        for t0 in range(0, sn, NT):
                            tn = min(NT, sn - t0)
                            for mc in range(4):
                                ub = uT.next()
                                fw.load("pool", ub, ub[:, 0:tn], PXT[PFT + mc * 128:PFT + (mc + 1) * 128, seq0 + t0:seq0 + t0 + tn])
                                for lc in range(tn // 128):
                                    p_ = pu.next()
                                    fw.op("pe", lambda e: e.matmul(p_[:, 0:256], lhsT=ub[:, lc * 128:(lc + 1) * 128], rhs=c64[:], start=True, stop=True),
                                          reads=[ub, c64], writes=[p_])
                                    glc = t0 // 128 + lc
                                    if (lc + mc) % 2 == 0:
                                        fw.op("dve", lambda e: e.tensor_copy(out=UC[:, glc, mc * 128:(mc + 1) * 128], in_=p_[:, 0:128]), reads=[p_], writes=[UC])
                                        fw.op("dve", lambda e: e.tensor_copy(out=US[:, glc, mc * 128:(mc + 1) * 128], in_=p_[:, 128:256]), reads=[p_], writes=[US])
                                    else:
                                        fw.op("act", lambda e: e.activation(out=UC[:, glc, mc * 128:(mc + 1) * 128], in_=p_[:, 0:128], func=AF.Copy), reads=[p_], writes=[UC])
                                        fw.op("act", lambda e: e.activation(out=US[:, glc, mc * 128:(mc + 1) * 128], in_=p_[:, 128:256], func=AF.Copy), reads=[p_], writes=[US])
                        for k0 in range(0, sn, 512):
                            kn = min(512, sn - k0)
                            for lg0 in range(0, nlc, LG):
                                lgn = min(LG, nlc - lg0)
                                tb = tabr.next()
                                for i2 in range(2):
                                    fw.load("sp" if i2 == 0 else "act", tb[i2], tb[i2][:, 0:lgn, 0:kn],
                                            tab[i2, lg0 * 128:(lg0 + lgn) * 128, k0:k0 + kn].rearrange("(c p) k -> p c k", p=128))
                                for mc in range(4):
                                    fns = []
                                    for l_ in range(lgn):
                                        lc = lg0 + l_
                                        fns.append(lambda e, l_=l_, lc=lc: e.matmul(py[mc][:, 0:kn], lhsT=UC[:, lc, mc * 128:(mc + 1) * 128], rhs=tb[0][:, l_, 0:kn],
                                                                                    start=(lc == 0), stop=False))
                                        fns.append(lambda e, l_=l_, lc=lc: e.matmul(py[mc][:, 0:kn], lhsT=US[:, lc, mc * 128:(mc + 1) * 128], rhs=tb[1][:, l_, 0:kn],
                                                                                    start=False, stop=(lc == nlc - 1)))
                                    fw.ops("pe", fns, reads=[UC, US, tb[0], tb[1]], writes=[py[mc]])
                            for mc in range(4):
                                ob = fo.next()
                                fw.op("act" if mc % 2 else "dve",
                                      (lambda e: e.activation(out=ob[:, 0:kn], in_=py[mc][:, 0:kn], func=AF.Copy)) if mc % 2 else
                                      (lambda e: e.tensor_copy(out=ob[:, 0:kn], in_=py[mc][:, 0:kn])), reads=[py[mc]], writes=[ob])
                                fw.store("sp", ob, YT[1536 + mc * 128:1536 + (mc + 1) * 128, seq0 + k0:seq0 + k0 + kn], ob[:, 0:kn])
                fw.barrier()
                if "stop_p4" in dbg:
                    break

                with ExitStack() as es5:
                    xb_ = fw.sb([128, 16, NT], F32, es=es5)
                    yx = fw.sb([128, 16, NT], BF16, es=es5)
                    H = fw.sb([128, FC, NT], BF16, es=es5)
                    wring = Ring([fw.sb([128, 16, 256], BF16, es=es5) for _ in range(4)])
                    wdr = Ring([fw.sb([128, FC, 128], BF16, es=es5) for _ in range(2)])
                    sq = H
                    rstd = fw.sb([128, NT], F32, es=es5)
                    tmp = Ring([fw.sb([128, NT], F32, es=es5) for _ in range(2)])
                    sg = Ring([fw.sb([128, NT], F32, es=es5) for _ in range(2)])
                    pss = fw.ps([128, NT], F32, es=es5)
                    pA = Ring([fw.ps([128, NT], F32, es=es5) for _ in range(3)])
                    pG = Ring([fw.ps([128, NT], F32, es=es5) for _ in range(2)])
                    pU = Ring([fw.ps([128, NT], F32, es=es5) for _ in range(2)])
                    YTv = YT.rearrange("(c p) t -> p c t", p=128)
                    for (col0, n, is_ctx, first, last) in tiles:
                        if is_ctx and last_layer:
                            continue
                        r = 1 if is_ctx else 0
                        fw.load("sp", xb_, xb_[:, :, 0:n], XTv[:, :, col0:col0 + n])
                        fw.load("act", yx, yx[:, :, 0:n], YTv[:, :, col0:col0 + n])
                        for g in range(8):
                            wb = wring.next()
                            fw.load("sp" if g % 2 else "act", wb, wb[:], WOUT_b[g])
                            for j in range(2):
                                dc = g * 2 + j
                                po = pA.next()
                                fw.ops("pe", [lambda e, kc=kc: e.matmul(po[:, 0:n], lhsT=wb[:, kc, j * 128:(j + 1) * 128], rhs=yx[:, kc, 0:n],
                                                                        start=(kc == 0), stop=(kc == 15)) for kc in range(16)], reads=[wb, yx], writes=[po])
                                fw.op("dve", lambda e: e.scalar_tensor_tensor(out=xb_[:, dc, 0:n], in0=po[:, 0:n], scalar=mods[:, 32 + dc, r:r + 1], in1=xb_[:, dc, 0:n],
                                                                              op0=ALU.mult, op1=ALU.add), reads=[po, mods, xb_], writes=[xb_])
                        fw.op("act", lambda e: e.activation(out=sq[:, 0:16, 0:n], in_=xb_[:, :, 0:n], func=AF.Square), reads=[xb_], writes=[sq])
                        fw.ops("pe", [lambda e, dc=dc: e.matmul(pss[:, 0:n], lhsT=onesb[:], rhs=sq[:, dc, 0:n], start=(dc == 0), stop=(dc == 15))
                                      for dc in range(16)], reads=[sq, onesb], writes=[pss])
                        fw.op("act", lambda e: e.activation(out=rstd[:, 0:n], in_=pss[:, 0:n], func=AF.Sqrt, bias=RMS_EPS, scale=1.0 / D), reads=[pss], writes=[rstd])
                        fw.op("dve", lambda e: e.reciprocal(out=rstd[:, 0:n], in_=rstd[:, 0:n]), reads=[rstd], writes=[rstd])
                        for dc in range(16):
                            tb = tmp.next()
                            fw.op("dve", lambda e, dc=dc: e.scalar_tensor_tensor(out=tb[:, 0:n], in0=xb_[:, dc, 0:n], scalar=A2[:, dc, r:r + 1], in1=rstd[:, 0:n],
                                                                                 op0=ALU.mult, op1=ALU.mult), reads=[xb_, A2, rstd], writes=[tb])
                            fw.op("act", lambda e, dc=dc: e.activation(out=yx[:, dc, 0:n], in_=tb[:, 0:n], func=AF.Identity, bias=mods[:, 48 + dc, r:r + 1], scale=1.0),
                                  reads=[tb, mods], writes=[yx])
                        for g in range(22):
                            wg_ = wring.next()
                            wu_ = wring.next()
                            fw.load("sp", wg_, wg_[:], WG_b[g])
                            fw.load("act", wu_, wu_[:], WU_b[g])
                            for j in range(2):
                                fc = g * 2 + j
                                pg = pG.next()
                                pu_ = pU.next()
                                fw.ops("pe", [lambda e, kc=kc: e.matmul(pg[:, 0:n], lhsT=wg_[:, kc, j * 128:(j + 1) * 128], rhs=yx[:, kc, 0:n],
                                                                        start=(kc == 0), stop=(kc == 15)) for kc in range(16)], reads=[wg_, yx], writes=[pg])
                                fw.ops("pe", [lambda e, kc=kc: e.matmul(pu_[:, 0:n], lhsT=wu_[:, kc, j * 128:(j + 1) * 128], rhs=yx[:, kc, 0:n],
                                                                        start=(kc == 0), stop=(kc == 15)) for kc in range(16)], reads=[wu_, yx], writes=[pu_])
                                s_ = sg.next()
                                fw.op("act", lambda e: e.activation(out=s_[:, 0:n], in_=pg[:, 0:n], func=AF.Silu), reads=[pg], writes=[s_])
                                fw.op("dve", lambda e: e.tensor_tensor(out=H[:, fc, 0:n], in0=pu_[:, 0:n], in1=s_[:, 0:n], op=ALU.mult), reads=[pu_, s_], writes=[H])
                        for dc in range(16):
                            wd_ = wdr.next()
                            fw.load("sp" if dc % 2 else "act", wd_, wd_[:], WD_b[dc])
                            if True:
                                po = pA.next()
                                fw.ops("pe", [lambda e, fc=fc: e.matmul(po[:, 0:n], lhsT=wd_[:, fc, :], rhs=H[:, fc, 0:n],
                                                                        start=(fc == 0), stop=(fc == FC - 1)) for fc in range(FC)], reads=[wd_, H], writes=[po])
                                fw.op("dve", lambda e: e.scalar_tensor_tensor(out=xb_[:, dc, 0:n], in0=po[:, 0:n], scalar=mods[:, 80 + dc, r:r + 1], in1=xb_[:, dc, 0:n],
                                                                              op0=ALU.mult, op1=ALU.add), reads=[po, mods, xb_], writes=[xb_])
                        fw.store("pool", xb_, XTv[:, :, col0:col0 + n], xb_[:, :, 0:n])
                fw.barrier()

        with ExitStack() as es7:
            xb_ = fw.sb([128, 16, NT], F32, es=es7)
            sq = fw.sb([128, 16, NT], BF16, es=es7)
            rstd = fw.sb([128, NT], F32, es=es7)
            xo = fw.sb([128, 16, NT], F32, es=es7)
            pss = fw.ps([128, NT], F32, es=es7)
            ptr = Ring([fw.ps([128, 512], F32, es=es7) for _ in range(4)])
            orow = Ring([fw.sb([128, D], F32, es=es7) for _ in range(2)])
            for (col0, n, is_ctx, first, last) in tiles:
                if is_ctx:
                    continue
                fw.load("sp", xb_, xb_[:, :, 0:n], XTv[:, :, col0:col0 + n])
                fw.op("act", lambda e: e.activation(out=sq[:, :, 0:n], in_=xb_[:, :, 0:n], func=AF.Square), reads=[xb_], writes=[sq])
                fw.ops("pe", [lambda e, dc=dc: e.matmul(pss[:, 0:n], lhsT=onesb[:], rhs=sq[:, dc, 0:n], start=(dc == 0), stop=(dc == 15))
                              for dc in range(16)], reads=[sq, onesb], writes=[pss])
                fw.op("act", lambda e: e.activation(out=rstd[:, 0:n], in_=pss[:, 0:n], func=AF.Sqrt, bias=RMS_EPS, scale=1.0 / D), reads=[pss], writes=[rstd])
                fw.op("dve", lambda e: e.reciprocal(out=rstd[:, 0:n], in_=rstd[:, 0:n]), reads=[rstd], writes=[rstd])
                for dc in range(16):
                    fw.op("dve", lambda e, dc=dc: e.scalar_tensor_tensor(out=xo[:, dc, 0:n], in0=xb_[:, dc, 0:n], scalar=nf_fm[:, dc:dc + 1], in1=rstd[:, 0:n],
                                                                         op0=ALU.mult, op1=ALU.mult), reads=[xb_, nf_fm, rstd], writes=[xo])
                for tb in range(n // 128):
                    ob = orow.next()
                    for g4 in range(4):
                        pt = ptr.next()
                        fw.ops("pe", [lambda e, j=j: e.transpose(pt[:, j * 128:(j + 1) * 128], xo[:, g4 * 4 + j, tb * 128:(tb + 1) * 128], ident[:])
                                      for j in range(4)], reads=[xo, ident], writes=[pt])
                        fw.op("dve" if g4 % 2 == 0 else "act",
                              (lambda e: e.tensor_copy(out=ob[:, g4 * 512:(g4 + 1) * 512], in_=pt[:])) if g4 % 2 == 0 else
                              (lambda e: e.activation(out=ob[:, g4 * 512:(g4 + 1) * 512], in_=pt[:], func=AF.Copy)), reads=[pt], writes=[ob])
                    t0 = col0 - CTX + tb * 128
                    fw.store("sp", ob, out_ap[t0:t0 + 128, :], ob[:])
        fw.barrier()
    return nc, fw


def fourier_tables(n):
    idx = np.arange(n, dtype=np.int64)
    ang = (2.0 * np.pi / n) * ((idx[:, None] * idx[None, :]) % n).astype(np.float64)
    sc = 1.0 / np.sqrt(64.0 * n)
    tab = np.stack([np.cos(ang) * sc, -np.sin(ang) * sc], 0)
    return tab.astype(np.float32).astype(ml_dtypes.bfloat16)


def c64_table():
    idx = np.arange(64)
    ang = 2.0 * np.pi * ((idx[:, None] * idx[None, :]) % 64) / 64.0
    t = np.zeros((128, 256), np.float32)
    for h in range(2):
        t[h * 64:(h + 1) * 64, h * 64:(h + 1) * 64] = np.cos(ang)
        t[h * 64:(h + 1) * 64, 128 + h * 64:128 + (h + 1) * 64] = np.sin(ang)
    return t.astype(ml_dtypes.bfloat16)


_CACHE = {}


def make_inputs(b, x, c, ctx, c_ctx, W, T, CTX):
    m = {"x": np.ascontiguousarray(x[b]), "ctx": np.ascontiguousarray(ctx[b]),
         "cvec": np.ascontiguousarray(np.stack([c[b], c_ctx], 0))}
    m.update(W)
    return m


def kernel(x, c, ctx, c_ctx, w_mod, b_mod, norm_mix, w_in, rw_shift, dec_w0, dec_up, iclr_a0, iclr_up, k_k, k_a, r_k,
           ln_w, ln_b, g_up, conv_w, w_out, norm_ffn, w_gate, w_up, w_down, norm_final, _dbg=()):
    x = np.asarray(x)
    B, T, _ = x.shape
    CTX = ctx.shape[1]
    DEPTH = w_mod.shape[0]
    f = lambda a: np.ascontiguousarray(np.asarray(a, dtype=np.float32))
    W = dict(w_mod=f(w_mod), b_mod=f(b_mod), norm_mix=f(norm_mix), w_in=f(w_in), rw_shift=f(rw_shift), dec_w0=f(dec_w0),
             dec_up=f(dec_up), iclr_a0=f(iclr_a0), iclr_up=f(iclr_up), k_k=f(k_k), k_a=f(k_a),
             r_k=f(r_k).reshape(DEPTH, D_RWKV), ln_w=f(ln_w), ln_b=f(ln_b), g_up=f(g_up), conv_w=f(conv_w), w_out=f(w_out),
             norm_ffn=f(norm_ffn), w_gate=f(w_gate), w_up=f(w_up), w_down=f(w_down), norm_final=f(norm_final).reshape(1, D),
             ftab_x=fourier_tables(T), ftab_c=fourier_tables(CTX), c64=c64_table())
    key = (T, CTX, DEPTH, tuple(_dbg))
    nc, fw = build(T, CTX, DEPTH, _dbg)
    in_maps = [make_inputs(b, x, np.asarray(c), np.asarray(ctx), np.asarray(c_ctx), W, T, CTX) for b in range(B)]
    res = run_bass_kernel_spmd(nc, in_maps, core_ids=list(range(B)))
    if _dbg:
        return res
    return np.stack([res.results[b]["out"] for b in range(B)], 0).astype(np.float32)
```

```python
import numpy as np
import ml_dtypes
from contextlib import ExitStack
import concourse.bass as bass
import concourse.mybir as mybir
from concourse.bass_utils import run_bass_kernel_spmd

F32 = mybir.dt.float32
BF16 = mybir.dt.bfloat16
AF = mybir.ActivationFunctionType
ALU = mybir.AluOpType

D = 2048
DC = 16
HD = 64
D_RWKV = 1024
D_FF = 5632
FC = 44
R0, K0, V0, WD0, AD0, GD0, RW_COLS = 0, 1024, 2048, 3072, 3168, 3264, 3520
D_IN = 5568
PCG, PCX, PCB, PFT, PEND = 3584, 4096, 4608, 5120, 5632
RMS_EPS = 1e-6
GN_EPS = 64e-5
CH = 128
NT = 512


class Tr:
    __slots__ = ("w", "r", "sem", "dcnt", "const")

    def __init__(self, const=False):
        self.w = None
        self.r = {}
        self.sem = None
        self.dcnt = 0
        self.const = const


class Buf:
    def __init__(self, tile):
        self.t = tile
        self.tr = Tr()

    def __getitem__(self, k):
        return self.t[k]


class FW:
    ENG = ("pe", "act", "dve", "pool", "sp")

    def __init__(self, nc):
        self.nc = nc
        self.es = ExitStack()
        self.eng = {"pe": nc.tensor, "act": nc.scalar, "dve": nc.vector, "pool": nc.gpsimd, "sp": nc.sync}
        self.sem = {e: self.es.enter_context(nc.semaphore("sem_" + e)) for e in self.ENG}
        self.cnt = {e: 0 for e in self.ENG}
        self.seen = {e: {} for e in self.ENG}
        self.nsem = 0
        self.n_inst = 0
        self.dma_trs = []
        self.sem_pool = [[], []]
        self.uid = 0

    def sb(self, shape, dt, es=None, name=None):
        self.uid += 1
        return Buf((es or self.es).enter_context(self.nc.sbuf_tensor("%s_%d" % (name or "sb", self.uid), shape, dt)))

    def ps(self, shape, dt=F32, es=None, name=None):
        self.uid += 1
        return Buf((es or self.es).enter_context(self.nc.psum_tensor("%s_%d" % (name or "ps", self.uid), shape, dt)))

    def dram(self, name, shape, dt, kind="Internal"):
        return self.nc.dram_tensor(name, shape, dt, kind=kind).ap()

    def _wait(self, e, tok):
        if tok is None:
            return
        kind, key, val = tok
        if kind == "e":
            if self.seen[e].get(key, 0) >= val:
                return
            self.eng[e].wait_ge(self.sem[key], val)
            self.seen[e][key] = val
        else:
            k = id(key)
            if self.seen[e].get(k, 0) >= val:
                return
            self.eng[e].wait_ge(key, val)
            self.seen[e][k] = val

    def _deps(self, e, reads, writes):
        for t in reads:
            self._wait(e, t.w)
        for t in writes:
            self._wait(e, t.w)
            for tok in t.r.values():
                self._wait(e, tok)

    def _commit(self, tok, reads, writes):
        key = tok[1] if tok[0] == "e" else id(tok[1])
        for t in reads:
            if not t.const:
                t.r[key] = tok
        for t in writes:
            t.w = tok
            t.r = {}

    @staticmethod
    def _trs(bufs):
        return [b.tr if isinstance(b, Buf) else b for b in bufs]

    def op(self, e, fn, reads=(), writes=()):
        reads = self._trs(reads)
        writes = self._trs(writes)
        self._deps(e, reads, writes)
        ins = fn(self.eng[e])
        self.cnt[e] += 1
        ins.then_inc(self.sem[e], 1)
        self._commit(("e", e, self.cnt[e]), reads, writes)
        self.n_inst += 1

    def ops(self, e, fns, reads=(), writes=()):
        reads = self._trs(reads)
        writes = self._trs(writes)
        self._deps(e, reads, writes)
        ins = None
        for fn in fns:
            ins = fn(self.eng[e])
            self.n_inst += 1
        self.cnt[e] += 1
        ins.then_inc(self.sem[e], 1)
        self._commit(("e", e, self.cnt[e]), reads, writes)

    def dma(self, q, out, in_, own, reads=(), writes=()):
        own = own.tr if isinstance(own, Buf) else own
        reads = self._trs(reads)
        writes = self._trs(writes)
        sw = 1 if q == "pool" else 0
        if own.sem is None:
            own.sem = [None, None]
            own.dcnt = [0, 0]
            self.dma_trs.append(own)
        if own.sem[sw] is None:
            if self.sem_pool[sw]:
                own.sem[sw], own.dcnt[sw] = self.sem_pool[sw].pop()
            else:
                self.nsem += 1
                own.sem[sw] = self.es.enter_context(self.nc.semaphore("dsem%d" % self.nsem))
                own.dcnt[sw] = 0
        for i in (0, 1):
            if own.sem[i] is not None and own.dcnt[i] > 0:
                self._wait(q, ("d", own.sem[i], own.dcnt[i]))
        self._deps(q, reads, writes)
        ins = self.eng[q].dma_start(out=out, in_=in_)
        own.dcnt[sw] += 16
        ins.then_inc(own.sem[sw], 16)
        self._commit(("d", own.sem[sw], own.dcnt[sw]), reads, writes)
        self.n_inst += 1

    def load(self, q, buf, dst, src):
        self.dma(q, dst, src, buf, writes=[buf])

    def store(self, q, buf, dst, src):
        self.dma(q, dst, src, buf, reads=[buf])

    def barrier(self):
        for e in self.ENG:
            for f in self.ENG:
                if f != e and self.cnt[f] > 0:
                    self._wait(e, ("e", f, self.cnt[f]))
            for t in self.dma_trs:
                for i in (0, 1):
                    if t.sem[i] is not None and t.dcnt[i] > 0:
                        self._wait(e, ("d", t.sem[i], t.dcnt[i]))
        for t in self.dma_trs:
            for i in (0, 1):
                if t.sem[i] is not None:
                    self.sem_pool[i].append((t.sem[i], t.dcnt[i]))
            t.sem = None
            t.dcnt = 0
        self.dma_trs = []


class Ring:
    def __init__(self, bufs):
        self.bufs = bufs
        self.i = 0

    def next(self):
        b = self.bufs[self.i % len(self.bufs)]
        self.i += 1
        return b


def build(T, CTX, DEPTH, dbg=()):
    nc = bass.Bass("TRN2", target_bir_lowering=False)
    fw = FW(nc)
    TT = CTX + T
    L = DEPTH

    def din(name, shape, dt=F32):
        return nc.dram_tensor(name, shape, dt, kind="ExternalInput").ap()

    x_in = din("x", [T, D])
    ctx_in = din("ctx", [CTX, D])
    cvec_in = din("cvec", [2, D])
    w_mod = din("w_mod", [L, D, 6 * D])
    b_mod = din("b_mod", [L, 6 * D])
    norm_mix = din("norm_mix", [L, D])
    w_in = din("w_in", [L, D, D_IN])
    rw_shift = din("rw_shift", [L, 3, RW_COLS])
    dec_w0 = din("dec_w0", [L, 2, D_RWKV])
    dec_up = din("dec_up", [L, 2, 96, D_RWKV])
    iclr_a0 = din("iclr_a0", [L, 2, D_RWKV])
    iclr_up = din("iclr_up", [L, 2, 96, D_RWKV])
    k_k = din("k_k", [L, D_RWKV])
    k_a = din("k_a", [L, D_RWKV])
    r_k = din("r_k", [L, D_RWKV])
    ln_w = din("ln_w", [L, D_RWKV])
    ln_b = din("ln_b", [L, D_RWKV])
    g_up = din("g_up", [L, 256, D_RWKV])
    conv_w = din("conv_w", [L, 3, 512])
    w_out = din("w_out", [L, D, D])
    norm_ffn = din("norm_ffn", [L, D])
    w_gate = din("w_gate", [L, D, D_FF])
    w_up = din("w_up", [L, D, D_FF])
    w_down = din("w_down", [L, D_FF, D])
    norm_final = din("norm_final", [1, D])
    ftab_x = din("ftab_x", [2, T, T], BF16)
    ftab_c = din("ftab_c", [2, CTX, CTX], BF16)
    c64_in = din("c64", [128, 256], BF16)
    out_ap = nc.dram_tensor("out", [T, D], F32, kind="ExternalOutput").ap()

    def scratch(name, shape, dt):
        kind = "ExternalOutput" if name in dbg else "Internal"
        return nc.dram_tensor(name, shape, dt, kind=kind).ap()

    XT = scratch("XT", [D, TT], F32)
    PXT = scratch("PXT", [PEND, TT], F32)
    YF = scratch("YF", [D_RWKV, TT], F32)
    YT = scratch("YT", [D, TT], BF16)
    MODS = scratch("MODS", [128, 96 * 2], F32)
    WIN_b = scratch("WIN_b", [11, 128, 16, 512], BF16)
    WOUT_b = scratch("WOUT_b", [8, 128, 16, 256], BF16)
    WG_b = scratch("WG_b", [22, 128, 16, 256], BF16)
    WU_b = scratch("WU_b", [22, 128, 16, 256], BF16)
    WD_b = scratch("WD_b", [16, 128, FC, 128], BF16)

    XTv = XT.rearrange("(c p) t -> p c t", p=128)

    tiles = []
    for c0 in range(0, CTX, NT):
        n = min(NT, CTX - c0)
        tiles.append((c0, n, True, c0 == 0, c0 + n == CTX))
    for c0 in range(0, T, NT):
        n = min(NT, T - c0)
        tiles.append((CTX + c0, n, False, c0 == 0, c0 + n == T))

    with fw.es:
        ident = fw.sb([128, 128], F32, name="ident")
        identb = fw.sb([128, 128], BF16, name="identb")
        onesb = fw.sb([128, 128], BF16, name="onesb")
        blk = fw.sb([128, 128], F32, name="blk")
        iot = fw.sb([128, 128], F32, name="iot")
        m_f = fw.sb([128, 256], F32, name="m_f")
        m_b = fw.sb([128, 256], F32, name="m_b")
        mt_f = fw.sb([128, 128], F32, name="mt_f")
        mt_b = fw.sb([128, 128], F32, name="mt_b")
        cmask = fw.sb([128, NT], F32, name="cmask")
        c64 = fw.sb([128, 256], BF16, name="c64")
        sc_fm = fw.sb([128, 16, 2], F32, name="sc_fm")
        nf_fm = fw.sb([128, 16], F32, name="nf_fm")
        for b in (ident, identb, onesb, blk, iot, m_f, m_b, mt_f, mt_b, cmask, c64):
            b.tr.const = True

        P = fw.eng["pool"]
        fw.op("pool", lambda e: e.iota(iot[:], pattern=[[1, 128]], base=0, channel_multiplier=-1,
                                       allow_small_or_imprecise_dtypes=True), writes=[iot])
        fw.op("dve", lambda e: e.tensor_single_scalar(out=ident[:], in_=iot[:], scalar=0.0, op=ALU.is_equal),
              reads=[iot], writes=[ident])
        fw.op("dve", lambda e: e.tensor_copy(out=identb[:], in_=ident[:]), reads=[ident], writes=[identb])
        fw.op("dve", lambda e: e.memset(onesb[:], 1.0), writes=[onesb])
        fw.op("dve", lambda e: e.memset(blk[:], 0.0), writes=[blk])
        fw.op("dve", lambda e: e.memset(blk[0:64, 0:64], 1.0), writes=[blk])
        fw.op("dve", lambda e: e.memset(blk[64:128, 64:128], 1.0), writes=[blk])
        fw.op("dve", lambda e: e.tensor_single_scalar(out=m_f[:, 0:128], in_=iot[:], scalar=0.0, op=ALU.is_gt), reads=[iot], writes=[m_f])
        fw.op("dve", lambda e: e.tensor_single_scalar(out=m_f[:, 128:256], in_=iot[:], scalar=0.0, op=ALU.is_ge), reads=[iot], writes=[m_f])
        fw.op("dve", lambda e: e.tensor_single_scalar(out=m_b[:, 0:128], in_=iot[:], scalar=0.0, op=ALU.is_lt), reads=[iot], writes=[m_b])
        fw.op("dve", lambda e: e.tensor_single_scalar(out=m_b[:, 128:256], in_=iot[:], scalar=0.0, op=ALU.is_le), reads=[iot], writes=[m_b])
        fw.op("dve", lambda e: e.tensor_single_scalar(out=mt_f[:], in_=iot[:], scalar=0.0, op=ALU.is_lt), reads=[iot], writes=[mt_f])
        fw.op("dve", lambda e: e.tensor_single_scalar(out=mt_b[:], in_=iot[:], scalar=0.0, op=ALU.is_gt), reads=[iot], writes=[mt_b])
        fw.op("dve", lambda e: e.memset(cmask[:], 1.0), writes=[cmask])
        for c in range(NT // CH):
            fw.op("dve", lambda e, c=c: e.memset(cmask[:, c * CH:c * CH + 1], 0.0), writes=[cmask])
        fw.load("sp", c64, c64[:], c64_in[:, :])

        def rows_to_fm(es, rows, R, nq, dst, dst_q0=0):
            psr = fw.ps([128, 16, R], F32, es=es, name="psr")
            for q0 in range(0, nq, 16):
                qn = min(16, nq - q0)
                fw.ops("pe", [lambda e, q=q: e.matmul(psr[:, q - q0, :], lhsT=rows[0:R, q * 128:(q + 1) * 128],
                                                      rhs=ident[0:R, 0:R], start=True, stop=True)
                              for q in range(q0, q0 + qn)], reads=[rows, ident], writes=[psr])
                fw.op("dve", lambda e: e.tensor_copy(out=dst[:, dst_q0 + q0:dst_q0 + q0 + qn, 0:R], in_=psr[:, 0:qn, :]),
                      reads=[psr], writes=[dst])

        with ExitStack() as es:
            crow = fw.sb([2, D], F32, es=es)
            fw.load("sp", crow, crow[:], cvec_in[:, :])
            fw.op("act", lambda e: e.activation(out=crow[:], in_=crow[:], func=AF.Silu), reads=[crow], writes=[crow])
            rows_to_fm(es, crow, 2, 16, sc_fm)
            nrow = fw.sb([1, D], F32, es=es)
            fw.load("sp", nrow, nrow[:], norm_final[:, :])
            nf3 = fw.sb([128, 16, 1], F32, es=es)
            rows_to_fm(es, nrow, 1, 16, nf3)
            fw.op("dve", lambda e: e.tensor_copy(out=nf_fm[:], in_=nf3[:, :, 0]), reads=[nf3], writes=[nf_fm])

            inb = [fw.sb([128, D], F32, es=es) for _ in range(2)]
            stg = [fw.sb([128, 16, 128], F32, es=es) for _ in range(2)]
            pst = [fw.ps([128, 4, 128], F32, es=es) for _ in range(4)]
            k = 0
            for (src, n0, c0) in ((ctx_in, CTX, 0), (x_in, T, CTX)):
                for tb in range(n0 // 128):
                    ib = inb[k % 2]
                    sg = stg[k % 2]
                    fw.load("sp", ib, ib[:], src[tb * 128:(tb + 1) * 128, :])
                    for g4 in range(4):
                        pt = pst[g4 % 4]
                        fw.ops("pe", [lambda e, j=j: e.transpose(pt[:, j, :], ib[:, (g4 * 4 + j) * 128:(g4 * 4 + j + 1) * 128], ident[:])
                                      for j in range(4)], reads=[ib, ident], writes=[pt])
                        fw.op("dve" if g4 % 2 == 0 else "act",
                              (lambda e: e.tensor_copy(out=sg[:, g4 * 4:(g4 + 1) * 4, :], in_=pt[:])) if g4 % 2 == 0 else
                              (lambda e: e.activation(out=sg[:, g4 * 4:(g4 + 1) * 4, :], in_=pt[:], func=AF.Copy)),
                              reads=[pt], writes=[sg])
                    fw.store("pool", sg, XTv[:, :, c0 + tb * 128:c0 + (tb + 1) * 128], sg[:])
                    k += 1
        fw.barrier()

        for li in range(L):
            last_layer = (li == L - 1)
            with ExitStack() as esl:
              with ExitStack() as es:
                st16 = Ring([fw.sb([128, 16, 512], BF16, es=es) for _ in range(3)])
                for g in range(11):
                    b = st16.next()
                    pc0 = g * 512
                    if g < 6:
                        fw.load("pool", b, b[:], w_in[li, :, pc0:pc0 + 512].rearrange("(c p) n -> p c n", p=128))
                    elif g == 6:
                        fw.load("pool", b, b[:, :, 0:448], w_in[li, :, pc0:pc0 + 448].rearrange("(c p) n -> p c n", p=128))
                    else:
                        fw.load("pool", b, b[:], w_in[li, :, pc0 - 64:pc0 - 64 + 512].rearrange("(c p) n -> p c n", p=128))
                    fw.store("sp", b, WIN_b[g], b[:])
                for (wsrc, wdst, ng) in ((w_out, WOUT_b, 4), (w_gate, WG_b, 11), (w_up, WU_b, 11)):
                    for g in range(ng):
                        b = st16.next()
                        fw.load("pool", b, b[:], wsrc[li, :, g * 512:(g + 1) * 512].rearrange("(c p) n -> p c n", p=128))
                        fw.store("sp", b, wdst[2 * g], b[:, :, 0:256])
                        fw.store("sp", b, wdst[2 * g + 1], b[:, :, 256:512])
                st44 = Ring([fw.sb([128, FC, 256], BF16, es=es) for _ in range(2)])
                for g in range(8):
                    b = st44.next()
                    fw.load("pool", b, b[:], w_down[li, :, g * 256:(g + 1) * 256].rearrange("(c p) n -> p c n", p=128))
                    fw.store("sp", b, WD_b[2 * g], b[:, :, 0:128])
                    fw.store("sp", b, WD_b[2 * g + 1], b[:, :, 128:256])
                fw.barrier()
              vfm = fw.sb([128, 96, 24], F32, name="vfm", es=esl)
              cw_fm = fw.sb([128, 4, 3], F32, name="cw_fm", es=esl)
              mods = fw.sb([128, 96, 2], F32, name="mods", es=esl)
              A1 = fw.sb([128, 16, 2], F32, name="A1", es=esl)
              A2 = fw.sb([128, 16, 2], F32, name="A2", es=esl)
              omk = fw.sb([128, 8], F32, name="omk", es=esl)
              omk2 = fw.sb([128, 8], F32, name="omk2", es=esl)
              with ExitStack() as es:
                NR = 24
                rowb = fw.sb([NR, 6 * D], F32, es=es)
                fw.op("pool", lambda e: e.memset(rowb[:], 0.0), writes=[rowb])
                rowspec = [(0, b_mod[li:li + 1, :], 6 * D), (1, norm_mix[li:li + 1, :], D), (2, norm_ffn[li:li + 1, :], D),
                           (3, rw_shift[li, :, 0:3072], 3072), (6, rw_shift[li, :, WD0:WD0 + 96], 96),
                           (9, rw_shift[li, :, AD0:AD0 + 96], 96), (12, rw_shift[li, :, GD0:GD0 + 256], 256),
                           (15, dec_w0[li], D_RWKV), (17, iclr_a0[li], D_RWKV), (19, k_k[li:li + 1, :], D_RWKV),
                           (20, k_a[li:li + 1, :], D_RWKV), (21, r_k[li:li + 1, :], D_RWKV), (22, ln_w[li:li + 1, :], D_RWKV),
                           (23, ln_b[li:li + 1, :], D_RWKV)]
                for (r0, src, ln) in rowspec:
                    nr = src.shape[0]
                    fw.dma("sp", rowb[r0:r0 + nr, 0:ln], src, rowb, writes=[rowb])
                crow3 = fw.sb([3, 512], F32, es=es)
                fw.load("sp", crow3, crow3[:], conv_w[li])
                rows_to_fm(es, rowb, NR, 96, vfm)
                rows_to_fm(es, crow3, 3, 4, cw_fm)

                wm = Ring([fw.sb([128, 16, 512], F32, es=es) for _ in range(2)])
                psm = Ring([fw.ps([128, 4, 2], F32, es=es) for _ in range(2)])
                for g in range(24):
                    b = wm.next()
                    fw.load("sp" if g % 2 == 0 else "act", b, b[:], w_mod[li, :, g * 512:(g + 1) * 512].rearrange("(c p) n -> p c n", p=128))
                    pm = psm.next()
                    fns = []
                    for j in range(4):
                        for kc in range(16):
                            fns.append(lambda e, j=j, kc=kc: e.matmul(pm[:, j, :], lhsT=b[:, kc, j * 128:(j + 1) * 128], rhs=sc_fm[:, kc, :],
                                                                      start=(kc == 0), stop=(kc == 15)))
                    fw.ops("pe", fns, reads=[b, sc_fm], writes=[pm])
                    for r in range(2):
                        fw.op("dve", lambda e, r=r: e.tensor_tensor(out=mods[:, g * 4:(g + 1) * 4, r], in0=pm[:, :, r],
                                                                    in1=vfm[:, g * 4:(g + 1) * 4, 0], op=ALU.add),
                              reads=[pm, vfm], writes=[mods])
                for r in range(2):
                    fw.op("dve", lambda e, r=r: e.scalar_tensor_tensor(out=A1[:, :, r], in0=mods[:, 16:32, r], scalar=1.0, in1=vfm[:, 0:16, 1],
                                                                       op0=ALU.add, op1=ALU.mult), reads=[mods, vfm], writes=[A1])
                    fw.op("dve", lambda e, r=r: e.scalar_tensor_tensor(out=A2[:, :, r], in0=mods[:, 64:80, r], scalar=1.0, in1=vfm[:, 0:16, 2],
                                                                       op0=ALU.add, op1=ALU.mult), reads=[mods, vfm], writes=[A2])
                if "MODS" in dbg:
                    fw.store("sp", mods, MODS, mods[:].rearrange("p q r -> p (q r)"))
                fw.op("dve", lambda e: e.tensor_scalar(out=omk[:], in0=vfm[:, 0:8, 20], scalar1=-1.0, scalar2=1.0, op0=ALU.mult, op1=ALU.add),
                      reads=[vfm], writes=[omk])
                fw.op("dve", lambda e: e.tensor_scalar(out=omk2[:], in0=omk[:], scalar1=2.0, scalar2=None, op0=ALU.mult), reads=[omk], writes=[omk2])
                fw.barrier()
              if True:
                with ExitStack() as es1:
                    xs = fw.sb([128, 16, NT], F32, es=es1)
                    sq = fw.sb([128, 16, NT], BF16, es=es1)
                    rstd = fw.sb([128, NT], F32, es=es1)
                    tmp = Ring([fw.sb([128, NT], F32, es=es1) for _ in range(2)])
                    xn = fw.sb([128, 16, 2 * NT], BF16, es=es1)
                    wr = Ring([fw.sb([128, 16, 512], BF16, es=es1) for _ in range(2)])
                    pss = fw.ps([128, NT], F32, es=es1)
                    pso = Ring([fw.ps([128, 2 * NT], F32, es=es1) for _ in range(3)])
                    ost = Ring([fw.sb([128, 2 * NT], F32, es=es1) for _ in range(3)])
                    supers = []
                    for c0 in range(0, CTX, 2 * NT):
                        supers.append((c0, min(2 * NT, CTX - c0), 1))
                    for c0 in range(0, T, 2 * NT):
                        supers.append((CTX + c0, min(2 * NT, T - c0), 0))

                    def norm_mod(col0, n, r, Amod, sh_q0, dst, dst0, src_buf=None):
                        xb = src_buf
                        if xb is None:
                            xb = xs
                            fw.load("sp", xs, xs[:, :, 0:n], XTv[:, :, col0:col0 + n])
                        fw.op("act", lambda e: e.activation(out=sq[:, :, 0:n], in_=xb[:, :, 0:n], func=AF.Square), reads=[xb], writes=[sq])
                        fw.ops("pe", [lambda e, dc=dc: e.matmul(pss[:, 0:n], lhsT=onesb[:], rhs=sq[:, dc, 0:n], start=(dc == 0), stop=(dc == 15))
                                      for dc in range(16)], reads=[sq, onesb], writes=[pss])
                        fw.op("act", lambda e: e.activation(out=rstd[:, 0:n], in_=pss[:, 0:n], func=AF.Sqrt, bias=RMS_EPS, scale=1.0 / D),
                              reads=[pss], writes=[rstd])
                        fw.op("dve", lambda e: e.reciprocal(out=rstd[:, 0:n], in_=rstd[:, 0:n]), reads=[rstd], writes=[rstd])
                        for dc in range(16):
                            tb = tmp.next()
                            fw.op("dve", lambda e, dc=dc: e.scalar_tensor_tensor(out=tb[:, 0:n], in0=xb[:, dc, 0:n], scalar=Amod[:, dc, r:r + 1],
                                                                                 in1=rstd[:, 0:n], op0=ALU.mult, op1=ALU.mult),
                                  reads=[xb, Amod, rstd], writes=[tb])
                            fw.op("act", lambda e, dc=dc: e.activation(out=dst[:, dc, dst0:dst0 + n], in_=tb[:, 0:n], func=AF.Identity,
                                                                       bias=mods[:, sh_q0 + dc, r:r + 1], scale=1.0),
                                  reads=[tb, mods], writes=[dst])

                    ev = 0
                    for (s0, sn, r) in supers:
                        for o in range(0, sn, NT):
                            norm_mod(s0 + o, min(NT, sn - o), r, A1, 0, xn, o)
                        for g in range(11):
                            wb = wr.next()
                            fw.load("act", wb, wb[:], WIN_b[g])
                            for j in range(4):
                                pq = g * 4 + j
                                if pq == 27:
                                    pass
                                po = pso.next()
                                fns = []
                                for h0 in range(0, sn, NT):
                                    hn = min(NT, sn - h0)
                                    for kc in range(16):
                                        fns.append(lambda e, h0=h0, hn=hn, kc=kc, j=j: e.matmul(po[:, h0:h0 + hn], lhsT=wb[:, kc, j * 128:(j + 1) * 128],
                                                                                               rhs=xn[:, kc, h0:h0 + hn], start=(kc == 0), stop=(kc == 15)))
                                fw.ops("pe", fns, reads=[wb, xn], writes=[po])
                                ob = ost.next()
                                if ev % 2 == 0:
                                    fw.op("dve", lambda e: e.tensor_copy(out=ob[:, 0:sn], in_=po[:, 0:sn]), reads=[po], writes=[ob])
                                else:
                                    fw.op("act", lambda e: e.activation(out=ob[:, 0:sn], in_=po[:, 0:sn], func=AF.Copy), reads=[po], writes=[ob])
                                ev += 1
                                fw.store("sp", ob, PXT[pq * 128:(pq + 1) * 128, s0:s0 + sn], ob[:, 0:sn])
                fw.barrier()
                if "stop_p1" in dbg:
                    break

                with ExitStack() as es2:
                    HP = 8
                    SL = 9
                    for d_ in dbg:
                        if d_.startswith('sl='):
                            SL = int(d_[3:])
                    SDT = F32 if 'scan32' in dbg else BF16
                    identS = ident if 'scan32' in dbg else identb
                    IDT = BF16 if 'inv16' in dbg else F32
                    decw = fw.sb([96, 2, D_RWKV], F32, es=es2)
                    iclw = fw.sb([96, 2, D_RWKV], F32, es=es2)
                    gupw = fw.sb([128, 2, D_RWKV], BF16, es=es2)
                    for d in range(2):
                        fw.dma("sp", decw[:, d, :], dec_up[li, d], decw, writes=[decw])
                        fw.dma("sp", iclw[:, d, :], iclr_up[li, d], iclw, writes=[iclw])
                    fw.load("pool", gupw, gupw[:], g_up[li].rearrange("(c p) n -> p c n", p=128))
                    for b in (decw, iclw, gupw):
                        pass
                    S2g = [fw.sb([128, 4, 128], F32, es=es2) for _ in range(2)]
                    S2bg = [fw.sb([128, 4, 128], SDT, es=es2) for _ in range(2)]
                    wdh = fw.sb([96, NT + 2], F32, es=es2)
                    adh = fw.sb([96, NT + 2], F32, es=es2)
                    gdh = fw.sb([128, 2, NT + 2], F32, es=es2)
                    wdt = fw.sb([96, NT], F32, es=es2)
                    ads = fw.sb([96, NT], F32, es=es2)
                    gds = fw.sb([128, 2, NT], BF16, es=es2)
                    gtmp = fw.sb([128, 2, NT], F32, es=es2)
                    rkv_h = Ring([[fw.sb([128, NT + 2], F32, es=es2) for _ in range(3)] for _ in range(2)])
                    r_s = [fw.sb([128, NT], F32, es=es2) for _ in range(4)]
                    ks_s = [fw.sb([128, NT], F32, es=es2) for _ in range(4)]
                    v_s = Ring([fw.sb([128, NT], F32, es=es2) for _ in range(2)])
                    vb_s = [fw.sb([128, NT], SDT, es=es2) for _ in range(4)]
                    strm = [fw.sb([128, NT // CH, 4, CH], SDT, es=es2) for _ in range(4)]
                    gC = [fw.sb([128, NT // CH], F32, es=es2) for _ in range(4)]
                    ybuf = fw.sb([128, 4, NT], F32, es=es2)
                    tw = Ring([fw.sb([128, NT], F32, es=es2) for _ in range(12)])
                    LCt = fw.sb([128, NT // CH], F32, es=es2)
                    TTall = fw.sb([128, 4, 3, 128], SDT, es=es2)
                    GBK = fw.sb([128, 8, 2, 256], SDT, es=es2)
                    PkAll = [fw.sb([128, 8, 2, 128], IDT, es=es2) for _ in range(2)]
                    Zt = [fw.sb([128, 8, 64], IDT, es=es2) for _ in range(2)]
                    UP = fw.sb([128, 4, 128], SDT, es=es2)
                    stm = fw.sb([128, 4, 128], F32, es=es2)
                    psA = Ring([fw.ps([128, 512], F32, es=es2) for _ in range(4)])
                    psZ = Ring([fw.ps([128, 512], F32, es=es2) for _ in range(4)])

                    yio = Ring([fw.sb([128, NT], F32, es=es2) for _ in range(2)])
                    yob = Ring([fw.sb([128, NT], BF16, es=es2) for _ in range(2)])

                    for hg_ in range(2):
                        fw.op("pool", lambda e, hg_=hg_: e.memset(S2g[hg_][:], 0.0), writes=[S2g[hg_]])
                        fw.op("pool", lambda e, hg_=hg_: e.memset(S2bg[hg_][:], 0.0), writes=[S2bg[hg_]])

                    ptmp = Ring([fw.sb([128, NT], F32, es=es2) for _ in range(2)])

                    def shift3(eng, dst, src, n, wq, rows, row0, dv=None, sv_=None):
                        dv = dv or (lambda: dst[0:rows, 0:n])
                        sv_ = sv_ or (lambda a, b_: src[0:rows, a:b_])
                        w = lambda tap: vfm[0:rows, wq, row0 + tap:row0 + tap + 1]
                        fw.op(eng, lambda e: e.tensor_scalar(out=dv(), in0=sv_(1, n + 1), scalar1=w(1), scalar2=None, op0=ALU.mult), reads=[src, vfm], writes=[dst])
                        for tap, a in ((0, 0), (2, 2)):
                            if eng == "dve":
                                fw.op(eng, lambda e: e.scalar_tensor_tensor(out=dv(), in0=sv_(a, a + n), scalar=w(tap), in1=dv(), op0=ALU.mult, op1=ALU.add),
                                      reads=[src, vfm, dst], writes=[dst])
                            else:
                                pt_ = ptmp.next()
                                fw.op(eng, lambda e: e.tensor_scalar(out=pt_[0:rows, 0:n], in0=sv_(a, a + n), scalar1=w(tap), scalar2=None, op0=ALU.mult),
                                      reads=[src, vfm], writes=[pt_])
                                fw.op(eng, lambda e: e.tensor_tensor(out=dv(), in0=dv(), in1=pt_[0:rows, 0:n], op=ALU.add), reads=[dst, pt_], writes=[dst])

                    def load_halo(q, buf, dst_rows, row_a, row_b, col0, n, first, last):
                        lo = 1 if first else 0
                        hi = n + 1 if last else n + 2
                        if first:
                            fw.op("pool", lambda e: e.memset(dst_rows[:, 0:1], 0.0), writes=[buf])
                        if last:
                            fw.op("pool", lambda e: e.memset(dst_rows[:, n + 1:n + 2], 0.0), writes=[buf])
                        fw.dma(q, dst_rows[:, lo:hi], PXT[row_a:row_b, col0 - 1 + lo:col0 - 1 + hi], buf, writes=[buf])

                    for pas in (0, 1):
                        if pas == 0:
                            order = list(tiles)
                        else:
                            ctx_t = [t for t in tiles if t[2]]
                            x_t = [t for t in tiles if not t[2]]
                            order = ctx_t[::-1] + x_t[::-1]
                        msk = m_f if pas == 0 else m_b
                        mskT = mt_f if pas == 0 else mt_b
                        for (col0, n, is_ctx, first, last) in order:
                            ncn = n // CH
                            skip_fin = is_ctx and last_layer
                            load_halo("sp", wdh, wdh[0:96, 0:n + 2], WD0, WD0 + 96, col0, n, first, last)
                            load_halo("sp", adh, adh[0:96, 0:n + 2], AD0, AD0 + 96, col0, n, first, last)
                            shift3("pool", wdt, wdh, n, 0, 96, 6)
                            fw.op("act", lambda e: e.activation(out=wdt[:, 0:n], in_=wdt[:, 0:n], func=AF.Tanh), reads=[wdt], writes=[wdt])
                            shift3("pool", ads, adh, n, 0, 96, 9)
                            if pas == 1 and not skip_fin:
                                for c2 in range(2):
                                    load_halo("sp", gdh, gdh[:, c2, 0:n + 2], GD0 + c2 * 128, GD0 + (c2 + 1) * 128, col0, n, first, last)
                                for c2 in range(2):
                                    shift3("pool", gtmp, gdh, n, c2, 128, 12, dv=lambda c2=c2: gtmp[:, c2, 0:n], sv_=lambda a, b_, c2=c2: gdh[:, c2, a:b_])
                                fw.op("act", lambda e: e.activation(out=gds[:, :, 0:n], in_=gtmp[:, :, 0:n], func=AF.Sigmoid), reads=[gtmp], writes=[gds])
                            for hg in range(2):
                              hps = list(range(hg * 4, hg * 4 + 4))
                              if True:
                                for hp in hps:
                                    hb = rkv_h.next()
                                    for i3, base in enumerate((R0, K0, V0)):
                                        load_halo("sp" if i3 != 1 else "act", hb[i3], hb[i3][:, 0:n + 2], base + hp * 128, base + (hp + 1) * 128, col0, n, first, last)
                                    rr, vv = r_s[hp % 4], v_s.next()
                                    kk_ = tw.next()
                                    shift3("dve", rr, hb[0], n, R0 // 128 + hp, 128, 3)
                                    shift3("pool", kk_, hb[1], n, K0 // 128 + hp, 128, 3)
                                    shift3("dve", vv, hb[2], n, V0 // 128 + hp, 128, 3)
                                    fw.op("act", lambda e: e.activation(out=vb_s[hp % 4][:, 0:n], in_=vv[:, 0:n], func=AF.Copy), reads=[vv], writes=[vb_s[hp % 4]])
                                    t1 = tw.next()
                                    fw.op("dve", lambda e: e.tensor_scalar(out=t1[:, 0:n], in0=kk_[:, 0:n], scalar1=vfm[:, hp, 19:20], scalar2=None, op0=ALU.mult),
                                          reads=[kk_, vfm], writes=[t1])
                                    t2 = tw.next()
                                    fw.op("act", lambda e: e.activation(out=t2[:, 0:n], in_=t1[:, 0:n], func=AF.Square), reads=[t1], writes=[t2])
                                    pa = psA.next()
                                    fw.op("pe", lambda e: e.matmul(pa[:, 0:n], lhsT=blk[:], rhs=t2[:, 0:n], start=True, stop=True), reads=[blk, t2], writes=[pa])
                                    fw.op("act", lambda e: e.activation(out=t2[:, 0:n], in_=pa[:, 0:n], func=AF.Sqrt), reads=[pa], writes=[t2])
                                    fw.op("dve", lambda e: e.tensor_scalar(out=t2[:, 0:n], in0=t2[:, 0:n], scalar1=1e-12, scalar2=None, op0=ALU.max), reads=[t2], writes=[t2])
                                    fw.op("dve", lambda e: e.reciprocal(out=t2[:, 0:n], in_=t2[:, 0:n]), reads=[t2], writes=[t2])
                                    kkn = t1
                                    fw.op("dve", lambda e: e.tensor_tensor(out=kkn[:, 0:n], in0=t1[:, 0:n], in1=t2[:, 0:n], op=ALU.mult), reads=[t1, t2], writes=[kkn])
                                    d = pas
                                    pw = psA.next()
                                    fw.op("pe", lambda e: e.matmul(pw[:, 0:n], lhsT=decw[:, d, hp * 128:(hp + 1) * 128], rhs=wdt[:, 0:n], start=True, stop=True),
                                          reads=[decw, wdt], writes=[pw])
                                    lw = tw.next()
                                    fw.op("act", lambda e: e.activation(out=lw[:, 0:n], in_=pw[:, 0:n], func=AF.Sigmoid, bias=vfm[:, hp, 15 + d:16 + d], scale=1.0),
                                          reads=[pw, vfm], writes=[lw])
                                    fw.op("dve", lambda e: e.tensor_scalar(out=lw[:, 0:n], in0=lw[:, 0:n], scalar1=-0.6065306597126334, scalar2=None, op0=ALU.mult),
                                          reads=[lw], writes=[lw])
                                    pa2 = psA.next()
                                    fw.op("pe", lambda e: e.matmul(pa2[:, 0:n], lhsT=iclw[:, d, hp * 128:(hp + 1) * 128], rhs=ads[:, 0:n], start=True, stop=True),
                                          reads=[iclw, ads], writes=[pa2])
                                    aa = tw.next()
                                    fw.op("act", lambda e: e.activation(out=aa[:, 0:n], in_=pa2[:, 0:n], func=AF.Sigmoid, bias=vfm[:, hp, 17 + d:18 + d], scale=1.0),
                                          reads=[pa2, vfm], writes=[aa])
                                    kd = tw.next()
                                    fw.op("dve", lambda e: e.tensor_scalar(out=kd[:, 0:n], in0=aa[:, 0:n], scalar1=vfm[:, hp, 20:21], scalar2=omk[:, hp:hp + 1],
                                                                           op0=ALU.mult, op1=ALU.add), reads=[aa, vfm, omk], writes=[kd])
                                    fw.op("dve", lambda e: e.tensor_tensor(out=kd[:, 0:n], in0=kd[:, 0:n], in1=kk_[:, 0:n], op=ALU.mult), reads=[kd, kk_], writes=[kd])
                                    if pas == 1 and not skip_fin:
                                        pa3 = psA.next()
                                        fw.op("pe", lambda e: e.matmul(pa3[:, 0:n], lhsT=iclw[:, 0, hp * 128:(hp + 1) * 128], rhs=ads[:, 0:n], start=True, stop=True),
                                              reads=[iclw, ads], writes=[pa3])
                                        af = tw.next()
                                        fw.op("act", lambda e: e.activation(out=af[:, 0:n], in_=pa3[:, 0:n], func=AF.Sigmoid, bias=vfm[:, hp, 17:18], scale=1.0),
                                              reads=[pa3, vfm], writes=[af])
                                        fw.op("dve", lambda e: e.tensor_tensor(out=af[:, 0:n], in0=af[:, 0:n], in1=aa[:, 0:n], op=ALU.add), reads=[af, aa], writes=[af])
                                        fw.op("dve", lambda e: e.tensor_scalar(out=af[:, 0:n], in0=af[:, 0:n], scalar1=vfm[:, hp, 20:21], scalar2=omk2[:, hp:hp + 1],
                                                                               op0=ALU.mult, op1=ALU.add), reads=[af, vfm, omk2], writes=[af])
                                        fw.op("dve", lambda e: e.tensor_tensor(out=af[:, 0:n], in0=af[:, 0:n], in1=kk_[:, 0:n], op=ALU.mult), reads=[af, kk_], writes=[af])
                                        fw.op("dve", lambda e: e.scalar_tensor_tensor(out=ks_s[hp % 4][:, 0:n], in0=af[:, 0:n], scalar=vfm[:, hp, 21:22], in1=rr[:, 0:n],
                                                                                      op0=ALU.mult, op1=ALU.mult), reads=[af, vfm, rr], writes=[ks_s[hp % 4]])
                                    Lc = tw.next()
                                    fw.op("dve", lambda e: e.tensor_tensor_scan(out=Lc[:, 0:n], data0=cmask[:, 0:n], data1=lw[:, 0:n], initial=0.0,
                                                                                op0=ALU.mult, op1=ALU.add), reads=[cmask, lw], writes=[Lc])
                                    Ginc, Gexc, Ginv = tw.next(), tw.next(), tw.next()
                                    st = strm[hp % 4]
                                    if pas == 0:
                                        fw.op("act", lambda e: e.activation(out=Ginc[:, 0:n], in_=Lc[:, 0:n], func=AF.Exp), reads=[Lc], writes=[Ginc])
                                        fw.op("act", lambda e: e.activation(out=Ginv[:, 0:n], in_=Lc[:, 0:n], func=AF.Exp, scale=-1.0), reads=[Lc], writes=[Ginv])
                                        fw.op("dve", lambda e: e.tensor_tensor(out=Gexc[:, 0:n], in0=Lc[:, 0:n], in1=lw[:, 0:n], op=ALU.subtract), reads=[Lc, lw], writes=[Gexc])
                                        fw.op("act", lambda e: e.activation(out=Gexc[:, 0:n], in_=Gexc[:, 0:n], func=AF.Exp), reads=[Gexc], writes=[Gexc])
                                        for c in range(ncn):
                                            fw.op("pool", lambda e, c=c: e.tensor_copy(out=gC[hp % 4][:, c:c + 1], in_=Ginc[:, c * CH + CH - 1:c * CH + CH]),
                                                  reads=[Ginc], writes=[gC[hp % 4]])
                                    else:
                                        for c in range(ncn):
                                            fw.op("dve", lambda e, c=c: e.tensor_scalar(out=Gexc[:, c * CH:(c + 1) * CH], in0=Lc[:, c * CH:(c + 1) * CH],
                                                                                        scalar1=Lc[:, c * CH + CH - 1:c * CH + CH], scalar2=None, op0=ALU.subtract),
                                                  reads=[Lc], writes=[Gexc])
                                            fw.op("act", lambda e, c=c: e.activation(out=gC[hp % 4][:, c:c + 1], in_=Lc[:, c * CH + CH - 1:c * CH + CH], func=AF.Exp),
                                                  reads=[Lc], writes=[gC[hp % 4]])
                                        fw.op("dve", lambda e: e.tensor_tensor(out=Ginc[:, 0:n], in0=lw[:, 0:n], in1=Gexc[:, 0:n], op=ALU.subtract), reads=[lw, Gexc], writes=[Ginc])
                                        fw.op("act", lambda e: e.activation(out=Ginv[:, 0:n], in_=Ginc[:, 0:n], func=AF.Exp, scale=-1.0), reads=[Ginc], writes=[Ginv])
                                        fw.op("act", lambda e: e.activation(out=Ginc[:, 0:n], in_=Ginc[:, 0:n], func=AF.Exp), reads=[Ginc], writes=[Ginc])
                                        fw.op("act", lambda e: e.activation(out=Gexc[:, 0:n], in_=Gexc[:, 0:n], func=AF.Exp, scale=-1.0), reads=[Gexc], writes=[Gexc])
                                    sv = lambda i4: st[:, 0:ncn, i4, :]
                                    v3 = lambda b_: b_[:, 0:n].rearrange("p (c t) -> p c t", t=CH)
                                    fw.op("dve", lambda e: e.scalar_tensor_tensor(out=sv(0), in0=v3(kkn), scalar=-1.0, in1=v3(Gexc), op0=ALU.mult, op1=ALU.mult),
                                          reads=[kkn, Gexc], writes=[st])
                                    fw.op("pool", lambda e: e.tensor_tensor(out=sv(1), in0=v3(rr), in1=v3(Ginc), op=ALU.mult), reads=[rr, Ginc], writes=[st])
                                    fw.op("dve", lambda e: e.tensor_tensor(out=aa[:, 0:n], in0=aa[:, 0:n], in1=kkn[:, 0:n], op=ALU.mult), reads=[aa, kkn], writes=[aa])
                                    fw.op("dve", lambda e: e.tensor_tensor(out=sv(2), in0=v3(aa), in1=v3(Ginv), op=ALU.mult), reads=[aa, Ginv], writes=[st])
                                    fw.op("pool", lambda e: e.tensor_tensor(out=sv(3), in0=v3(kd), in1=v3(Ginv), op=ALU.mult), reads=[kd, Ginv], writes=[st])

                                if 'no_scan' in dbg:
                                    continue
                                crange = list(range(ncn)) if pas == 0 else list(range(ncn))[::-1]
                                S2_, S2b_ = S2g[hg], S2bg[hg]
                                NLEV = 7
                                for c in crange:
                                    cs = slice(c * CH, (c + 1) * CH)
                                    for g in range(4):
                                        st = strm[g]
                                        pb = psA.next()
                                        fw.ops("pe", [lambda e: e.matmul(pb[:, 0:128], lhsT=vb_s[g][:, cs], rhs=identS[:], start=True, stop=True),
                                                      lambda e: e.matmul(pb[:, 128:256], lhsT=st[:, c, 2, :], rhs=identS[:], start=True, stop=True),
                                                      lambda e: e.matmul(pb[:, 256:384], lhsT=st[:, c, 3, :], rhs=identS[:], start=True, stop=True)],
                                               reads=[vb_s[g], st, identS], writes=[pb])
                                        fw.op("act", lambda e: e.activation(out=TTall[:, g, :, :], in_=pb[:, 0:384].rearrange("p (a b) -> p a b", b=128), func=AF.Copy),
                                              reads=[pb], writes=[TTall])
                                    for k in range(8):
                                        g, h = k // 2, k % 2
                                        st = strm[g]
                                        hs = slice(h * 64, (h + 1) * 64)
                                        pg = psA.next()
                                        fw.ops("pe", [lambda e: e.matmul(pg[:, 0:256], lhsT=st[hs, c, 2, :], rhs=st[hs, c, 0:2, :], start=True, stop=True),
                                                      lambda e: e.matmul(pg[:, 256:512], lhsT=st[hs, c, 3, :], rhs=st[hs, c, 0:2, :], start=True, stop=True)],
                                               reads=[st], writes=[pg])
                                        fw.op("dve", lambda e: e.tensor_tensor(out=GBK[:, k, 0, :], in0=pg[:, 0:256], in1=msk[:], op=ALU.mult), reads=[pg, msk], writes=[GBK])
                                        fw.op("dve", lambda e: e.tensor_tensor(out=GBK[:, k, 1, :], in0=pg[:, 256:512], in1=msk[:], op=ALU.mult), reads=[pg, msk], writes=[GBK])
                                        fw.op("dve", lambda e: e.tensor_tensor(out=PkAll[0][:, k, 0, :], in0=pg[:, 0:128], in1=msk[:, 0:128], op=ALU.mult),
                                              reads=[pg, msk], writes=[PkAll[0]])
                                    for m in range(2):
                                        pt = psZ.next()
                                        fns = []
                                        for kk2 in range(4):
                                            k = m * 4 + kk2
                                            g, h = k // 2, k % 2
                                            fns.append(lambda e, kk2=kk2, g=g, h=h: e.matmul(pt[:, kk2 * 128:(kk2 + 1) * 128], lhsT=strm[g][h * 64:(h + 1) * 64, c, 0, :],
                                                                                            rhs=strm[g][h * 64:(h + 1) * 64, c, 2, :], start=True, stop=True))
                                        fw.ops("pe", fns, reads=[strm[(m * 4) // 2], strm[(m * 4) // 2 + 1]], writes=[pt])
                                        for kk2 in range(4):
                                            k = m * 4 + kk2
                                            fw.op("dve", lambda e, kk2=kk2, k=k: e.tensor_tensor(out=PkAll[0][:, k, 1, :], in0=pt[:, kk2 * 128:(kk2 + 1) * 128], in1=mskT[:], op=ALU.mult),
                                                  reads=[pt, mskT], writes=[PkAll[0]])
                                    pX = psZ.next()
                                    fns = []
                                    for g in range(4):
                                        st = strm[g]
                                        for h in range(2):
                                            k = g * 2 + h
                                            fns.append(lambda e, g=g, h=h, k=k, st=st: e.matmul(pX[:, k * 64:(k + 1) * 64], lhsT=st[:, c, 0, :], rhs=S2b_[:, g, h * 64:(h + 1) * 64],
                                                                                              start=True, stop=False))
                                            fns.append(lambda e, g=g, h=h, k=k: e.matmul(pX[:, k * 64:(k + 1) * 64], lhsT=GBK[:, k, 1, 0:128], rhs=TTall[:, g, 0, h * 64:(h + 1) * 64],
                                                                                       start=False, stop=True))
                                    fw.ops("pe", fns, reads=[strm[0], strm[1], strm[2], strm[3], S2b_, GBK, TTall], writes=[pX])
                                    fw.op("act", lambda e: e.activation(out=Zt[0][:], in_=pX[:].rearrange("p (a b) -> p a b", b=64), func=AF.Copy), reads=[pX], writes=[Zt[0]])
                                    for lv in range(NLEV):
                                        Pc, Pn = PkAll[lv % 2], PkAll[(lv + 1) % 2]
                                        zi, zo = Zt[lv % 2], Zt[(lv + 1) % 2]
                                        pz = psZ.next()
                                        fw.ops("pe", [lambda e, k=k: e.matmul(pz[:, k * 64:(k + 1) * 64], lhsT=Pc[:, k, 0, :], rhs=zi[:, k, :], start=True, stop=True)
                                                      for k in range(8)], reads=[Pc, zi], writes=[pz])
                                        if lv == NLEV - 1:
                                            fw.op("dve", lambda e: e.tensor_tensor(out=UP[:].rearrange("p g (h i) -> p (g h) i", i=64), in0=pz[:].rearrange("p (a b) -> p a b", b=64),
                                                                                   in1=zi[:], op=ALU.add), reads=[pz, zi], writes=[UP])
                                            break
                                        fw.op("dve", lambda e: e.tensor_tensor(out=zo[:], in0=pz[:].rearrange("p (a b) -> p a b", b=64), in1=zi[:], op=ALU.add),
                                              reads=[pz, zi], writes=[zo])
                                        if lv < NLEV - 2:
                                            for m in range(4):
                                                pq = psA.next()
                                                fns = []
                                                for k in (2 * m, 2 * m + 1):
                                                    o = (k % 2) * 256
                                                    fns.append(lambda e, k=k, o=o: e.matmul(pq[:, o:o + 128], lhsT=Pc[:, k, 1, :], rhs=Pc[:, k, 0, :], start=True, stop=True))
                                                    fns.append(lambda e, k=k, o=o: e.matmul(pq[:, o + 128:o + 256], lhsT=Pc[:, k, 0, :], rhs=Pc[:, k, 1, :], start=True, stop=True))
                                                fw.ops("pe", fns, reads=[Pc], writes=[pq])
                                                dst = Pn[:, 2 * m:2 * m + 2, :, :]
                                                src = pq[:].rearrange("p (a b c) -> p a b c", a=2, b=2)
                                                if m % 2 == 0:
                                                    fw.op("act", lambda e: e.activation(out=dst, in_=src, func=AF.Copy), reads=[pq], writes=[Pn])
                                                else:
                                                    fw.op("dve", lambda e: e.tensor_copy(out=dst, in_=src), reads=[pq], writes=[Pn])
                                        else:
                                            for m in range(2):
                                                pq = psA.next()
                                                fw.ops("pe", [lambda e, k=k: e.matmul(pq[:, (k % 4) * 128:(k % 4 + 1) * 128], lhsT=Pc[:, k, 1, :], rhs=Pc[:, k, 0, :], start=True, stop=True)
                                                              for k in range(4 * m, 4 * m + 4)], reads=[Pc], writes=[pq])
                                                dst = Pn[:, 4 * m:4 * m + 4, 0, :]
                                                src = pq[:].rearrange("p (a b) -> p a b", b=128)
                                                if m % 2 == 0:
                                                    fw.op("act", lambda e: e.activation(out=dst, in_=src, func=AF.Copy), reads=[pq], writes=[Pn])
                                                else:
                                                    fw.op("dve", lambda e: e.tensor_copy(out=dst, in_=src), reads=[pq], writes=[Pn])
                                    pY = psZ.next()
                                    fns = []
                                    for g in range(4):
                                        st = strm[g]
                                        for h in range(2):
                                            k = g * 2 + h
                                            ro = slice(h * 64, (h + 1) * 64)
                                            co = slice(g * 128, (g + 1) * 128)
                                            fns.append(lambda e, g=g, ro=ro, co=co, st=st: e.matmul(pY[ro, co], lhsT=S2b_[:, g, ro], rhs=st[:, c, 1, :], start=True, stop=False))
                                            fns.append(lambda e, g=g, k=k, ro=ro, co=co: e.matmul(pY[ro, co], lhsT=UP[:, g, ro], rhs=GBK[:, k, 0, 128:256], start=False, stop=False))
                                            fns.append(lambda e, g=g, k=k, ro=ro, co=co: e.matmul(pY[ro, co], lhsT=TTall[:, g, 0, ro], rhs=GBK[:, k, 1, 128:256], start=False, stop=True))
                                    fw.ops("pe", fns, reads=[strm[0], strm[1], strm[2], strm[3], S2b_, UP, GBK, TTall], writes=[pY])
                                    fw.op("act", lambda e: e.activation(out=ybuf[:, :, cs], in_=pY[:].rearrange("p (a b) -> p a b", b=128), func=AF.Copy), reads=[pY], writes=[ybuf])
                                    pS = psZ.next()
                                    fns = []
                                    for g in range(4):
                                        co = slice(g * 128, (g + 1) * 128)
                                        fns.append(lambda e, g=g, co=co: e.matmul(pS[:, co], lhsT=TTall[:, g, 1, :], rhs=UP[:, g, :], start=True, stop=False))
                                        fns.append(lambda e, g=g, co=co: e.matmul(pS[:, co], lhsT=TTall[:, g, 2, :], rhs=TTall[:, g, 0, :], start=False, stop=True))
                                    fw.ops("pe", fns, reads=[TTall, UP], writes=[pS])
                                    fw.op("dve", lambda e: e.tensor_tensor(out=stm[:], in0=pS[:].rearrange("p (a b) -> p a b", b=128), in1=S2_[:], op=ALU.add),
                                          reads=[pS, S2_], writes=[stm])
                                    for g in range(4):
                                        fw.op("dve", lambda e, g=g: e.scalar_tensor_tensor(out=S2_[:, g, :], in0=stm[:, g, :], scalar=gC[g][:, c:c + 1], in1=blk[:],
                                                                                           op0=ALU.mult, op1=ALU.mult), reads=[stm, gC[g], blk], writes=[S2_])
                                    fw.op("pool", lambda e: e.tensor_copy(out=S2b_[:], in_=S2_[:]), reads=[S2_], writes=[S2b_])

                                for hp in hps:
                                    rows = slice(hp * 128, (hp + 1) * 128)
                                    if pas == 0:
                                        fw.store("sp", ybuf, YF[rows, col0:col0 + n], ybuf[:, hp % 4, 0:n])
                                        continue
                                    if skip_fin or 'no_fin' in dbg:
                                        continue
                                    yf = yio.next()
                                    fw.load("sp", yf, yf[:, 0:n], YF[rows, col0:col0 + n])
                                    y = yf
                                    fw.op("dve", lambda e: e.tensor_tensor(out=y[:, 0:n], in0=yf[:, 0:n], in1=ybuf[:, hp % 4, 0:n], op=ALU.add), reads=[yf, ybuf], writes=[y])
                                    pm_ = psA.next()
                                    fw.op("pe", lambda e: e.matmul(pm_[:, 0:n], lhsT=blk[:], rhs=y[:, 0:n], start=True, stop=True), reads=[blk, y], writes=[pm_])
                                    dd = tw.next()
                                    fw.op("dve", lambda e: e.scalar_tensor_tensor(out=dd[:, 0:n], in0=pm_[:, 0:n], scalar=-1.0 / 64, in1=y[:, 0:n], op0=ALU.mult, op1=ALU.add),
                                          reads=[pm_, y], writes=[dd])
                                    d2 = tw.next()
                                    fw.op("act", lambda e: e.activation(out=d2[:, 0:n], in_=dd[:, 0:n], func=AF.Square), reads=[dd], writes=[d2])
                                    pv_ = psA.next()
                                    fw.op("pe", lambda e: e.matmul(pv_[:, 0:n], lhsT=blk[:], rhs=d2[:, 0:n], start=True, stop=True), reads=[blk, d2], writes=[pv_])
                                    fw.op("act", lambda e: e.activation(out=d2[:, 0:n], in_=pv_[:, 0:n], func=AF.Sqrt, bias=GN_EPS, scale=1.0 / 64), reads=[pv_], writes=[d2])
                                    fw.op("dve", lambda e: e.reciprocal(out=d2[:, 0:n], in_=d2[:, 0:n]), reads=[d2], writes=[d2])
                                    fw.op("dve", lambda e: e.tensor_tensor(out=dd[:, 0:n], in0=dd[:, 0:n], in1=d2[:, 0:n], op=ALU.mult), reads=[dd, d2], writes=[dd])
                                    fw.op("dve", lambda e: e.tensor_scalar(out=dd[:, 0:n], in0=dd[:, 0:n], scalar1=vfm[:, hp, 22:23], scalar2=vfm[:, hp, 23:24],
                                                                           op0=ALU.mult, op1=ALU.add), reads=[dd, vfm], writes=[dd])
                                    pb_ = psA.next()
                                    fw.op("pe", lambda e: e.matmul(pb_[:, 0:n], lhsT=blk[:], rhs=ks_s[hp % 4][:, 0:n], start=True, stop=True), reads=[blk, ks_s[hp % 4]], writes=[pb_])
                                    fw.op("dve", lambda e: e.tensor_tensor(out=d2[:, 0:n], in0=pb_[:, 0:n], in1=vb_s[hp % 4][:, 0:n], op=ALU.mult), reads=[pb_, vb_s[hp % 4]], writes=[d2])
                                    fw.op("dve", lambda e: e.tensor_tensor(out=dd[:, 0:n], in0=dd[:, 0:n], in1=d2[:, 0:n], op=ALU.add), reads=[dd, d2], writes=[dd])
                                    pg_ = psA.next()
                                    fw.ops("pe", [lambda e, c2=c2: e.matmul(pg_[:, 0:n], lhsT=gupw[:, c2, hp * 128:(hp + 1) * 128], rhs=gds[:, c2, 0:n],
                                                                            start=(c2 == 0), stop=(c2 == 1)) for c2 in range(2)],
                                           reads=[gupw, gds], writes=[pg_])
                                    yo = yob.next()
                                    fw.op("dve", lambda e: e.tensor_tensor(out=yo[:, 0:n], in0=pg_[:, 0:n], in1=dd[:, 0:n], op=ALU.mult), reads=[pg_, dd], writes=[yo])
                                    fw.store("sp", yo, YT[rows, col0:col0 + n], yo[:, 0:n])
                        fw.barrier()
                        if pas == 0:
                            for hg_ in range(2):
                                fw.op("pool", lambda e, hg_=hg_: e.memset(S2g[hg_][:], 0.0), writes=[S2g[hg_]])
                                fw.op("pool", lambda e, hg_=hg_: e.memset(S2bg[hg_][:], 0.0), writes=[S2bg[hg_]])
                fw.barrier()
                if "stop_p2" in dbg:
                    break

                with ExitStack() as es3:
                    cin = Ring([[fw.sb([128, NT], F32, es=es3) for _ in range(3)] for _ in range(2)])
                    cu = Ring([fw.sb([128, NT], F32, es=es3) for _ in range(2)])
                    co = Ring([fw.sb([128, NT], F32, es=es3) for _ in range(2)])
                    cob = Ring([fw.sb([128, NT], BF16, es=es3) for _ in range(2)])
                    for (col0, n, is_ctx, first, last) in tiles:
                        if is_ctx and last_layer:
                            continue
                        for q in range(4):
                            ib = cin.next()
                            for i3, base in enumerate((PCG, PCX, PCB)):
                                fw.load("sp", ib[i3], ib[i3][:, 0:n], PXT[base + q * 128:base + (q + 1) * 128, col0:col0 + n])
                            u = cu.next()
                            o = co.next()
                            fw.op("pool", lambda e: e.tensor_tensor(out=u[:, 0:n], in0=ib[0][:, 0:n], in1=ib[1][:, 0:n], op=ALU.mult), reads=[ib[0], ib[1]], writes=[u])
                            fw.op("dve", lambda e: e.tensor_scalar(out=o[:, 0:n], in0=u[:, 0:n], scalar1=cw_fm[:, q, 1:2], scalar2=None, op0=ALU.mult),
                                  reads=[u, cw_fm], writes=[o])
                            if is_ctx:
                                assert first and last
                                W_ = n
                            else:
                                W_ = 64
                            u3 = u[:, 0:n].rearrange("p (r w) -> p r w", w=W_)
                            o3 = o[:, 0:n].rearrange("p (r w) -> p r w", w=W_)
                            fw.op("dve", lambda e: e.scalar_tensor_tensor(out=o3[:, :, 1:W_], in0=u3[:, :, 0:W_ - 1], scalar=cw_fm[:, q, 0:1], in1=o3[:, :, 1:W_],
                                                                          op0=ALU.mult, op1=ALU.add), reads=[u, cw_fm, o], writes=[o])
                            fw.op("dve", lambda e: e.scalar_tensor_tensor(out=o3[:, :, 0:W_ - 1], in0=u3[:, :, 1:W_], scalar=cw_fm[:, q, 2:3], in1=o3[:, :, 0:W_ - 1],
                                                                          op0=ALU.mult, op1=ALU.add), reads=[u, cw_fm, o], writes=[o])
                            ob = cob.next()
                            fw.op("pool", lambda e: e.tensor_tensor(out=ob[:, 0:n], in0=o[:, 0:n], in1=ib[2][:, 0:n], op=ALU.mult), reads=[o, ib[2]], writes=[ob])
                            fw.store("sp", ob, YT[1024 + q * 128:1024 + (q + 1) * 128, col0:col0 + n], ob[:, 0:n])
                fw.barrier()

                with ExitStack() as es4:
                    LCmax = T // 128
                    UC = fw.sb([128, LCmax, 512], BF16, es=es4)
                    US = fw.sb([128, LCmax, 512], BF16, es=es4)
                    uT = Ring([fw.sb([128, NT], BF16, es=es4) for _ in range(3)])
                    pu = Ring([fw.ps([128, 512], F32, es=es4) for _ in range(2)])
                    py = [fw.ps([128, 512], F32, es=es4) for _ in range(4)]
                    LG = 8
                    tabr = Ring([[fw.sb([128, LG, 512], BF16, es=es4) for _ in range(2)] for _ in range(2)])
                    fo = Ring([fw.sb([128, 512], BF16, es=es4) for _ in range(3)])
                    for (seq0, sn, tab, is_ctx) in ((0, CTX, ftab_c, True), (CTX, T, ftab_x, False)):
                        if is_ctx and last_layer:
                            continue
                        nlc = sn // 128
                        for t0 in range(0, sn, NT):
                            tn = min(NT, sn - t0)
                            for mc in range(4):
                                ub = uT.next()
                                fw.load("pool", ub, ub[:, 0:tn], PXT[PFT + mc * 128:PFT + (mc + 1) * 128, seq0 + t0:seq0 + t0 + tn])
                                for lc in range(tn // 128):
                                    p_ = pu.next()
                                    fw.op("pe", lambda e: e.matmul(p_[:, 0:256], lhsT=ub[:, lc * 128:(lc + 1) * 128], rhs=c64[:], start=True, stop=True),
                                          reads=[ub, c64], writes=[p_])
                                    glc = t0 // 128 + lc
                                    if (lc + mc) % 2 == 0:
                                        fw.op("dve", lambda e: e.tensor_copy(out=UC[:, glc, mc * 128:(mc + 1) * 128], in_=p_[:, 0:128]), reads=[p_], writes=[UC])
                                        fw.op("dve", lambda e: e.tensor_copy(out=US[:, glc, mc * 128:(mc + 1) * 128], in_=p_[:, 128:256]), reads=[p_], writes=[US])
                                    else:
                                        fw.op("act", lambda e: e.activation(out=UC[:, glc, mc * 128:(mc + 1) * 128], in_=p_[:, 0:128], func=AF.Copy), reads=[p_], writes=[UC])
                                        fw.op("act", lambda e: e.activation(out=US[:, glc, mc * 128:(mc + 1) * 128], in_=p_[:, 128:256], func=AF.Copy), reads=[p_], writes=[US])
                        for k0 in range(0, sn, 512):
                            kn = min(512, sn - k0)
                            for lg0 in range(0, nlc, LG):
                                lgn = min(LG, nlc - lg0)
                                tb = tabr.next()
                                for i2 in range(2):
                                    fw.load("sp" if i2 == 0 else "act", tb[i2], tb[i2][:, 0:lgn, 0:kn],
                                            tab[i2, lg0 * 128:(lg0 + lgn) * 128, k0:k0 + kn].rearrange("(c p) k -> p c k", p=128))
                                for mc in range(4):
                                    fns = []
                                    for l_ in range(lgn):
                                        lc = lg0 + l_
                                        fns.append(lambda e, l_=l_, lc=lc: e.matmul(py[mc][:, 0:kn], lhsT=UC[:, lc, mc * 128:(mc + 1) * 128], rhs=tb[0][:, l_, 0:kn],
                                                                                    start=(lc == 0), stop=False))
                                        fns.append(lambda e, l_=l_, lc=lc: e.matmul(py[mc][:, 0:kn], lhsT=US[:, lc, mc * 128:(mc + 1) * 128], rhs=tb[1][:, l_, 0:kn],
                                                                                    start=False, stop=(lc == nlc - 1)))
                                    fw.ops("pe", fns, reads=[UC, US, tb[0], tb[1]], writes=[py[mc]])
                            for mc in range(4):
                                ob = fo.next()
                                fw.op("act" if mc % 2 else "dve",
                                      (lambda e: e.activation(out=ob[:, 0:kn], in_=py[mc][:, 0:kn], func=AF.Copy)) if mc % 2 else
                                      (lambda e: e.tensor_copy(out=ob[:, 0:kn], in_=py[mc][:, 0:kn])), reads=[py[mc]], writes=[ob])
                                fw.store("sp", ob, YT[1536 + mc * 128:1536 + (mc + 1) * 128, seq0 + k0:seq0 + k0 + kn], ob[:, 0:kn])
                fw.barrier()
                if "stop_p4" in dbg:
                    break

                with ExitStack() as es5:
                    xb_ = fw.sb([128, 16, NT], F32, es=es5)
                    yx = fw.sb([128, 16, NT], BF16, es=es5)
                    H = fw.sb([128, FC, NT], BF16, es=es5)
                    wring = Ring([fw.sb([128, 16, 256], BF16, es=es5) for _ in range(4)])
                    wdr = Ring([fw.sb([128, FC, 128], BF16, es=es5) for _ in range(2)])
                    sq = H
                    rstd = fw.sb([128, NT], F32, es=es5)
                    tmp = Ring([fw.sb([128, NT], F32, es=es5) for _ in range(2)])
                    sg = Ring([fw.sb([128, NT], F32, es=es5) for _ in range(2)])
                    pss = fw.ps([128, NT], F32, es=es5)
                    pA = Ring([fw.ps([128, NT], F32, es=es5) for _ in range(3)])
                    pG = Ring([fw.ps([128, NT], F32, es=es5) for _ in range(2)])
                    pU = Ring([fw.ps([128, NT], F32, es=es5) for _ in range(2)])
                    YTv = YT.rearrange("(c p) t -> p c t", p=128)
                    for (col0, n, is_ctx, first, last) in tiles:
                        if is_ctx and last_layer:
                            continue
                        r = 1 if is_ctx else 0
                        fw.load("sp", xb_, xb_[:, :, 0:n], XTv[:, :, col0:col0 + n])
                        fw.load("act", yx, yx[:, :, 0:n], YTv[:, :, col0:col0 + n])
                        for g in range(8):
                            wb = wring.next()
                            fw.load("sp" if g % 2 else "act", wb, wb[:], WOUT_b[g])
                            for j in range(2):
                                dc = g * 2 + j
                                po = pA.next()
                                fw.ops("pe", [lambda e, kc=kc: e.matmul(po[:, 0:n], lhsT=wb[:, kc, j * 128:(j + 1) * 128], rhs=yx[:, kc, 0:n],
                                                                        start=(kc == 0), stop=(kc == 15)) for kc in range(16)], reads=[wb, yx], writes=[po])
                                fw.op("dve", lambda e: e.scalar_tensor_tensor(out=xb_[:, dc, 0:n], in0=po[:, 0:n], scalar=mods[:, 32 + dc, r:r + 1], in1=xb_[:, dc, 0:n],
                                                                              op0=ALU.mult, op1=ALU.add), reads=[po, mods, xb_], writes=[xb_])
                        fw.op("act", lambda e: e.activation(out=sq[:, 0:16, 0:n], in_=xb_[:, :, 0:n], func=AF.Square), reads=[xb_], writes=[sq])
                        fw.ops("pe", [lambda e, dc=dc: e.matmul(pss[:, 0:n], lhsT=onesb[:], rhs=sq[:, dc, 0:n], start=(dc == 0), stop=(dc == 15))
                                      for dc in range(16)], reads=[sq, onesb], writes=[pss])
                        fw.op("act", lambda e: e.activation(out=rstd[:, 0:n], in_=pss[:, 0:n], func=AF.Sqrt, bias=RMS_EPS, scale=1.0 / D), reads=[pss], writes=[rstd])
                        fw.op("dve", lambda e: e.reciprocal(out=rstd[:, 0:n], in_=rstd[:, 0:n]), reads=[rstd], writes=[rstd])
                        for dc in range(16):
                            tb = tmp.next()
                            fw.op("dve", lambda e, dc=dc: e.scalar_tensor_tensor(out=tb[:, 0:n], in0=xb_[:, dc, 0:n], scalar=A2[:, dc, r:r + 1], in1=rstd[:, 0:n],
                                                                                 op0=ALU.mult, op1=ALU.mult), reads=[xb_, A2, rstd], writes=[tb])
                            fw.op("act", lambda e, dc=dc: e.activation(out=yx[:, dc, 0:n], in_=tb[:, 0:n], func=AF.Identity, bias=mods[:, 48 + dc, r:r + 1], scale=1.0),
                                  reads=[tb, mods], writes=[yx])
                        for g in range(22):
                            wg_ = wring.next()
                            wu_ = wring.next()
                            fw.load("sp", wg_, wg_[:], WG_b[g])
                            fw.load("act", wu_, wu_[:], WU_b[g])
                            for j in range(2):
                                fc = g * 2 + j
                                pg = pG.next()
                                pu_ = pU.next()
                                fw.ops("pe", [lambda e, kc=kc: e.matmul(pg[:, 0:n], lhsT=wg_[:, kc, j * 128:(j + 1) * 128], rhs=yx[:, kc, 0:n],
                                                                        start=(kc == 0), stop=(kc == 15)) for kc in range(16)], reads=[wg_, yx], writes=[pg])
                                fw.ops("pe", [lambda e, kc=kc: e.matmul(pu_[:, 0:n], lhsT=wu_[:, kc, j * 128:(j + 1) * 128], rhs=yx[:, kc, 0:n],
                                                                        start=(kc == 0), stop=(kc == 15)) for kc in range(16)], reads=[wu_, yx], writes=[pu_])
                                s_ = sg.next()
                                fw.op("act", lambda e: e.activation(out=s_[:, 0:n], in_=pg[:, 0:n], func=AF.Silu), reads=[pg], writes=[s_])
                                fw.op("dve", lambda e: e.tensor_tensor(out=H[:, fc, 0:n], in0=pu_[:, 0:n], in1=s_[:, 0:n], op=ALU.mult), reads=[pu_, s_], writes=[H])
                        for dc in range(16):
                            wd_ = wdr.next()
                            fw.load("sp" if dc % 2 else "act", wd_, wd_[:], WD_b[dc])
                            if True:
                                po = pA.next()
                                fw.ops("pe", [lambda e, fc=fc: e.matmul(po[:, 0:n], lhsT=wd_[:, fc, :], rhs=H[:, fc, 0:n],
                                                                        start=(fc == 0), stop=(fc == FC - 1)) for fc in range(FC)], reads=[wd_, H], writes=[po])
                                fw.op("dve", lambda e: e.scalar_tensor_tensor(out=xb_[:, dc, 0:n], in0=po[:, 0:n], scalar=mods[:, 80 + dc, r:r + 1], in1=xb_[:, dc, 0:n],
                                                                              op0=ALU.mult, op1=ALU.add), reads=[po, mods, xb_], writes=[xb_])
                        fw.store("pool", xb_, XTv[:, :, col0:col0 + n], xb_[:, :, 0:n])
                fw.barrier()

        with ExitStack() as es7:
            xb_ = fw.sb([128, 16, NT], F32, es=es7)
            sq = fw.sb([128, 16, NT], BF16, es=es7)
            rstd = fw.sb([128, NT], F32, es=es7)
            xo = fw.sb([128, 16, NT], F32, es=es7)
            pss = fw.ps([128, NT], F32, es=es7)
            ptr = Ring([fw.ps([128, 512], F32, es=es7) for _ in range(4)])
            orow = Ring([fw.sb([128, D], F32, es=es7) for _ in range(2)])
            for (col0, n, is_ctx, first, last) in tiles:
                if is_ctx:
                    continue
                fw.load("sp", xb_, xb_[:, :, 0:n], XTv[:, :, col0:col0 + n])
                fw.op("act", lambda e: e.activation(out=sq[:, :, 0:n], in_=xb_[:, :, 0:n], func=AF.Square), reads=[xb_], writes=[sq])
                fw.ops("pe", [lambda e, dc=dc: e.matmul(pss[:, 0:n], lhsT=onesb[:], rhs=sq[:, dc, 0:n], start=(dc == 0), stop=(dc == 15))
                              for dc in range(16)], reads=[sq, onesb], writes=[pss])
                fw.op("act", lambda e: e.activation(out=rstd[:, 0:n], in_=pss[:, 0:n], func=AF.Sqrt, bias=RMS_EPS, scale=1.0 / D), reads=[pss], writes=[rstd])
                fw.op("dve", lambda e: e.reciprocal(out=rstd[:, 0:n], in_=rstd[:, 0:n]), reads=[rstd], writes=[rstd])
                for dc in range(16):
                    fw.op("dve", lambda e, dc=dc: e.scalar_tensor_tensor(out=xo[:, dc, 0:n], in0=xb_[:, dc, 0:n], scalar=nf_fm[:, dc:dc + 1], in1=rstd[:, 0:n],
                                                                         op0=ALU.mult, op1=ALU.mult), reads=[xb_, nf_fm, rstd], writes=[xo])
                for tb in range(n // 128):
                    ob = orow.next()
                    for g4 in range(4):
                        pt = ptr.next()
                        fw.ops("pe", [lambda e, j=j: e.transpose(pt[:, j * 128:(j + 1) * 128], xo[:, g4 * 4 + j, tb * 128:(tb + 1) * 128], ident[:])
                                      for j in range(4)], reads=[xo, ident], writes=[pt])
                        fw.op("dve" if g4 % 2 == 0 else "act",
                              (lambda e: e.tensor_copy(out=ob[:, g4 * 512:(g4 + 1) * 512], in_=pt[:])) if g4 % 2 == 0 else
                              (lambda e: e.activation(out=ob[:, g4 * 512:(g4 + 1) * 512], in_=pt[:], func=AF.Copy)), reads=[pt], writes=[ob])
                    t0 = col0 - CTX + tb * 128
                    fw.store("sp", ob, out_ap[t0:t0 + 128, :], ob[:])
        fw.barrier()
    return nc, fw


def fourier_tables(n):
    idx = np.arange(n, dtype=np.int64)
    ang = (2.0 * np.pi / n) * ((idx[:, None] * idx[None, :]) % n).astype(np.float64)
    sc = 1.0 / np.sqrt(64.0 * n)
    tab = np.stack([np.cos(ang) * sc, -np.sin(ang) * sc], 0)
    return tab.astype(np.float32).astype(ml_dtypes.bfloat16)


def c64_table():
    idx = np.arange(64)
    ang = 2.0 * np.pi * ((idx[:, None] * idx[None, :]) % 64) / 64.0
    t = np.zeros((128, 256), np.float32)
    for h in range(2):
        t[h * 64:(h + 1) * 64, h * 64:(h + 1) * 64] = np.cos(ang)
        t[h * 64:(h + 1) * 64, 128 + h * 64:128 + (h + 1) * 64] = np.sin(ang)
    return t.astype(ml_dtypes.bfloat16)


_CACHE = {}


def make_inputs(b, x, c, ctx, c_ctx, W, T, CTX):
    m = {"x": np.ascontiguousarray(x[b]), "ctx": np.ascontiguousarray(ctx[b]),
         "cvec": np.ascontiguousarray(np.stack([c[b], c_ctx], 0))}
    m.update(W)
    return m


def kernel(x, c, ctx, c_ctx, w_mod, b_mod, norm_mix, w_in, rw_shift, dec_w0, dec_up, iclr_a0, iclr_up, k_k, k_a, r_k,
           ln_w, ln_b, g_up, conv_w, w_out, norm_ffn, w_gate, w_up, w_down, norm_final, _dbg=()):
    x = np.asarray(x)
    B, T, _ = x.shape
    CTX = ctx.shape[1]
    DEPTH = w_mod.shape[0]
    f = lambda a: np.ascontiguousarray(np.asarray(a, dtype=np.float32))
    W = dict(w_mod=f(w_mod), b_mod=f(b_mod), norm_mix=f(norm_mix), w_in=f(w_in), rw_shift=f(rw_shift), dec_w0=f(dec_w0),
             dec_up=f(dec_up), iclr_a0=f(iclr_a0), iclr_up=f(iclr_up), k_k=f(k_k), k_a=f(k_a),
             r_k=f(r_k).reshape(DEPTH, D_RWKV), ln_w=f(ln_w), ln_b=f(ln_b), g_up=f(g_up), conv_w=f(conv_w), w_out=f(w_out),
             norm_ffn=f(norm_ffn), w_gate=f(w_gate), w_up=f(w_up), w_down=f(w_down), norm_final=f(norm_final).reshape(1, D),
             ftab_x=fourier_tables(T), ftab_c=fourier_tables(CTX), c64=c64_table())
    key = (T, CTX, DEPTH, tuple(_dbg))
    nc, fw = build(T, CTX, DEPTH, _dbg)
    in_maps = [make_inputs(b, x, np.asarray(c), np.asarray(ctx), np.asarray(c_ctx), W, T, CTX) for b in range(B)]
    res = run_bass_kernel_spmd(nc, in_maps, core_ids=list(range(B)))
    if _dbg:
        return res
    return np.stack([res.results[b]["out"] for b in range(B)], 0).astype(np.float32)
```

```python
import numpy as np
import ml_dtypes
from contextlib import ExitStack
import concourse.bass as bass
import concourse.mybir as mybir
from concourse.bass_utils import run_bass_kernel_spmd

F32 = mybir.dt.float32
BF16 = mybir.dt.bfloat16
AF = mybir.ActivationFunctionType
ALU = mybir.AluOpType

D = 2048
DC = 16
HD = 64
D_RWKV = 1024
D_FF = 5632
FC = 44
R0, K0, V0, WD0, AD0, GD0, RW_COLS = 0, 1024, 2048, 3072, 3168, 3264, 3520
D_IN = 5568
PCG, PCX, PCB, PFT, PEND = 3584, 4096, 4608, 5120, 5632
RMS_EPS = 1e-6
GN_EPS = 64e-5
CH = 128
NT = 512


class Tr:
    __slots__ = ("w", "r", "sem", "dcnt", "const")

    def __init__(self, const=False):
        self.w = None
        self.r = {}
        self.sem = None
        self.dcnt = 0
        self.const = const


class Buf:
    def __init__(self, tile):
        self.t = tile
        self.tr = Tr()

    def __getitem__(self, k):
        return self.t[k]


class FW:
    ENG = ("pe", "act", "dve", "pool", "sp")

    def __init__(self, nc):
        self.nc = nc
        self.es = ExitStack()
        self.eng = {"pe": nc.tensor, "act": nc.scalar, "dve": nc.vector, "pool": nc.gpsimd, "sp": nc.sync}
        self.sem = {e: self.es.enter_context(nc.semaphore("sem_" + e)) for e in self.ENG}
        self.cnt = {e: 0 for e in self.ENG}
        self.seen = {e: {} for e in self.ENG}
        self.nsem = 0
        self.n_inst = 0
        self.dma_trs = []
        self.sem_pool = [[], []]
        self.uid = 0

    def sb(self, shape, dt, es=None, name=None):
        self.uid += 1
        return Buf((es or self.es).enter_context(self.nc.sbuf_tensor("%s_%d" % (name or "sb", self.uid), shape, dt)))

    def ps(self, shape, dt=F32, es=None, name=None):
        self.uid += 1
        return Buf((es or self.es).enter_context(self.nc.psum_tensor("%s_%d" % (name or "ps", self.uid), shape, dt)))

    def dram(self, name, shape, dt, kind="Internal"):
        return self.nc.dram_tensor(name, shape, dt, kind=kind).ap()

    def _wait(self, e, tok):
        if tok is None:
            return
        kind, key, val = tok
        if kind == "e":
            if self.seen[e].get(key, 0) >= val:
                return
            self.eng[e].wait_ge(self.sem[key], val)
            self.seen[e][key] = val
        else:
            k = id(key)
            if self.seen[e].get(k, 0) >= val:
                return
            self.eng[e].wait_ge(key, val)
            self.seen[e][k] = val

    def _deps(self, e, reads, writes):
        for t in reads:
            self._wait(e, t.w)
        for t in writes:
            self._wait(e, t.w)
            for tok in t.r.values():
                self._wait(e, tok)

    def _commit(self, tok, reads, writes):
        key = tok[1] if tok[0] == "e" else id(tok[1])
        for t in reads:
            if not t.const:
                t.r[key] = tok
        for t in writes:
            t.w = tok
            t.r = {}

    @staticmethod
    def _trs(bufs):
        return [b.tr if isinstance(b, Buf) else b for b in bufs]

    def op(self, e, fn, reads=(), writes=()):
        reads = self._trs(reads)
        writes = self._trs(writes)
        self._deps(e, reads, writes)
        ins = fn(self.eng[e])
        self.cnt[e] += 1
        ins.then_inc(self.sem[e], 1)
        self._commit(("e", e, self.cnt[e]), reads, writes)
        self.n_inst += 1

    def ops(self, e, fns, reads=(), writes=()):
        reads = self._trs(reads)
        writes = self._trs(writes)
        self._deps(e, reads, writes)
        ins = None
        for fn in fns:
            ins = fn(self.eng[e])
            self.n_inst += 1
        self.cnt[e] += 1
        ins.then_inc(self.sem[e], 1)
        self._commit(("e", e, self.cnt[e]), reads, writes)

    def dma(self, q, out, in_, own, reads=(), writes=()):
        own = own.tr if isinstance(own, Buf) else own
        reads = self._trs(reads)
        writes = self._trs(writes)
        sw = 1 if q == "pool" else 0
        if own.sem is None:
            own.sem = [None, None]
            own.dcnt = [0, 0]
            self.dma_trs.append(own)
        if own.sem[sw] is None:
            if self.sem_pool[sw]:
                own.sem[sw], own.dcnt[sw] = self.sem_pool[sw].pop()
            else:
                self.nsem += 1
                own.sem[sw] = self.es.enter_context(self.nc.semaphore("dsem%d" % self.nsem))
                own.dcnt[sw] = 0
        for i in (0, 1):
            if own.sem[i] is not None and own.dcnt[i] > 0:
                self._wait(q, ("d", own.sem[i], own.dcnt[i]))
        self._deps(q, reads, writes)
        ins = self.eng[q].dma_start(out=out, in_=in_)
        own.dcnt[sw] += 16
        ins.then_inc(own.sem[sw], 16)
        self._commit(("d", own.sem[sw], own.dcnt[sw]), reads, writes)
        self.n_inst += 1

    def load(self, q, buf, dst, src):
        self.dma(q, dst, src, buf, writes=[buf])

    def store(self, q, buf, dst, src):
        self.dma(q, dst, src, buf, reads=[buf])

    def barrier(self):
        for e in self.ENG:
            for f in self.ENG:
                if f != e and self.cnt[f] > 0:
                    self._wait(e, ("e", f, self.cnt[f]))
            for t in self.dma_trs:
                for i in (0, 1):
                    if t.sem[i] is not None and t.dcnt[i] > 0:
                        self._wait(e, ("d", t.sem[i], t.dcnt[i]))
        for t in self.dma_trs:
            for i in (0, 1):
                if t.sem[i] is not None:
                    self.sem_pool[i].append((t.sem[i], t.dcnt[i]))
            t.sem = None
            t.dcnt = 0
        self.dma_trs = []


class Ring:
    def __init__(self, bufs):
        self.bufs = bufs
        self.i = 0

    def next(self):
        b = self.bufs[self.i % len(self.bufs)]
        self.i += 1
        return b


def build(T, CTX, DEPTH, dbg=()):
    nc = bass.Bass("TRN2", target_bir_lowering=False)
    fw = FW(nc)
    TT = CTX + T
    L = DEPTH

    def din(name, shape, dt=F32):
        return nc.dram_tensor(name, shape, dt, kind="ExternalInput").ap()

    x_in = din("x", [T, D])
    ctx_in = din("ctx", [CTX, D])
    cvec_in = din("cvec", [2, D])
    w_mod = din("w_mod", [L, D, 6 * D])
    b_mod = din("b_mod", [L, 6 * D])
    norm_mix = din("norm_mix", [L, D])
    w_in = din("w_in", [L, D, D_IN])
    rw_shift = din("rw_shift", [L, 3, RW_COLS])
    dec_w0 = din("dec_w0", [L, 2, D_RWKV])
    dec_up = din("dec_up", [L, 2, 96, D_RWKV])
    iclr_a0 = din("iclr_a0", [L, 2, D_RWKV])
    iclr_up = din("iclr_up", [L, 2, 96, D_RWKV])
    k_k = din("k_k", [L, D_RWKV])
    k_a = din("k_a", [L, D_RWKV])
    r_k = din("r_k", [L, D_RWKV])
    ln_w = din("ln_w", [L, D_RWKV])
    ln_b = din("ln_b", [L, D_RWKV])
    g_up = din("g_up", [L, 256, D_RWKV])
    conv_w = din("conv_w", [L, 3, 512])
    w_out = din("w_out", [L, D, D])
    norm_ffn = din("norm_ffn", [L, D])
    w_gate = din("w_gate", [L, D, D_FF])
    w_up = din("w_up", [L, D, D_FF])
    w_down = din("w_down", [L, D_FF, D])
    norm_final = din("norm_final", [1, D])
    ftab_x = din("ftab_x", [2, T, T], BF16)
    ftab_c = din("ftab_c", [2, CTX, CTX], BF16)
    c64_in = din("c64", [128, 256], BF16)
    out_ap = nc.dram_tensor("out", [T, D], F32, kind="ExternalOutput").ap()

    def scratch(name, shape, dt):
        kind = "ExternalOutput" if name in dbg else "Internal"
        return nc.dram_tensor(name, shape, dt, kind=kind).ap()

    XT = scratch("XT", [D, TT], F32)
    PXT = scratch("PXT", [PEND, TT], F32)
    YF = scratch("YF", [D_RWKV, TT], F32)
    YT = scratch("YT", [D, TT], BF16)
    MODS = scratch("MODS", [128, 96 * 2], F32)
    WIN_b = scratch("WIN_b", [11, 128, 16, 512], BF16)
    WOUT_b = scratch("WOUT_b", [8, 128, 16, 256], BF16)
    WG_b = scratch("WG_b", [22, 128, 16, 256], BF16)
    WU_b = scratch("WU_b", [22, 128, 16, 256], BF16)
    WD_b = scratch("WD_b", [16, 128, FC, 128], BF16)

    XTv = XT.rearrange("(c p) t -> p c t", p=128)

    tiles = []
    for c0 in range(0, CTX, NT):
        n = min(NT, CTX - c0)
        tiles.append((c0, n, True, c0 == 0, c0 + n == CTX))
    for c0 in range(0, T, NT):
        n = min(NT, T - c0)
        tiles.append((CTX + c0, n, False, c0 == 0, c0 + n == T))

    with fw.es:
        ident = fw.sb([128, 128], F32, name="ident")
        identb = fw.sb([128, 128], BF16, name="identb")
        onesb = fw.sb([128, 128], BF16, name="onesb")
        blk = fw.sb([128, 128], F32, name="blk")
        blkb = fw.sb([128, 128], BF16, name="blkb")
        iot = fw.sb([128, 128], F32, name="iot")
        m_f = fw.sb([128, 256], F32, name="m_f")
        m_b = fw.sb([128, 256], F32, name="m_b")
        mt_f = fw.sb([128, 128], F32, name="mt_f")
        mt_b = fw.sb([128, 128], F32, name="mt_b")
        cmask = fw.sb([128, NT], F32, name="cmask")
        c64 = fw.sb([128, 256], BF16, name="c64")
        sc_fm = fw.sb([128, 16, 2], F32, name="sc_fm")
        nf_fm = fw.sb([128, 16], F32, name="nf_fm")
        for b in (blkb, ident, identb, onesb, blk, iot, m_f, m_b, mt_f, mt_b, cmask, c64):
            b.tr.const = True

        P = fw.eng["pool"]
        fw.op("pool", lambda e: e.iota(iot[:], pattern=[[1, 128]], base=0, channel_multiplier=-1,
                                       allow_small_or_imprecise_dtypes=True), writes=[iot])
        fw.op("dve", lambda e: e.tensor_single_scalar(out=ident[:], in_=iot[:], scalar=0.0, op=ALU.is_equal),
              reads=[iot], writes=[ident])
        fw.op("dve", lambda e: e.tensor_copy(out=identb[:], in_=ident[:]), reads=[ident], writes=[identb])
        fw.op("dve", lambda e: e.memset(onesb[:], 1.0), writes=[onesb])
        fw.op("dve", lambda e: e.memset(blk[:], 0.0), writes=[blk])
        fw.op("dve", lambda e: e.memset(blk[0:64, 0:64], 1.0), writes=[blk])
        fw.op("dve", lambda e: e.memset(blk[64:128, 64:128], 1.0), writes=[blk])
        fw.op("dve", lambda e: e.tensor_copy(out=blkb[:], in_=blk[:]), reads=[blk], writes=[blkb])
        fw.op("dve", lambda e: e.tensor_single_scalar(out=m_f[:, 0:128], in_=iot[:], scalar=0.0, op=ALU.is_gt), reads=[iot], writes=[m_f])
        fw.op("dve", lambda e: e.tensor_single_scalar(out=m_f[:, 128:256], in_=iot[:], scalar=0.0, op=ALU.is_ge), reads=[iot], writes=[m_f])
        fw.op("dve", lambda e: e.tensor_single_scalar(out=m_b[:, 0:128], in_=iot[:], scalar=0.0, op=ALU.is_lt), reads=[iot], writes=[m_b])
        fw.op("dve", lambda e: e.tensor_single_scalar(out=m_b[:, 128:256], in_=iot[:], scalar=0.0, op=ALU.is_le), reads=[iot], writes=[m_b])
        fw.op("dve", lambda e: e.tensor_single_scalar(out=mt_f[:], in_=iot[:], scalar=0.0, op=ALU.is_lt), reads=[iot], writes=[mt_f])
        fw.op("dve", lambda e: e.tensor_single_scalar(out=mt_b[:], in_=iot[:], scalar=0.0, op=ALU.is_gt), reads=[iot], writes=[mt_b])
        fw.op("dve", lambda e: e.memset(cmask[:], 1.0), writes=[cmask])
        for c in range(NT // CH):
            fw.op("dve", lambda e, c=c: e.memset(cmask[:, c * CH:c * CH + 1], 0.0), writes=[cmask])
        fw.load("sp", c64, c64[:], c64_in[:, :])

        def rows_to_fm(es, rows, R, nq, dst, dst_q0=0):
            psr = fw.ps([128, 16, R], F32, es=es, name="psr")
            for q0 in range(0, nq, 16):
                qn = min(16, nq - q0)
                fw.ops("pe", [lambda e, q=q: e.matmul(psr[:, q - q0, :], lhsT=rows[0:R, q * 128:(q + 1) * 128],
                                                      rhs=ident[0:R, 0:R], start=True, stop=True)
                              for q in range(q0, q0 + qn)], reads=[rows, ident], writes=[psr])
                fw.op("dve", lambda e: e.tensor_copy(out=dst[:, dst_q0 + q0:dst_q0 + q0 + qn, 0:R], in_=psr[:, 0:qn, :]),
                      reads=[psr], writes=[dst])

        with ExitStack() as es:
            crow = fw.sb([2, D], F32, es=es)
            fw.load("sp", crow, crow[:], cvec_in[:, :])
            fw.op("act", lambda e: e.activation(out=crow[:], in_=crow[:], func=AF.Silu), reads=[crow], writes=[crow])
            rows_to_fm(es, crow, 2, 16, sc_fm)
            nrow = fw.sb([1, D], F32, es=es)
            fw.load("sp", nrow, nrow[:], norm_final[:, :])
            nf3 = fw.sb([128, 16, 1], F32, es=es)
            rows_to_fm(es, nrow, 1, 16, nf3)
            fw.op("dve", lambda e: e.tensor_copy(out=nf_fm[:], in_=nf3[:, :, 0]), reads=[nf3], writes=[nf_fm])

            inb = [fw.sb([128, D], F32, es=es) for _ in range(2)]
            stg = [fw.sb([128, 16, 128], F32, es=es) for _ in range(2)]
            pst = [fw.ps([128, 4, 128], F32, es=es) for _ in range(4)]
            k = 0
            for (src, n0, c0) in ((ctx_in, CTX, 0), (x_in, T, CTX)):
                for tb in range(n0 // 128):
                    ib = inb[k % 2]
                    sg = stg[k % 2]
                    fw.load("sp", ib, ib[:], src[tb * 128:(tb + 1) * 128, :])
                    for g4 in range(4):
                        pt = pst[g4 % 4]
                        fw.ops("pe", [lambda e, j=j: e.transpose(pt[:, j, :], ib[:, (g4 * 4 + j) * 128:(g4 * 4 + j + 1) * 128], ident[:])
                                      for j in range(4)], reads=[ib, ident], writes=[pt])
                        fw.op("dve" if g4 % 2 == 0 else "act",
                              (lambda e: e.tensor_copy(out=sg[:, g4 * 4:(g4 + 1) * 4, :], in_=pt[:])) if g4 % 2 == 0 else
                              (lambda e: e.activation(out=sg[:, g4 * 4:(g4 + 1) * 4, :], in_=pt[:], func=AF.Copy)),
                              reads=[pt], writes=[sg])
                    fw.store("pool", sg, XTv[:, :, c0 + tb * 128:c0 + (tb + 1) * 128], sg[:])
                    k += 1
        fw.barrier()

        for li in range(L):
            last_layer = (li == L - 1)
            with ExitStack() as esl:
              with ExitStack() as es:
                st16 = Ring([fw.sb([128, 16, 512], BF16, es=es) for _ in range(3)])
                for g in range(11):
                    b = st16.next()
                    pc0 = g * 512
                    if g < 6:
                        fw.load("pool", b, b[:], w_in[li, :, pc0:pc0 + 512].rearrange("(c p) n -> p c n", p=128))
                    elif g == 6:
                        fw.load("pool", b, b[:, :, 0:448], w_in[li, :, pc0:pc0 + 448].rearrange("(c p) n -> p c n", p=128))
                    else:
                        fw.load("pool", b, b[:], w_in[li, :, pc0 - 64:pc0 - 64 + 512].rearrange("(c p) n -> p c n", p=128))
                    fw.store("sp", b, WIN_b[g], b[:])
                for (wsrc, wdst, ng) in ((w_out, WOUT_b, 4), (w_gate, WG_b, 11), (w_up, WU_b, 11)):
                    for g in range(ng):
                        b = st16.next()
                        fw.load("pool", b, b[:], wsrc[li, :, g * 512:(g + 1) * 512].rearrange("(c p) n -> p c n", p=128))
                        fw.store("sp", b, wdst[2 * g], b[:, :, 0:256])
                        fw.store("sp", b, wdst[2 * g + 1], b[:, :, 256:512])
                st44 = Ring([fw.sb([128, FC, 256], BF16, es=es) for _ in range(2)])
                for g in range(8):
                    b = st44.next()
                    fw.load("pool", b, b[:], w_down[li, :, g * 256:(g + 1) * 256].rearrange("(c p) n -> p c n", p=128))
                    fw.store("sp", b, WD_b[2 * g], b[:, :, 0:128])
                    fw.store("sp", b, WD_b[2 * g + 1], b[:, :, 128:256])
                fw.barrier()
              vfm = fw.sb([128, 96, 24], F32, name="vfm", es=esl)
              cw_fm = fw.sb([128, 4, 3], F32, name="cw_fm", es=esl)
              mods = fw.sb([128, 96, 2], F32, name="mods", es=esl)
              A1 = fw.sb([128, 16, 2], F32, name="A1", es=esl)
              A2 = fw.sb([128, 16, 2], F32, name="A2", es=esl)
              omk = fw.sb([128, 8], F32, name="omk", es=esl)
              omk2 = fw.sb([128, 8], F32, name="omk2", es=esl)
              with ExitStack() as es:
                NR = 24
                rowb = fw.sb([NR, 6 * D], F32, es=es)
                fw.op("pool", lambda e: e.memset(rowb[:], 0.0), writes=[rowb])
                rowspec = [(0, b_mod[li:li + 1, :], 6 * D), (1, norm_mix[li:li + 1, :], D), (2, norm_ffn[li:li + 1, :], D),
                           (3, rw_shift[li, :, 0:3072], 3072), (6, rw_shift[li, :, WD0:WD0 + 96], 96),
                           (9, rw_shift[li, :, AD0:AD0 + 96], 96), (12, rw_shift[li, :, GD0:GD0 + 256], 256),
                           (15, dec_w0[li], D_RWKV), (17, iclr_a0[li], D_RWKV), (19, k_k[li:li + 1, :], D_RWKV),
                           (20, k_a[li:li + 1, :], D_RWKV), (21, r_k[li:li + 1, :], D_RWKV), (22, ln_w[li:li + 1, :], D_RWKV),
                           (23, ln_b[li:li + 1, :], D_RWKV)]
                for (r0, src, ln) in rowspec:
                    nr = src.shape[0]
                    fw.dma("sp", rowb[r0:r0 + nr, 0:ln], src, rowb, writes=[rowb])
                crow3 = fw.sb([3, 512], F32, es=es)
                fw.load("sp", crow3, crow3[:], conv_w[li])
                rows_to_fm(es, rowb, NR, 96, vfm)
                rows_to_fm(es, crow3, 3, 4, cw_fm)

                wm = Ring([fw.sb([128, 16, 512], F32, es=es) for _ in range(2)])
                psm = Ring([fw.ps([128, 4, 2], F32, es=es) for _ in range(2)])
                for g in range(24):
                    b = wm.next()
                    fw.load("sp" if g % 2 == 0 else "act", b, b[:], w_mod[li, :, g * 512:(g + 1) * 512].rearrange("(c p) n -> p c n", p=128))
                    pm = psm.next()
                    fns = []
                    for j in range(4):
                        for kc in range(16):
                            fns.append(lambda e, j=j, kc=kc: e.matmul(pm[:, j, :], lhsT=b[:, kc, j * 128:(j + 1) * 128], rhs=sc_fm[:, kc, :],
                                                                      start=(kc == 0), stop=(kc == 15)))
                    fw.ops("pe", fns, reads=[b, sc_fm], writes=[pm])
                    for r in range(2):
                        fw.op("dve", lambda e, r=r: e.tensor_tensor(out=mods[:, g * 4:(g + 1) * 4, r], in0=pm[:, :, r],
                                                                    in1=vfm[:, g * 4:(g + 1) * 4, 0], op=ALU.add),
                              reads=[pm, vfm], writes=[mods])
                for r in range(2):
                    fw.op("dve", lambda e, r=r: e.scalar_tensor_tensor(out=A1[:, :, r], in0=mods[:, 16:32, r], scalar=1.0, in1=vfm[:, 0:16, 1],
                                                                       op0=ALU.add, op1=ALU.mult), reads=[mods, vfm], writes=[A1])
                    fw.op("dve", lambda e, r=r: e.scalar_tensor_tensor(out=A2[:, :, r], in0=mods[:, 64:80, r], scalar=1.0, in1=vfm[:, 0:16, 2],
                                                                       op0=ALU.add, op1=ALU.mult), reads=[mods, vfm], writes=[A2])
                if "MODS" in dbg:
                    fw.store("sp", mods, MODS, mods[:].rearrange("p q r -> p (q r)"))
                fw.op("dve", lambda e: e.tensor_scalar(out=omk[:], in0=vfm[:, 0:8, 20], scalar1=-1.0, scalar2=1.0, op0=ALU.mult, op1=ALU.add),
                      reads=[vfm], writes=[omk])
                fw.op("dve", lambda e: e.tensor_scalar(out=omk2[:], in0=omk[:], scalar1=2.0, scalar2=None, op0=ALU.mult), reads=[omk], writes=[omk2])
                fw.barrier()
              if True:
                with ExitStack() as es1:
                    xs = fw.sb([128, 16, NT], F32, es=es1)
                    sq = fw.sb([128, 16, NT], BF16, es=es1)
                    rstd = fw.sb([128, NT], F32, es=es1)
                    tmp = Ring([fw.sb([128, NT], F32, es=es1) for _ in range(2)])
                    xn = fw.sb([128, 16, 2 * NT], BF16, es=es1)
                    wr = Ring([fw.sb([128, 16, 512], BF16, es=es1) for _ in range(2)])
                    pss = fw.ps([128, NT], F32, es=es1)
                    pso = Ring([fw.ps([128, 2 * NT], F32, es=es1) for _ in range(3)])
                    ost = Ring([fw.sb([128, 2 * NT], F32, es=es1) for _ in range(3)])
                    supers = []
                    for c0 in range(0, CTX, 2 * NT):
                        supers.append((c0, min(2 * NT, CTX - c0), 1))
                    for c0 in range(0, T, 2 * NT):
                        supers.append((CTX + c0, min(2 * NT, T - c0), 0))

                    def norm_mod(col0, n, r, Amod, sh_q0, dst, dst0, src_buf=None):
                        xb = src_buf
                        if xb is None:
                            xb = xs
                            fw.load("sp", xs, xs[:, :, 0:n], XTv[:, :, col0:col0 + n])
                        fw.op("act", lambda e: e.activation(out=sq[:, :, 0:n], in_=xb[:, :, 0:n], func=AF.Square), reads=[xb], writes=[sq])
                        fw.ops("pe", [lambda e, dc=dc: e.matmul(pss[:, 0:n], lhsT=onesb[:], rhs=sq[:, dc, 0:n], start=(dc == 0), stop=(dc == 15))
                                      for dc in range(16)], reads=[sq, onesb], writes=[pss])
                        fw.op("act", lambda e: e.activation(out=rstd[:, 0:n], in_=pss[:, 0:n], func=AF.Sqrt, bias=RMS_EPS, scale=1.0 / D),
                              reads=[pss], writes=[rstd])
                        fw.op("dve", lambda e: e.reciprocal(out=rstd[:, 0:n], in_=rstd[:, 0:n]), reads=[rstd], writes=[rstd])
                        for dc in range(16):
                            tb = tmp.next()
                            fw.op("dve", lambda e, dc=dc: e.scalar_tensor_tensor(out=tb[:, 0:n], in0=xb[:, dc, 0:n], scalar=Amod[:, dc, r:r + 1],
                                                                                 in1=rstd[:, 0:n], op0=ALU.mult, op1=ALU.mult),
                                  reads=[xb, Amod, rstd], writes=[tb])
                            fw.op("act", lambda e, dc=dc: e.activation(out=dst[:, dc, dst0:dst0 + n], in_=tb[:, 0:n], func=AF.Identity,
                                                                       bias=mods[:, sh_q0 + dc, r:r + 1], scale=1.0),
                                  reads=[tb, mods], writes=[dst])

                    ev = 0
                    for (s0, sn, r) in supers:
                        for o in range(0, sn, NT):
                            norm_mod(s0 + o, min(NT, sn - o), r, A1, 0, xn, o)
                        for g in range(11):
                            wb = wr.next()
                            fw.load("act", wb, wb[:], WIN_b[g])
                            for j in range(4):
                                pq = g * 4 + j
                                if pq == 27:
                                    pass
                                po = pso.next()
                                fns = []
                                for h0 in range(0, sn, NT):
                                    hn = min(NT, sn - h0)
                                    for kc in range(16):
                                        fns.append(lambda e, h0=h0, hn=hn, kc=kc, j=j: e.matmul(po[:, h0:h0 + hn], lhsT=wb[:, kc, j * 128:(j + 1) * 128],
                                                                                               rhs=xn[:, kc, h0:h0 + hn], start=(kc == 0), stop=(kc == 15)))
                                fw.ops("pe", fns, reads=[wb, xn], writes=[po])
                                ob = ost.next()
                                if ev % 2 == 0:
                                    fw.op("dve", lambda e: e.tensor_copy(out=ob[:, 0:sn], in_=po[:, 0:sn]), reads=[po], writes=[ob])
                                else:
                                    fw.op("act", lambda e: e.activation(out=ob[:, 0:sn], in_=po[:, 0:sn], func=AF.Copy), reads=[po], writes=[ob])
                                ev += 1
                                fw.store("sp", ob, PXT[pq * 128:(pq + 1) * 128, s0:s0 + sn], ob[:, 0:sn])
                fw.barrier()
                if "stop_p1" in dbg:
                    break

                with ExitStack() as es2:
                    HP = 8
                    SL = 9
                    for d_ in dbg:
                        if d_.startswith('sl='):
                            SL = int(d_[3:])
                    SDT = F32 if 'scan32' in dbg else BF16
                    identS = ident if 'scan32' in dbg else identb
                    IDT = BF16 if 'inv16' in dbg else F32
                    decw = fw.sb([96, 2, D_RWKV], F32, es=es2)
                    iclw = fw.sb([96, 2, D_RWKV], F32, es=es2)
                    gupw = fw.sb([128, 2, D_RWKV], BF16, es=es2)
                    for d in range(2):
                        fw.dma("sp", decw[:, d, :], dec_up[li, d], decw, writes=[decw])
                        fw.dma("sp", iclw[:, d, :], iclr_up[li, d], iclw, writes=[iclw])
                    fw.load("pool", gupw, gupw[:], g_up[li].rearrange("(c p) n -> p c n", p=128))
                    for b in (decw, iclw, gupw):
                        pass
                    S2g = [fw.sb([128, 4, 128], F32, es=es2) for _ in range(2)]
                    S2bg = [fw.sb([128, 4, 128], SDT, es=es2) for _ in range(2)]
                    wdh = fw.sb([96, NT + 2], F32, es=es2)
                    adh = fw.sb([96, NT + 2], F32, es=es2)
                    gdh = fw.sb([128, 2, NT + 2], F32, es=es2)
                    wdt2 = [fw.sb([96, NT], F32, es=es2) for _ in range(2)]
                    ads2 = [fw.sb([96, NT], F32, es=es2) for _ in range(2)]
                    gds2 = [fw.sb([128, 2, NT], BF16, es=es2) for _ in range(2)]
                    gtmp = fw.sb([128, 2, NT], F32, es=es2)
                    rkv_h = Ring([[fw.sb([128, NT + 2], F32, es=es2) for _ in range(3)] for _ in range(1)])
                    r_s = Ring([fw.sb([128, NT], F32, es=es2) for _ in range(1)])
                    ks_s2 = [[fw.sb([128, NT], BF16, es=es2) for _ in range(4)] for _ in range(2)]
                    rkb = Ring([[fw.sb([128, NT + 2], BF16, es=es2) for _ in range(3)] for _ in range(1)])
                    dgr = Ring([fw.sb([128, 9, 128], BF16, es=es2) for _ in range(2)])
                    vb_s2 = [[fw.sb([128, NT], SDT, es=es2) for _ in range(4)] for _ in range(2)]
                    strm2 = [[fw.sb([128, NT // CH, 4, CH], SDT, es=es2) for _ in range(4)] for _ in range(2)]
                    gC2 = [[fw.sb([128, NT // CH], F32, es=es2) for _ in range(4)] for _ in range(2)]
                    ybuf = fw.sb([128, 4, NT], F32, es=es2)
                    tw = Ring([fw.sb([128, NT], F32, es=es2) for _ in range(11)])
                    TTall = fw.sb([128, 4, 3, 128], SDT, es=es2)
                    GBK = fw.sb([128, 8, 2, 256], SDT, es=es2)
                    PkAll = [fw.sb([128, 8, 2, 128], IDT, es=es2) for _ in range(2)]
                    Zt = [fw.sb([128, 8, 64], IDT, es=es2) for _ in range(2)]
                    UP = fw.sb([128, 4, 128], SDT, es=es2)
                    stm = fw.sb([128, 4, 128], F32, es=es2)
                    psA = Ring([fw.ps([128, 512], F32, es=es2) for _ in range(3)])
                    psZ = Ring([fw.ps([128, 512], F32, es=es2) for _ in range(3)])
                    psP = Ring([fw.ps([128, 512], F32, es=es2) for _ in range(2)])

                    yio = Ring([fw.sb([128, NT], F32, es=es2) for _ in range(2)])
                    yob = Ring([fw.sb([128, NT], BF16, es=es2) for _ in range(2)])

                    for hg_ in range(2):
                        fw.op("pool", lambda e, hg_=hg_: e.memset(S2g[hg_][:], 0.0), writes=[S2g[hg_]])
                        fw.op("pool", lambda e, hg_=hg_: e.memset(S2bg[hg_][:], 0.0), writes=[S2bg[hg_]])

                    ptmp = Ring([fw.sb([128, NT], F32, es=es2) for _ in range(2)])

                    def shift3(eng, dst, src, n, wq, rows, row0, dv=None, sv_=None):
                        dv = dv or (lambda: dst[0:rows, 0:n])
                        sv_ = sv_ or (lambda a, b_: src[0:rows, a:b_])
                        w = lambda tap: vfm[0:rows, wq, row0 + tap:row0 + tap + 1]
                        fw.op(eng, lambda e: e.tensor_scalar(out=dv(), in0=sv_(1, n + 1), scalar1=w(1), scalar2=None, op0=ALU.mult), reads=[src, vfm], writes=[dst])
                        for tap, a in ((0, 0), (2, 2)):
                            if eng == "dve":
                                fw.op(eng, lambda e: e.scalar_tensor_tensor(out=dv(), in0=sv_(a, a + n), scalar=w(tap), in1=dv(), op0=ALU.mult, op1=ALU.add),
                                      reads=[src, vfm, dst], writes=[dst])
                            else:
                                pt_ = ptmp.next()
                                fw.op(eng, lambda e: e.tensor_scalar(out=pt_[0:rows, 0:n], in0=sv_(a, a + n), scalar1=w(tap), scalar2=None, op0=ALU.mult),
                                      reads=[src, vfm], writes=[pt_])
                                fw.op(eng, lambda e: e.tensor_tensor(out=dv(), in0=dv(), in1=pt_[0:rows, 0:n], op=ALU.add), reads=[dst, pt_], writes=[dst])

                    def load_halo(q, buf, dst_rows, row_a, row_b, col0, n, first, last):
                        lo = 1 if first else 0
                        hi = n + 1 if last else n + 2
                        if first:
                            fw.op("pool", lambda e: e.memset(dst_rows[:, 0:1], 0.0), writes=[buf])
                        if last:
                            fw.op("pool", lambda e: e.memset(dst_rows[:, n + 1:n + 2], 0.0), writes=[buf])
                        fw.dma(q, dst_rows[:, lo:hi], PXT[row_a:row_b, col0 - 1 + lo:col0 - 1 + hi], buf, writes=[buf])

                    for pas in (0, 1):
                        if pas == 0:
                            order = list(tiles)
                        else:
                            ctx_t = [t for t in tiles if t[2]]
                            x_t = [t for t in tiles if not t[2]]
                            order = ctx_t[::-1] + x_t[::-1]
                        msk = m_f if pas == 0 else m_b
                        mskT = mt_f if pas == 0 else mt_b
                        pump_ref = [lambda k: None]
                        pump_k = [4]

                        def unpack(tile):
                            (col0, n, is_ctx, first, last) = tile
                            return col0, n, is_ctx, first, last, n // CH, (is_ctx and last_layer)
                        def lora_shared(tile, lp):
                            col0, n, is_ctx, first, last, ncn, skip_fin = unpack(tile)
                            wdt, ads, gds = wdt2[lp], ads2[lp], gds2[lp]
                            load_halo("sp", wdh, wdh[0:96, 0:n + 2], WD0, WD0 + 96, col0, n, first, last)
                            load_halo("sp", adh, adh[0:96, 0:n + 2], AD0, AD0 + 96, col0, n, first, last)
                            shift3("pool", wdt, wdh, n, 0, 96, 6)
                            fw.op("act", lambda e: e.activation(out=wdt[:, 0:n], in_=wdt[:, 0:n], func=AF.Tanh), reads=[wdt], writes=[wdt])
                            shift3("pool", ads, adh, n, 0, 96, 9)
                            if pas == 1 and not skip_fin:
                                for c2 in range(2):
                                    load_halo("sp", gdh, gdh[:, c2, 0:n + 2], GD0 + c2 * 128, GD0 + (c2 + 1) * 128, col0, n, first, last)
                                for c2 in range(2):
                                    shift3("pool", gtmp, gdh, n, c2, 128, 12, dv=lambda c2=c2: gtmp[:, c2, 0:n], sv_=lambda a, b_, c2=c2: gdh[:, c2, a:b_])
                                fw.op("act", lambda e: e.activation(out=gds[:, :, 0:n], in_=gtmp[:, :, 0:n], func=AF.Sigmoid), reads=[gtmp], writes=[gds])
                        def prep_hp(tile, hp, sset, lp):
                            col0, n, is_ctx, first, last, ncn, skip_fin = unpack(tile)
                            wdt, ads, gds = wdt2[lp], ads2[lp], gds2[lp]
                            strm, vb_s, gC, ks_s = strm2[sset], vb_s2[sset], gC2[sset], ks_s2[sset]
                            hb = rkv_h.next()
                            for i3, base in enumerate((R0, K0, V0)):
                                load_halo("sp" if i3 != 1 else "act", hb[i3], hb[i3][:, 0:n + 2], base + hp * 128, base + (hp + 1) * 128, col0, n, first, last)
                                yield
                            rr = r_s.next()
                            kk_ = tw.next()
                            hbb = rkb.next()
                            dg = dgr.next()
                            for i3, q_ in enumerate((R0 // 128 + hp, K0 // 128 + hp, V0 // 128 + hp)):
                                fw.op("act", lambda e: e.activation(out=hbb[i3][:, 0:n + 2], in_=hb[i3][:, 0:n + 2], func=AF.Copy), reads=[hb[i3]], writes=[hbb[i3]])
                                yield
                                for tap in range(3):
                                    fw.op("dve", lambda e: e.tensor_scalar(out=dg[:, i3 * 3 + tap, :], in0=identb[:], scalar1=vfm[:, q_, 3 + tap:4 + tap], scalar2=None, op0=ALU.mult),
                                          reads=[identb, vfm], writes=[dg])
                                yield
                                psh = psP.next()
                                fw.ops("pe", [lambda e, tap=tap: e.matmul(psh[:, 0:n], lhsT=dg[:, i3 * 3 + tap, :], rhs=hbb[i3][:, tap:tap + n], start=(tap == 0), stop=(tap == 2))
                                              for tap in range(3)], reads=[dg, hbb[i3]], writes=[psh])
                                yield
                                dst_ = (rr, kk_, vb_s[hp % 4])[i3]
                                fw.op("act", lambda e: e.activation(out=dst_[:, 0:n], in_=psh[:, 0:n], func=AF.Copy), reads=[psh], writes=[dst_])
                                yield
                            t1 = tw.next()
                            fw.op("dve", lambda e: e.tensor_scalar(out=t1[:, 0:n], in0=kk_[:, 0:n], scalar1=vfm[:, hp, 19:20], scalar2=None, op0=ALU.mult),
                                  reads=[kk_, vfm], writes=[t1])
                            yield
                            t2 = tw.next()
                            fw.op("act", lambda e: e.activation(out=t2[:, 0:n], in_=t1[:, 0:n], func=AF.Square), reads=[t1], writes=[t2])
                            yield
                            pa = psP.next()
                            fw.op("pe", lambda e: e.matmul(pa[:, 0:n], lhsT=blk[:], rhs=t2[:, 0:n], start=True, stop=True), reads=[blk, t2], writes=[pa])
                            yield
                            fw.op("act", lambda e: e.activation(out=t2[:, 0:n], in_=pa[:, 0:n], func=AF.Sqrt), reads=[pa], writes=[t2])
                            yield
                            fw.op("dve", lambda e: e.tensor_scalar(out=t2[:, 0:n], in0=t2[:, 0:n], scalar1=1e-12, scalar2=None, op0=ALU.max), reads=[t2], writes=[t2])
                            yield
                            fw.op("dve", lambda e: e.reciprocal(out=t2[:, 0:n], in_=t2[:, 0:n]), reads=[t2], writes=[t2])
                            yield
                            kkn = t1
                            fw.op("dve", lambda e: e.tensor_tensor(out=kkn[:, 0:n], in0=t1[:, 0:n], in1=t2[:, 0:n], op=ALU.mult), reads=[t1, t2], writes=[kkn])
                            yield
                            d = pas
                            pw = psP.next()
                            fw.op("pe", lambda e: e.matmul(pw[:, 0:n], lhsT=decw[:, d, hp * 128:(hp + 1) * 128], rhs=wdt[:, 0:n], start=True, stop=True),
                                  reads=[decw, wdt], writes=[pw])
                            yield
                            lw = tw.next()
                            fw.op("act", lambda e: e.activation(out=lw[:, 0:n], in_=pw[:, 0:n], func=AF.Sigmoid, bias=vfm[:, hp, 15 + d:16 + d], scale=1.0),
                                  reads=[pw, vfm], writes=[lw])
                            yield
                            fw.op("dve", lambda e: e.tensor_scalar(out=lw[:, 0:n], in0=lw[:, 0:n], scalar1=-0.6065306597126334, scalar2=None, op0=ALU.mult),
                                  reads=[lw], writes=[lw])
                            yield
                            pa2 = psP.next()
                            fw.op("pe", lambda e: e.matmul(pa2[:, 0:n], lhsT=iclw[:, d, hp * 128:(hp + 1) * 128], rhs=ads[:, 0:n], start=True, stop=True),
                                  reads=[iclw, ads], writes=[pa2])
                            yield
                            aa = tw.next()
                            fw.op("act", lambda e: e.activation(out=aa[:, 0:n], in_=pa2[:, 0:n], func=AF.Sigmoid, bias=vfm[:, hp, 17 + d:18 + d], scale=1.0),
                                  reads=[pa2, vfm], writes=[aa])
                            yield
                            kd = tw.next()
                            fw.op("dve", lambda e: e.tensor_scalar(out=kd[:, 0:n], in0=aa[:, 0:n], scalar1=vfm[:, hp, 20:21], scalar2=omk[:, hp:hp + 1],
                                                                   op0=ALU.mult, op1=ALU.add), reads=[aa, vfm, omk], writes=[kd])
                            yield
                            fw.op("dve", lambda e: e.tensor_tensor(out=kd[:, 0:n], in0=kd[:, 0:n], in1=kk_[:, 0:n], op=ALU.mult), reads=[kd, kk_], writes=[kd])
                            yield
                            if pas == 1 and not skip_fin:
                                pa3 = psP.next()
                                fw.op("pe", lambda e: e.matmul(pa3[:, 0:n], lhsT=iclw[:, 0, hp * 128:(hp + 1) * 128], rhs=ads[:, 0:n], start=True, stop=True),
                                      reads=[iclw, ads], writes=[pa3])
                                yield
                                af = tw.next()
                                fw.op("act", lambda e: e.activation(out=af[:, 0:n], in_=pa3[:, 0:n], func=AF.Sigmoid, bias=vfm[:, hp, 17:18], scale=1.0),
                                      reads=[pa3, vfm], writes=[af])
                                yield
                                fw.op("dve", lambda e: e.tensor_tensor(out=af[:, 0:n], in0=af[:, 0:n], in1=aa[:, 0:n], op=ALU.add), reads=[af, aa], writes=[af])
                                yield
                                fw.op("dve", lambda e: e.tensor_scalar(out=af[:, 0:n], in0=af[:, 0:n], scalar1=vfm[:, hp, 20:21], scalar2=omk2[:, hp:hp + 1],
                                                                       op0=ALU.mult, op1=ALU.add), reads=[af, vfm, omk2], writes=[af])
                                yield
                                fw.op("dve", lambda e: e.tensor_tensor(out=af[:, 0:n], in0=af[:, 0:n], in1=kk_[:, 0:n], op=ALU.mult), reads=[af, kk_], writes=[af])
                                yield
                                fw.op("dve", lambda e: e.scalar_tensor_tensor(out=ks_s[hp % 4][:, 0:n], in0=af[:, 0:n], scalar=vfm[:, hp, 21:22], in1=rr[:, 0:n],
                                                                              op0=ALU.mult, op1=ALU.mult), reads=[af, vfm, rr], writes=[ks_s[hp % 4]])
                                yield
                            Lc = tw.next()
                            fw.op("dve", lambda e: e.tensor_tensor_scan(out=Lc[:, 0:n], data0=cmask[:, 0:n], data1=lw[:, 0:n], initial=0.0,
                                                                        op0=ALU.mult, op1=ALU.add), reads=[cmask, lw], writes=[Lc])
                            yield
                            Ginc, Gexc, Ginv = tw.next(), tw.next(), tw.next()
                            st = strm[hp % 4]
                            if pas == 0:
                                fw.op("act", lambda e: e.activation(out=Ginc[:, 0:n], in_=Lc[:, 0:n], func=AF.Exp), reads=[Lc], writes=[Ginc])
                                yield
                                fw.op("act", lambda e: e.activation(out=Ginv[:, 0:n], in_=Lc[:, 0:n], func=AF.Exp, scale=-1.0), reads=[Lc], writes=[Ginv])
                                yield
                                fw.op("dve", lambda e: e.tensor_tensor(out=Gexc[:, 0:n], in0=Lc[:, 0:n], in1=lw[:, 0:n], op=ALU.subtract), reads=[Lc, lw], writes=[Gexc])
                                yield
                                fw.op("act", lambda e: e.activation(out=Gexc[:, 0:n], in_=Gexc[:, 0:n], func=AF.Exp), reads=[Gexc], writes=[Gexc])
                                yield
                                for c in range(ncn):
                                    fw.op("pool", lambda e, c=c: e.tensor_copy(out=gC[hp % 4][:, c:c + 1], in_=Ginc[:, c * CH + CH - 1:c * CH + CH]),
                                          reads=[Ginc], writes=[gC[hp % 4]])
                                    yield
                            else:
                                for c in range(ncn):
                                    fw.op("dve", lambda e, c=c: e.tensor_scalar(out=Gexc[:, c * CH:(c + 1) * CH], in0=Lc[:, c * CH:(c + 1) * CH],
                                                                                scalar1=Lc[:, c * CH + CH - 1:c * CH + CH], scalar2=None, op0=ALU.subtract),
                                          reads=[Lc], writes=[Gexc])
                                    yield
                                    fw.op("act", lambda e, c=c: e.activation(out=gC[hp % 4][:, c:c + 1], in_=Lc[:, c * CH + CH - 1:c * CH + CH], func=AF.Exp),
                                          reads=[Lc], writes=[gC[hp % 4]])
                                    yield
                                fw.op("dve", lambda e: e.tensor_tensor(out=Ginc[:, 0:n], in0=lw[:, 0:n], in1=Gexc[:, 0:n], op=ALU.subtract), reads=[lw, Gexc], writes=[Ginc])
                                yield
                                fw.op("act", lambda e: e.activation(out=Ginv[:, 0:n], in_=Ginc[:, 0:n], func=AF.Exp, scale=-1.0), reads=[Ginc], writes=[Ginv])
                                yield
                                fw.op("act", lambda e: e.activation(out=Ginc[:, 0:n], in_=Ginc[:, 0:n], func=AF.Exp), reads=[Ginc], writes=[Ginc])
                                yield
                                fw.op("act", lambda e: e.activation(out=Gexc[:, 0:n], in_=Gexc[:, 0:n], func=AF.Exp, scale=-1.0), reads=[Gexc], writes=[Gexc])
                                yield
                            sv = lambda i4: st[:, 0:ncn, i4, :]
                            v3 = lambda b_: b_[:, 0:n].rearrange("p (c t) -> p c t", t=CH)
                            fw.op("dve", lambda e: e.scalar_tensor_tensor(out=sv(0), in0=v3(kkn), scalar=-1.0, in1=v3(Gexc), op0=ALU.mult, op1=ALU.mult),
                                  reads=[kkn, Gexc], writes=[st])
                            yield
                            fw.op("pool", lambda e: e.tensor_tensor(out=sv(1), in0=v3(rr), in1=v3(Ginc), op=ALU.mult), reads=[rr, Ginc], writes=[st])
                            yield
                            fw.op("dve", lambda e: e.tensor_tensor(out=aa[:, 0:n], in0=aa[:, 0:n], in1=kkn[:, 0:n], op=ALU.mult), reads=[aa, kkn], writes=[aa])
                            yield
                            fw.op("dve", lambda e: e.tensor_tensor(out=sv(2), in0=v3(aa), in1=v3(Ginv), op=ALU.mult), reads=[aa, Ginv], writes=[st])
                            yield
                            fw.op("pool", lambda e: e.tensor_tensor(out=sv(3), in0=v3(kd), in1=v3(Ginv), op=ALU.mult), reads=[kd, Ginv], writes=[st])
                            yield

                        def scan_chunk(c, hg, sset):
                            strm, vb_s, gC = strm2[sset], vb_s2[sset], gC2[sset]
                            S2_, S2b_ = S2g[hg], S2bg[hg]
                            NLEV = 7
                            if True:
                                cs = slice(c * CH, (c + 1) * CH)
                                for g in range(4):
                                    st = strm[g]
                                    pb = psA.next()
                                    fw.ops("pe", [lambda e: e.matmul(pb[:, 0:128], lhsT=vb_s[g][:, cs], rhs=identS[:], start=True, stop=True),
                                                  lambda e: e.matmul(pb[:, 128:256], lhsT=st[:, c, 2, :], rhs=identS[:], start=True, stop=True),
                                                  lambda e: e.matmul(pb[:, 256:384], lhsT=st[:, c, 3, :], rhs=identS[:], start=True, stop=True)],
                                           reads=[vb_s[g], st, identS], writes=[pb])
                                    fw.op("act", lambda e: e.activation(out=TTall[:, g, :, :], in_=pb[:, 0:384].rearrange("p (a b) -> p a b", b=128), func=AF.Copy),
                                          reads=[pb], writes=[TTall])
                                pump_ref[0](pump_k[0])
                                for k in range(8):
                                    g, h = k // 2, k % 2
                                    st = strm[g]
                                    hs = slice(h * 64, (h + 1) * 64)
                                    pg = psA.next()
                                    fw.ops("pe", [lambda e: e.matmul(pg[:, 0:256], lhsT=st[hs, c, 2, :], rhs=st[hs, c, 0:2, :], start=True, stop=True),
                                                  lambda e: e.matmul(pg[:, 256:512], lhsT=st[hs, c, 3, :], rhs=st[hs, c, 0:2, :], start=True, stop=True)],
                                           reads=[st], writes=[pg])
                                    fw.op("dve", lambda e: e.tensor_tensor(out=GBK[:, k, 0, :], in0=pg[:, 0:256], in1=msk[:], op=ALU.mult), reads=[pg, msk], writes=[GBK])
                                    fw.op("dve", lambda e: e.tensor_tensor(out=GBK[:, k, 1, :], in0=pg[:, 256:512], in1=msk[:], op=ALU.mult), reads=[pg, msk], writes=[GBK])
                                    fw.op("dve", lambda e: e.tensor_tensor(out=PkAll[0][:, k, 0, :], in0=pg[:, 0:128], in1=msk[:, 0:128], op=ALU.mult),
                                          reads=[pg, msk], writes=[PkAll[0]])
                                for m in range(2):
                                    pt = psZ.next()
                                    fns = []
                                    for kk2 in range(4):
                                        k = m * 4 + kk2
                                        g, h = k // 2, k % 2
                                        fns.append(lambda e, kk2=kk2, g=g, h=h: e.matmul(pt[:, kk2 * 128:(kk2 + 1) * 128], lhsT=strm[g][h * 64:(h + 1) * 64, c, 0, :],
                                                                                        rhs=strm[g][h * 64:(h + 1) * 64, c, 2, :], start=True, stop=True))
                                    fw.ops("pe", fns, reads=[strm[(m * 4) // 2], strm[(m * 4) // 2 + 1]], writes=[pt])
                                    for kk2 in range(4):
                                        k = m * 4 + kk2
                                        fw.op("dve", lambda e, kk2=kk2, k=k: e.tensor_tensor(out=PkAll[0][:, k, 1, :], in0=pt[:, kk2 * 128:(kk2 + 1) * 128], in1=mskT[:], op=ALU.mult),
                                              reads=[pt, mskT], writes=[PkAll[0]])
                                pump_ref[0](pump_k[0])
                                pX = psZ.next()
                                fns = []
                                for g in range(4):
                                    st = strm[g]
                                    for h in range(2):
                                        k = g * 2 + h
                                        fns.append(lambda e, g=g, h=h, k=k, st=st: e.matmul(pX[:, k * 64:(k + 1) * 64], lhsT=st[:, c, 0, :], rhs=S2b_[:, g, h * 64:(h + 1) * 64],
                                                                                          start=True, stop=False))
                                        fns.append(lambda e, g=g, h=h, k=k: e.matmul(pX[:, k * 64:(k + 1) * 64], lhsT=GBK[:, k, 1, 0:128], rhs=TTall[:, g, 0, h * 64:(h + 1) * 64],
                                                                                   start=False, stop=True))
                                fw.ops("pe", fns, reads=[strm[0], strm[1], strm[2], strm[3], S2b_, GBK, TTall], writes=[pX])
                                fw.op("act", lambda e: e.activation(out=Zt[0][:], in_=pX[:].rearrange("p (a b) -> p a b", b=64), func=AF.Copy), reads=[pX], writes=[Zt[0]])
                                pump_ref[0](pump_k[0])
                                for lv in range(NLEV):
                                    if lv > 0:
                                        pump_ref[0](pump_k[0])
                                    Pc, Pn = PkAll[lv % 2], PkAll[(lv + 1) % 2]
                                    zi, zo = Zt[lv % 2], Zt[(lv + 1) % 2]
                                    pz = psZ.next()
                                    fw.ops("pe", [lambda e, k=k: e.matmul(pz[:, k * 64:(k + 1) * 64], lhsT=Pc[:, k, 0, :], rhs=zi[:, k, :], start=True, stop=True)
                                                  for k in range(8)], reads=[Pc, zi], writes=[pz])
                                    if lv == NLEV - 1:
                                        fw.op("dve", lambda e: e.tensor_tensor(out=UP[:].rearrange("p g (h i) -> p (g h) i", i=64), in0=pz[:].rearrange("p (a b) -> p a b", b=64),
                                                                               in1=zi[:], op=ALU.add), reads=[pz, zi], writes=[UP])
                                        break
                                    fw.op("dve", lambda e: e.tensor_tensor(out=zo[:], in0=pz[:].rearrange("p (a b) -> p a b", b=64), in1=zi[:], op=ALU.add),
                                          reads=[pz, zi], writes=[zo])
                                    if lv < NLEV - 2:
                                        for m in range(4):
                                            pq = psA.next()
                                            fns = []
                                            for k in (2 * m, 2 * m + 1):
                                                o = (k % 2) * 256
                                                fns.append(lambda e, k=k, o=o: e.matmul(pq[:, o:o + 128], lhsT=Pc[:, k, 1, :], rhs=Pc[:, k, 0, :], start=True, stop=True))
                                                fns.append(lambda e, k=k, o=o: e.matmul(pq[:, o + 128:o + 256], lhsT=Pc[:, k, 0, :], rhs=Pc[:, k, 1, :], start=True, stop=True))
                                            fw.ops("pe", fns, reads=[Pc], writes=[pq])
                                            dst = Pn[:, 2 * m:2 * m + 2, :, :]
                                            src = pq[:].rearrange("p (a b c) -> p a b c", a=2, b=2)
                                            if m % 2 == 0:
                                                fw.op("act", lambda e: e.activation(out=dst, in_=src, func=AF.Copy), reads=[pq], writes=[Pn])
                                            else:
                                                fw.op("dve", lambda e: e.tensor_copy(out=dst, in_=src), reads=[pq], writes=[Pn])
                                    else:
                                        for m in range(2):
                                            pq = psA.next()
                                            fw.ops("pe", [lambda e, k=k: e.matmul(pq[:, (k % 4) * 128:(k % 4 + 1) * 128], lhsT=Pc[:, k, 1, :], rhs=Pc[:, k, 0, :], start=True, stop=True)
                                                          for k in range(4 * m, 4 * m + 4)], reads=[Pc], writes=[pq])
                                            dst = Pn[:, 4 * m:4 * m + 4, 0, :]
                                            src = pq[:].rearrange("p (a b) -> p a b", b=128)
                                            if m % 2 == 0:
                                                fw.op("act", lambda e: e.activation(out=dst, in_=src, func=AF.Copy), reads=[pq], writes=[Pn])
                                            else:
                                                fw.op("dve", lambda e: e.tensor_copy(out=dst, in_=src), reads=[pq], writes=[Pn])
                                pump_ref[0](pump_k[0])
                                pY = psZ.next()
                                fns = []
                                for g in range(4):
                                    st = strm[g]
                                    for h in range(2):
                                        k = g * 2 + h
                                        ro = slice(h * 64, (h + 1) * 64)
                                        co = slice(g * 128, (g + 1) * 128)
                                        fns.append(lambda e, g=g, ro=ro, co=co, st=st: e.matmul(pY[ro, co], lhsT=S2b_[:, g, ro], rhs=st[:, c, 1, :], start=True, stop=False))
                                        fns.append(lambda e, g=g, k=k, ro=ro, co=co: e.matmul(pY[ro, co], lhsT=UP[:, g, ro], rhs=GBK[:, k, 0, 128:256], start=False, stop=False))
                                        fns.append(lambda e, g=g, k=k, ro=ro, co=co: e.matmul(pY[ro, co], lhsT=TTall[:, g, 0, ro], rhs=GBK[:, k, 1, 128:256], start=False, stop=True))
                                fw.ops("pe", fns, reads=[strm[0], strm[1], strm[2], strm[3], S2b_, UP, GBK, TTall], writes=[pY])
                                fw.op("act", lambda e: e.activation(out=ybuf[:, :, cs], in_=pY[:].rearrange("p (a b) -> p a b", b=128), func=AF.Copy), reads=[pY], writes=[ybuf])
                                pump_ref[0](pump_k[0])
                                pS = psZ.next()
                                fns = []
                                for g in range(4):
                                    co = slice(g * 128, (g + 1) * 128)
                                    fns.append(lambda e, g=g, co=co: e.matmul(pS[:, co], lhsT=TTall[:, g, 1, :], rhs=UP[:, g, :], start=True, stop=False))
                                    fns.append(lambda e, g=g, co=co: e.matmul(pS[:, co], lhsT=TTall[:, g, 2, :], rhs=TTall[:, g, 0, :], start=False, stop=True))
                                fw.ops("pe", fns, reads=[TTall, UP], writes=[pS])
                                fw.op("dve", lambda e: e.tensor_tensor(out=stm[:], in0=pS[:].rearrange("p (a b) -> p a b", b=128), in1=S2_[:], op=ALU.add),
                                      reads=[pS, S2_], writes=[stm])
                                for g in range(4):
                                    fw.op("dve", lambda e, g=g: e.scalar_tensor_tensor(out=S2_[:, g, :], in0=stm[:, g, :], scalar=gC[g][:, c:c + 1], in1=blk[:],
                                                                                       op0=ALU.mult, op1=ALU.mult), reads=[stm, gC[g], blk], writes=[S2_])
                                fw.op("pool", lambda e: e.tensor_copy(out=S2b_[:], in_=S2_[:]), reads=[S2_], writes=[S2b_])
                                pump_ref[0](pump_k[0])

                        def outputs(tile, hg, sset, lp):
                            col0, n, is_ctx, first, last, ncn, skip_fin = unpack(tile)
                            hps = list(range(hg * 4, hg * 4 + 4))
                            gds = gds2[lp]
                            vb_s, ks_s = vb_s2[sset], ks_s2[sset]
                            for hp in hps:
                                rows = slice(hp * 128, (hp + 1) * 128)
                                if pas == 0:
                                    fw.store("sp", ybuf, YF[rows, col0:col0 + n], ybuf[:, hp % 4, 0:n])
                                    continue
                                if skip_fin or 'no_fin' in dbg:
                                    continue
                                yf = yio.next()
                                fw.load("sp", yf, yf[:, 0:n], YF[rows, col0:col0 + n])
                                y = yf
                                fw.op("dve", lambda e: e.tensor_tensor(out=y[:, 0:n], in0=yf[:, 0:n], in1=ybuf[:, hp % 4, 0:n], op=ALU.add), reads=[yf, ybuf], writes=[y])
                                pm_ = psA.next()
                                fw.op("pe", lambda e: e.matmul(pm_[:, 0:n], lhsT=blk[:], rhs=y[:, 0:n], start=True, stop=True), reads=[blk, y], writes=[pm_])
                                dd = tw.next()
                                fw.op("dve", lambda e: e.scalar_tensor_tensor(out=dd[:, 0:n], in0=pm_[:, 0:n], scalar=-1.0 / 64, in1=y[:, 0:n], op0=ALU.mult, op1=ALU.add),
                                      reads=[pm_, y], writes=[dd])
                                d2 = tw.next()
                                fw.op("act", lambda e: e.activation(out=d2[:, 0:n], in_=dd[:, 0:n], func=AF.Square), reads=[dd], writes=[d2])
                                pv_ = psA.next()
                                fw.op("pe", lambda e: e.matmul(pv_[:, 0:n], lhsT=blk[:], rhs=d2[:, 0:n], start=True, stop=True), reads=[blk, d2], writes=[pv_])
                                fw.op("act", lambda e: e.activation(out=d2[:, 0:n], in_=pv_[:, 0:n], func=AF.Sqrt, bias=GN_EPS, scale=1.0 / 64), reads=[pv_], writes=[d2])
                                fw.op("dve", lambda e: e.reciprocal(out=d2[:, 0:n], in_=d2[:, 0:n]), reads=[d2], writes=[d2])
                                fw.op("dve", lambda e: e.tensor_tensor(out=dd[:, 0:n], in0=dd[:, 0:n], in1=d2[:, 0:n], op=ALU.mult), reads=[dd, d2], writes=[dd])
                                fw.op("dve", lambda e: e.tensor_scalar(out=dd[:, 0:n], in0=dd[:, 0:n], scalar1=vfm[:, hp, 22:23], scalar2=vfm[:, hp, 23:24],
                                                                       op0=ALU.mult, op1=ALU.add), reads=[dd, vfm], writes=[dd])
                                pb_ = psA.next()
                                fw.op("pe", lambda e: e.matmul(pb_[:, 0:n], lhsT=blkb[:], rhs=ks_s[hp % 4][:, 0:n], start=True, stop=True), reads=[blkb, ks_s[hp % 4]], writes=[pb_])
                                fw.op("dve", lambda e: e.tensor_tensor(out=d2[:, 0:n], in0=pb_[:, 0:n], in1=vb_s[hp % 4][:, 0:n], op=ALU.mult), reads=[pb_, vb_s[hp % 4]], writes=[d2])
                                fw.op("dve", lambda e: e.tensor_tensor(out=dd[:, 0:n], in0=dd[:, 0:n], in1=d2[:, 0:n], op=ALU.add), reads=[dd, d2], writes=[dd])
                                pg_ = psA.next()
                                fw.ops("pe", [lambda e, c2=c2: e.matmul(pg_[:, 0:n], lhsT=gupw[:, c2, hp * 128:(hp + 1) * 128], rhs=gds[:, c2, 0:n],
                                                                        start=(c2 == 0), stop=(c2 == 1)) for c2 in range(2)],
                                       reads=[gupw, gds], writes=[pg_])
                                yo = yob.next()
                                fw.op("dve", lambda e: e.tensor_tensor(out=yo[:, 0:n], in0=pg_[:, 0:n], in1=dd[:, 0:n], op=ALU.mult), reads=[pg_, dd], writes=[yo])
                                fw.store("sp", yo, YT[rows, col0:col0 + n], yo[:, 0:n])
                        items = [(ti, hg) for ti in range(len(order)) for hg in range(2)]
                        lora_shared(order[0], 0)
                        for g in range(4):
                            for _ in prep_hp(order[0], g, 0, 0):
                                pass
                        pending = [None]

                        def pump(k):
                            gen = pending[0]
                            if gen is None:
                                return
                            for _ in range(k):
                                try:
                                    next(gen)
                                except StopIteration:
                                    pending[0] = None
                                    return

                        def chain_prep(nx, sset):
                            for g in range(4):
                                for _ in prep_hp(order[nx[0]], nx[1] * 4 + g, sset, nx[0] % 2):
                                    yield

                        pump_ref[0] = pump
                        for j, (ti, hg) in enumerate(items):
                            tile = order[ti]
                            ncn = tile[1] // CH
                            nxt = items[j + 1] if j + 1 < len(items) else None
                            if nxt is not None and nxt[0] != ti:
                                lora_shared(order[nxt[0]], nxt[0] % 2)
                            pending[0] = chain_prep(nxt, (j + 1) % 2) if nxt is not None else None
                            pump_k[0] = max(1, (4 * 62) // (ncn * 14) + 1)
                            crange = list(range(ncn)) if pas == 0 else list(range(ncn))[::-1]
                            for c in crange:
                                scan_chunk(c, hg, j % 2)
                            pump(100000)
                            outputs(tile, hg, j % 2, ti % 2)
                        fw.barrier()
                        if pas == 0:
                            for hg_ in range(2):
                                fw.op("pool", lambda e, hg_=hg_: e.memset(S2g[hg_][:], 0.0), writes=[S2g[hg_]])
                                fw.op("pool", lambda e, hg_=hg_: e.memset(S2bg[hg_][:], 0.0), writes=[S2bg[hg_]])
                fw.barrier()
                if "stop_p2" in dbg:
                    break

                with ExitStack() as es3:
                    cin = Ring([[fw.sb([128, NT], F32, es=es3) for _ in range(3)] for _ in range(2)])
                    cu = Ring([fw.sb([128, NT], F32, es=es3) for _ in range(2)])
                    co = Ring([fw.sb([128, NT], F32, es=es3) for _ in range(2)])
                    cob = Ring([fw.sb([128, NT], BF16, es=es3) for _ in range(2)])
                    for (col0, n, is_ctx, first, last) in tiles:
                        if is_ctx and last_layer:
                            continue
                        for q in range(4):
                            ib = cin.next()
                            for i3, base in enumerate((PCG, PCX, PCB)):
                                fw.load("sp", ib[i3], ib[i3][:, 0:n], PXT[base + q * 128:base + (q + 1) * 128, col0:col0 + n])
                            u = cu.next()
                            o = co.next()
                            fw.op("pool", lambda e: e.tensor_tensor(out=u[:, 0:n], in0=ib[0][:, 0:n], in1=ib[1][:, 0:n], op=ALU.mult), reads=[ib[0], ib[1]], writes=[u])
                            fw.op("dve", lambda e: e.tensor_scalar(out=o[:, 0:n], in0=u[:, 0:n], scalar1=cw_fm[:, q, 1:2], scalar2=None, op0=ALU.mult),
                                  reads=[u, cw_fm], writes=[o])
                            if is_ctx:
                                assert first and last
                                W_ = n
                            else:
                                W_ = 64
                            u3 = u[:, 0:n].rearrange("p (r w) -> p r w", w=W_)
                            o3 = o[:, 0:n].rearrange("p (r w) -> p r w", w=W_)
                            fw.op("dve", lambda e: e.scalar_tensor_tensor(out=o3[:, :, 1:W_], in0=u3[:, :, 0:W_ - 1], scalar=cw_fm[:, q, 0:1], in1=o3[:, :, 1:W_],
                                                                          op0=ALU.mult, op1=ALU.add), reads=[u, cw_fm, o], writes=[o])
                            fw.op("dve", lambda e: e.scalar_tensor_tensor(out=o3[:, :, 0:W_ - 1], in0=u3[:, :, 1:W_], scalar=cw_fm[:, q, 2:3], in1=o3[:, :, 0:W_ - 1],
                                                                          op0=ALU.mult, op1=ALU.add), reads=[u, cw_fm, o], writes=[o])
                            ob = cob.next()
                            fw.op("pool", lambda e: e.tensor_tensor(out=ob[:, 0:n], in0=o[:, 0:n], in1=ib[2][:, 0:n], op=ALU.mult), reads=[o, ib[2]], writes=[ob])
                            fw.store("sp", ob, YT[1024 + q * 128:1024 + (q + 1) * 128, col0:col0 + n], ob[:, 0:n])
                fw.barrier()

                with ExitStack() as es4:
                    LCmax = T // 128
                    UC = fw.sb([128, LCmax, 512], BF16, es=es4)
                    US = fw.sb([128, LCmax, 512], BF16, es=es4)
                    uT = Ring([fw.sb([128, NT], BF16, es=es4) for _ in range(3)])
                    pu = Ring([fw.ps([128, 512], F32, es=es4) for _ in range(2)])
                    py = [fw.ps([128, 512], F32, es=es4) for _ in range(4)]
                    LG = 8
                    tabr = Ring([[fw.sb([128, LG, 512], BF16, es=es4) for _ in range(2)] for _ in range(2)])
                    fo = Ring([fw.sb([128, 512], BF16, es=es4) for _ in range(3)])
                    for (seq0, sn, tab, is_ctx) in ((0, CTX, ftab_c, True), (CTX, T, ftab_x, False)):
                        if is_ctx and last_layer:
                            continue
                        nlc = sn // 128
                        for t0 in range(0, sn, NT):
                            tn = min(NT, sn - t0)
                            for mc in range(4):
                                ub = uT.next()
                                fw.load("pool", ub, ub[:, 0:tn], PXT[PFT + mc * 128:PFT + (mc + 1) * 128, seq0 + t0:seq0 + t0 + tn])
                                for lc in range(tn // 128):
                                    p_ = pu.next()
                                    fw.op("pe", lambda e: e.matmul(p_[:, 0:256], lhsT=ub[:, lc * 128:(lc + 1) * 128], rhs=c64[:], start=True, stop=True),
                                          reads=[ub, c64], writes=[p_])
                                    glc = t0 // 128 + lc
                                    if (lc + mc) % 2 == 0:
                                        fw.op("dve", lambda e: e.tensor_copy(out=UC[:, glc, mc * 128:(mc + 1) * 128], in_=p_[:, 0:128]), reads=[p_], writes=[UC])
                                        fw.op("dve", lambda e: e.tensor_copy(out=US[:, glc, mc * 128:(mc + 1) * 128], in_=p_[:, 128:256]), reads=[p_], writes=[US])
                                    else:
                                        fw.op("act", lambda e: e.activation(out=UC[:, glc, mc * 128:(mc + 1) * 128], in_=p_[:, 0:128], func=AF.Copy), reads=[p_], writes=[UC])
                                        fw.op("act", lambda e: e.activation(out=US[:, glc, mc * 128:(mc + 1) * 128], in_=p_[:, 128:256], func=AF.Copy), reads=[p_], writes=[US])
                        for k0 in range(0, sn, 512):
                            kn = min(512, sn - k0)
                            for lg0 in range(0, nlc, LG):
                                lgn = min(LG, nlc - lg0)
                                tb = tabr.next()
                                for i2 in range(2):
                                    fw.load("sp" if i2 == 0 else "act", tb[i2], tb[i2][:, 0:lgn, 0:kn],
                                            tab[i2, lg0 * 128:(lg0 + lgn) * 128, k0:k0 + kn].rearrange("(c p) k -> p c k", p=128))
                                for mc in range(4):
                                    fns = []
                                    for l_ in range(lgn):
                                        lc = lg0 + l_
                                        fns.append(lambda e, l_=l_, lc=lc: e.matmul(py[mc][:, 0:kn], lhsT=UC[:, lc, mc * 128:(mc + 1) * 128], rhs=tb[0][:, l_, 0:kn],
                                                                                    start=(lc == 0), stop=False))
                                        fns.append(lambda e, l_=l_, lc=lc: e.matmul(py[mc][:, 0:kn], lhsT=US[:, lc, mc * 128:(mc + 1) * 128], rhs=tb[1][:, l_, 0:kn],
                                                                                    start=False, stop=(lc == nlc - 1)))
                                    fw.ops("pe", fns, reads=[UC, US, tb[0], tb[1]], writes=[py[mc]])
                            for mc in range(4):
                                ob = fo.next()
                                fw.op("act" if mc % 2 else "dve",
                                      (lambda e: e.activation(out=ob[:, 0:kn], in_=py[mc][:, 0:kn], func=AF.Copy)) if mc % 2 else
                                      (lambda e: e.tensor_copy(out=ob[:, 0:kn], in_=py[mc][:, 0:kn])), reads=[py[mc]], writes=[ob])
                                fw.store("sp", ob, YT[1536 + mc * 128:1536 + (mc + 1) * 128, seq0 + k0:seq0 + k0 + kn], ob[:, 0:kn])
                fw.barrier()
                if "stop_p4" in dbg:
                    break

                with ExitStack() as es5:
                    xb_ = fw.sb([128, 16, NT], F32, es=es5)
                    yx = fw.sb([128, 16, NT], BF16, es=es5)
                    H = fw.sb([128, FC, NT], BF16, es=es5)
                    wring = Ring([fw.sb([128, 16, 256], BF16, es=es5) for _ in range(4)])
                    wdr = Ring([fw.sb([128, FC, 128], BF16, es=es5) for _ in range(2)])
                    sq = H
                    rstd = fw.sb([128, NT], F32, es=es5)
                    tmp = Ring([fw.sb([128, NT], F32, es=es5) for _ in range(2)])
                    sg = Ring([fw.sb([128, NT], F32, es=es5) for _ in range(2)])
                    pss = fw.ps([128, NT], F32, es=es5)
                    pA = Ring([fw.ps([128, NT], F32, es=es5) for _ in range(3)])
                    pG = Ring([fw.ps([128, NT], F32, es=es5) for _ in range(2)])
                    pU = Ring([fw.ps([128, NT], F32, es=es5) for _ in range(2)])
                    YTv = YT.rearrange("(c p) t -> p c t", p=128)
                    for (col0, n, is_ctx, first, last) in tiles:
                        if is_ctx and last_layer:
                            continue
                        r = 1 if is_ctx else 0
                        fw.load("sp", xb_, xb_[:, :, 0:n], XTv[:, :, col0:col0 + n])
                        fw.load("act", yx, yx[:, :, 0:n], YTv[:, :, col0:col0 + n])
                        for g in range(8):
                            wb = wring.next()
                            fw.load("sp" if g % 2 else "act", wb, wb[:], WOUT_b[g])
                            for j in range(2):
                                dc = g * 2 + j
                                po = pA.next()
                                fw.ops("pe", [lambda e, kc=kc: e.matmul(po[:, 0:n], lhsT=wb[:, kc, j * 128:(j + 1) * 128], rhs=yx[:, kc, 0:n],
                                                                        start=(kc == 0), stop=(kc == 15)) for kc in range(16)], reads=[wb, yx], writes=[po])
                                fw.op("dve", lambda e: e.scalar_tensor_tensor(out=xb_[:, dc, 0:n], in0=po[:, 0:n], scalar=mods[:, 32 + dc, r:r + 1], in1=xb_[:, dc, 0:n],
                                                                              op0=ALU.mult, op1=ALU.add), reads=[po, mods, xb_], writes=[xb_])
                        fw.op("act", lambda e: e.activation(out=sq[:, 0:16, 0:n], in_=xb_[:, :, 0:n], func=AF.Square), reads=[xb_], writes=[sq])
                        fw.ops("pe", [lambda e, dc=dc: e.matmul(pss[:, 0:n], lhsT=onesb[:], rhs=sq[:, dc, 0:n], start=(dc == 0), stop=(dc == 15))
                                      for dc in range(16)], reads=[sq, onesb], writes=[pss])
                        fw.op("act", lambda e: e.activation(out=rstd[:, 0:n], in_=pss[:, 0:n], func=AF.Sqrt, bias=RMS_EPS, scale=1.0 / D), reads=[pss], writes=[rstd])
                        fw.op("dve", lambda e: e.reciprocal(out=rstd[:, 0:n], in_=rstd[:, 0:n]), reads=[rstd], writes=[rstd])
                        for dc in range(16):
                            tb = tmp.next()
                            fw.op("dve", lambda e, dc=dc: e.scalar_tensor_tensor(out=tb[:, 0:n], in0=xb_[:, dc, 0:n], scalar=A2[:, dc, r:r + 1], in1=rstd[:, 0:n],
                                                                                 op0=ALU.mult, op1=ALU.mult), reads=[xb_, A2, rstd], writes=[tb])
                            fw.op("act", lambda e, dc=dc: e.activation(out=yx[:, dc, 0:n], in_=tb[:, 0:n], func=AF.Identity, bias=mods[:, 48 + dc, r:r + 1], scale=1.0),
                                  reads=[tb, mods], writes=[yx])
                        for g in range(22):
                            wg_ = wring.next()
                            wu_ = wring.next()
                            fw.load("sp", wg_, wg_[:], WG_b[g])
                            fw.load("act", wu_, wu_[:], WU_b[g])
                            for j in range(2):
                                fc = g * 2 + j
                                pg = pG.next()
                                pu_ = pU.next()
                                fw.ops("pe", [lambda e, kc=kc: e.matmul(pg[:, 0:n], lhsT=wg_[:, kc, j * 128:(j + 1) * 128], rhs=yx[:, kc, 0:n],
                                                                        start=(kc == 0), stop=(kc == 15)) for kc in range(16)], reads=[wg_, yx], writes=[pg])
                                fw.ops("pe", [lambda e, kc=kc: e.matmul(pu_[:, 0:n], lhsT=wu_[:, kc, j * 128:(j + 1) * 128], rhs=yx[:, kc, 0:n],
                                                                        start=(kc == 0), stop=(kc == 15)) for kc in range(16)], reads=[wu_, yx], writes=[pu_])
                                s_ = sg.next()
                                fw.op("act", lambda e: e.activation(out=s_[:, 0:n], in_=pg[:, 0:n], func=AF.Silu), reads=[pg], writes=[s_])
                                fw.op("dve", lambda e: e.tensor_tensor(out=H[:, fc, 0:n], in0=pu_[:, 0:n], in1=s_[:, 0:n], op=ALU.mult), reads=[pu_, s_], writes=[H])
                        for dc in range(16):
                            wd_ = wdr.next()
                            fw.load("sp" if dc % 2 else "act", wd_, wd_[:], WD_b[dc])
                            if True:
                                po = pA.next()
                                fw.ops("pe", [lambda e, fc=fc: e.matmul(po[:, 0:n], lhsT=wd_[:, fc, :], rhs=H[:, fc, 0:n],
                                                                        start=(fc == 0), stop=(fc == FC - 1)) for fc in range(FC)], reads=[wd_, H], writes=[po])
                                fw.op("dve", lambda e: e.scalar_tensor_tensor(out=xb_[:, dc, 0:n], in0=po[:, 0:n], scalar=mods[:, 80 + dc, r:r + 1], in1=xb_[:, dc, 0:n],
                                                                              op0=ALU.mult, op1=ALU.add), reads=[po, mods, xb_], writes=[xb_])
                        fw.store("pool", xb_, XTv[:, :, col0:col0 + n], xb_[:, :, 0:n])
                fw.barrier()

        with ExitStack() as es7:
            xb_ = fw.sb([128, 16, NT], F32, es=es7)
            sq = fw.sb([128, 16, NT], BF16, es=es7)
            rstd = fw.sb([128, NT], F32, es=es7)
            xo = fw.sb([128, 16, NT], F32, es=es7)
            pss = fw.ps([128, NT], F32, es=es7)
            ptr = Ring([fw.ps([128, 512], F32, es=es7) for _ in range(4)])
            orow = Ring([fw.sb([128, D], F32, es=es7) for _ in range(2)])
            for (col0, n, is_ctx, first, last) in tiles:
                if is_ctx:
                    continue
                fw.load("sp", xb_, xb_[:, :, 0:n], XTv[:, :, col0:col0 + n])
                fw.op("act", lambda e: e.activation(out=sq[:, :, 0:n], in_=xb_[:, :, 0:n], func=AF.Square), reads=[xb_], writes=[sq])
                fw.ops("pe", [lambda e, dc=dc: e.matmul(pss[:, 0:n], lhsT=onesb[:], rhs=sq[:, dc, 0:n], start=(dc == 0), stop=(dc == 15))
                              for dc in range(16)], reads=[sq, onesb], writes=[pss])
                fw.op("act", lambda e: e.activation(out=rstd[:, 0:n], in_=pss[:, 0:n], func=AF.Sqrt, bias=RMS_EPS, scale=1.0 / D), reads=[pss], writes=[rstd])
                fw.op("dve", lambda e: e.reciprocal(out=rstd[:, 0:n], in_=rstd[:, 0:n]), reads=[rstd], writes=[rstd])
                for dc in range(16):
                    fw.op("dve", lambda e, dc=dc: e.scalar_tensor_tensor(out=xo[:, dc, 0:n], in0=xb_[:, dc, 0:n], scalar=nf_fm[:, dc:dc + 1], in1=rstd[:, 0:n],
                                                                         op0=ALU.mult, op1=ALU.mult), reads=[xb_, nf_fm, rstd], writes=[xo])
                for tb in range(n // 128):
                    ob = orow.next()
                    for g4 in range(4):
                        pt = ptr.next()
                        fw.ops("pe", [lambda e, j=j: e.transpose(pt[:, j * 128:(j + 1) * 128], xo[:, g4 * 4 + j, tb * 128:(tb + 1) * 128], ident[:])
                                      for j in range(4)], reads=[xo, ident], writes=[pt])
                        fw.op("dve" if g4 % 2 == 0 else "act",
                              (lambda e: e.tensor_copy(out=ob[:, g4 * 512:(g4 + 1) * 512], in_=pt[:])) if g4 % 2 == 0 else
                              (lambda e: e.activation(out=ob[:, g4 * 512:(g4 + 1) * 512], in_=pt[:], func=AF.Copy)), reads=[pt], writes=[ob])
                    t0 = col0 - CTX + tb * 128
                    fw.store("sp", ob, out_ap[t0:t0 + 128, :], ob[:])
        fw.barrier()
    return nc, fw


def fourier_tables(n):
    idx = np.arange(n, dtype=np.int64)
    ang = (2.0 * np.pi / n) * ((idx[:, None] * idx[None, :]) % n).astype(np.float64)
    sc = 1.0 / np.sqrt(64.0 * n)
    tab = np.stack([np.cos(ang) * sc, -np.sin(ang) * sc], 0)
    return tab.astype(np.float32).astype(ml_dtypes.bfloat16)


def c64_table():
    idx = np.arange(64)
    ang = 2.0 * np.pi * ((idx[:, None] * idx[None, :]) % 64) / 64.0
    t = np.zeros((128, 256), np.float32)
    for h in range(2):
        t[h * 64:(h + 1) * 64, h * 64:(h + 1) * 64] = np.cos(ang)
        t[h * 64:(h + 1) * 64, 128 + h * 64:128 + (h + 1) * 64] = np.sin(ang)
    return t.astype(ml_dtypes.bfloat16)


_CACHE = {}


def make_inputs(b, x, c, ctx, c_ctx, W, T, CTX):
    m = {"x": np.ascontiguousarray(x[b]), "ctx": np.ascontiguousarray(ctx[b]),
         "cvec": np.ascontiguousarray(np.stack([c[b], c_ctx], 0))}
    m.update(W)
    return m


def kernel(x, c, ctx, c_ctx, w_mod, b_mod, norm_mix, w_in, rw_shift, dec_w0, dec_up, iclr_a0, iclr_up, k_k, k_a, r_k,
           ln_w, ln_b, g_up, conv_w, w_out, norm_ffn, w_gate, w_up, w_down, norm_final, _dbg=()):
    x = np.asarray(x)
    B, T, _ = x.shape
    CTX = ctx.shape[1]
    DEPTH = w_mod.shape[0]
    f = lambda a: np.ascontiguousarray(np.asarray(a, dtype=np.float32))
    W = dict(w_mod=f(w_mod), b_mod=f(b_mod), norm_mix=f(norm_mix), w_in=f(w_in), rw_shift=f(rw_shift), dec_w0=f(dec_w0),
             dec_up=f(dec_up), iclr_a0=f(iclr_a0), iclr_up=f(iclr_up), k_k=f(k_k), k_a=f(k_a),
             r_k=f(r_k).reshape(DEPTH, D_RWKV), ln_w=f(ln_w), ln_b=f(ln_b), g_up=f(g_up), conv_w=f(conv_w), w_out=f(w_out),
             norm_ffn=f(norm_ffn), w_gate=f(w_gate), w_up=f(w_up), w_down=f(w_down), norm_final=f(norm_final).reshape(1, D),
             ftab_x=fourier_tables(T), ftab_c=fourier_tables(CTX), c64=c64_table())
    key = (T, CTX, DEPTH, tuple(_dbg))
    nc, fw = build(T, CTX, DEPTH, _dbg)
    in_maps = [make_inputs(b, x, np.asarray(c), np.asarray(ctx), np.asarray(c_ctx), W, T, CTX) for b in range(B)]
    res = run_bass_kernel_spmd(nc, in_maps, core_ids=list(range(B)))
    if _dbg:
        return res
    return np.stack([res.results[b]["out"] for b in range(B)], 0).astype(np.float32)
```

```python
import numpy as np
import ml_dtypes
from contextlib import ExitStack
import concourse.bass as bass
import concourse.mybir as mybir
from concourse.bass_utils import run_bass_kernel_spmd

F32 = mybir.dt.float32
BF16 = mybir.dt.bfloat16
AF = mybir.ActivationFunctionType
ALU = mybir.AluOpType

D = 2048
DC = 16
HD = 64
D_RWKV = 1024
D_FF = 5632
FC = 44
R0, K0, V0, WD0, AD0, GD0, RW_COLS = 0, 1024, 2048, 3072, 3168, 3264, 3520
D_IN = 5568
PCG, PCX, PCB, PFT, PEND = 3584, 4096, 4608, 5120, 5632
RMS_EPS = 1e-6
GN_EPS = 64e-5
CH = 128
NT = 512


class Tr:
    __slots__ = ("w", "r", "sem", "dcnt", "const")

    def __init__(self, const=False):
        self.w = None
        self.r = {}
        self.sem = None
        self.dcnt = 0
        self.const = const


class Buf:
    def __init__(self, tile):
        self.t = tile
        self.tr = Tr()

    def __getitem__(self, k):
        return self.t[k]


class FW:
    ENG = ("pe", "act", "dve", "pool", "sp")

    def __init__(self, nc):
        self.nc = nc
        self.es = ExitStack()
        self.eng = {"pe": nc.tensor, "act": nc.scalar, "dve": nc.vector, "pool": nc.gpsimd, "sp": nc.sync}
        self.sem = {e: self.es.enter_context(nc.semaphore("sem_" + e)) for e in self.ENG}
        self.cnt = {e: 0 for e in self.ENG}
        self.seen = {e: {} for e in self.ENG}
        self.nsem = 0
        self.n_inst = 0
        self.dma_trs = []
        self.sem_pool = [[], []]
        self.uid = 0

    def sb(self, shape, dt, es=None, name=None):
        self.uid += 1
        return Buf((es or self.es).enter_context(self.nc.sbuf_tensor("%s_%d" % (name or "sb", self.uid), shape, dt)))

    def ps(self, shape, dt=F32, es=None, name=None):
        self.uid += 1
        return Buf((es or self.es).enter_context(self.nc.psum_tensor("%s_%d" % (name or "ps", self.uid), shape, dt)))

    def dram(self, name, shape, dt, kind="Internal"):
        return self.nc.dram_tensor(name, shape, dt, kind=kind).ap()

    def _wait(self, e, tok):
        if tok is None:
            return
        kind, key, val = tok
        if kind == "e":
            if self.seen[e].get(key, 0) >= val:
                return
            self.eng[e].wait_ge(self.sem[key], val)
            self.seen[e][key] = val
        else:
            k = id(key)
            if self.seen[e].get(k, 0) >= val:
                return
            self.eng[e].wait_ge(key, val)
            self.seen[e][k] = val

    def _deps(self, e, reads, writes):
        for t in reads:
            self._wait(e, t.w)
        for t in writes:
            self._wait(e, t.w)
            for tok in t.r.values():
                self._wait(e, tok)

    def _commit(self, tok, reads, writes):
        key = tok[1] if tok[0] == "e" else id(tok[1])
        for t in reads:
            if not t.const:
                t.r[key] = tok
        for t in writes:
            t.w = tok
            t.r = {}

    @staticmethod
    def _trs(bufs):
        return [b.tr if isinstance(b, Buf) else b for b in bufs]

    def op(self, e, fn, reads=(), writes=()):
        reads = self._trs(reads)
        writes = self._trs(writes)
        self._deps(e, reads, writes)
        ins = fn(self.eng[e])
        self.cnt[e] += 1
        ins.then_inc(self.sem[e], 1)
        self._commit(("e", e, self.cnt[e]), reads, writes)
        self.n_inst += 1

    def ops(self, e, fns, reads=(), writes=()):
        reads = self._trs(reads)
        writes = self._trs(writes)
        self._deps(e, reads, writes)
        ins = None
        for fn in fns:
            ins = fn(self.eng[e])
            self.n_inst += 1
        self.cnt[e] += 1
        ins.then_inc(self.sem[e], 1)
        self._commit(("e", e, self.cnt[e]), reads, writes)

    def dma(self, q, out, in_, own, reads=(), writes=()):
        own = own.tr if isinstance(own, Buf) else own
        reads = self._trs(reads)
        writes = self._trs(writes)
        sw = 1 if q == "pool" else 0
        if own.sem is None:
            own.sem = [None, None]
            own.dcnt = [0, 0]
            self.dma_trs.append(own)
        if own.sem[sw] is None:
            if self.sem_pool[sw]:
                own.sem[sw], own.dcnt[sw] = self.sem_pool[sw].pop()
            else:
                self.nsem += 1
                own.sem[sw] = self.es.enter_context(self.nc.semaphore("dsem%d" % self.nsem))
                own.dcnt[sw] = 0
        for i in (0, 1):
            if own.sem[i] is not None and own.dcnt[i] > 0:
                self._wait(q, ("d", own.sem[i], own.dcnt[i]))
        self._deps(q, reads, writes)
        ins = self.eng[q].dma_start(out=out, in_=in_)
        own.dcnt[sw] += 16
        ins.then_inc(own.sem[sw], 16)
        self._commit(("d", own.sem[sw], own.dcnt[sw]), reads, writes)
        self.n_inst += 1

    def load(self, q, buf, dst, src):
        self.dma(q, dst, src, buf, writes=[buf])

    def store(self, q, buf, dst, src):
        self.dma(q, dst, src, buf, reads=[buf])

    def barrier(self):
        for e in self.ENG:
            for f in self.ENG:
                if f != e and self.cnt[f] > 0:
                    self._wait(e, ("e", f, self.cnt[f]))
            for t in self.dma_trs:
                for i in (0, 1):
                    if t.sem[i] is not None and t.dcnt[i] > 0:
                        self._wait(e, ("d", t.sem[i], t.dcnt[i]))
        for t in self.dma_trs:
            for i in (0, 1):
                if t.sem[i] is not None:
                    self.sem_pool[i].append((t.sem[i], t.dcnt[i]))
            t.sem = None
            t.dcnt = 0
        self.dma_trs = []


class Ring:
    def __init__(self, bufs):
        self.bufs = bufs
        self.i = 0

    def next(self):
        b = self.bufs[self.i % len(self.bufs)]
        self.i += 1
        return b


def build(T, CTX, DEPTH, dbg=()):
    nc = bass.Bass("TRN2", target_bir_lowering=False)
    fw = FW(nc)
    TT = CTX + T
    L = DEPTH

    def din(name, shape, dt=F32):
        return nc.dram_tensor(name, shape, dt, kind="ExternalInput").ap()

    x_in = din("x", [T, D])
    ctx_in = din("ctx", [CTX, D])
    cvec_in = din("cvec", [2, D])
    w_mod = din("w_mod", [L, D, 6 * D])
    b_mod = din("b_mod", [L, 6 * D])
    norm_mix = din("norm_mix", [L, D])
    w_in = din("w_in", [L, D, D_IN])
    rw_shift = din("rw_shift", [L, 3, RW_COLS])
    dec_w0 = din("dec_w0", [L, 2, D_RWKV])
    dec_up = din("dec_up", [L, 2, 96, D_RWKV])
    iclr_a0 = din("iclr_a0", [L, 2, D_RWKV])
    iclr_up = din("iclr_up", [L, 2, 96, D_RWKV])
    k_k = din("k_k", [L, D_RWKV])
    k_a = din("k_a", [L, D_RWKV])
    r_k = din("r_k", [L, D_RWKV])
    ln_w = din("ln_w", [L, D_RWKV])
    ln_b = din("ln_b", [L, D_RWKV])
    g_up = din("g_up", [L, 256, D_RWKV])
    conv_w = din("conv_w", [L, 3, 512])
    w_out = din("w_out", [L, D, D])
    norm_ffn = din("norm_ffn", [L, D])
    w_gate = din("w_gate", [L, D, D_FF])
    w_up = din("w_up", [L, D, D_FF])
    w_down = din("w_down", [L, D_FF, D])
    norm_final = din("norm_final", [1, D])
    ftab_x = din("ftab_x", [2, T, T], BF16)
    ftab_c = din("ftab_c", [2, CTX, CTX], BF16)
    c64_in = din("c64", [128, 256], BF16)
    out_ap = nc.dram_tensor("out", [T, D], F32, kind="ExternalOutput").ap()

    def scratch(name, shape, dt):
        kind = "ExternalOutput" if name in dbg else "Internal"
        return nc.dram_tensor(name, shape, dt, kind=kind).ap()

    XT = scratch("XT", [D, TT], F32)
    PXT = scratch("PXT", [PEND, TT], F32)
    YF = scratch("YF", [D_RWKV, TT], F32)
    YT = scratch("YT", [D, TT], BF16)
    MODS = scratch("MODS", [128, 96 * 2], F32)
    WIN_b = scratch("WIN_b", [11, 128, 16, 512], BF16)
    WOUT_b = scratch("WOUT_b", [8, 128, 16, 256], BF16)
    WG_b = scratch("WG_b", [22, 128, 16, 256], BF16)
    WU_b = scratch("WU_b", [22, 128, 16, 256], BF16)
    WD_b = scratch("WD_b", [16, 128, FC, 128], BF16)

    XTv = XT.rearrange("(c p) t -> p c t", p=128)

    tiles = []
    for c0 in range(0, CTX, NT):
        n = min(NT, CTX - c0)
        tiles.append((c0, n, True, c0 == 0, c0 + n == CTX))
    for c0 in range(0, T, NT):
        n = min(NT, T - c0)
        tiles.append((CTX + c0, n, False, c0 == 0, c0 + n == T))

    with fw.es:
        ident = fw.sb([128, 128], F32, name="ident")
        identb = fw.sb([128, 128], BF16, name="identb")
        onesb = fw.sb([128, 128], BF16, name="onesb")
        blk = fw.sb([128, 128], F32, name="blk")
        blkb = fw.sb([128, 128], BF16, name="blkb")
        iot = fw.sb([128, 128], F32, name="iot")
        m_f = fw.sb([128, 256], F32, name="m_f")
        m_b = fw.sb([128, 256], F32, name="m_b")
        mt_f = fw.sb([128, 128], F32, name="mt_f")
        mt_b = fw.sb([128, 128], F32, name="mt_b")
        cmask = fw.sb([128, NT], F32, name="cmask")
        c64 = fw.sb([128, 256], BF16, name="c64")
        sc_fm = fw.sb([128, 16, 2], F32, name="sc_fm")
        nf_fm = fw.sb([128, 16], F32, name="nf_fm")
        for b in (blkb, ident, identb, onesb, blk, iot, m_f, m_b, mt_f, mt_b, cmask, c64):
            b.tr.const = True

        P = fw.eng["pool"]
        fw.op("pool", lambda e: e.iota(iot[:], pattern=[[1, 128]], base=0, channel_multiplier=-1,
                                       allow_small_or_imprecise_dtypes=True), writes=[iot])
        fw.op("dve", lambda e: e.tensor_single_scalar(out=ident[:], in_=iot[:], scalar=0.0, op=ALU.is_equal),
              reads=[iot], writes=[ident])
        fw.op("dve", lambda e: e.tensor_copy(out=identb[:], in_=ident[:]), reads=[ident], writes=[identb])
        fw.op("dve", lambda e: e.memset(onesb[:], 1.0), writes=[onesb])
        fw.op("dve", lambda e: e.memset(blk[:], 0.0), writes=[blk])
        fw.op("dve", lambda e: e.memset(blk[0:64, 0:64], 1.0), writes=[blk])
        fw.op("dve", lambda e: e.memset(blk[64:128, 64:128], 1.0), writes=[blk])
        fw.op("dve", lambda e: e.tensor_copy(out=blkb[:], in_=blk[:]), reads=[blk], writes=[blkb])
        fw.op("dve", lambda e: e.tensor_single_scalar(out=m_f[:, 0:128], in_=iot[:], scalar=0.0, op=ALU.is_gt), reads=[iot], writes=[m_f])
        fw.op("dve", lambda e: e.tensor_single_scalar(out=m_f[:, 128:256], in_=iot[:], scalar=0.0, op=ALU.is_ge), reads=[iot], writes=[m_f])
        fw.op("dve", lambda e: e.tensor_single_scalar(out=m_b[:, 0:128], in_=iot[:], scalar=0.0, op=ALU.is_lt), reads=[iot], writes=[m_b])
        fw.op("dve", lambda e: e.tensor_single_scalar(out=m_b[:, 128:256], in_=iot[:], scalar=0.0, op=ALU.is_le), reads=[iot], writes=[m_b])
        fw.op("dve", lambda e: e.tensor_single_scalar(out=mt_f[:], in_=iot[:], scalar=0.0, op=ALU.is_lt), reads=[iot], writes=[mt_f])
        fw.op("dve", lambda e: e.tensor_single_scalar(out=mt_b[:], in_=iot[:], scalar=0.0, op=ALU.is_gt), reads=[iot], writes=[mt_b])
        fw.op("dve", lambda e: e.memset(cmask[:], 1.0), writes=[cmask])
        for c in range(NT // CH):
            fw.op("dve", lambda e, c=c: e.memset(cmask[:, c * CH:c * CH + 1], 0.0), writes=[cmask])
        fw.load("sp", c64, c64[:], c64_in[:, :])

        def rows_to_fm(es, rows, R, nq, dst, dst_q0=0):
            psr = fw.ps([128, 16, R], F32, es=es, name="psr")
            for q0 in range(0, nq, 16):
                qn = min(16, nq - q0)
                fw.ops("pe", [lambda e, q=q: e.matmul(psr[:, q - q0, :], lhsT=rows[0:R, q * 128:(q + 1) * 128],
                                                      rhs=ident[0:R, 0:R], start=True, stop=True)
                              for q in range(q0, q0 + qn)], reads=[rows, ident], writes=[psr])
                fw.op("dve", lambda e: e.tensor_copy(out=dst[:, dst_q0 + q0:dst_q0 + q0 + qn, 0:R], in_=psr[:, 0:qn, :]),
                      reads=[psr], writes=[dst])

        with ExitStack() as es:
            crow = fw.sb([2, D], F32, es=es)
            fw.load("sp", crow, crow[:], cvec_in[:, :])
            fw.op("act", lambda e: e.activation(out=crow[:], in_=crow[:], func=AF.Silu), reads=[crow], writes=[crow])
            rows_to_fm(es, crow, 2, 16, sc_fm)
            nrow = fw.sb([1, D], F32, es=es)
            fw.load("sp", nrow, nrow[:], norm_final[:, :])
            nf3 = fw.sb([128, 16, 1], F32, es=es)
            rows_to_fm(es, nrow, 1, 16, nf3)
            fw.op("dve", lambda e: e.tensor_copy(out=nf_fm[:], in_=nf3[:, :, 0]), reads=[nf3], writes=[nf_fm])

            inb = [fw.sb([128, D], F32, es=es) for _ in range(2)]
            stg = [fw.sb([128, 16, 128], F32, es=es) for _ in range(2)]
            pst = [fw.ps([128, 4, 128], F32, es=es) for _ in range(4)]
            k = 0
            for (src, n0, c0) in ((ctx_in, CTX, 0), (x_in, T, CTX)):
                for tb in range(n0 // 128):
                    ib = inb[k % 2]
                    sg = stg[k % 2]
                    fw.load("sp", ib, ib[:], src[tb * 128:(tb + 1) * 128, :])
                    for g4 in range(4):
                        pt = pst[g4 % 4]
                        fw.ops("pe", [lambda e, j=j: e.transpose(pt[:, j, :], ib[:, (g4 * 4 + j) * 128:(g4 * 4 + j + 1) * 128], ident[:])
                                      for j in range(4)], reads=[ib, ident], writes=[pt])
                        fw.op("dve" if g4 % 2 == 0 else "act",
                              (lambda e: e.tensor_copy(out=sg[:, g4 * 4:(g4 + 1) * 4, :], in_=pt[:])) if g4 % 2 == 0 else
                              (lambda e: e.activation(out=sg[:, g4 * 4:(g4 + 1) * 4, :], in_=pt[:], func=AF.Copy)),
                              reads=[pt], writes=[sg])
                    fw.store("pool", sg, XTv[:, :, c0 + tb * 128:c0 + (tb + 1) * 128], sg[:])
                    k += 1
        fw.barrier()

        for li in range(L):
            last_layer = (li == L - 1)
            with ExitStack() as esl:
              with ExitStack() as es:
                st16 = Ring([fw.sb([128, 16, 512], BF16, es=es) for _ in range(3)])
                for g in range(11):
                    b = st16.next()
                    pc0 = g * 512
                    if g < 6:
                        fw.load("pool", b, b[:], w_in[li, :, pc0:pc0 + 512].rearrange("(c p) n -> p c n", p=128))
                    elif g == 6:
                        fw.load("pool", b, b[:, :, 0:448], w_in[li, :, pc0:pc0 + 448].rearrange("(c p) n -> p c n", p=128))
                    else:
                        fw.load("pool", b, b[:], w_in[li, :, pc0 - 64:pc0 - 64 + 512].rearrange("(c p) n -> p c n", p=128))
                    fw.store("sp", b, WIN_b[g], b[:])
                for (wsrc, wdst, ng) in ((w_out, WOUT_b, 4), (w_gate, WG_b, 11), (w_up, WU_b, 11)):
                    for g in range(ng):
                        b = st16.next()
                        fw.load("pool", b, b[:], wsrc[li, :, g * 512:(g + 1) * 512].rearrange("(c p) n -> p c n", p=128))
                        fw.store("sp", b, wdst[2 * g], b[:, :, 0:256])
                        fw.store("sp", b, wdst[2 * g + 1], b[:, :, 256:512])
                st44 = Ring([fw.sb([128, FC, 256], BF16, es=es) for _ in range(2)])
                for g in range(8):
                    b = st44.next()
                    fw.load("pool", b, b[:], w_down[li, :, g * 256:(g + 1) * 256].rearrange("(c p) n -> p c n", p=128))
                    fw.store("sp", b, WD_b[2 * g], b[:, :, 0:128])
                    fw.store("sp", b, WD_b[2 * g + 1], b[:, :, 128:256])
                fw.barrier()
              vfm = fw.sb([128, 96, 24], F32, name="vfm", es=esl)
              cw_fm = fw.sb([128, 4, 3], F32, name="cw_fm", es=esl)
              mods = fw.sb([128, 96, 2], F32, name="mods", es=esl)
              A1 = fw.sb([128, 16, 2], F32, name="A1", es=esl)
              A2 = fw.sb([128, 16, 2], F32, name="A2", es=esl)
              omk = fw.sb([128, 8], F32, name="omk", es=esl)
              omk2 = fw.sb([128, 8], F32, name="omk2", es=esl)
              with ExitStack() as es:
                NR = 24
                rowb = fw.sb([NR, 6 * D], F32, es=es)
                fw.op("pool", lambda e: e.memset(rowb[:], 0.0), writes=[rowb])
                rowspec = [(0, b_mod[li:li + 1, :], 6 * D), (1, norm_mix[li:li + 1, :], D), (2, norm_ffn[li:li + 1, :], D),
                           (3, rw_shift[li, :, 0:3072], 3072), (6, rw_shift[li, :, WD0:WD0 + 96], 96),
                           (9, rw_shift[li, :, AD0:AD0 + 96], 96), (12, rw_shift[li, :, GD0:GD0 + 256], 256),
                           (15, dec_w0[li], D_RWKV), (17, iclr_a0[li], D_RWKV), (19, k_k[li:li + 1, :], D_RWKV),
                           (20, k_a[li:li + 1, :], D_RWKV), (21, r_k[li:li + 1, :], D_RWKV), (22, ln_w[li:li + 1, :], D_RWKV),
                           (23, ln_b[li:li + 1, :], D_RWKV)]
                for (r0, src, ln) in rowspec:
                    nr = src.shape[0]
                    fw.dma("sp", rowb[r0:r0 + nr, 0:ln], src, rowb, writes=[rowb])
                crow3 = fw.sb([3, 512], F32, es=es)
                fw.load("sp", crow3, crow3[:], conv_w[li])
                rows_to_fm(es, rowb, NR, 96, vfm)
                rows_to_fm(es, crow3, 3, 4, cw_fm)

                wm = Ring([fw.sb([128, 16, 512], F32, es=es) for _ in range(2)])
                psm = Ring([fw.ps([128, 4, 2], F32, es=es) for _ in range(2)])
                for g in range(24):
                    b = wm.next()
                    fw.load("sp" if g % 2 == 0 else "act", b, b[:], w_mod[li, :, g * 512:(g + 1) * 512].rearrange("(c p) n -> p c n", p=128))
                    pm = psm.next()
                    fns = []
                    for j in range(4):
                        for kc in range(16):
                            fns.append(lambda e, j=j, kc=kc: e.matmul(pm[:, j, :], lhsT=b[:, kc, j * 128:(j + 1) * 128], rhs=sc_fm[:, kc, :],
                                                                      start=(kc == 0), stop=(kc == 15)))
                    fw.ops("pe", fns, reads=[b, sc_fm], writes=[pm])
                    for r in range(2):
                        fw.op("dve", lambda e, r=r: e.tensor_tensor(out=mods[:, g * 4:(g + 1) * 4, r], in0=pm[:, :, r],
                                                                    in1=vfm[:, g * 4:(g + 1) * 4, 0], op=ALU.add),
                              reads=[pm, vfm], writes=[mods])
                for r in range(2):
                    fw.op("dve", lambda e, r=r: e.scalar_tensor_tensor(out=A1[:, :, r], in0=mods[:, 16:32, r], scalar=1.0, in1=vfm[:, 0:16, 1],
                                                                       op0=ALU.add, op1=ALU.mult), reads=[mods, vfm], writes=[A1])
                    fw.op("dve", lambda e, r=r: e.scalar_tensor_tensor(out=A2[:, :, r], in0=mods[:, 64:80, r], scalar=1.0, in1=vfm[:, 0:16, 2],
                                                                       op0=ALU.add, op1=ALU.mult), reads=[mods, vfm], writes=[A2])
                if "MODS" in dbg:
                    fw.store("sp", mods, MODS, mods[:].rearrange("p q r -> p (q r)"))
                fw.op("dve", lambda e: e.tensor_scalar(out=omk[:], in0=vfm[:, 0:8, 20], scalar1=-1.0, scalar2=1.0, op0=ALU.mult, op1=ALU.add),
                      reads=[vfm], writes=[omk])
                fw.op("dve", lambda e: e.tensor_scalar(out=omk2[:], in0=omk[:], scalar1=2.0, scalar2=None, op0=ALU.mult), reads=[omk], writes=[omk2])
                fw.barrier()
              if True:
                with ExitStack() as es1:
                    xs = fw.sb([128, 16, NT], F32, es=es1)
                    sq = fw.sb([128, 16, NT], BF16, es=es1)
                    rstd = fw.sb([128, NT], F32, es=es1)
                    tmp = Ring([fw.sb([128, NT], F32, es=es1) for _ in range(2)])
                    xn = fw.sb([128, 16, 2 * NT], BF16, es=es1)
                    wr = Ring([fw.sb([128, 16, 512], BF16, es=es1) for _ in range(2)])
                    pss = fw.ps([128, NT], F32, es=es1)
                    pso = Ring([fw.ps([128, 2 * NT], F32, es=es1) for _ in range(3)])
                    ost = Ring([fw.sb([128, 2 * NT], F32, es=es1) for _ in range(3)])
                    supers = []
                    for c0 in range(0, CTX, 2 * NT):
                        supers.append((c0, min(2 * NT, CTX - c0), 1))
                    for c0 in range(0, T, 2 * NT):
                        supers.append((CTX + c0, min(2 * NT, T - c0), 0))

                    def norm_mod(col0, n, r, Amod, sh_q0, dst, dst0, src_buf=None):
                        xb = src_buf
                        if xb is None:
                            xb = xs
                            fw.load("sp", xs, xs[:, :, 0:n], XTv[:, :, col0:col0 + n])
                        fw.op("act", lambda e: e.activation(out=sq[:, :, 0:n], in_=xb[:, :, 0:n], func=AF.Square), reads=[xb], writes=[sq])
                        fw.ops("pe", [lambda e, dc=dc: e.matmul(pss[:, 0:n], lhsT=onesb[:], rhs=sq[:, dc, 0:n], start=(dc == 0), stop=(dc == 15))
                                      for dc in range(16)], reads=[sq, onesb], writes=[pss])
                        fw.op("act", lambda e: e.activation(out=rstd[:, 0:n], in_=pss[:, 0:n], func=AF.Sqrt, bias=RMS_EPS, scale=1.0 / D),
                              reads=[pss], writes=[rstd])
                        fw.op("dve", lambda e: e.reciprocal(out=rstd[:, 0:n], in_=rstd[:, 0:n]), reads=[rstd], writes=[rstd])
                        for dc in range(16):
                            tb = tmp.next()
                            fw.op("dve", lambda e, dc=dc: e.scalar_tensor_tensor(out=tb[:, 0:n], in0=xb[:, dc, 0:n], scalar=Amod[:, dc, r:r + 1],
                                                                                 in1=rstd[:, 0:n], op0=ALU.mult, op1=ALU.mult),
                                  reads=[xb, Amod, rstd], writes=[tb])
                            fw.op("act", lambda e, dc=dc: e.activation(out=dst[:, dc, dst0:dst0 + n], in_=tb[:, 0:n], func=AF.Identity,
                                                                       bias=mods[:, sh_q0 + dc, r:r + 1], scale=1.0),
                                  reads=[tb, mods], writes=[dst])

                    ev = 0
                    for (s0, sn, r) in supers:
                        for o in range(0, sn, NT):
                            norm_mod(s0 + o, min(NT, sn - o), r, A1, 0, xn, o)
                        for g in range(11):
                            wb = wr.next()
                            fw.load("act", wb, wb[:], WIN_b[g])
                            for j in range(4):
                                pq = g * 4 + j
                                if pq == 27:
                                    pass
                                po = pso.next()
                                fns = []
                                for h0 in range(0, sn, NT):
                                    hn = min(NT, sn - h0)
                                    for kc in range(16):
                                        fns.append(lambda e, h0=h0, hn=hn, kc=kc, j=j: e.matmul(po[:, h0:h0 + hn], lhsT=wb[:, kc, j * 128:(j + 1) * 128],
                                                                                               rhs=xn[:, kc, h0:h0 + hn], start=(kc == 0), stop=(kc == 15)))
                                fw.ops("pe", fns, reads=[wb, xn], writes=[po])
                                ob = ost.next()
                                if ev % 2 == 0:
                                    fw.op("dve", lambda e: e.tensor_copy(out=ob[:, 0:sn], in_=po[:, 0:sn]), reads=[po], writes=[ob])
                                else:
                                    fw.op("act", lambda e: e.activation(out=ob[:, 0:sn], in_=po[:, 0:sn], func=AF.Copy), reads=[po], writes=[ob])
                                ev += 1
                                fw.store("sp", ob, PXT[pq * 128:(pq + 1) * 128, s0:s0 + sn], ob[:, 0:sn])
                fw.barrier()
                if "stop_p1" in dbg:
                    break

                with ExitStack() as es2:
                    HP = 8
                    SL = 9
                    for d_ in dbg:
                        if d_.startswith('sl='):
                            SL = int(d_[3:])
                    SDT = F32 if 'scan32' in dbg else BF16
                    identS = ident if 'scan32' in dbg else identb
                    IDT = BF16 if 'inv16' in dbg else F32
                    decw = fw.sb([96, 2, D_RWKV], F32, es=es2)
                    iclw = fw.sb([96, 2, D_RWKV], F32, es=es2)
                    gupw = fw.sb([128, 2, D_RWKV], BF16, es=es2)
                    for d in range(2):
                        fw.dma("sp", decw[:, d, :], dec_up[li, d], decw, writes=[decw])
                        fw.dma("sp", iclw[:, d, :], iclr_up[li, d], iclw, writes=[iclw])
                    fw.load("pool", gupw, gupw[:], g_up[li].rearrange("(c p) n -> p c n", p=128))
                    for b in (decw, iclw, gupw):
                        pass
                    S2g = [fw.sb([128, 4, 128], F32, es=es2) for _ in range(2)]
                    S2bg = [fw.sb([128, 4, 128], SDT, es=es2) for _ in range(2)]
                    wdh = fw.sb([96, NT + 2], F32, es=es2)
                    adh = fw.sb([96, NT + 2], F32, es=es2)
                    gdh = fw.sb([128, 2, NT + 2], F32, es=es2)
                    wdt2 = [fw.sb([96, NT], F32, es=es2) for _ in range(2)]
                    ads2 = [fw.sb([96, NT], F32, es=es2) for _ in range(2)]
                    gds2 = [fw.sb([128, 2, NT], BF16, es=es2) for _ in range(2)]
                    rkv_h = Ring([[fw.sb([128, NT + 2], F32, es=es2) for _ in range(3)] for _ in range(1)])
                    r_s = Ring([fw.sb([128, NT], F32, es=es2) for _ in range(1)])
                    ks_s2 = [[fw.sb([128, NT], BF16, es=es2) for _ in range(4)] for _ in range(2)]
                    rkb = Ring([[fw.sb([128, NT + 2], BF16, es=es2) for _ in range(3)] for _ in range(1)])
                    dgr = Ring([fw.sb([128, 9, 128], BF16, es=es2) for _ in range(2)])
                    dgfr = Ring([fw.sb([128, 3, 128], F32, es=es2) for _ in range(1)])
                    vb_s2 = [[fw.sb([128, NT], SDT, es=es2) for _ in range(4)] for _ in range(2)]
                    strm2 = [[fw.sb([128, NT // CH, 4, CH], SDT, es=es2) for _ in range(4)] for _ in range(2)]
                    gC2 = [[fw.sb([128, NT // CH], F32, es=es2) for _ in range(4)] for _ in range(2)]
                    ybuf = fw.sb([128, 4, NT], F32, es=es2)
                    tw = Ring([fw.sb([128, NT], F32, es=es2) for _ in range(11)])
                    TTall = fw.sb([128, 4, 3, 128], SDT, es=es2)
                    GBK = fw.sb([128, 8, 2, 256], SDT, es=es2)
                    PkAll = [fw.sb([128, 8, 2, 128], IDT, es=es2) for _ in range(2)]
                    Zt = [fw.sb([128, 8, 64], IDT, es=es2) for _ in range(2)]
                    UP = fw.sb([128, 4, 128], SDT, es=es2)
                    stm = fw.sb([128, 4, 128], F32, es=es2)
                    psA = Ring([fw.ps([128, 512], F32, es=es2) for _ in range(3)])
                    psZ = Ring([fw.ps([128, 512], F32, es=es2) for _ in range(3)])
                    psP = Ring([fw.ps([128, 512], F32, es=es2) for _ in range(2)])

                    yio = Ring([fw.sb([128, NT], F32, es=es2) for _ in range(2)])
                    yob = Ring([fw.sb([128, NT], BF16, es=es2) for _ in range(2)])

                    for hg_ in range(2):
                        fw.op("pool", lambda e, hg_=hg_: e.memset(S2g[hg_][:], 0.0), writes=[S2g[hg_]])
                        fw.op("pool", lambda e, hg_=hg_: e.memset(S2bg[hg_][:], 0.0), writes=[S2bg[hg_]])

                    ptmp = Ring([fw.sb([128, NT], F32, es=es2) for _ in range(2)])

                    def shift3(eng, dst, src, n, wq, rows, row0, dv=None, sv_=None):
                        dv = dv or (lambda: dst[0:rows, 0:n])
                        sv_ = sv_ or (lambda a, b_: src[0:rows, a:b_])
                        w = lambda tap: vfm[0:rows, wq, row0 + tap:row0 + tap + 1]
                        fw.op(eng, lambda e: e.tensor_scalar(out=dv(), in0=sv_(1, n + 1), scalar1=w(1), scalar2=None, op0=ALU.mult), reads=[src, vfm], writes=[dst])
                        for tap, a in ((0, 0), (2, 2)):
                            if eng == "dve":
                                fw.op(eng, lambda e: e.scalar_tensor_tensor(out=dv(), in0=sv_(a, a + n), scalar=w(tap), in1=dv(), op0=ALU.mult, op1=ALU.add),
                                      reads=[src, vfm, dst], writes=[dst])
                            else:
                                pt_ = ptmp.next()
                                fw.op(eng, lambda e: e.tensor_scalar(out=pt_[0:rows, 0:n], in0=sv_(a, a + n), scalar1=w(tap), scalar2=None, op0=ALU.mult),
                                      reads=[src, vfm], writes=[pt_])
                                fw.op(eng, lambda e: e.tensor_tensor(out=dv(), in0=dv(), in1=pt_[0:rows, 0:n], op=ALU.add), reads=[dst, pt_], writes=[dst])

                    def load_halo(q, buf, dst_rows, row_a, row_b, col0, n, first, last):
                        lo = 1 if first else 0
                        hi = n + 1 if last else n + 2
                        if first:
                            fw.op("pool", lambda e: e.memset(dst_rows[:, 0:1], 0.0), writes=[buf])
                        if last:
                            fw.op("pool", lambda e: e.memset(dst_rows[:, n + 1:n + 2], 0.0), writes=[buf])
                        fw.dma(q, dst_rows[:, lo:hi], PXT[row_a:row_b, col0 - 1 + lo:col0 - 1 + hi], buf, writes=[buf])

                    for pas in (0, 1):
                        if pas == 0:
                            order = list(tiles)
                        else:
                            ctx_t = [t for t in tiles if t[2]]
                            x_t = [t for t in tiles if not t[2]]
                            order = ctx_t[::-1] + x_t[::-1]
                        msk = m_f if pas == 0 else m_b
                        mskT = mt_f if pas == 0 else mt_b
                        pump_ref = [lambda k: None]
                        pump_k = [4]

                        def unpack(tile):
                            (col0, n, is_ctx, first, last) = tile
                            return col0, n, is_ctx, first, last, n // CH, (is_ctx and last_layer)
                        def lora_shared(tile, lp):
                            col0, n, is_ctx, first, last, ncn, skip_fin = unpack(tile)
                            wdt, ads, gds = wdt2[lp], ads2[lp], gds2[lp]

                            def pe_shift(src, rows, wq, row0, evac, sf=None):
                                dgf = dgfr.next()
                                for tap in range(3):
                                    fw.op("dve", lambda e, tap=tap: e.tensor_scalar(out=dgf[0:rows, tap, 0:rows], in0=ident[0:rows, 0:rows],
                                                                                   scalar1=vfm[0:rows, wq, row0 + tap:row0 + tap + 1], scalar2=None, op0=ALU.mult),
                                          reads=[ident, vfm], writes=[dgf])
                                psh = psP.next()
                                fw.ops("pe", [lambda e, tap=tap: e.matmul(psh[0:rows, 0:n], lhsT=dgf[0:rows, tap, 0:rows], rhs=(sf(tap, tap + n) if sf else src[0:rows, tap:tap + n]),
                                                                         start=(tap == 0), stop=(tap == 2)) for tap in range(3)], reads=[dgf, src], writes=[psh])
                                evac(psh)

                            load_halo("sp", wdh, wdh[0:96, 0:n + 2], WD0, WD0 + 96, col0, n, first, last)
                            load_halo("sp", adh, adh[0:96, 0:n + 2], AD0, AD0 + 96, col0, n, first, last)
                            pe_shift(wdh, 96, 0, 6, lambda psh: fw.op("act", lambda e: e.activation(out=wdt[:, 0:n], in_=psh[0:96, 0:n], func=AF.Tanh), reads=[psh], writes=[wdt]))
                            pe_shift(adh, 96, 0, 9, lambda psh: fw.op("act", lambda e: e.activation(out=ads[:, 0:n], in_=psh[0:96, 0:n], func=AF.Copy), reads=[psh], writes=[ads]))
                            if pas == 1 and not skip_fin:
                                for c2 in range(2):
                                    load_halo("sp", gdh, gdh[:, c2, 0:n + 2], GD0 + c2 * 128, GD0 + (c2 + 1) * 128, col0, n, first, last)
                                for c2 in range(2):
                                    pe_shift(gdh, 128, c2, 12,
                                             sf=lambda a_, b_, c2=c2: gdh[:, c2, a_:b_], evac=lambda psh, c2=c2: fw.op("act", lambda e: e.activation(out=gds[:, c2, 0:n], in_=psh[:, 0:n], func=AF.Sigmoid), reads=[psh], writes=[gds]))
                        def prep_hp(tile, hp, sset, lp):
                            col0, n, is_ctx, first, last, ncn, skip_fin = unpack(tile)
                            wdt, ads, gds = wdt2[lp], ads2[lp], gds2[lp]
                            strm, vb_s, gC, ks_s = strm2[sset], vb_s2[sset], gC2[sset], ks_s2[sset]
                            hb = rkv_h.next()
                            for i3, base in enumerate((R0, K0, V0)):
                                load_halo("sp" if i3 != 1 else "act", hb[i3], hb[i3][:, 0:n + 2], base + hp * 128, base + (hp + 1) * 128, col0, n, first, last)
                                yield
                            rr = r_s.next()
                            kk_ = tw.next()
                            hbb = rkb.next()
                            dg = dgr.next()
                            for i3, q_ in enumerate((R0 // 128 + hp, K0 // 128 + hp, V0 // 128 + hp)):
                                fw.op("act", lambda e: e.activation(out=hbb[i3][:, 0:n + 2], in_=hb[i3][:, 0:n + 2], func=AF.Copy), reads=[hb[i3]], writes=[hbb[i3]])
                                yield
                                for tap in range(3):
                                    fw.op("dve", lambda e: e.tensor_scalar(out=dg[:, i3 * 3 + tap, :], in0=identb[:], scalar1=vfm[:, q_, 3 + tap:4 + tap], scalar2=None, op0=ALU.mult),
                                          reads=[identb, vfm], writes=[dg])
                                yield
                                psh = psP.next()
                                fw.ops("pe", [lambda e, tap=tap: e.matmul(psh[:, 0:n], lhsT=dg[:, i3 * 3 + tap, :], rhs=hbb[i3][:, tap:tap + n], start=(tap == 0), stop=(tap == 2))
                                              for tap in range(3)], reads=[dg, hbb[i3]], writes=[psh])
                                yield
                                dst_ = (rr, kk_, vb_s[hp % 4])[i3]
                                fw.op("act", lambda e: e.activation(out=dst_[:, 0:n], in_=psh[:, 0:n], func=AF.Copy), reads=[psh], writes=[dst_])
                                yield
                            t1 = tw.next()
                            fw.op("dve", lambda e: e.tensor_scalar(out=t1[:, 0:n], in0=kk_[:, 0:n], scalar1=vfm[:, hp, 19:20], scalar2=None, op0=ALU.mult),
                                  reads=[kk_, vfm], writes=[t1])
                            yield
                            t2 = tw.next()
                            fw.op("act", lambda e: e.activation(out=t2[:, 0:n], in_=t1[:, 0:n], func=AF.Square), reads=[t1], writes=[t2])
                            yield
                            pa = psP.next()
                            fw.op("pe", lambda e: e.matmul(pa[:, 0:n], lhsT=blk[:], rhs=t2[:, 0:n], start=True, stop=True), reads=[blk, t2], writes=[pa])
                            yield
                            fw.op("act", lambda e: e.activation(out=t2[:, 0:n], in_=pa[:, 0:n], func=AF.Sqrt), reads=[pa], writes=[t2])
                            yield
                            fw.op("dve", lambda e: e.tensor_scalar(out=t2[:, 0:n], in0=t2[:, 0:n], scalar1=1e-12, scalar2=None, op0=ALU.max), reads=[t2], writes=[t2])
                            yield
                            fw.op("dve", lambda e: e.reciprocal(out=t2[:, 0:n], in_=t2[:, 0:n]), reads=[t2], writes=[t2])
                            yield
                            kkn = t1
                            fw.op("dve", lambda e: e.tensor_tensor(out=kkn[:, 0:n], in0=t1[:, 0:n], in1=t2[:, 0:n], op=ALU.mult), reads=[t1, t2], writes=[kkn])
                            yield
                            d = pas
                            pw = psP.next()
                            fw.op("pe", lambda e: e.matmul(pw[:, 0:n], lhsT=decw[:, d, hp * 128:(hp + 1) * 128], rhs=wdt[:, 0:n], start=True, stop=True),
                                  reads=[decw, wdt], writes=[pw])
                            yield
                            lw = tw.next()
                            fw.op("act", lambda e: e.activation(out=lw[:, 0:n], in_=pw[:, 0:n], func=AF.Sigmoid, bias=vfm[:, hp, 15 + d:16 + d], scale=1.0),
                                  reads=[pw, vfm], writes=[lw])
                            yield
                            fw.op("dve", lambda e: e.tensor_scalar(out=lw[:, 0:n], in0=lw[:, 0:n], scalar1=-0.6065306597126334, scalar2=None, op0=ALU.mult),
                                  reads=[lw], writes=[lw])
                            yield
                            pa2 = psP.next()
                            fw.op("pe", lambda e: e.matmul(pa2[:, 0:n], lhsT=iclw[:, d, hp * 128:(hp + 1) * 128], rhs=ads[:, 0:n], start=True, stop=True),
                                  reads=[iclw, ads], writes=[pa2])
                            yield
                            aa = tw.next()
                            fw.op("act", lambda e: e.activation(out=aa[:, 0:n], in_=pa2[:, 0:n], func=AF.Sigmoid, bias=vfm[:, hp, 17 + d:18 + d], scale=1.0),
                                  reads=[pa2, vfm], writes=[aa])
                            yield
                            kd = tw.next()
                            fw.op("dve", lambda e: e.tensor_scalar(out=kd[:, 0:n], in0=aa[:, 0:n], scalar1=vfm[:, hp, 20:21], scalar2=omk[:, hp:hp + 1],
                                                                   op0=ALU.mult, op1=ALU.add), reads=[aa, vfm, omk], writes=[kd])
                            yield
                            fw.op("dve", lambda e: e.tensor_tensor(out=kd[:, 0:n], in0=kd[:, 0:n], in1=kk_[:, 0:n], op=ALU.mult), reads=[kd, kk_], writes=[kd])
                            yield
                            if pas == 1 and not skip_fin:
                                pa3 = psP.next()
                                fw.op("pe", lambda e: e.matmul(pa3[:, 0:n], lhsT=iclw[:, 0, hp * 128:(hp + 1) * 128], rhs=ads[:, 0:n], start=True, stop=True),
                                      reads=[iclw, ads], writes=[pa3])
                                yield
                                af = tw.next()
                                fw.op("act", lambda e: e.activation(out=af[:, 0:n], in_=pa3[:, 0:n], func=AF.Sigmoid, bias=vfm[:, hp, 17:18], scale=1.0),
                                      reads=[pa3, vfm], writes=[af])
                                yield
                                fw.op("dve", lambda e: e.tensor_tensor(out=af[:, 0:n], in0=af[:, 0:n], in1=aa[:, 0:n], op=ALU.add), reads=[af, aa], writes=[af])
                                yield
                                fw.op("dve", lambda e: e.tensor_scalar(out=af[:, 0:n], in0=af[:, 0:n], scalar1=vfm[:, hp, 20:21], scalar2=omk2[:, hp:hp + 1],
                                                                       op0=ALU.mult, op1=ALU.add), reads=[af, vfm, omk2], writes=[af])
                                yield
                                fw.op("dve", lambda e: e.tensor_tensor(out=af[:, 0:n], in0=af[:, 0:n], in1=kk_[:, 0:n], op=ALU.mult), reads=[af, kk_], writes=[af])
                                yield
                                fw.op("dve", lambda e: e.scalar_tensor_tensor(out=ks_s[hp % 4][:, 0:n], in0=af[:, 0:n], scalar=vfm[:, hp, 21:22], in1=rr[:, 0:n],
                                                                              op0=ALU.mult, op1=ALU.mult), reads=[af, vfm, rr], writes=[ks_s[hp % 4]])
                                yield
                            Lc = tw.next()
                            fw.op("dve", lambda e: e.tensor_tensor_scan(out=Lc[:, 0:n], data0=cmask[:, 0:n], data1=lw[:, 0:n], initial=0.0,
                                                                        op0=ALU.mult, op1=ALU.add), reads=[cmask, lw], writes=[Lc])
                            yield
                            Ginc, Gexc, Ginv = tw.next(), tw.next(), tw.next()
                            st = strm[hp % 4]
                            if pas == 0:
                                fw.op("act", lambda e: e.activation(out=Ginc[:, 0:n], in_=Lc[:, 0:n], func=AF.Exp), reads=[Lc], writes=[Ginc])
                                yield
                                fw.op("act", lambda e: e.activation(out=Ginv[:, 0:n], in_=Lc[:, 0:n], func=AF.Exp, scale=-1.0), reads=[Lc], writes=[Ginv])
                                yield
                                fw.op("dve", lambda e: e.tensor_tensor(out=Gexc[:, 0:n], in0=Lc[:, 0:n], in1=lw[:, 0:n], op=ALU.subtract), reads=[Lc, lw], writes=[Gexc])
                                yield
                                fw.op("act", lambda e: e.activation(out=Gexc[:, 0:n], in_=Gexc[:, 0:n], func=AF.Exp), reads=[Gexc], writes=[Gexc])
                                yield
                                for c in range(ncn):
                                    fw.op("pool", lambda e, c=c: e.tensor_copy(out=gC[hp % 4][:, c:c + 1], in_=Ginc[:, c * CH + CH - 1:c * CH + CH]),
                                          reads=[Ginc], writes=[gC[hp % 4]])
                                    yield
                            else:
                                for c in range(ncn):
                                    fw.op("dve", lambda e, c=c: e.tensor_scalar(out=Gexc[:, c * CH:(c + 1) * CH], in0=Lc[:, c * CH:(c + 1) * CH],
                                                                                scalar1=Lc[:, c * CH + CH - 1:c * CH + CH], scalar2=None, op0=ALU.subtract),
                                          reads=[Lc], writes=[Gexc])
                                    yield
                                    fw.op("act", lambda e, c=c: e.activation(out=gC[hp % 4][:, c:c + 1], in_=Lc[:, c * CH + CH - 1:c * CH + CH], func=AF.Exp),
                                          reads=[Lc], writes=[gC[hp % 4]])
                                    yield
                                fw.op("dve", lambda e: e.tensor_tensor(out=Ginc[:, 0:n], in0=lw[:, 0:n], in1=Gexc[:, 0:n], op=ALU.subtract), reads=[lw, Gexc], writes=[Ginc])
                                yield
                                fw.op("act", lambda e: e.activation(out=Ginv[:, 0:n], in_=Ginc[:, 0:n], func=AF.Exp, scale=-1.0), reads=[Ginc], writes=[Ginv])
                                yield
                                fw.op("act", lambda e: e.activation(out=Ginc[:, 0:n], in_=Ginc[:, 0:n], func=AF.Exp), reads=[Ginc], writes=[Ginc])
                                yield
                                fw.op("act", lambda e: e.activation(out=Gexc[:, 0:n], in_=Gexc[:, 0:n], func=AF.Exp, scale=-1.0), reads=[Gexc], writes=[Gexc])
                                yield
                            sv = lambda i4: st[:, 0:ncn, i4, :]
                            v3 = lambda b_: b_[:, 0:n].rearrange("p (c t) -> p c t", t=CH)
                            fw.op("dve", lambda e: e.scalar_tensor_tensor(out=sv(0), in0=v3(kkn), scalar=-1.0, in1=v3(Gexc), op0=ALU.mult, op1=ALU.mult),
                                  reads=[kkn, Gexc], writes=[st])
                            yield
                            fw.op("pool", lambda e: e.tensor_tensor(out=sv(1), in0=v3(rr), in1=v3(Ginc), op=ALU.mult), reads=[rr, Ginc], writes=[st])
                            yield
                            fw.op("dve", lambda e: e.tensor_tensor(out=aa[:, 0:n], in0=aa[:, 0:n], in1=kkn[:, 0:n], op=ALU.mult), reads=[aa, kkn], writes=[aa])
                            yield
                            fw.op("dve", lambda e: e.tensor_tensor(out=sv(2), in0=v3(aa), in1=v3(Ginv), op=ALU.mult), reads=[aa, Ginv], writes=[st])
                            yield
                            fw.op("pool", lambda e: e.tensor_tensor(out=sv(3), in0=v3(kd), in1=v3(Ginv), op=ALU.mult), reads=[kd, Ginv], writes=[st])
                            yield

                        def scan_chunk(c, hg, sset):
                            strm, vb_s, gC = strm2[sset], vb_s2[sset], gC2[sset]
                            S2_, S2b_ = S2g[hg], S2bg[hg]
                            NLEV = 7
                            if True:
                                cs = slice(c * CH, (c + 1) * CH)
                                for g in range(4):
                                    st = strm[g]
                                    pb = psA.next()
                                    fw.ops("pe", [lambda e: e.matmul(pb[:, 0:128], lhsT=vb_s[g][:, cs], rhs=identS[:], start=True, stop=True),
                                                  lambda e: e.matmul(pb[:, 128:256], lhsT=st[:, c, 2, :], rhs=identS[:], start=True, stop=True),
                                                  lambda e: e.matmul(pb[:, 256:384], lhsT=st[:, c, 3, :], rhs=identS[:], start=True, stop=True)],
                                           reads=[vb_s[g], st, identS], writes=[pb])
                                    fw.op("act", lambda e: e.activation(out=TTall[:, g, :, :], in_=pb[:, 0:384].rearrange("p (a b) -> p a b", b=128), func=AF.Copy),
                                          reads=[pb], writes=[TTall])
                                pump_ref[0](pump_k[0])
                                for k in range(8):
                                    g, h = k // 2, k % 2
                                    st = strm[g]
                                    hs = slice(h * 64, (h + 1) * 64)
                                    pg = psA.next()
                                    fw.ops("pe", [lambda e: e.matmul(pg[:, 0:256], lhsT=st[hs, c, 2, :], rhs=st[hs, c, 0:2, :], start=True, stop=True),
                                                  lambda e: e.matmul(pg[:, 256:512], lhsT=st[hs, c, 3, :], rhs=st[hs, c, 0:2, :], start=True, stop=True)],
                                           reads=[st], writes=[pg])
                                    fw.op("dve", lambda e: e.tensor_tensor(out=GBK[:, k, 0, :], in0=pg[:, 0:256], in1=msk[:], op=ALU.mult), reads=[pg, msk], writes=[GBK])
                                    fw.op("dve", lambda e: e.tensor_tensor(out=GBK[:, k, 1, :], in0=pg[:, 256:512], in1=msk[:], op=ALU.mult), reads=[pg, msk], writes=[GBK])
                                    fw.op("dve", lambda e: e.tensor_tensor(out=PkAll[0][:, k, 0, :], in0=pg[:, 0:128], in1=msk[:, 0:128], op=ALU.mult),
                                          reads=[pg, msk], writes=[PkAll[0]])
                                for m in range(2):
                                    pt = psZ.next()
                                    fns = []
                                    for kk2 in range(4):
                                        k = m * 4 + kk2
                                        g, h = k // 2, k % 2
                                        fns.append(lambda e, kk2=kk2, g=g, h=h: e.matmul(pt[:, kk2 * 128:(kk2 + 1) * 128], lhsT=strm[g][h * 64:(h + 1) * 64, c, 0, :],
                                                                                        rhs=strm[g][h * 64:(h + 1) * 64, c, 2, :], start=True, stop=True))
                                    fw.ops("pe", fns, reads=[strm[(m * 4) // 2], strm[(m * 4) // 2 + 1]], writes=[pt])
                                    for kk2 in range(4):
                                        k = m * 4 + kk2
                                        fw.op("dve", lambda e, kk2=kk2, k=k: e.tensor_tensor(out=PkAll[0][:, k, 1, :], in0=pt[:, kk2 * 128:(kk2 + 1) * 128], in1=mskT[:], op=ALU.mult),
                                              reads=[pt, mskT], writes=[PkAll[0]])
                                pump_ref[0](pump_k[0])
                                pX = psZ.next()
                                fns = []
                                for g in range(4):
                                    st = strm[g]
                                    for h in range(2):
                                        k = g * 2 + h
                                        fns.append(lambda e, g=g, h=h, k=k, st=st: e.matmul(pX[:, k * 64:(k + 1) * 64], lhsT=st[:, c, 0, :], rhs=S2b_[:, g, h * 64:(h + 1) * 64],
                                                                                          start=True, stop=False))
                                        fns.append(lambda e, g=g, h=h, k=k: e.matmul(pX[:, k * 64:(k + 1) * 64], lhsT=GBK[:, k, 1, 0:128], rhs=TTall[:, g, 0, h * 64:(h + 1) * 64],
                                                                                   start=False, stop=True))
                                fw.ops("pe", fns, reads=[strm[0], strm[1], strm[2], strm[3], S2b_, GBK, TTall], writes=[pX])
                                fw.op("act", lambda e: e.activation(out=Zt[0][:], in_=pX[:].rearrange("p (a b) -> p a b", b=64), func=AF.Copy), reads=[pX], writes=[Zt[0]])
                                pump_ref[0](pump_k[0])
                                for lv in range(NLEV):
                                    if lv > 0:
                                        pump_ref[0](pump_k[0])
                                    Pc, Pn = PkAll[lv % 2], PkAll[(lv + 1) % 2]
                                    zi, zo = Zt[lv % 2], Zt[(lv + 1) % 2]
                                    pz = psZ.next()
                                    fw.ops("pe", [lambda e, k=k: e.matmul(pz[:, k * 64:(k + 1) * 64], lhsT=Pc[:, k, 0, :], rhs=zi[:, k, :], start=True, stop=True)
                                                  for k in range(8)], reads=[Pc, zi], writes=[pz])
                                    if lv == NLEV - 1:
                                        fw.op("dve", lambda e: e.tensor_tensor(out=UP[:].rearrange("p g (h i) -> p (g h) i", i=64), in0=pz[:].rearrange("p (a b) -> p a b", b=64),
                                                                               in1=zi[:], op=ALU.add), reads=[pz, zi], writes=[UP])
                                        break
                                    fw.op("dve", lambda e: e.tensor_tensor(out=zo[:], in0=pz[:].rearrange("p (a b) -> p a b", b=64), in1=zi[:], op=ALU.add),
                                          reads=[pz, zi], writes=[zo])
                                    if lv < NLEV - 2:
                                        for m in range(4):
                                            pq = psA.next()
                                            fns = []
                                            for k in (2 * m, 2 * m + 1):
                                                o = (k % 2) * 256
                                                fns.append(lambda e, k=k, o=o: e.matmul(pq[:, o:o + 128], lhsT=Pc[:, k, 1, :], rhs=Pc[:, k, 0, :], start=True, stop=True))
                                                fns.append(lambda e, k=k, o=o: e.matmul(pq[:, o + 128:o + 256], lhsT=Pc[:, k, 0, :], rhs=Pc[:, k, 1, :], start=True, stop=True))
                                            fw.ops("pe", fns, reads=[Pc], writes=[pq])
                                            dst = Pn[:, 2 * m:2 * m + 2, :, :]
                                            src = pq[:].rearrange("p (a b c) -> p a b c", a=2, b=2)
                                            if m % 2 == 0:
                                                fw.op("act", lambda e: e.activation(out=dst, in_=src, func=AF.Copy), reads=[pq], writes=[Pn])
                                            else:
                                                fw.op("dve", lambda e: e.tensor_copy(out=dst, in_=src), reads=[pq], writes=[Pn])
                                    else:
                                        for m in range(2):
                                            pq = psA.next()
                                            fw.ops("pe", [lambda e, k=k: e.matmul(pq[:, (k % 4) * 128:(k % 4 + 1) * 128], lhsT=Pc[:, k, 1, :], rhs=Pc[:, k, 0, :], start=True, stop=True)
                                                          for k in range(4 * m, 4 * m + 4)], reads=[Pc], writes=[pq])
                                            dst = Pn[:, 4 * m:4 * m + 4, 0, :]
                                            src = pq[:].rearrange("p (a b) -> p a b", b=128)
                                            if m % 2 == 0:
                                                fw.op("act", lambda e: e.activation(out=dst, in_=src, func=AF.Copy), reads=[pq], writes=[Pn])
                                            else:
                                                fw.op("dve", lambda e: e.tensor_copy(out=dst, in_=src), reads=[pq], writes=[Pn])
                                pump_ref[0](pump_k[0])
                                pY = psZ.next()
                                fns = []
                                for g in range(4):
                                    st = strm[g]
                                    for h in range(2):
                                        k = g * 2 + h
                                        ro = slice(h * 64, (h + 1) * 64)
                                        co = slice(g * 128, (g + 1) * 128)
                                        fns.append(lambda e, g=g, ro=ro, co=co, st=st: e.matmul(pY[ro, co], lhsT=S2b_[:, g, ro], rhs=st[:, c, 1, :], start=True, stop=False))
                                        fns.append(lambda e, g=g, k=k, ro=ro, co=co: e.matmul(pY[ro, co], lhsT=UP[:, g, ro], rhs=GBK[:, k, 0, 128:256], start=False, stop=False))
                                        fns.append(lambda e, g=g, k=k, ro=ro, co=co: e.matmul(pY[ro, co], lhsT=TTall[:, g, 0, ro], rhs=GBK[:, k, 1, 128:256], start=False, stop=True))
                                fw.ops("pe", fns, reads=[strm[0], strm[1], strm[2], strm[3], S2b_, UP, GBK, TTall], writes=[pY])
                                fw.op("act", lambda e: e.activation(out=ybuf[:, :, cs], in_=pY[:].rearrange("p (a b) -> p a b", b=128), func=AF.Copy), reads=[pY], writes=[ybuf])
                                pump_ref[0](pump_k[0])
                                pS = psZ.next()
                                fns = []
                                for g in range(4):
                                    co = slice(g * 128, (g + 1) * 128)
                                    fns.append(lambda e, g=g, co=co: e.matmul(pS[:, co], lhsT=TTall[:, g, 1, :], rhs=UP[:, g, :], start=True, stop=False))
                                    fns.append(lambda e, g=g, co=co: e.matmul(pS[:, co], lhsT=TTall[:, g, 2, :], rhs=TTall[:, g, 0, :], start=False, stop=True))
                                fw.ops("pe", fns, reads=[TTall, UP], writes=[pS])
                                fw.op("dve", lambda e: e.tensor_tensor(out=stm[:], in0=pS[:].rearrange("p (a b) -> p a b", b=128), in1=S2_[:], op=ALU.add),
                                      reads=[pS, S2_], writes=[stm])
                                for g in range(4):
                                    fw.op("dve", lambda e, g=g: e.scalar_tensor_tensor(out=S2_[:, g, :], in0=stm[:, g, :], scalar=gC[g][:, c:c + 1], in1=blk[:],
                                                                                       op0=ALU.mult, op1=ALU.mult), reads=[stm, gC[g], blk], writes=[S2_])
                                fw.op("pool", lambda e: e.tensor_copy(out=S2b_[:], in_=S2_[:]), reads=[S2_], writes=[S2b_])
                                pump_ref[0](pump_k[0])

                        def outputs(tile, hg, sset, lp):
                            col0, n, is_ctx, first, last, ncn, skip_fin = unpack(tile)
                            hps = list(range(hg * 4, hg * 4 + 4))
                            gds = gds2[lp]
                            vb_s, ks_s = vb_s2[sset], ks_s2[sset]
                            for hp in hps:
                                rows = slice(hp * 128, (hp + 1) * 128)
                                if pas == 0:
                                    fw.store("sp", ybuf, YF[rows, col0:col0 + n], ybuf[:, hp % 4, 0:n])
                                    continue
                                if skip_fin or 'no_fin' in dbg:
                                    continue
                                yf = yio.next()
                                fw.load("sp", yf, yf[:, 0:n], YF[rows, col0:col0 + n])
                                y = yf
                                fw.op("dve", lambda e: e.tensor_tensor(out=y[:, 0:n], in0=yf[:, 0:n], in1=ybuf[:, hp % 4, 0:n], op=ALU.add), reads=[yf, ybuf], writes=[y])
                                pm_ = psA.next()
                                fw.op("pe", lambda e: e.matmul(pm_[:, 0:n], lhsT=blk[:], rhs=y[:, 0:n], start=True, stop=True), reads=[blk, y], writes=[pm_])
                                dd = tw.next()
                                fw.op("dve", lambda e: e.scalar_tensor_tensor(out=dd[:, 0:n], in0=pm_[:, 0:n], scalar=-1.0 / 64, in1=y[:, 0:n], op0=ALU.mult, op1=ALU.add),
                                      reads=[pm_, y], writes=[dd])
                                d2 = tw.next()
                                fw.op("act", lambda e: e.activation(out=d2[:, 0:n], in_=dd[:, 0:n], func=AF.Square), reads=[dd], writes=[d2])
                                pv_ = psA.next()
                                fw.op("pe", lambda e: e.matmul(pv_[:, 0:n], lhsT=blk[:], rhs=d2[:, 0:n], start=True, stop=True), reads=[blk, d2], writes=[pv_])
                                fw.op("act", lambda e: e.activation(out=d2[:, 0:n], in_=pv_[:, 0:n], func=AF.Sqrt, bias=GN_EPS, scale=1.0 / 64), reads=[pv_], writes=[d2])
                                fw.op("dve", lambda e: e.reciprocal(out=d2[:, 0:n], in_=d2[:, 0:n]), reads=[d2], writes=[d2])
                                fw.op("dve", lambda e: e.tensor_tensor(out=dd[:, 0:n], in0=dd[:, 0:n], in1=d2[:, 0:n], op=ALU.mult), reads=[dd, d2], writes=[dd])
                                fw.op("dve", lambda e: e.tensor_scalar(out=dd[:, 0:n], in0=dd[:, 0:n], scalar1=vfm[:, hp, 22:23], scalar2=vfm[:, hp, 23:24],
                                                                       op0=ALU.mult, op1=ALU.add), reads=[dd, vfm], writes=[dd])
                                pb_ = psA.next()
                                fw.op("pe", lambda e: e.matmul(pb_[:, 0:n], lhsT=blkb[:], rhs=ks_s[hp % 4][:, 0:n], start=True, stop=True), reads=[blkb, ks_s[hp % 4]], writes=[pb_])
                                fw.op("dve", lambda e: e.tensor_tensor(out=d2[:, 0:n], in0=pb_[:, 0:n], in1=vb_s[hp % 4][:, 0:n], op=ALU.mult), reads=[pb_, vb_s[hp % 4]], writes=[d2])
                                fw.op("dve", lambda e: e.tensor_tensor(out=dd[:, 0:n], in0=dd[:, 0:n], in1=d2[:, 0:n], op=ALU.add), reads=[dd, d2], writes=[dd])
                                pg_ = psA.next()
                                fw.ops("pe", [lambda e, c2=c2: e.matmul(pg_[:, 0:n], lhsT=gupw[:, c2, hp * 128:(hp + 1) * 128], rhs=gds[:, c2, 0:n],
                                                                        start=(c2 == 0), stop=(c2 == 1)) for c2 in range(2)],
                                       reads=[gupw, gds], writes=[pg_])
                                yo = yob.next()
                                fw.op("dve", lambda e: e.tensor_tensor(out=yo[:, 0:n], in0=pg_[:, 0:n], in1=dd[:, 0:n], op=ALU.mult), reads=[pg_, dd], writes=[yo])
                                fw.store("sp", yo, YT[rows, col0:col0 + n], yo[:, 0:n])
                        items = [(ti, hg) for ti in range(len(order)) for hg in range(2)]
                        lora_shared(order[0], 0)
                        for g in range(4):
                            for _ in prep_hp(order[0], g, 0, 0):
                                pass
                        pending = [None]

                        def pump(k):
                            gen = pending[0]
                            if gen is None:
                                return
                            for _ in range(k):
                                try:
                                    next(gen)
                                except StopIteration:
                                    pending[0] = None
                                    return

                        def chain_prep(nx, sset):
                            for g in range(4):
                                for _ in prep_hp(order[nx[0]], nx[1] * 4 + g, sset, nx[0] % 2):
                                    yield

                        pump_ref[0] = pump
                        for j, (ti, hg) in enumerate(items):
                            tile = order[ti]
                            ncn = tile[1] // CH
                            nxt = items[j + 1] if j + 1 < len(items) else None
                            if nxt is not None and nxt[0] != ti:
                                lora_shared(order[nxt[0]], nxt[0] % 2)
                            pending[0] = chain_prep(nxt, (j + 1) % 2) if nxt is not None else None
                            pump_k[0] = max(1, (4 * 62) // (ncn * 14) + 1)
                            crange = list(range(ncn)) if pas == 0 else list(range(ncn))[::-1]
                            for c in crange:
                                scan_chunk(c, hg, j % 2)
                            pump(100000)
                            outputs(tile, hg, j % 2, ti % 2)
                        fw.barrier()
                        if pas == 0:
                            for hg_ in range(2):
                                fw.op("pool", lambda e, hg_=hg_: e.memset(S2g[hg_][:], 0.0), writes=[S2g[hg_]])
                                fw.op("pool", lambda e, hg_=hg_: e.memset(S2bg[hg_][:], 0.0), writes=[S2bg[hg_]])
                fw.barrier()
                if "stop_p2" in dbg:
                    break

                with ExitStack() as es3:
                    cin = Ring([[fw.sb([128, NT], F32, es=es3) for _ in range(3)] for _ in range(2)])
                    cu = Ring([fw.sb([128, NT], F32, es=es3) for _ in range(2)])
                    co = Ring([fw.sb([128, NT], F32, es=es3) for _ in range(2)])
                    cob = Ring([fw.sb([128, NT], BF16, es=es3) for _ in range(2)])
                    for (col0, n, is_ctx, first, last) in tiles:
                        if is_ctx and last_layer:
                            continue
                        for q in range(4):
                            ib = cin.next()
                            for i3, base in enumerate((PCG, PCX, PCB)):
                                fw.load("sp", ib[i3], ib[i3][:, 0:n], PXT[base + q * 128:base + (q + 1) * 128, col0:col0 + n])
                            u = cu.next()
                            o = co.next()
                            fw.op("pool", lambda e: e.tensor_tensor(out=u[:, 0:n], in0=ib[0][:, 0:n], in1=ib[1][:, 0:n], op=ALU.mult), reads=[ib[0], ib[1]], writes=[u])
                            fw.op("dve", lambda e: e.tensor_scalar(out=o[:, 0:n], in0=u[:, 0:n], scalar1=cw_fm[:, q, 1:2], scalar2=None, op0=ALU.mult),
                                  reads=[u, cw_fm], writes=[o])
                            if is_ctx:
                                assert first and last
                                W_ = n
                            else:
                                W_ = 64
                            u3 = u[:, 0:n].rearrange("p (r w) -> p r w", w=W_)
                            o3 = o[:, 0:n].rearrange("p (r w) -> p r w", w=W_)
                            fw.op("dve", lambda e: e.scalar_tensor_tensor(out=o3[:, :, 1:W_], in0=u3[:, :, 0:W_ - 1], scalar=cw_fm[:, q, 0:1], in1=o3[:, :, 1:W_],
                                                                          op0=ALU.mult, op1=ALU.add), reads=[u, cw_fm, o], writes=[o])
                            fw.op("dve", lambda e: e.scalar_tensor_tensor(out=o3[:, :, 0:W_ - 1], in0=u3[:, :, 1:W_], scalar=cw_fm[:, q, 2:3], in1=o3[:, :, 0:W_ - 1],
                                                                          op0=ALU.mult, op1=ALU.add), reads=[u, cw_fm, o], writes=[o])
                            ob = cob.next()
                            fw.op("pool", lambda e: e.tensor_tensor(out=ob[:, 0:n], in0=o[:, 0:n], in1=ib[2][:, 0:n], op=ALU.mult), reads=[o, ib[2]], writes=[ob])
                            fw.store("sp", ob, YT[1024 + q * 128:1024 + (q + 1) * 128, col0:col0 + n], ob[:, 0:n])
                fw.barrier()

                with ExitStack() as es4:
                    LCmax = T // 128
                    UC = fw.sb([128, LCmax, 512], BF16, es=es4)
                    US = fw.sb([128, LCmax, 512], BF16, es=es4)
                    uT = Ring([fw.sb([128, NT], BF16, es=es4) for _ in range(3)])
                    pu = Ring([fw.ps([128, 512], F32, es=es4) for _ in range(2)])
                    py = [fw.ps([128, 512], F32, es=es4) for _ in range(4)]
                    LG = 8
                    tabr = Ring([[fw.sb([128, LG, 512], BF16, es=es4) for _ in range(2)] for _ in range(2)])
                    fo = Ring([fw.sb([128, 512], BF16, es=es4) for _ in range(3)])
                    for (seq0, sn, tab, is_ctx) in ((0, CTX, ftab_c, True), (CTX, T, ftab_x, False)):
                        if is_ctx and last_layer:
                            continue
                        nlc = sn // 128
                        for t0 in range(0, sn, NT):
                            tn = min(NT, sn - t0)
                            for mc in range(4):
                                ub = uT.next()
                                fw.load("pool", ub, ub[:, 0:tn], PXT[PFT + mc * 128:PFT + (mc + 1) * 128, seq0 + t0:seq0 + t0 + tn])
                                for lc in range(tn // 128):
                                    p_ = pu.next()
                                    fw.op("pe", lambda e: e.matmul(p_[:, 0:256], lhsT=ub[:, lc * 128:(lc + 1) * 128], rhs=c64[:], start=True, stop=True),
                                          reads=[ub, c64], writes=[p_])
                                    glc = t0 // 128 + lc
                                    if (lc + mc) % 2 == 0:
                                        fw.op("dve", lambda e: e.tensor_copy(out=UC[:, glc, mc * 128:(mc + 1) * 128], in_=p_[:, 0:128]), reads=[p_], writes=[UC])
                                        fw.op("dve", lambda e: e.tensor_copy(out=US[:, glc, mc * 128:(mc + 1) * 128], in_=p_[:, 128:256]), reads=[p_], writes=[US])
                                    else:
                                        fw.op("act", lambda e: e.activation(out=UC[:, glc, mc * 128:(mc + 1) * 128], in_=p_[:, 0:128], func=AF.Copy), reads=[p_], writes=[UC])
                                        fw.op("act", lambda e: e.activation(out=US[:, glc, mc * 128:(mc + 1) * 128], in_=p_[:, 128:256], func=AF.Copy), reads=[p_], writes=[US])
                        for k0 in range(0, sn, 512):
                            kn = min(512, sn - k0)
                            for lg0 in range(0, nlc, LG):
                                lgn = min(LG, nlc - lg0)
                                tb = tabr.next()
                                for i2 in range(2):
                                    fw.load("sp" if i2 == 0 else "act", tb[i2], tb[i2][:, 0:lgn, 0:kn],
                                            tab[i2, lg0 * 128:(lg0 + lgn) * 128, k0:k0 + kn].rearrange("(c p) k -> p c k", p=128))
                                for mc in range(4):
                                    fns = []
                                    for l_ in range(lgn):
                                        lc = lg0 + l_
                                        fns.append(lambda e, l_=l_, lc=lc: e.matmul(py[mc][:, 0:kn], lhsT=UC[:, lc, mc * 128:(mc + 1) * 128], rhs=tb[0][:, l_, 0:kn],
                                                                                    start=(lc == 0), stop=False))
                                        fns.append(lambda e, l_=l_, lc=lc: e.matmul(py[mc][:, 0:kn], lhsT=US[:, lc, mc * 128:(mc + 1) * 128], rhs=tb[1][:, l_, 0:kn],
                                                                                    start=False, stop=(lc == nlc - 1)))
                                    fw.ops("pe", fns, reads=[UC, US, tb[0], tb[1]], writes=[py[mc]])
                            for mc in range(4):
                                ob = fo.next()
                                fw.op("act" if mc % 2 else "dve",
                                      (lambda e: e.activation(out=ob[:, 0:kn], in_=py[mc][:, 0:kn], func=AF.Copy)) if mc % 2 else
                                      (lambda e: e.tensor_copy(out=ob[:, 0:kn], in_=py[mc][:, 0:kn])), reads=[py[mc]], writes=[ob])
                                fw.store("sp", ob, YT[1536 + mc * 128:1536 + (mc + 1) * 128, seq0 + k0:seq0 + k0 + kn], ob[:, 0:kn])
                fw.barrier()
                if "stop_p4" in dbg:
                    break

                with ExitStack() as es5:
                    xb_ = fw.sb([128, 16, NT], F32, es=es5)
                    yx = fw.sb([128, 16, NT], BF16, es=es5)
                    H = fw.sb([128, FC, NT], BF16, es=es5)
                    wring = Ring([fw.sb([128, 16, 256], BF16, es=es5) for _ in range(4)])
                    wdr = Ring([fw.sb([128, FC, 128], BF16, es=es5) for _ in range(2)])
                    sq = H
                    rstd = fw.sb([128, NT], F32, es=es5)
                    tmp = Ring([fw.sb([128, NT], F32, es=es5) for _ in range(2)])
                    sg = Ring([fw.sb([128, NT], F32, es=es5) for _ in range(2)])
                    pss = fw.ps([128, NT], F32, es=es5)
                    pA = Ring([fw.ps([128, NT], F32, es=es5) for _ in range(3)])
                    pG = Ring([fw.ps([128, NT], F32, es=es5) for _ in range(2)])
                    pU = Ring([fw.ps([128, NT], F32, es=es5) for _ in range(2)])
                    YTv = YT.rearrange("(c p) t -> p c t", p=128)
                    for (col0, n, is_ctx, first, last) in tiles:
                        if is_ctx and last_layer:
                            continue
                        r = 1 if is_ctx else 0
                        fw.load("sp", xb_, xb_[:, :, 0:n], XTv[:, :, col0:col0 + n])
                        fw.load("act", yx, yx[:, :, 0:n], YTv[:, :, col0:col0 + n])
                        for g in range(8):
                            wb = wring.next()
                            fw.load("sp" if g % 2 else "act", wb, wb[:], WOUT_b[g])
                            for j in range(2):
                                dc = g * 2 + j
                                po = pA.next()
                                fw.ops("pe", [lambda e, kc=kc: e.matmul(po[:, 0:n], lhsT=wb[:, kc, j * 128:(j + 1) * 128], rhs=yx[:, kc, 0:n],
                                                                        start=(kc == 0), stop=(kc == 15)) for kc in range(16)], reads=[wb, yx], writes=[po])
                                fw.op("dve", lambda e: e.scalar_tensor_tensor(out=xb_[:, dc, 0:n], in0=po[:, 0:n], scalar=mods[:, 32 + dc, r:r + 1], in1=xb_[:, dc, 0:n],
                                                                              op0=ALU.mult, op1=ALU.add), reads=[po, mods, xb_], writes=[xb_])
                        fw.op("act", lambda e: e.activation(out=sq[:, 0:16, 0:n], in_=xb_[:, :, 0:n], func=AF.Square), reads=[xb_], writes=[sq])
                        fw.ops("pe", [lambda e, dc=dc: e.matmul(pss[:, 0:n], lhsT=onesb[:], rhs=sq[:, dc, 0:n], start=(dc == 0), stop=(dc == 15))
                                      for dc in range(16)], reads=[sq, onesb], writes=[pss])
                        fw.op("act", lambda e: e.activation(out=rstd[:, 0:n], in_=pss[:, 0:n], func=AF.Sqrt, bias=RMS_EPS, scale=1.0 / D), reads=[pss], writes=[rstd])
                        fw.op("dve", lambda e: e.reciprocal(out=rstd[:, 0:n], in_=rstd[:, 0:n]), reads=[rstd], writes=[rstd])
                        for dc in range(16):
                            tb = tmp.next()
                            fw.op("dve", lambda e, dc=dc: e.scalar_tensor_tensor(out=tb[:, 0:n], in0=xb_[:, dc, 0:n], scalar=A2[:, dc, r:r + 1], in1=rstd[:, 0:n],
                                                                                 op0=ALU.mult, op1=ALU.mult), reads=[xb_, A2, rstd], writes=[tb])
                            fw.op("act", lambda e, dc=dc: e.activation(out=yx[:, dc, 0:n], in_=tb[:, 0:n], func=AF.Identity, bias=mods[:, 48 + dc, r:r + 1], scale=1.0),
                                  reads=[tb, mods], writes=[yx])
                        for g in range(22):
                            wg_ = wring.next()
                            wu_ = wring.next()
                            fw.load("sp", wg_, wg_[:], WG_b[g])
                            fw.load("act", wu_, wu_[:], WU_b[g])
                            for j in range(2):
                                fc = g * 2 + j
                                pg = pG.next()
                                pu_ = pU.next()
                                fw.ops("pe", [lambda e, kc=kc: e.matmul(pg[:, 0:n], lhsT=wg_[:, kc, j * 128:(j + 1) * 128], rhs=yx[:, kc, 0:n],
                                                                        start=(kc == 0), stop=(kc == 15)) for kc in range(16)], reads=[wg_, yx], writes=[pg])
                                fw.ops("pe", [lambda e, kc=kc: e.matmul(pu_[:, 0:n], lhsT=wu_[:, kc, j * 128:(j + 1) * 128], rhs=yx[:, kc, 0:n],
                                                                        start=(kc == 0), stop=(kc == 15)) for kc in range(16)], reads=[wu_, yx], writes=[pu_])
                                s_ = sg.next()
                                fw.op("act", lambda e: e.activation(out=s_[:, 0:n], in_=pg[:, 0:n], func=AF.Silu), reads=[pg], writes=[s_])
                                fw.op("dve", lambda e: e.tensor_tensor(out=H[:, fc, 0:n], in0=pu_[:, 0:n], in1=s_[:, 0:n], op=ALU.mult), reads=[pu_, s_], writes=[H])
                        for dc in range(16):
                            wd_ = wdr.next()
                            fw.load("sp" if dc % 2 else "act", wd_, wd_[:], WD_b[dc])
                            if True:
                                po = pA.next()
                                fw.ops("pe", [lambda e, fc=fc: e.matmul(po[:, 0:n], lhsT=wd_[:, fc, :], rhs=H[:, fc, 0:n],
                                                                        start=(fc == 0), stop=(fc == FC - 1)) for fc in range(FC)], reads=[wd_, H], writes=[po])
                                fw.op("dve", lambda e: e.scalar_tensor_tensor(out=xb_[:, dc, 0:n], in0=po[:, 0:n], scalar=mods[:, 80 + dc, r:r + 1], in1=xb_[:, dc, 0:n],
                                                                              op0=ALU.mult, op1=ALU.add), reads=[po, mods, xb_], writes=[xb_])
                        fw.store("pool", xb_, XTv[:, :, col0:col0 + n], xb_[:, :, 0:n])
                fw.barrier()

        with ExitStack() as es7:
            xb_ = fw.sb([128, 16, NT], F32, es=es7)
            sq = fw.sb([128, 16, NT], BF16, es=es7)
            rstd = fw.sb([128, NT], F32, es=es7)
            xo = fw.sb([128, 16, NT], F32, es=es7)
            pss = fw.ps([128, NT], F32, es=es7)
            ptr = Ring([fw.ps([128, 512], F32, es=es7) for _ in range(4)])
            orow = Ring([fw.sb([128, D], F32, es=es7) for _ in range(2)])
            for (col0, n, is_ctx, first, last) in tiles:
                if is_ctx:
                    continue
                fw.load("sp", xb_, xb_[:, :, 0:n], XTv[:, :, col0:col0 + n])
                fw.op("act", lambda e: e.activation(out=sq[:, :, 0:n], in_=xb_[:, :, 0:n], func=AF.Square), reads=[xb_], writes=[sq])
                fw.ops("pe", [lambda e, dc=dc: e.matmul(pss[:, 0:n], lhsT=onesb[:], rhs=sq[:, dc, 0:n], start=(dc == 0), stop=(dc == 15))
                              for dc in range(16)], reads=[sq, onesb], writes=[pss])
                fw.op("act", lambda e: e.activation(out=rstd[:, 0:n], in_=pss[:, 0:n], func=AF.Sqrt, bias=RMS_EPS, scale=1.0 / D), reads=[pss], writes=[rstd])
                fw.op("dve", lambda e: e.reciprocal(out=rstd[:, 0:n], in_=rstd[:, 0:n]), reads=[rstd], writes=[rstd])
                for dc in range(16):
                    fw.op("dve", lambda e, dc=dc: e.scalar_tensor_tensor(out=xo[:, dc, 0:n], in0=xb_[:, dc, 0:n], scalar=nf_fm[:, dc:dc + 1], in1=rstd[:, 0:n],
                                                                         op0=ALU.mult, op1=ALU.mult), reads=[xb_, nf_fm, rstd], writes=[xo])
                for tb in range(n // 128):
                    ob = orow.next()
                    for g4 in range(4):
                        pt = ptr.next()
                        fw.ops("pe", [lambda e, j=j: e.transpose(pt[:, j * 128:(j + 1) * 128], xo[:, g4 * 4 + j, tb * 128:(tb + 1) * 128], ident[:])
                                      for j in range(4)], reads=[xo, ident], writes=[pt])
                        fw.op("dve" if g4 % 2 == 0 else "act",
                              (lambda e: e.tensor_copy(out=ob[:, g4 * 512:(g4 + 1) * 512], in_=pt[:])) if g4 % 2 == 0 else
                              (lambda e: e.activation(out=ob[:, g4 * 512:(g4 + 1) * 512], in_=pt[:], func=AF.Copy)), reads=[pt], writes=[ob])
                    t0 = col0 - CTX + tb * 128
                    fw.store("sp", ob, out_ap[t0:t0 + 128, :], ob[:])
        fw.barrier()
    return nc, fw


def fourier_tables(n):
    idx = np.arange(n, dtype=np.int64)
    ang = (2.0 * np.pi / n) * ((idx[:, None] * idx[None, :]) % n).astype(np.float64)
    sc = 1.0 / np.sqrt(64.0 * n)
    tab = np.stack([np.cos(ang) * sc, -np.sin(ang) * sc], 0)
    return tab.astype(np.float32).astype(ml_dtypes.bfloat16)


def c64_table():
    idx = np.arange(64)
    ang = 2.0 * np.pi * ((idx[:, None] * idx[None, :]) % 64) / 64.0
    t = np.zeros((128, 256), np.float32)
    for h in range(2):
        t[h * 64:(h + 1) * 64, h * 64:(h + 1) * 64] = np.cos(ang)
        t[h * 64:(h + 1) * 64, 128 + h * 64:128 + (h + 1) * 64] = np.sin(ang)
    return t.astype(ml_dtypes.bfloat16)


_CACHE = {}


def make_inputs(b, x, c, ctx, c_ctx, W, T, CTX):
    m = {"x": np.ascontiguousarray(x[b]), "ctx": np.ascontiguousarray(ctx[b]),
         "cvec": np.ascontiguousarray(np.stack([c[b], c_ctx], 0))}
    m.update(W)
    return m


def kernel(x, c, ctx, c_ctx, w_mod, b_mod, norm_mix, w_in, rw_shift, dec_w0, dec_up, iclr_a0, iclr_up, k_k, k_a, r_k,
           ln_w, ln_b, g_up, conv_w, w_out, norm_ffn, w_gate, w_up, w_down, norm_final, _dbg=()):
    x = np.asarray(x)
    B, T, _ = x.shape
    CTX = ctx.shape[1]
    DEPTH = w_mod.shape[0]
    f = lambda a: np.ascontiguousarray(np.asarray(a, dtype=np.float32))
    W = dict(w_mod=f(w_mod), b_mod=f(b_mod), norm_mix=f(norm_mix), w_in=f(w_in), rw_shift=f(rw_shift), dec_w0=f(dec_w0),
             dec_up=f(dec_up), iclr_a0=f(iclr_a0), iclr_up=f(iclr_up), k_k=f(k_k), k_a=f(k_a),
             r_k=f(r_k).reshape(DEPTH, D_RWKV), ln_w=f(ln_w), ln_b=f(ln_b), g_up=f(g_up), conv_w=f(conv_w), w_out=f(w_out),
             norm_ffn=f(norm_ffn), w_gate=f(w_gate), w_up=f(w_up), w_down=f(w_down), norm_final=f(norm_final).reshape(1, D),
             ftab_x=fourier_tables(T), ftab_c=fourier_tables(CTX), c64=c64_table())
    key = (T, CTX, DEPTH, tuple(_dbg))
    nc, fw = build(T, CTX, DEPTH, _dbg)
    in_maps = [make_inputs(b, x, np.asarray(c), np.asarray(ctx), np.asarray(c_ctx), W, T, CTX) for b in range(B)]
    res = run_bass_kernel_spmd(nc, in_maps, core_ids=list(range(B)))
    if _dbg:
        return res
    return np.stack([res.results[b]["out"] for b in range(B)], 0).astype(np.float32)
```

```python
import numpy as np
import ml_dtypes
from contextlib import ExitStack
import concourse.bass as bass
import concourse.mybir as mybir
from concourse.bass_utils import run_bass_kernel_spmd

F32 = mybir.dt.float32
BF16 = mybir.dt.bfloat16
AF = mybir.ActivationFunctionType
ALU = mybir.AluOpType

D = 2048
DC = 16
HD = 64
D_RWKV = 1024
D_FF = 5632
FC = 44
R0, K0, V0, WD0, AD0, GD0, RW_COLS = 0, 1024, 2048, 3072, 3168, 3264, 3520
D_IN = 5568
PCG, PCX, PCB, PFT, PEND = 3584, 4096, 4608, 5120, 5632
RMS_EPS = 1e-6
GN_EPS = 64e-5
CH = 128
NT = 512


class Tr:
    __slots__ = ("w", "r", "sem", "dcnt", "const")

    def __init__(self, const=False):
        self.w = None
        self.r = {}
        self.sem = None
        self.dcnt = 0
        self.const = const


class Buf:
    def __init__(self, tile):
        self.t = tile
        self.tr = Tr()

    def __getitem__(self, k):
        return self.t[k]


class FW:
    ENG = ("pe", "act", "dve", "pool", "sp")

    def __init__(self, nc):
        self.nc = nc
        self.es = ExitStack()
        self.eng = {"pe": nc.tensor, "act": nc.scalar, "dve": nc.vector, "pool": nc.gpsimd, "sp": nc.sync}
        self.sem = {e: self.es.enter_context(nc.semaphore("sem_" + e)) for e in self.ENG}
        self.cnt = {e: 0 for e in self.ENG}
        self.seen = {e: {} for e in self.ENG}
        self.nsem = 0
        self.n_inst = 0
        self.dma_trs = []
        self.sem_pool = [[], []]
        self.uid = 0

    def sb(self, shape, dt, es=None, name=None):
        self.uid += 1
        return Buf((es or self.es).enter_context(self.nc.sbuf_tensor("%s_%d" % (name or "sb", self.uid), shape, dt)))

    def ps(self, shape, dt=F32, es=None, name=None):
        self.uid += 1
        return Buf((es or self.es).enter_context(self.nc.psum_tensor("%s_%d" % (name or "ps", self.uid), shape, dt)))

    def dram(self, name, shape, dt, kind="Internal"):
        return self.nc.dram_tensor(name, shape, dt, kind=kind).ap()

    def _wait(self, e, tok):
        if tok is None:
            return
        kind, key, val = tok
        if kind == "e":
            if self.seen[e].get(key, 0) >= val:
                return
            self.eng[e].wait_ge(self.sem[key], val)
            self.seen[e][key] = val
        else:
            k = id(key)
            if self.seen[e].get(k, 0) >= val:
                return
            self.eng[e].wait_ge(key, val)
            self.seen[e][k] = val

    def _deps(self, e, reads, writes):
        for t in reads:
            self._wait(e, t.w)
        for t in writes:
            self._wait(e, t.w)
            for tok in t.r.values():
                self._wait(e, tok)

    def _commit(self, tok, reads, writes):
        key = tok[1] if tok[0] == "e" else id(tok[1])
        for t in reads:
            if not t.const:
                t.r[key] = tok
        for t in writes:
            t.w = tok
            t.r = {}

    @staticmethod
    def _trs(bufs):
        return [b.tr if isinstance(b, Buf) else b for b in bufs]

    def op(self, e, fn, reads=(), writes=()):
        reads = self._trs(reads)
        writes = self._trs(writes)
        self._deps(e, reads, writes)
        ins = fn(self.eng[e])
        self.cnt[e] += 1
        ins.then_inc(self.sem[e], 1)
        self._commit(("e", e, self.cnt[e]), reads, writes)
        self.n_inst += 1

    def ops(self, e, fns, reads=(), writes=()):
        reads = self._trs(reads)
        writes = self._trs(writes)
        self._deps(e, reads, writes)
        ins = None
        for fn in fns:
            ins = fn(self.eng[e])
            self.n_inst += 1
        self.cnt[e] += 1
        ins.then_inc(self.sem[e], 1)
        self._commit(("e", e, self.cnt[e]), reads, writes)

    def dma(self, q, out, in_, own, reads=(), writes=()):
        own = own.tr if isinstance(own, Buf) else own
        reads = self._trs(reads)
        writes = self._trs(writes)
        sw = 1 if q == "pool" else 0
        if own.sem is None:
            own.sem = [None, None]
            own.dcnt = [0, 0]
            self.dma_trs.append(own)
        if own.sem[sw] is None:
            if self.sem_pool[sw]:
                own.sem[sw], own.dcnt[sw] = self.sem_pool[sw].pop()
            else:
                self.nsem += 1
                own.sem[sw] = self.es.enter_context(self.nc.semaphore("dsem%d" % self.nsem))
                own.dcnt[sw] = 0
        for i in (0, 1):
            if own.sem[i] is not None and own.dcnt[i] > 0:
                self._wait(q, ("d", own.sem[i], own.dcnt[i]))
        self._deps(q, reads, writes)
        ins = self.eng[q].dma_start(out=out, in_=in_)
        own.dcnt[sw] += 16
        ins.then_inc(own.sem[sw], 16)
        self._commit(("d", own.sem[sw], own.dcnt[sw]), reads, writes)
        self.n_inst += 1

    def load(self, q, buf, dst, src):
        self.dma(q, dst, src, buf, writes=[buf])

    def store(self, q, buf, dst, src):
        self.dma(q, dst, src, buf, reads=[buf])

    def barrier(self):
        for e in self.ENG:
            for f in self.ENG:
                if f != e and self.cnt[f] > 0:
                    self._wait(e, ("e", f, self.cnt[f]))
            for t in self.dma_trs:
                for i in (0, 1):
                    if t.sem[i] is not None and t.dcnt[i] > 0:
                        self._wait(e, ("d", t.sem[i], t.dcnt[i]))
        for t in self.dma_trs:
            for i in (0, 1):
                if t.sem[i] is not None:
                    self.sem_pool[i].append((t.sem[i], t.dcnt[i]))
            t.sem = None
            t.dcnt = 0
        self.dma_trs = []


class Ring:
    def __init__(self, bufs):
        self.bufs = bufs
        self.i = 0

    def next(self):
        b = self.bufs[self.i % len(self.bufs)]
        self.i += 1
        return b


def build(T, CTX, DEPTH, dbg=()):
    nc = bass.Bass("TRN2", target_bir_lowering=False)
    fw = FW(nc)
    TT = CTX + T
    L = DEPTH

    def din(name, shape, dt=F32):
        return nc.dram_tensor(name, shape, dt, kind="ExternalInput").ap()

    x_in = din("x", [T, D])
    ctx_in = din("ctx", [CTX, D])
    cvec_in = din("cvec", [2, D])
    w_mod = din("w_mod", [L, D, 6 * D])
    b_mod = din("b_mod", [L, 6 * D])
    norm_mix = din("norm_mix", [L, D])
    w_in = din("w_in", [L, D, D_IN])
    rw_shift = din("rw_shift", [L, 3, RW_COLS])
    dec_w0 = din("dec_w0", [L, 2, D_RWKV])
    dec_up = din("dec_up", [L, 2, 96, D_RWKV])
    iclr_a0 = din("iclr_a0", [L, 2, D_RWKV])
    iclr_up = din("iclr_up", [L, 2, 96, D_RWKV])
    k_k = din("k_k", [L, D_RWKV])
    k_a = din("k_a", [L, D_RWKV])
    r_k = din("r_k", [L, D_RWKV])
    ln_w = din("ln_w", [L, D_RWKV])
    ln_b = din("ln_b", [L, D_RWKV])
    g_up = din("g_up", [L, 256, D_RWKV])
    conv_w = din("conv_w", [L, 3, 512])
    w_out = din("w_out", [L, D, D])
    norm_ffn = din("norm_ffn", [L, D])
    w_gate = din("w_gate", [L, D, D_FF])
    w_up = din("w_up", [L, D, D_FF])
    w_down = din("w_down", [L, D_FF, D])
    norm_final = din("norm_final", [1, D])
    ftab_x = din("ftab_x", [2, T, T], BF16)
    ftab_c = din("ftab_c", [2, CTX, CTX], BF16)
    c64_in = din("c64", [128, 256], BF16)
    out_ap = nc.dram_tensor("out", [T, D], F32, kind="ExternalOutput").ap()

    def scratch(name, shape, dt):
        kind = "ExternalOutput" if name in dbg else "Internal"
        return nc.dram_tensor(name, shape, dt, kind=kind).ap()

    XT = scratch("XT", [D, TT], F32)
    PXT = scratch("PXT", [PEND, TT], F32)
    YF = scratch("YF", [D_RWKV, TT], F32)
    YT = scratch("YT", [D, TT], BF16)
    MODS = scratch("MODS", [128, 96 * 2], F32)
    WIN_b = scratch("WIN_b", [11, 128, 16, 512], BF16)
    WOUT_b = scratch("WOUT_b", [8, 128, 16, 256], BF16)
    WG_b = scratch("WG_b", [22, 128, 16, 256], BF16)
    WU_b = scratch("WU_b", [22, 128, 16, 256], BF16)
    WD_b = scratch("WD_b", [16, 128, FC, 128], BF16)

    XTv = XT.rearrange("(c p) t -> p c t", p=128)

    tiles = []
    for c0 in range(0, CTX, NT):
        n = min(NT, CTX - c0)
        tiles.append((c0, n, True, c0 == 0, c0 + n == CTX))
    for c0 in range(0, T, NT):
        n = min(NT, T - c0)
        tiles.append((CTX + c0, n, False, c0 == 0, c0 + n == T))

    with fw.es:
        ident = fw.sb([128, 128], F32, name="ident")
        identb = fw.sb([128, 128], BF16, name="identb")
        onesb = fw.sb([128, 128], BF16, name="onesb")
        blk = fw.sb([128, 128], F32, name="blk")
        blkb = fw.sb([128, 128], BF16, name="blkb")
        iot = fw.sb([128, 128], F32, name="iot")
        m_f = fw.sb([128, 256], F32, name="m_f")
        m_b = fw.sb([128, 256], F32, name="m_b")
        mt_f = fw.sb([128, 128], F32, name="mt_f")
        mt_b = fw.sb([128, 128], F32, name="mt_b")
        cmask = fw.sb([128, NT], F32, name="cmask")
        c64 = fw.sb([128, 256], BF16, name="c64")
        sc_fm = fw.sb([128, 16, 2], F32, name="sc_fm")
        nf_fm = fw.sb([128, 16], F32, name="nf_fm")
        for b in (blkb, ident, identb, onesb, blk, iot, m_f, m_b, mt_f, mt_b, cmask, c64):
            b.tr.const = True

        P = fw.eng["pool"]
        fw.op("pool", lambda e: e.iota(iot[:], pattern=[[1, 128]], base=0, channel_multiplier=-1,
                                       allow_small_or_imprecise_dtypes=True), writes=[iot])
        fw.op("dve", lambda e: e.tensor_single_scalar(out=ident[:], in_=iot[:], scalar=0.0, op=ALU.is_equal),
              reads=[iot], writes=[ident])
        fw.op("dve", lambda e: e.tensor_copy(out=identb[:], in_=ident[:]), reads=[ident], writes=[identb])
        fw.op("dve", lambda e: e.memset(onesb[:], 1.0), writes=[onesb])
        fw.op("dve", lambda e: e.memset(blk[:], 0.0), writes=[blk])
        fw.op("dve", lambda e: e.memset(blk[0:64, 0:64], 1.0), writes=[blk])
        fw.op("dve", lambda e: e.memset(blk[64:128, 64:128], 1.0), writes=[blk])
        fw.op("dve", lambda e: e.tensor_copy(out=blkb[:], in_=blk[:]), reads=[blk], writes=[blkb])
        fw.op("dve", lambda e: e.tensor_single_scalar(out=m_f[:, 0:128], in_=iot[:], scalar=0.0, op=ALU.is_gt), reads=[iot], writes=[m_f])
        fw.op("dve", lambda e: e.tensor_single_scalar(out=m_f[:, 128:256], in_=iot[:], scalar=0.0, op=ALU.is_ge), reads=[iot], writes=[m_f])
        fw.op("dve", lambda e: e.tensor_single_scalar(out=m_b[:, 0:128], in_=iot[:], scalar=0.0, op=ALU.is_lt), reads=[iot], writes=[m_b])
        fw.op("dve", lambda e: e.tensor_single_scalar(out=m_b[:, 128:256], in_=iot[:], scalar=0.0, op=ALU.is_le), reads=[iot], writes=[m_b])
        fw.op("dve", lambda e: e.tensor_single_scalar(out=mt_f[:], in_=iot[:], scalar=0.0, op=ALU.is_lt), reads=[iot], writes=[mt_f])
        fw.op("dve", lambda e: e.tensor_single_scalar(out=mt_b[:], in_=iot[:], scalar=0.0, op=ALU.is_gt), reads=[iot], writes=[mt_b])
        fw.op("dve", lambda e: e.memset(cmask[:], 1.0), writes=[cmask])
        for c in range(NT // CH):
            fw.op("dve", lambda e, c=c: e.memset(cmask[:, c * CH:c * CH + 1], 0.0), writes=[cmask])
        fw.load("sp", c64, c64[:], c64_in[:, :])

        def rows_to_fm(es, rows, R, nq, dst, dst_q0=0):
            psr = fw.ps([128, 16, R], F32, es=es, name="psr")
            for q0 in range(0, nq, 16):
                qn = min(16, nq - q0)
                fw.ops("pe", [lambda e, q=q: e.matmul(psr[:, q - q0, :], lhsT=rows[0:R, q * 128:(q + 1) * 128],
                                                      rhs=ident[0:R, 0:R], start=True, stop=True)
                              for q in range(q0, q0 + qn)], reads=[rows, ident], writes=[psr])
                fw.op("dve", lambda e: e.tensor_copy(out=dst[:, dst_q0 + q0:dst_q0 + q0 + qn, 0:R], in_=psr[:, 0:qn, :]),
                      reads=[psr], writes=[dst])

        with ExitStack() as es:
            crow = fw.sb([2, D], F32, es=es)
            fw.load("sp", crow, crow[:], cvec_in[:, :])
            fw.op("act", lambda e: e.activation(out=crow[:], in_=crow[:], func=AF.Silu), reads=[crow], writes=[crow])
            rows_to_fm(es, crow, 2, 16, sc_fm)
            nrow = fw.sb([1, D], F32, es=es)
            fw.load("sp", nrow, nrow[:], norm_final[:, :])
            nf3 = fw.sb([128, 16, 1], F32, es=es)
            rows_to_fm(es, nrow, 1, 16, nf3)
            fw.op("dve", lambda e: e.tensor_copy(out=nf_fm[:], in_=nf3[:, :, 0]), reads=[nf3], writes=[nf_fm])

            inb = [fw.sb([128, D], F32, es=es) for _ in range(2)]
            stg = [fw.sb([128, 16, 128], F32, es=es) for _ in range(2)]
            pst = [fw.ps([128, 4, 128], F32, es=es) for _ in range(4)]
            k = 0
            for (src, n0, c0) in ((ctx_in, CTX, 0), (x_in, T, CTX)):
                for tb in range(n0 // 128):
                    ib = inb[k % 2]
                    sg = stg[k % 2]
                    fw.load("sp", ib, ib[:], src[tb * 128:(tb + 1) * 128, :])
                    for g4 in range(4):
                        pt = pst[g4 % 4]
                        fw.ops("pe", [lambda e, j=j: e.transpose(pt[:, j, :], ib[:, (g4 * 4 + j) * 128:(g4 * 4 + j + 1) * 128], ident[:])
                                      for j in range(4)], reads=[ib, ident], writes=[pt])
                        fw.op("dve" if g4 % 2 == 0 else "act",
                              (lambda e: e.tensor_copy(out=sg[:, g4 * 4:(g4 + 1) * 4, :], in_=pt[:])) if g4 % 2 == 0 else
                              (lambda e: e.activation(out=sg[:, g4 * 4:(g4 + 1) * 4, :], in_=pt[:], func=AF.Copy)),
                              reads=[pt], writes=[sg])
                    fw.store("pool", sg, XTv[:, :, c0 + tb * 128:c0 + (tb + 1) * 128], sg[:])
                    k += 1
        fw.barrier()

        for li in range(L):
            last_layer = (li == L - 1)
            with ExitStack() as esl:
              with ExitStack() as es:
                st16 = Ring([fw.sb([128, 16, 512], BF16, es=es) for _ in range(3)])
                for g in range(11):
                    b = st16.next()
                    pc0 = g * 512
                    if g < 6:
                        fw.load("pool", b, b[:], w_in[li, :, pc0:pc0 + 512].rearrange("(c p) n -> p c n", p=128))
                    elif g == 6:
                        fw.load("pool", b, b[:, :, 0:448], w_in[li, :, pc0:pc0 + 448].rearrange("(c p) n -> p c n", p=128))
                    else:
                        fw.load("pool", b, b[:], w_in[li, :, pc0 - 64:pc0 - 64 + 512].rearrange("(c p) n -> p c n", p=128))
                    fw.store("sp", b, WIN_b[g], b[:])
                for (wsrc, wdst, ng) in ((w_out, WOUT_b, 4), (w_gate, WG_b, 11), (w_up, WU_b, 11)):
                    for g in range(ng):
                        b = st16.next()
                        fw.load("pool", b, b[:], wsrc[li, :, g * 512:(g + 1) * 512].rearrange("(c p) n -> p c n", p=128))
                        fw.store("sp", b, wdst[2 * g], b[:, :, 0:256])
                        fw.store("sp", b, wdst[2 * g + 1], b[:, :, 256:512])
                st44 = Ring([fw.sb([128, FC, 256], BF16, es=es) for _ in range(2)])
                for g in range(8):
                    b = st44.next()
                    fw.load("pool", b, b[:], w_down[li, :, g * 256:(g + 1) * 256].rearrange("(c p) n -> p c n", p=128))
                    fw.store("sp", b, WD_b[2 * g], b[:, :, 0:128])
                    fw.store("sp", b, WD_b[2 * g + 1], b[:, :, 128:256])
                fw.barrier()
              vfm = fw.sb([128, 96, 24], F32, name="vfm", es=esl)
              cw_fm = fw.sb([128, 4, 3], F32, name="cw_fm", es=esl)
              mods = fw.sb([128, 96, 2], F32, name="mods", es=esl)
              A1 = fw.sb([128, 16, 2], F32, name="A1", es=esl)
              A2 = fw.sb([128, 16, 2], F32, name="A2", es=esl)
              omk = fw.sb([128, 8], F32, name="omk", es=esl)
              omk2 = fw.sb([128, 8], F32, name="omk2", es=esl)
              with ExitStack() as es:
                NR = 24
                rowb = fw.sb([NR, 6 * D], F32, es=es)
                fw.op("pool", lambda e: e.memset(rowb[:], 0.0), writes=[rowb])
                rowspec = [(0, b_mod[li:li + 1, :], 6 * D), (1, norm_mix[li:li + 1, :], D), (2, norm_ffn[li:li + 1, :], D),
                           (3, rw_shift[li, :, 0:3072], 3072), (6, rw_shift[li, :, WD0:WD0 + 96], 96),
                           (9, rw_shift[li, :, AD0:AD0 + 96], 96), (12, rw_shift[li, :, GD0:GD0 + 256], 256),
                           (15, dec_w0[li], D_RWKV), (17, iclr_a0[li], D_RWKV), (19, k_k[li:li + 1, :], D_RWKV),
                           (20, k_a[li:li + 1, :], D_RWKV), (21, r_k[li:li + 1, :], D_RWKV), (22, ln_w[li:li + 1, :], D_RWKV),
                           (23, ln_b[li:li + 1, :], D_RWKV)]
                for (r0, src, ln) in rowspec:
                    nr = src.shape[0]
                    fw.dma("sp", rowb[r0:r0 + nr, 0:ln], src, rowb, writes=[rowb])
                crow3 = fw.sb([3, 512], F32, es=es)
                fw.load("sp", crow3, crow3[:], conv_w[li])
                rows_to_fm(es, rowb, NR, 96, vfm)
                rows_to_fm(es, crow3, 3, 4, cw_fm)

                wm = Ring([fw.sb([128, 16, 512], F32, es=es) for _ in range(2)])
                psm = Ring([fw.ps([128, 4, 2], F32, es=es) for _ in range(2)])
                for g in range(24):
                    b = wm.next()
                    fw.load("sp" if g % 2 == 0 else "act", b, b[:], w_mod[li, :, g * 512:(g + 1) * 512].rearrange("(c p) n -> p c n", p=128))
                    pm = psm.next()
                    fns = []
                    for j in range(4):
                        for kc in range(16):
                            fns.append(lambda e, j=j, kc=kc: e.matmul(pm[:, j, :], lhsT=b[:, kc, j * 128:(j + 1) * 128], rhs=sc_fm[:, kc, :],
                                                                      start=(kc == 0), stop=(kc == 15)))
                    fw.ops("pe", fns, reads=[b, sc_fm], writes=[pm])
                    for r in range(2):
                        fw.op("dve", lambda e, r=r: e.tensor_tensor(out=mods[:, g * 4:(g + 1) * 4, r], in0=pm[:, :, r],
                                                                    in1=vfm[:, g * 4:(g + 1) * 4, 0], op=ALU.add),
                              reads=[pm, vfm], writes=[mods])
                for r in range(2):
                    fw.op("dve", lambda e, r=r: e.scalar_tensor_tensor(out=A1[:, :, r], in0=mods[:, 16:32, r], scalar=1.0, in1=vfm[:, 0:16, 1],
                                                                       op0=ALU.add, op1=ALU.mult), reads=[mods, vfm], writes=[A1])
                    fw.op("dve", lambda e, r=r: e.scalar_tensor_tensor(out=A2[:, :, r], in0=mods[:, 64:80, r], scalar=1.0, in1=vfm[:, 0:16, 2],
                                                                       op0=ALU.add, op1=ALU.mult), reads=[mods, vfm], writes=[A2])
                if "MODS" in dbg:
                    fw.store("sp", mods, MODS, mods[:].rearrange("p q r -> p (q r)"))
                fw.op("dve", lambda e: e.tensor_scalar(out=omk[:], in0=vfm[:, 0:8, 20], scalar1=-1.0, scalar2=1.0, op0=ALU.mult, op1=ALU.add),
                      reads=[vfm], writes=[omk])
                fw.op("dve", lambda e: e.tensor_scalar(out=omk2[:], in0=omk[:], scalar1=2.0, scalar2=None, op0=ALU.mult), reads=[omk], writes=[omk2])
                fw.barrier()
              if True:
                with ExitStack() as es1:
                    xs = fw.sb([128, 16, NT], F32, es=es1)
                    sq = fw.sb([128, 16, NT], BF16, es=es1)
                    rstd = fw.sb([128, NT], F32, es=es1)
                    tmp = Ring([fw.sb([128, NT], F32, es=es1) for _ in range(2)])
                    xn2 = [fw.sb([128, 16, 2 * NT], BF16, es=es1) for _ in range(2)]
                    wr = Ring([fw.sb([128, 16, 512], BF16, es=es1) for _ in range(2)])
                    pss = fw.ps([128, NT], F32, es=es1)
                    pso = Ring([fw.ps([128, 2 * NT], F32, es=es1) for _ in range(3)])
                    ost = Ring([fw.sb([128, 2 * NT], F32, es=es1) for _ in range(3)])
                    supers = []
                    for c0 in range(0, CTX, 2 * NT):
                        supers.append((c0, min(2 * NT, CTX - c0), 1))
                    for c0 in range(0, T, 2 * NT):
                        supers.append((CTX + c0, min(2 * NT, T - c0), 0))

                    def norm_mod(col0, n, r, Amod, sh_q0, dst, dst0, src_buf=None):
                        xb = src_buf
                        if xb is None:
                            xb = xs
                            fw.load("sp", xs, xs[:, :, 0:n], XTv[:, :, col0:col0 + n])
                            yield
                        fw.op("act", lambda e: e.activation(out=sq[:, :, 0:n], in_=xb[:, :, 0:n], func=AF.Square), reads=[xb], writes=[sq])
                        yield
                        fw.ops("pe", [lambda e, dc=dc: e.matmul(pss[:, 0:n], lhsT=onesb[:], rhs=sq[:, dc, 0:n], start=(dc == 0), stop=(dc == 15))
                                      for dc in range(16)], reads=[sq, onesb], writes=[pss])
                        yield
                        fw.op("act", lambda e: e.activation(out=rstd[:, 0:n], in_=pss[:, 0:n], func=AF.Sqrt, bias=RMS_EPS, scale=1.0 / D),
                              reads=[pss], writes=[rstd])
                        yield
                        fw.op("dve", lambda e: e.reciprocal(out=rstd[:, 0:n], in_=rstd[:, 0:n]), reads=[rstd], writes=[rstd])
                        yield
                        for dc in range(16):
                            tb = tmp.next()
                            fw.op("dve", lambda e, dc=dc: e.scalar_tensor_tensor(out=tb[:, 0:n], in0=xb[:, dc, 0:n], scalar=Amod[:, dc, r:r + 1],
                                                                                 in1=rstd[:, 0:n], op0=ALU.mult, op1=ALU.mult),
                                  reads=[xb, Amod, rstd], writes=[tb])
                            yield
                            fw.op("act", lambda e, dc=dc: e.activation(out=dst[:, dc, dst0:dst0 + n], in_=tb[:, 0:n], func=AF.Identity,
                                                                       bias=mods[:, sh_q0 + dc, r:r + 1], scale=1.0),
                                  reads=[tb, mods], writes=[dst])
                            yield

                    ev = 0
                    def norm_super(si):
                        (s0_, sn_, r_) = supers[si]
                        for o in range(0, sn_, NT):
                            for _ in norm_mod(s0_ + o, min(NT, sn_ - o), r_, A1, 0, xn2[si % 2], o):
                                yield

                    for _ in norm_super(0):
                        pass
                    for si, (s0, sn, r) in enumerate(supers):
                        xn = xn2[si % 2]
                        ngen = norm_super(si + 1) if si + 1 < len(supers) else None
                        for g in range(11):
                            wb = wr.next()
                            fw.load("act", wb, wb[:], WIN_b[g])
                            for j in range(4):
                                pq = g * 4 + j
                                if pq == 27:
                                    pass
                                po = pso.next()
                                fns = []
                                for h0 in range(0, sn, NT):
                                    hn = min(NT, sn - h0)
                                    for kc in range(16):
                                        fns.append(lambda e, h0=h0, hn=hn, kc=kc, j=j: e.matmul(po[:, h0:h0 + hn], lhsT=wb[:, kc, j * 128:(j + 1) * 128],
                                                                                               rhs=xn[:, kc, h0:h0 + hn], start=(kc == 0), stop=(kc == 15)))
                                fw.ops("pe", fns, reads=[wb, xn], writes=[po])
                                ob = ost.next()
                                if ev % 2 == 0:
                                    fw.op("dve", lambda e: e.tensor_copy(out=ob[:, 0:sn], in_=po[:, 0:sn]), reads=[po], writes=[ob])
                                else:
                                    fw.op("act", lambda e: e.activation(out=ob[:, 0:sn], in_=po[:, 0:sn], func=AF.Copy), reads=[po], writes=[ob])
                                ev += 1
                                fw.store("sp", ob, PXT[pq * 128:(pq + 1) * 128, s0:s0 + sn], ob[:, 0:sn])
                                if ngen is not None:
                                    for _ in range(2):
                                        try:
                                            next(ngen)
                                        except StopIteration:
                                            ngen = None
                                            break
                        if ngen is not None:
                            for _ in ngen:
                                pass
                fw.barrier()
                if "stop_p1" in dbg:
                    break

                with ExitStack() as es2:
                    HP = 8
                    SL = 9
                    for d_ in dbg:
                        if d_.startswith('sl='):
                            SL = int(d_[3:])
                    SDT = F32 if 'scan32' in dbg else BF16
                    identS = ident if 'scan32' in dbg else identb
                    IDT = BF16 if 'inv16' in dbg else F32
                    decw = fw.sb([96, 2, D_RWKV], F32, es=es2)
                    iclw = fw.sb([96, 2, D_RWKV], F32, es=es2)
                    gupw = fw.sb([128, 2, D_RWKV], BF16, es=es2)
                    for d in range(2):
                        fw.dma("sp", decw[:, d, :], dec_up[li, d], decw, writes=[decw])
                        fw.dma("sp", iclw[:, d, :], iclr_up[li, d], iclw, writes=[iclw])
                    fw.load("pool", gupw, gupw[:], g_up[li].rearrange("(c p) n -> p c n", p=128))
                    for b in (decw, iclw, gupw):
                        pass
                    S2g = [fw.sb([128, 4, 128], F32, es=es2) for _ in range(2)]
                    S2bg = [fw.sb([128, 4, 128], SDT, es=es2) for _ in range(2)]
                    wdh = fw.sb([96, NT + 2], F32, es=es2)
                    adh = fw.sb([96, NT + 2], F32, es=es2)
                    gdh = fw.sb([128, 2, NT + 2], F32, es=es2)
                    wdt2 = [fw.sb([96, NT], F32, es=es2) for _ in range(2)]
                    ads2 = [fw.sb([96, NT], F32, es=es2) for _ in range(2)]
                    gds2 = [fw.sb([128, 2, NT], BF16, es=es2) for _ in range(2)]
                    rkv_h = Ring([[fw.sb([128, NT + 2], F32, es=es2) for _ in range(3)] for _ in range(1)])
                    r_s = Ring([fw.sb([128, NT], F32, es=es2) for _ in range(1)])
                    ks_s2 = [[fw.sb([128, NT], BF16, es=es2) for _ in range(4)] for _ in range(2)]
                    rkb = Ring([[fw.sb([128, NT + 2], BF16, es=es2) for _ in range(3)] for _ in range(1)])
                    dgr = Ring([fw.sb([128, 9, 128], BF16, es=es2) for _ in range(2)])
                    dgfr = Ring([fw.sb([128, 3, 128], F32, es=es2) for _ in range(1)])
                    vb_s2 = [[fw.sb([128, NT], SDT, es=es2) for _ in range(4)] for _ in range(2)]
                    strm2 = [[fw.sb([128, NT // CH, 4, CH], SDT, es=es2) for _ in range(4)] for _ in range(2)]
                    gC2 = [[fw.sb([128, NT // CH], F32, es=es2) for _ in range(4)] for _ in range(2)]
                    ybuf = fw.sb([128, 4, NT], F32, es=es2)
                    tw = Ring([fw.sb([128, NT], F32, es=es2) for _ in range(11)])
                    TTall = fw.sb([128, 4, 3, 128], SDT, es=es2)
                    GBK = fw.sb([128, 8, 2, 256], SDT, es=es2)
                    PkAll = [fw.sb([128, 8, 2, 128], IDT, es=es2) for _ in range(2)]
                    Zt = [fw.sb([128, 8, 64], IDT, es=es2) for _ in range(2)]
                    UP = fw.sb([128, 4, 128], SDT, es=es2)
                    stm = fw.sb([128, 4, 128], F32, es=es2)
                    psA = Ring([fw.ps([128, 512], F32, es=es2) for _ in range(3)])
                    psZ = Ring([fw.ps([128, 512], F32, es=es2) for _ in range(3)])
                    psP = Ring([fw.ps([128, 512], F32, es=es2) for _ in range(2)])

                    yio = Ring([fw.sb([128, NT], F32, es=es2) for _ in range(2)])
                    yob = Ring([fw.sb([128, NT], BF16, es=es2) for _ in range(2)])

                    for hg_ in range(2):
                        fw.op("pool", lambda e, hg_=hg_: e.memset(S2g[hg_][:], 0.0), writes=[S2g[hg_]])
                        fw.op("pool", lambda e, hg_=hg_: e.memset(S2bg[hg_][:], 0.0), writes=[S2bg[hg_]])

                    ptmp = Ring([fw.sb([128, NT], F32, es=es2) for _ in range(2)])

                    def shift3(eng, dst, src, n, wq, rows, row0, dv=None, sv_=None):
                        dv = dv or (lambda: dst[0:rows, 0:n])
                        sv_ = sv_ or (lambda a, b_: src[0:rows, a:b_])
                        w = lambda tap: vfm[0:rows, wq, row0 + tap:row0 + tap + 1]
                        fw.op(eng, lambda e: e.tensor_scalar(out=dv(), in0=sv_(1, n + 1), scalar1=w(1), scalar2=None, op0=ALU.mult), reads=[src, vfm], writes=[dst])
                        for tap, a in ((0, 0), (2, 2)):
                            if eng == "dve":
                                fw.op(eng, lambda e: e.scalar_tensor_tensor(out=dv(), in0=sv_(a, a + n), scalar=w(tap), in1=dv(), op0=ALU.mult, op1=ALU.add),
                                      reads=[src, vfm, dst], writes=[dst])
                            else:
                                pt_ = ptmp.next()
                                fw.op(eng, lambda e: e.tensor_scalar(out=pt_[0:rows, 0:n], in0=sv_(a, a + n), scalar1=w(tap), scalar2=None, op0=ALU.mult),
                                      reads=[src, vfm], writes=[pt_])
                                fw.op(eng, lambda e: e.tensor_tensor(out=dv(), in0=dv(), in1=pt_[0:rows, 0:n], op=ALU.add), reads=[dst, pt_], writes=[dst])

                    def load_halo(q, buf, dst_rows, row_a, row_b, col0, n, first, last):
                        lo = 1 if first else 0
                        hi = n + 1 if last else n + 2
                        if first:
                            fw.op("pool", lambda e: e.memset(dst_rows[:, 0:1], 0.0), writes=[buf])
                        if last:
                            fw.op("pool", lambda e: e.memset(dst_rows[:, n + 1:n + 2], 0.0), writes=[buf])
                        fw.dma(q, dst_rows[:, lo:hi], PXT[row_a:row_b, col0 - 1 + lo:col0 - 1 + hi], buf, writes=[buf])

                    for pas in (0, 1):
                        if pas == 0:
                            order = list(tiles)
                        else:
                            ctx_t = [t for t in tiles if t[2]]
                            x_t = [t for t in tiles if not t[2]]
                            order = ctx_t[::-1] + x_t[::-1]
                        msk = m_f if pas == 0 else m_b
                        mskT = mt_f if pas == 0 else mt_b
                        pump_ref = [lambda k: None]
                        pump_k = [4]

                        def unpack(tile):
                            (col0, n, is_ctx, first, last) = tile
                            return col0, n, is_ctx, first, last, n // CH, (is_ctx and last_layer)
                        def lora_shared(tile, lp):
                            col0, n, is_ctx, first, last, ncn, skip_fin = unpack(tile)
                            wdt, ads, gds = wdt2[lp], ads2[lp], gds2[lp]

                            def pe_shift(src, rows, wq, row0, evac, sf=None):
                                dgf = dgfr.next()
                                for tap in range(3):
                                    fw.op("dve", lambda e, tap=tap: e.tensor_scalar(out=dgf[0:rows, tap, 0:rows], in0=ident[0:rows, 0:rows],
                                                                                   scalar1=vfm[0:rows, wq, row0 + tap:row0 + tap + 1], scalar2=None, op0=ALU.mult),
                                          reads=[ident, vfm], writes=[dgf])
                                psh = psP.next()
                                fw.ops("pe", [lambda e, tap=tap: e.matmul(psh[0:rows, 0:n], lhsT=dgf[0:rows, tap, 0:rows], rhs=(sf(tap, tap + n) if sf else src[0:rows, tap:tap + n]),
                                                                         start=(tap == 0), stop=(tap == 2)) for tap in range(3)], reads=[dgf, src], writes=[psh])
                                evac(psh)

                            load_halo("sp", wdh, wdh[0:96, 0:n + 2], WD0, WD0 + 96, col0, n, first, last)
                            load_halo("sp", adh, adh[0:96, 0:n + 2], AD0, AD0 + 96, col0, n, first, last)
                            pe_shift(wdh, 96, 0, 6, lambda psh: fw.op("act", lambda e: e.activation(out=wdt[:, 0:n], in_=psh[0:96, 0:n], func=AF.Tanh), reads=[psh], writes=[wdt]))
                            pe_shift(adh, 96, 0, 9, lambda psh: fw.op("act", lambda e: e.activation(out=ads[:, 0:n], in_=psh[0:96, 0:n], func=AF.Copy), reads=[psh], writes=[ads]))
                            if pas == 1 and not skip_fin:
                                for c2 in range(2):
                                    load_halo("sp", gdh, gdh[:, c2, 0:n + 2], GD0 + c2 * 128, GD0 + (c2 + 1) * 128, col0, n, first, last)
                                for c2 in range(2):
                                    pe_shift(gdh, 128, c2, 12,
                                             sf=lambda a_, b_, c2=c2: gdh[:, c2, a_:b_], evac=lambda psh, c2=c2: fw.op("act", lambda e: e.activation(out=gds[:, c2, 0:n], in_=psh[:, 0:n], func=AF.Sigmoid), reads=[psh], writes=[gds]))
                        def prep_hp(tile, hp, sset, lp):
                            col0, n, is_ctx, first, last, ncn, skip_fin = unpack(tile)
                            wdt, ads, gds = wdt2[lp], ads2[lp], gds2[lp]
                            strm, vb_s, gC, ks_s = strm2[sset], vb_s2[sset], gC2[sset], ks_s2[sset]
                            hb = rkv_h.next()
                            for i3, base in enumerate((R0, K0, V0)):
                                load_halo("sp" if i3 != 1 else "act", hb[i3], hb[i3][:, 0:n + 2], base + hp * 128, base + (hp + 1) * 128, col0, n, first, last)
                                yield
                            rr = r_s.next()
                            kk_ = tw.next()
                            hbb = rkb.next()
                            dg = dgr.next()
                            for i3, q_ in enumerate((R0 // 128 + hp, K0 // 128 + hp, V0 // 128 + hp)):
                                fw.op("act", lambda e: e.activation(out=hbb[i3][:, 0:n + 2], in_=hb[i3][:, 0:n + 2], func=AF.Copy), reads=[hb[i3]], writes=[hbb[i3]])
                                yield
                                for tap in range(3):
                                    fw.op("dve", lambda e: e.tensor_scalar(out=dg[:, i3 * 3 + tap, :], in0=identb[:], scalar1=vfm[:, q_, 3 + tap:4 + tap], scalar2=None, op0=ALU.mult),
                                          reads=[identb, vfm], writes=[dg])
                                yield
                                psh = psP.next()
                                fw.ops("pe", [lambda e, tap=tap: e.matmul(psh[:, 0:n], lhsT=dg[:, i3 * 3 + tap, :], rhs=hbb[i3][:, tap:tap + n], start=(tap == 0), stop=(tap == 2))
                                              for tap in range(3)], reads=[dg, hbb[i3]], writes=[psh])
                                yield
                                dst_ = (rr, kk_, vb_s[hp % 4])[i3]
                                fw.op("act", lambda e: e.activation(out=dst_[:, 0:n], in_=psh[:, 0:n], func=AF.Copy), reads=[psh], writes=[dst_])
                                yield
                            t1 = tw.next()
                            fw.op("dve", lambda e: e.tensor_scalar(out=t1[:, 0:n], in0=kk_[:, 0:n], scalar1=vfm[:, hp, 19:20], scalar2=None, op0=ALU.mult),
                                  reads=[kk_, vfm], writes=[t1])
                            yield
                            t2 = tw.next()
                            fw.op("act", lambda e: e.activation(out=t2[:, 0:n], in_=t1[:, 0:n], func=AF.Square), reads=[t1], writes=[t2])
                            yield
                            pa = psP.next()
                            fw.op("pe", lambda e: e.matmul(pa[:, 0:n], lhsT=blk[:], rhs=t2[:, 0:n], start=True, stop=True), reads=[blk, t2], writes=[pa])
                            yield
                            fw.op("act", lambda e: e.activation(out=t2[:, 0:n], in_=pa[:, 0:n], func=AF.Sqrt), reads=[pa], writes=[t2])
                            yield
                            fw.op("dve", lambda e: e.tensor_scalar(out=t2[:, 0:n], in0=t2[:, 0:n], scalar1=1e-12, scalar2=None, op0=ALU.max), reads=[t2], writes=[t2])
                            yield
                            fw.op("dve", lambda e: e.reciprocal(out=t2[:, 0:n], in_=t2[:, 0:n]), reads=[t2], writes=[t2])
                            yield
                            kkn = t1
                            fw.op("dve", lambda e: e.tensor_tensor(out=kkn[:, 0:n], in0=t1[:, 0:n], in1=t2[:, 0:n], op=ALU.mult), reads=[t1, t2], writes=[kkn])
                            yield
                            d = pas
                            pw = psP.next()
                            fw.op("pe", lambda e: e.matmul(pw[:, 0:n], lhsT=decw[:, d, hp * 128:(hp + 1) * 128], rhs=wdt[:, 0:n], start=True, stop=True),
                                  reads=[decw, wdt], writes=[pw])
                            yield
                            lw = tw.next()
                            fw.op("act", lambda e: e.activation(out=lw[:, 0:n], in_=pw[:, 0:n], func=AF.Sigmoid, bias=vfm[:, hp, 15 + d:16 + d], scale=1.0),
                                  reads=[pw, vfm], writes=[lw])
                            yield
                            fw.op("dve", lambda e: e.tensor_scalar(out=lw[:, 0:n], in0=lw[:, 0:n], scalar1=-0.6065306597126334, scalar2=None, op0=ALU.mult),
                                  reads=[lw], writes=[lw])
                            yield
                            pa2 = psP.next()
                            fw.op("pe", lambda e: e.matmul(pa2[:, 0:n], lhsT=iclw[:, d, hp * 128:(hp + 1) * 128], rhs=ads[:, 0:n], start=True, stop=True),
                                  reads=[iclw, ads], writes=[pa2])
                            yield
                            aa = tw.next()
                            fw.op("act", lambda e: e.activation(out=aa[:, 0:n], in_=pa2[:, 0:n], func=AF.Sigmoid, bias=vfm[:, hp, 17 + d:18 + d], scale=1.0),
                                  reads=[pa2, vfm], writes=[aa])
                            yield
                            kd = tw.next()
                            fw.op("dve", lambda e: e.tensor_scalar(out=kd[:, 0:n], in0=aa[:, 0:n], scalar1=vfm[:, hp, 20:21], scalar2=omk[:, hp:hp + 1],
                                                                   op0=ALU.mult, op1=ALU.add), reads=[aa, vfm, omk], writes=[kd])
                            yield
                            fw.op("dve", lambda e: e.tensor_tensor(out=kd[:, 0:n], in0=kd[:, 0:n], in1=kk_[:, 0:n], op=ALU.mult), reads=[kd, kk_], writes=[kd])
                            yield
                            if pas == 1 and not skip_fin:
                                pa3 = psP.next()
                                fw.op("pe", lambda e: e.matmul(pa3[:, 0:n], lhsT=iclw[:, 0, hp * 128:(hp + 1) * 128], rhs=ads[:, 0:n], start=True, stop=True),
                                      reads=[iclw, ads], writes=[pa3])
                                yield
                                af = tw.next()
                                fw.op("act", lambda e: e.activation(out=af[:, 0:n], in_=pa3[:, 0:n], func=AF.Sigmoid, bias=vfm[:, hp, 17:18], scale=1.0),
                                      reads=[pa3, vfm], writes=[af])
                                yield
                                fw.op("dve", lambda e: e.tensor_tensor(out=af[:, 0:n], in0=af[:, 0:n], in1=aa[:, 0:n], op=ALU.add), reads=[af, aa], writes=[af])
                                yield
                                fw.op("dve", lambda e: e.tensor_scalar(out=af[:, 0:n], in0=af[:, 0:n], scalar1=vfm[:, hp, 20:21], scalar2=omk2[:, hp:hp + 1],
                                                                       op0=ALU.mult, op1=ALU.add), reads=[af, vfm, omk2], writes=[af])
                                yield
                                fw.op("dve", lambda e: e.tensor_tensor(out=af[:, 0:n], in0=af[:, 0:n], in1=kk_[:, 0:n], op=ALU.mult), reads=[af, kk_], writes=[af])
                                yield
                                fw.op("dve", lambda e: e.scalar_tensor_tensor(out=ks_s[hp % 4][:, 0:n], in0=af[:, 0:n], scalar=vfm[:, hp, 21:22], in1=rr[:, 0:n],
                                                                              op0=ALU.mult, op1=ALU.mult), reads=[af, vfm, rr], writes=[ks_s[hp % 4]])
                                yield
                            Lc = tw.next()
                            fw.op("dve", lambda e: e.tensor_tensor_scan(out=Lc[:, 0:n], data0=cmask[:, 0:n], data1=lw[:, 0:n], initial=0.0,
                                                                        op0=ALU.mult, op1=ALU.add), reads=[cmask, lw], writes=[Lc])
                            yield
                            Ginc, Gexc, Ginv = tw.next(), tw.next(), tw.next()
                            st = strm[hp % 4]
                            if pas == 0:
                                fw.op("act", lambda e: e.activation(out=Ginc[:, 0:n], in_=Lc[:, 0:n], func=AF.Exp), reads=[Lc], writes=[Ginc])
                                yield
                                fw.op("act", lambda e: e.activation(out=Ginv[:, 0:n], in_=Lc[:, 0:n], func=AF.Exp, scale=-1.0), reads=[Lc], writes=[Ginv])
                                yield
                                fw.op("dve", lambda e: e.tensor_tensor(out=Gexc[:, 0:n], in0=Lc[:, 0:n], in1=lw[:, 0:n], op=ALU.subtract), reads=[Lc, lw], writes=[Gexc])
                                yield
                                fw.op("act", lambda e: e.activation(out=Gexc[:, 0:n], in_=Gexc[:, 0:n], func=AF.Exp), reads=[Gexc], writes=[Gexc])
                                yield
                                for c in range(ncn):
                                    fw.op("pool", lambda e, c=c: e.tensor_copy(out=gC[hp % 4][:, c:c + 1], in_=Ginc[:, c * CH + CH - 1:c * CH + CH]),
                                          reads=[Ginc], writes=[gC[hp % 4]])
                                    yield
                            else:
                                for c in range(ncn):
                                    fw.op("dve", lambda e, c=c: e.tensor_scalar(out=Gexc[:, c * CH:(c + 1) * CH], in0=Lc[:, c * CH:(c + 1) * CH],
                                                                                scalar1=Lc[:, c * CH + CH - 1:c * CH + CH], scalar2=None, op0=ALU.subtract),
                                          reads=[Lc], writes=[Gexc])
                                    yield
                                    fw.op("act", lambda e, c=c: e.activation(out=gC[hp % 4][:, c:c + 1], in_=Lc[:, c * CH + CH - 1:c * CH + CH], func=AF.Exp),
                                          reads=[Lc], writes=[gC[hp % 4]])
                                    yield
                                fw.op("dve", lambda e: e.tensor_tensor(out=Ginc[:, 0:n], in0=lw[:, 0:n], in1=Gexc[:, 0:n], op=ALU.subtract), reads=[lw, Gexc], writes=[Ginc])
                                yield
                                fw.op("act", lambda e: e.activation(out=Ginv[:, 0:n], in_=Ginc[:, 0:n], func=AF.Exp, scale=-1.0), reads=[Ginc], writes=[Ginv])
                                yield
                                fw.op("act", lambda e: e.activation(out=Ginc[:, 0:n], in_=Ginc[:, 0:n], func=AF.Exp), reads=[Ginc], writes=[Ginc])
                                yield
                                fw.op("act", lambda e: e.activation(out=Gexc[:, 0:n], in_=Gexc[:, 0:n], func=AF.Exp, scale=-1.0), reads=[Gexc], writes=[Gexc])
                                yield
                            sv = lambda i4: st[:, 0:ncn, i4, :]
                            v3 = lambda b_: b_[:, 0:n].rearrange("p (c t) -> p c t", t=CH)
                            fw.op("dve", lambda e: e.scalar_tensor_tensor(out=sv(0), in0=v3(kkn), scalar=-1.0, in1=v3(Gexc), op0=ALU.mult, op1=ALU.mult),
                                  reads=[kkn, Gexc], writes=[st])
                            yield
                            fw.op("pool", lambda e: e.tensor_tensor(out=sv(1), in0=v3(rr), in1=v3(Ginc), op=ALU.mult), reads=[rr, Ginc], writes=[st])
                            yield
                            fw.op("dve", lambda e: e.tensor_tensor(out=aa[:, 0:n], in0=aa[:, 0:n], in1=kkn[:, 0:n], op=ALU.mult), reads=[aa, kkn], writes=[aa])
                            yield
                            fw.op("dve", lambda e: e.tensor_tensor(out=sv(2), in0=v3(aa), in1=v3(Ginv), op=ALU.mult), reads=[aa, Ginv], writes=[st])
                            yield
                            fw.op("pool", lambda e: e.tensor_tensor(out=sv(3), in0=v3(kd), in1=v3(Ginv), op=ALU.mult), reads=[kd, Ginv], writes=[st])
                            yield

                        def scan_chunk(c, hg, sset):
                            strm, vb_s, gC = strm2[sset], vb_s2[sset], gC2[sset]
                            S2_, S2b_ = S2g[hg], S2bg[hg]
                            NLEV = 7
                            if True:
                                cs = slice(c * CH, (c + 1) * CH)
                                for g in range(4):
                                    st = strm[g]
                                    pb = psA.next()
                                    fw.ops("pe", [lambda e: e.matmul(pb[:, 0:128], lhsT=vb_s[g][:, cs], rhs=identS[:], start=True, stop=True),
                                                  lambda e: e.matmul(pb[:, 128:256], lhsT=st[:, c, 2, :], rhs=identS[:], start=True, stop=True),
                                                  lambda e: e.matmul(pb[:, 256:384], lhsT=st[:, c, 3, :], rhs=identS[:], start=True, stop=True)],
                                           reads=[vb_s[g], st, identS], writes=[pb])
                                    fw.op("act", lambda e: e.activation(out=TTall[:, g, :, :], in_=pb[:, 0:384].rearrange("p (a b) -> p a b", b=128), func=AF.Copy),
                                          reads=[pb], writes=[TTall])
                                pump_ref[0](pump_k[0])
                                for k in range(8):
                                    g, h = k // 2, k % 2
                                    st = strm[g]
                                    hs = slice(h * 64, (h + 1) * 64)
                                    pg = psA.next()
                                    fw.ops("pe", [lambda e: e.matmul(pg[:, 0:256], lhsT=st[hs, c, 2, :], rhs=st[hs, c, 0:2, :], start=True, stop=True),
                                                  lambda e: e.matmul(pg[:, 256:512], lhsT=st[hs, c, 3, :], rhs=st[hs, c, 0:2, :], start=True, stop=True)],
                                           reads=[st], writes=[pg])
                                    fw.op("dve", lambda e: e.tensor_tensor(out=GBK[:, k, 0, :], in0=pg[:, 0:256], in1=msk[:], op=ALU.mult), reads=[pg, msk], writes=[GBK])
                                    fw.op("dve", lambda e: e.tensor_tensor(out=GBK[:, k, 1, :], in0=pg[:, 256:512], in1=msk[:], op=ALU.mult), reads=[pg, msk], writes=[GBK])
                                    fw.op("dve", lambda e: e.tensor_tensor(out=PkAll[0][:, k, 0, :], in0=pg[:, 0:128], in1=msk[:, 0:128], op=ALU.mult),
                                          reads=[pg, msk], writes=[PkAll[0]])
                                for m in range(2):
                                    pt = psZ.next()
                                    fns = []
                                    for kk2 in range(4):
                                        k = m * 4 + kk2
                                        g, h = k // 2, k % 2
                                        fns.append(lambda e, kk2=kk2, g=g, h=h: e.matmul(pt[:, kk2 * 128:(kk2 + 1) * 128], lhsT=strm[g][h * 64:(h + 1) * 64, c, 0, :],
                                                                                        rhs=strm[g][h * 64:(h + 1) * 64, c, 2, :], start=True, stop=True))
                                    fw.ops("pe", fns, reads=[strm[(m * 4) // 2], strm[(m * 4) // 2 + 1]], writes=[pt])
                                    for kk2 in range(4):
                                        k = m * 4 + kk2
                                        fw.op("dve", lambda e, kk2=kk2, k=k: e.tensor_tensor(out=PkAll[0][:, k, 1, :], in0=pt[:, kk2 * 128:(kk2 + 1) * 128], in1=mskT[:], op=ALU.mult),
                                              reads=[pt, mskT], writes=[PkAll[0]])
                                pump_ref[0](pump_k[0])
                                pX = psZ.next()
                                fns = []
                                for g in range(4):
                                    st = strm[g]
                                    for h in range(2):
                                        k = g * 2 + h
                                        fns.append(lambda e, g=g, h=h, k=k, st=st: e.matmul(pX[:, k * 64:(k + 1) * 64], lhsT=st[:, c, 0, :], rhs=S2b_[:, g, h * 64:(h + 1) * 64],
                                                                                          start=True, stop=False))
                                        fns.append(lambda e, g=g, h=h, k=k: e.matmul(pX[:, k * 64:(k + 1) * 64], lhsT=GBK[:, k, 1, 0:128], rhs=TTall[:, g, 0, h * 64:(h + 1) * 64],
                                                                                   start=False, stop=True))
                                fw.ops("pe", fns, reads=[strm[0], strm[1], strm[2], strm[3], S2b_, GBK, TTall], writes=[pX])
                                fw.op("act", lambda e: e.activation(out=Zt[0][:], in_=pX[:].rearrange("p (a b) -> p a b", b=64), func=AF.Copy), reads=[pX], writes=[Zt[0]])
                                pump_ref[0](pump_k[0])
                                for lv in range(NLEV):
                                    if lv > 0:
                                        pump_ref[0](pump_k[0])
                                    Pc, Pn = PkAll[lv % 2], PkAll[(lv + 1) % 2]
                                    zi, zo = Zt[lv % 2], Zt[(lv + 1) % 2]
                                    pz = psZ.next()
                                    fw.ops("pe", [lambda e, k=k: e.matmul(pz[:, k * 64:(k + 1) * 64], lhsT=Pc[:, k, 0, :], rhs=zi[:, k, :], start=True, stop=True)
                                                  for k in range(8)], reads=[Pc, zi], writes=[pz])
                                    if lv == NLEV - 1:
                                        fw.op("dve", lambda e: e.tensor_tensor(out=UP[:].rearrange("p g (h i) -> p (g h) i", i=64), in0=pz[:].rearrange("p (a b) -> p a b", b=64),
                                                                               in1=zi[:], op=ALU.add), reads=[pz, zi], writes=[UP])
                                        break
                                    fw.op("dve", lambda e: e.tensor_tensor(out=zo[:], in0=pz[:].rearrange("p (a b) -> p a b", b=64), in1=zi[:], op=ALU.add),
                                          reads=[pz, zi], writes=[zo])
                                    if lv < NLEV - 2:
                                        for m in range(4):
                                            pq = psA.next()
                                            fns = []
                                            for k in (2 * m, 2 * m + 1):
                                                o = (k % 2) * 256
                                                fns.append(lambda e, k=k, o=o: e.matmul(pq[:, o:o + 128], lhsT=Pc[:, k, 1, :], rhs=Pc[:, k, 0, :], start=True, stop=True))
                                                fns.append(lambda e, k=k, o=o: e.matmul(pq[:, o + 128:o + 256], lhsT=Pc[:, k, 0, :], rhs=Pc[:, k, 1, :], start=True, stop=True))
                                            fw.ops("pe", fns, reads=[Pc], writes=[pq])
                                            dst = Pn[:, 2 * m:2 * m + 2, :, :]
                                            src = pq[:].rearrange("p (a b c) -> p a b c", a=2, b=2)
                                            if m % 2 == 0:
                                                fw.op("act", lambda e: e.activation(out=dst, in_=src, func=AF.Copy), reads=[pq], writes=[Pn])
                                            else:
                                                fw.op("dve", lambda e: e.tensor_copy(out=dst, in_=src), reads=[pq], writes=[Pn])
                                    else:
                                        for m in range(2):
                                            pq = psA.next()
                                            fw.ops("pe", [lambda e, k=k: e.matmul(pq[:, (k % 4) * 128:(k % 4 + 1) * 128], lhsT=Pc[:, k, 1, :], rhs=Pc[:, k, 0, :], start=True, stop=True)
                                                          for k in range(4 * m, 4 * m + 4)], reads=[Pc], writes=[pq])
                                            dst = Pn[:, 4 * m:4 * m + 4, 0, :]
                                            src = pq[:].rearrange("p (a b) -> p a b", b=128)
                                            if m % 2 == 0:
                                                fw.op("act", lambda e: e.activation(out=dst, in_=src, func=AF.Copy), reads=[pq], writes=[Pn])
                                            else:
                                                fw.op("dve", lambda e: e.tensor_copy(out=dst, in_=src), reads=[pq], writes=[Pn])
                                pump_ref[0](pump_k[0])
                                pY = psZ.next()
                                fns = []
                                for g in range(4):
                                    st = strm[g]
                                    for h in range(2):
                                        k = g * 2 + h
                                        ro = slice(h * 64, (h + 1) * 64)
                                        co = slice(g * 128, (g + 1) * 128)
                                        fns.append(lambda e, g=g, ro=ro, co=co, st=st: e.matmul(pY[ro, co], lhsT=S2b_[:, g, ro], rhs=st[:, c, 1, :], start=True, stop=False))
                                        fns.append(lambda e, g=g, k=k, ro=ro, co=co: e.matmul(pY[ro, co], lhsT=UP[:, g, ro], rhs=GBK[:, k, 0, 128:256], start=False, stop=False))
                                        fns.append(lambda e, g=g, k=k, ro=ro, co=co: e.matmul(pY[ro, co], lhsT=TTall[:, g, 0, ro], rhs=GBK[:, k, 1, 128:256], start=False, stop=True))
                                fw.ops("pe", fns, reads=[strm[0], strm[1], strm[2], strm[3], S2b_, UP, GBK, TTall], writes=[pY])
                                fw.op("act", lambda e: e.activation(out=ybuf[:, :, cs], in_=pY[:].rearrange("p (a b) -> p a b", b=128), func=AF.Copy), reads=[pY], writes=[ybuf])
                                pump_ref[0](pump_k[0])
                                pS = psZ.next()
                                fns = []
                                for g in range(4):
                                    co = slice(g * 128, (g + 1) * 128)
                                    fns.append(lambda e, g=g, co=co: e.matmul(pS[:, co], lhsT=TTall[:, g, 1, :], rhs=UP[:, g, :], start=True, stop=False))
                                    fns.append(lambda e, g=g, co=co: e.matmul(pS[:, co], lhsT=TTall[:, g, 2, :], rhs=TTall[:, g, 0, :], start=False, stop=True))
                                fw.ops("pe", fns, reads=[TTall, UP], writes=[pS])
                                fw.op("dve", lambda e: e.tensor_tensor(out=stm[:], in0=pS[:].rearrange("p (a b) -> p a b", b=128), in1=S2_[:], op=ALU.add),
                                      reads=[pS, S2_], writes=[stm])
                                for g in range(4):
                                    fw.op("dve", lambda e, g=g: e.scalar_tensor_tensor(out=S2_[:, g, :], in0=stm[:, g, :], scalar=gC[g][:, c:c + 1], in1=blk[:],
                                                                                       op0=ALU.mult, op1=ALU.mult), reads=[stm, gC[g], blk], writes=[S2_])
                                fw.op("pool", lambda e: e.tensor_copy(out=S2b_[:], in_=S2_[:]), reads=[S2_], writes=[S2b_])
                                pump_ref[0](pump_k[0])

                        def outputs(tile, hg, sset, lp):
                            col0, n, is_ctx, first, last, ncn, skip_fin = unpack(tile)
                            hps = list(range(hg * 4, hg * 4 + 4))
                            gds = gds2[lp]
                            vb_s, ks_s = vb_s2[sset], ks_s2[sset]
                            for hp in hps:
                                rows = slice(hp * 128, (hp + 1) * 128)
                                if pas == 0:
                                    fw.store("sp", ybuf, YF[rows, col0:col0 + n], ybuf[:, hp % 4, 0:n])
                                    continue
                                if skip_fin or 'no_fin' in dbg:
                                    continue
                                yf = yio.next()
                                fw.load("sp", yf, yf[:, 0:n], YF[rows, col0:col0 + n])
                                y = yf
                                fw.op("dve", lambda e: e.tensor_tensor(out=y[:, 0:n], in0=yf[:, 0:n], in1=ybuf[:, hp % 4, 0:n], op=ALU.add), reads=[yf, ybuf], writes=[y])
                                pm_ = psA.next()
                                fw.op("pe", lambda e: e.matmul(pm_[:, 0:n], lhsT=blk[:], rhs=y[:, 0:n], start=True, stop=True), reads=[blk, y], writes=[pm_])
                                dd = tw.next()
                                fw.op("dve", lambda e: e.scalar_tensor_tensor(out=dd[:, 0:n], in0=pm_[:, 0:n], scalar=-1.0 / 64, in1=y[:, 0:n], op0=ALU.mult, op1=ALU.add),
                                      reads=[pm_, y], writes=[dd])
                                d2 = tw.next()
                                fw.op("act", lambda e: e.activation(out=d2[:, 0:n], in_=dd[:, 0:n], func=AF.Square), reads=[dd], writes=[d2])
                                pv_ = psA.next()
                                fw.op("pe", lambda e: e.matmul(pv_[:, 0:n], lhsT=blk[:], rhs=d2[:, 0:n], start=True, stop=True), reads=[blk, d2], writes=[pv_])
                                fw.op("act", lambda e: e.activation(out=d2[:, 0:n], in_=pv_[:, 0:n], func=AF.Sqrt, bias=GN_EPS, scale=1.0 / 64), reads=[pv_], writes=[d2])
                                fw.op("dve", lambda e: e.reciprocal(out=d2[:, 0:n], in_=d2[:, 0:n]), reads=[d2], writes=[d2])
                                fw.op("dve", lambda e: e.tensor_tensor(out=dd[:, 0:n], in0=dd[:, 0:n], in1=d2[:, 0:n], op=ALU.mult), reads=[dd, d2], writes=[dd])
                                fw.op("dve", lambda e: e.tensor_scalar(out=dd[:, 0:n], in0=dd[:, 0:n], scalar1=vfm[:, hp, 22:23], scalar2=vfm[:, hp, 23:24],
                                                                       op0=ALU.mult, op1=ALU.add), reads=[dd, vfm], writes=[dd])
                                pb_ = psA.next()
                                fw.op("pe", lambda e: e.matmul(pb_[:, 0:n], lhsT=blkb[:], rhs=ks_s[hp % 4][:, 0:n], start=True, stop=True), reads=[blkb, ks_s[hp % 4]], writes=[pb_])
                                fw.op("dve", lambda e: e.tensor_tensor(out=d2[:, 0:n], in0=pb_[:, 0:n], in1=vb_s[hp % 4][:, 0:n], op=ALU.mult), reads=[pb_, vb_s[hp % 4]], writes=[d2])
                                fw.op("dve", lambda e: e.tensor_tensor(out=dd[:, 0:n], in0=dd[:, 0:n], in1=d2[:, 0:n], op=ALU.add), reads=[dd, d2], writes=[dd])
                                pg_ = psA.next()
                                fw.ops("pe", [lambda e, c2=c2: e.matmul(pg_[:, 0:n], lhsT=gupw[:, c2, hp * 128:(hp + 1) * 128], rhs=gds[:, c2, 0:n],
                                                                        start=(c2 == 0), stop=(c2 == 1)) for c2 in range(2)],
                                       reads=[gupw, gds], writes=[pg_])
                                yo = yob.next()
                                fw.op("dve", lambda e: e.tensor_tensor(out=yo[:, 0:n], in0=pg_[:, 0:n], in1=dd[:, 0:n], op=ALU.mult), reads=[pg_, dd], writes=[yo])
                                fw.store("sp", yo, YT[rows, col0:col0 + n], yo[:, 0:n])
                        items = [(ti, hg) for ti in range(len(order)) for hg in range(2)]
                        lora_shared(order[0], 0)
                        for g in range(4):
                            for _ in prep_hp(order[0], g, 0, 0):
                                pass
                        pending = [None]

                        def pump(k):
                            gen = pending[0]
                            if gen is None:
                                return
                            for _ in range(k):
                                try:
                                    next(gen)
                                except StopIteration:
                                    pending[0] = None
                                    return

                        def chain_prep(nx, sset):
                            for g in range(4):
                                for _ in prep_hp(order[nx[0]], nx[1] * 4 + g, sset, nx[0] % 2):
                                    yield

                        pump_ref[0] = pump
                        for j, (ti, hg) in enumerate(items):
                            tile = order[ti]
                            ncn = tile[1] // CH
                            nxt = items[j + 1] if j + 1 < len(items) else None
                            if nxt is not None and nxt[0] != ti:
                                lora_shared(order[nxt[0]], nxt[0] % 2)
                            pending[0] = chain_prep(nxt, (j + 1) % 2) if nxt is not None else None
                            pump_k[0] = max(1, (4 * 62) // (ncn * 14) + 1)
                            crange = list(range(ncn)) if pas == 0 else list(range(ncn))[::-1]
                            for c in crange:
                                scan_chunk(c, hg, j % 2)
                            pump(100000)
                            outputs(tile, hg, j % 2, ti % 2)
                        fw.barrier()
                        if pas == 0:
                            for hg_ in range(2):
                                fw.op("pool", lambda e, hg_=hg_: e.memset(S2g[hg_][:], 0.0), writes=[S2g[hg_]])
                                fw.op("pool", lambda e, hg_=hg_: e.memset(S2bg[hg_][:], 0.0), writes=[S2bg[hg_]])
                fw.barrier()
                if "stop_p2" in dbg:
                    break

                with ExitStack() as es3:
                    cin = Ring([[fw.sb([128, NT], F32, es=es3) for _ in range(3)] for _ in range(2)])
                    cu = Ring([fw.sb([128, NT], F32, es=es3) for _ in range(2)])
                    co = Ring([fw.sb([128, NT], F32, es=es3) for _ in range(2)])
                    cob = Ring([fw.sb([128, NT], BF16, es=es3) for _ in range(2)])
                    for (col0, n, is_ctx, first, last) in tiles:
                        if is_ctx and last_layer:
                            continue
                        for q in range(4):
                            ib = cin.next()
                            for i3, base in enumerate((PCG, PCX, PCB)):
                                fw.load("sp", ib[i3], ib[i3][:, 0:n], PXT[base + q * 128:base + (q + 1) * 128, col0:col0 + n])
                            u = cu.next()
                            o = co.next()
                            fw.op("pool", lambda e: e.tensor_tensor(out=u[:, 0:n], in0=ib[0][:, 0:n], in1=ib[1][:, 0:n], op=ALU.mult), reads=[ib[0], ib[1]], writes=[u])
                            fw.op("dve", lambda e: e.tensor_scalar(out=o[:, 0:n], in0=u[:, 0:n], scalar1=cw_fm[:, q, 1:2], scalar2=None, op0=ALU.mult),
                                  reads=[u, cw_fm], writes=[o])
                            if is_ctx:
                                assert first and last
                                W_ = n
                            else:
                                W_ = 64
                            u3 = u[:, 0:n].rearrange("p (r w) -> p r w", w=W_)
                            o3 = o[:, 0:n].rearrange("p (r w) -> p r w", w=W_)
                            fw.op("dve", lambda e: e.scalar_tensor_tensor(out=o3[:, :, 1:W_], in0=u3[:, :, 0:W_ - 1], scalar=cw_fm[:, q, 0:1], in1=o3[:, :, 1:W_],
                                                                          op0=ALU.mult, op1=ALU.add), reads=[u, cw_fm, o], writes=[o])
                            fw.op("dve", lambda e: e.scalar_tensor_tensor(out=o3[:, :, 0:W_ - 1], in0=u3[:, :, 1:W_], scalar=cw_fm[:, q, 2:3], in1=o3[:, :, 0:W_ - 1],
                                                                          op0=ALU.mult, op1=ALU.add), reads=[u, cw_fm, o], writes=[o])
                            ob = cob.next()
                            fw.op("pool", lambda e: e.tensor_tensor(out=ob[:, 0:n], in0=o[:, 0:n], in1=ib[2][:, 0:n], op=ALU.mult), reads=[o, ib[2]], writes=[ob])
                            fw.store("sp", ob, YT[1024 + q * 128:1024 + (q + 1) * 128, col0:col0 + n], ob[:, 0:n])
                fw.barrier()

                with ExitStack() as es4:
                    LCmax = T // 128
                    UC = fw.sb([128, LCmax, 512], BF16, es=es4)
                    US = fw.sb([128, LCmax, 512], BF16, es=es4)
                    uT = Ring([fw.sb([128, NT], BF16, es=es4) for _ in range(3)])
                    pu = Ring([fw.ps([128, 512], F32, es=es4) for _ in range(2)])
                    py = [fw.ps([128, 512], F32, es=es4) for _ in range(4)]
                    LG = 8
                    tabr = Ring([[fw.sb([128, LG, 512], BF16, es=es4) for _ in range(2)] for _ in range(2)])
                    fo = Ring([fw.sb([128, 512], BF16, es=es4) for _ in range(3)])
                    for (seq0, sn, tab, is_ctx) in ((0, CTX, ftab_c, True), (CTX, T, ftab_x, False)):
                        if is_ctx and last_layer:
                            continue
                        nlc = sn // 128
                        for t0 in range(0, sn, NT):
                            tn = min(NT, sn - t0)
                            for mc in range(4):
                                ub = uT.next()
                                fw.load("pool", ub, ub[:, 0:tn], PXT[PFT + mc * 128:PFT + (mc + 1) * 128, seq0 + t0:seq0 + t0 + tn])
                                for lc in range(tn // 128):
                                    p_ = pu.next()
                                    fw.op("pe", lambda e: e.matmul(p_[:, 0:256], lhsT=ub[:, lc * 128:(lc + 1) * 128], rhs=c64[:], start=True, stop=True),
                                          reads=[ub, c64], writes=[p_])
                                    glc = t0 // 128 + lc
                                    if (lc + mc) % 2 == 0:
                                        fw.op("dve", lambda e: e.tensor_copy(out=UC[:, glc, mc * 128:(mc + 1) * 128], in_=p_[:, 0:128]), reads=[p_], writes=[UC])
                                        fw.op("dve", lambda e: e.tensor_copy(out=US[:, glc, mc * 128:(mc + 1) * 128], in_=p_[:, 128:256]), reads=[p_], writes=[US])
                                    else:
                                        fw.op("act", lambda e: e.activation(out=UC[:, glc, mc * 128:(mc + 1) * 128], in_=p_[:, 0:128], func=AF.Copy), reads=[p_], writes=[UC])
                                        fw.op("act", lambda e: e.activation(out=US[:, glc, mc * 128:(mc + 1) * 128], in_=p_[:, 128:256], func=AF.Copy), reads=[p_], writes=[US])
                        for k0 in range(0, sn, 512):
                            kn = min(512, sn - k0)
                            for lg0 in range(0, nlc, LG):
                                lgn = min(LG, nlc - lg0)
                                tb = tabr.next()
                                for i2 in range(2):
                                    fw.load("sp" if i2 == 0 else "act", tb[i2], tb[i2][:, 0:lgn, 0:kn],
                                            tab[i2, lg0 * 128:(lg0 + lgn) * 128, k0:k0 + kn].rearrange("(c p) k -> p c k", p=128))
                                for mc in range(4):
                                    fns = []
                                    for l_ in range(lgn):
                                        lc = lg0 + l_
                                        fns.append(lambda e, l_=l_, lc=lc: e.matmul(py[mc][:, 0:kn], lhsT=UC[:, lc, mc * 128:(mc + 1) * 128], rhs=tb[0][:, l_, 0:kn],
                                                                                    start=(lc == 0), stop=False))
                                        fns.append(lambda e, l_=l_, lc=lc: e.matmul(py[mc][:, 0:kn], lhsT=US[:, lc, mc * 128:(mc + 1) * 128], rhs=tb[1][:, l_, 0:kn],
                                                                                    start=False, stop=(lc == nlc - 1)))
                                    fw.ops("pe", fns, reads=[UC, US, tb[0], tb[1]], writes=[py[mc]])
                            for mc in range(4):
                                ob = fo.next()
                                fw.op("act" if mc % 2 else "dve",
                                      (lambda e: e.activation(out=ob[:, 0:kn], in_=py[mc][:, 0:kn], func=AF.Copy)) if mc % 2 else
                                      (lambda e: e.tensor_copy(out=ob[:, 0:kn], in_=py[mc][:, 0:kn])), reads=[py[mc]], writes=[ob])
                                fw.store("sp", ob, YT[1536 + mc * 128:1536 + (mc + 1) * 128, seq0 + k0:seq0 + k0 + kn], ob[:, 0:kn])
                fw.barrier()
                if "stop_p4" in dbg:
                    break

                with ExitStack() as es5:
                    xb_ = fw.sb([128, 16, NT], F32, es=es5)
                    yx = fw.sb([128, 16, NT], BF16, es=es5)
                    H = fw.sb([128, FC, NT], BF16, es=es5)
                    wring = Ring([fw.sb([128, 16, 256], BF16, es=es5) for _ in range(4)])
                    wdr = Ring([fw.sb([128, FC, 128], BF16, es=es5) for _ in range(2)])
                    sq = H
                    rstd = fw.sb([128, NT], F32, es=es5)
                    tmp = Ring([fw.sb([128, NT], F32, es=es5) for _ in range(2)])
                    sg = Ring([fw.sb([128, NT], F32, es=es5) for _ in range(2)])
                    pss = fw.ps([128, NT], F32, es=es5)
                    pA = Ring([fw.ps([128, NT], F32, es=es5) for _ in range(3)])
                    pG = Ring([fw.ps([128, NT], F32, es=es5) for _ in range(2)])
                    pU = Ring([fw.ps([128, NT], F32, es=es5) for _ in range(2)])
                    YTv = YT.rearrange("(c p) t -> p c t", p=128)
                    for (col0, n, is_ctx, first, last) in tiles:
                        if is_ctx and last_layer:
                            continue
                        r = 1 if is_ctx else 0
                        fw.load("sp", xb_, xb_[:, :, 0:n], XTv[:, :, col0:col0 + n])
                        fw.load("act", yx, yx[:, :, 0:n], YTv[:, :, col0:col0 + n])
                        for g in range(8):
                            wb = wring.next()
                            fw.load("sp" if g % 2 else "act", wb, wb[:], WOUT_b[g])
                            for j in range(2):
                                dc = g * 2 + j
                                po = pA.next()
                                fw.ops("pe", [lambda e, kc=kc: e.matmul(po[:, 0:n], lhsT=wb[:, kc, j * 128:(j + 1) * 128], rhs=yx[:, kc, 0:n],
                                                                        start=(kc == 0), stop=(kc == 15)) for kc in range(16)], reads=[wb, yx], writes=[po])
                                fw.op("dve", lambda e: e.scalar_tensor_tensor(out=xb_[:, dc, 0:n], in0=po[:, 0:n], scalar=mods[:, 32 + dc, r:r + 1], in1=xb_[:, dc, 0:n],
                                                                              op0=ALU.mult, op1=ALU.add), reads=[po, mods, xb_], writes=[xb_])
                        fw.op("act", lambda e: e.activation(out=sq[:, 0:16, 0:n], in_=xb_[:, :, 0:n], func=AF.Square), reads=[xb_], writes=[sq])
                        fw.ops("pe", [lambda e, dc=dc: e.matmul(pss[:, 0:n], lhsT=onesb[:], rhs=sq[:, dc, 0:n], start=(dc == 0), stop=(dc == 15))
                                      for dc in range(16)], reads=[sq, onesb], writes=[pss])
                        fw.op("act", lambda e: e.activation(out=rstd[:, 0:n], in_=pss[:, 0:n], func=AF.Sqrt, bias=RMS_EPS, scale=1.0 / D), reads=[pss], writes=[rstd])
                        fw.op("dve", lambda e: e.reciprocal(out=rstd[:, 0:n], in_=rstd[:, 0:n]), reads=[rstd], writes=[rstd])
                        for dc in range(16):
                            tb = tmp.next()
                            fw.op("dve", lambda e, dc=dc: e.scalar_tensor_tensor(out=tb[:, 0:n], in0=xb_[:, dc, 0:n], scalar=A2[:, dc, r:r + 1], in1=rstd[:, 0:n],
                                                                                 op0=ALU.mult, op1=ALU.mult), reads=[xb_, A2, rstd], writes=[tb])
                            fw.op("act", lambda e, dc=dc: e.activation(out=yx[:, dc, 0:n], in_=tb[:, 0:n], func=AF.Identity, bias=mods[:, 48 + dc, r:r + 1], scale=1.0),
                                  reads=[tb, mods], writes=[yx])
                        for g in range(22):
                            wg_ = wring.next()
                            wu_ = wring.next()
                            fw.load("sp", wg_, wg_[:], WG_b[g])
                            fw.load("act", wu_, wu_[:], WU_b[g])
                            for j in range(2):
                                fc = g * 2 + j
                                pg = pG.next()
                                pu_ = pU.next()
                                fw.ops("pe", [lambda e, kc=kc: e.matmul(pg[:, 0:n], lhsT=wg_[:, kc, j * 128:(j + 1) * 128], rhs=yx[:, kc, 0:n],
                                                                        start=(kc == 0), stop=(kc == 15)) for kc in range(16)], reads=[wg_, yx], writes=[pg])
                                fw.ops("pe", [lambda e, kc=kc: e.matmul(pu_[:, 0:n], lhsT=wu_[:, kc, j * 128:(j + 1) * 128], rhs=yx[:, kc, 0:n],
                                                                        start=(kc == 0), stop=(kc == 15)) for kc in range(16)], reads=[wu_, yx], writes=[pu_])
                                s_ = sg.next()
                                fw.op("act", lambda e: e.activation(out=s_[:, 0:n], in_=pg[:, 0:n], func=AF.Silu), reads=[pg], writes=[s_])
                                fw.op("dve", lambda e: e.tensor_tensor(out=H[:, fc, 0:n], in0=pu_[:, 0:n], in1=s_[:, 0:n], op=ALU.mult), reads=[pu_, s_], writes=[H])
                        for dc in range(16):
                            wd_ = wdr.next()
                            fw.load("sp" if dc % 2 else "act", wd_, wd_[:], WD_b[dc])
                            if True:
                                po = pA.next()
                                fw.ops("pe", [lambda e, fc=fc: e.matmul(po[:, 0:n], lhsT=wd_[:, fc, :], rhs=H[:, fc, 0:n],
                                                                        start=(fc == 0), stop=(fc == FC - 1)) for fc in range(FC)], reads=[wd_, H], writes=[po])
                                fw.op("dve", lambda e: e.scalar_tensor_tensor(out=xb_[:, dc, 0:n], in0=po[:, 0:n], scalar=mods[:, 80 + dc, r:r + 1], in1=xb_[:, dc, 0:n],
                                                                              op0=ALU.mult, op1=ALU.add), reads=[po, mods, xb_], writes=[xb_])
                        fw.store("pool", xb_, XTv[:, :, col0:col0 + n], xb_[:, :, 0:n])
                fw.barrier()

        with ExitStack() as es7:
            xb_ = fw.sb([128, 16, NT], F32, es=es7)
            sq = fw.sb([128, 16, NT], BF16, es=es7)
            rstd = fw.sb([128, NT], F32, es=es7)
            xo = fw.sb([128, 16, NT], F32, es=es7)
            pss = fw.ps([128, NT], F32, es=es7)
            ptr = Ring([fw.ps([128, 512], F32, es=es7) for _ in range(4)])
            orow = Ring([fw.sb([128, D], F32, es=es7) for _ in range(2)])
            for (col0, n, is_ctx, first, last) in tiles:
                if is_ctx:
                    continue
                fw.load("sp", xb_, xb_[:, :, 0:n], XTv[:, :, col0:col0 + n])
                fw.op("act", lambda e: e.activation(out=sq[:, :, 0:n], in_=xb_[:, :, 0:n], func=AF.Square), reads=[xb_], writes=[sq])
                fw.ops("pe", [lambda e, dc=dc: e.matmul(pss[:, 0:n], lhsT=onesb[:], rhs=sq[:, dc, 0:n], start=(dc == 0), stop=(dc == 15))
                              for dc in range(16)], reads=[sq, onesb], writes=[pss])
                fw.op("act", lambda e: e.activation(out=rstd[:, 0:n], in_=pss[:, 0:n], func=AF.Sqrt, bias=RMS_EPS, scale=1.0 / D), reads=[pss], writes=[rstd])
                fw.op("dve", lambda e: e.reciprocal(out=rstd[:, 0:n], in_=rstd[:, 0:n]), reads=[rstd], writes=[rstd])
                for dc in range(16):
                    fw.op("dve", lambda e, dc=dc: e.scalar_tensor_tensor(out=xo[:, dc, 0:n], in0=xb_[:, dc, 0:n], scalar=nf_fm[:, dc:dc + 1], in1=rstd[:, 0:n],
                                                                         op0=ALU.mult, op1=ALU.mult), reads=[xb_, nf_fm, rstd], writes=[xo])
                for tb in range(n // 128):
                    ob = orow.next()
                    for g4 in range(4):
                        pt = ptr.next()
                        fw.ops("pe", [lambda e, j=j: e.transpose(pt[:, j * 128:(j + 1) * 128], xo[:, g4 * 4 + j, tb * 128:(tb + 1) * 128], ident[:])
                                      for j in range(4)], reads=[xo, ident], writes=[pt])
                        fw.op("dve" if g4 % 2 == 0 else "act",
                              (lambda e: e.tensor_copy(out=ob[:, g4 * 512:(g4 + 1) * 512], in_=pt[:])) if g4 % 2 == 0 else
                              (lambda e: e.activation(out=ob[:, g4 * 512:(g4 + 1) * 512], in_=pt[:], func=AF.Copy)), reads=[pt], writes=[ob])
                    t0 = col0 - CTX + tb * 128
                    fw.store("sp", ob, out_ap[t0:t0 + 128, :], ob[:])
        fw.barrier()
    return nc, fw


def fourier_tables(n):
    idx = np.arange(n, dtype=np.int64)
    ang = (2.0 * np.pi / n) * ((idx[:, None] * idx[None, :]) % n).astype(np.float64)
    sc = 1.0 / np.sqrt(64.0 * n)
    tab = np.stack([np.cos(ang) * sc, -np.sin(ang) * sc], 0)
    return tab.astype(np.float32).astype(ml_dtypes.bfloat16)


def c64_table():
    idx = np.arange(64)
    ang = 2.0 * np.pi * ((idx[:, None] * idx[None, :]) % 64) / 64.0
    t = np.zeros((128, 256), np.float32)
    for h in range(2):
        t[h * 64:(h + 1) * 64, h * 64:(h + 1) * 64] = np.cos(ang)
        t[h * 64:(h + 1) * 64, 128 + h * 64:128 + (h + 1) * 64] = np.sin(ang)
    return t.astype(ml_dtypes.bfloat16)


_CACHE = {}


def make_inputs(b, x, c, ctx, c_ctx, W, T, CTX):
    m = {"x": np.ascontiguousarray(x[b]), "ctx": np.ascontiguousarray(ctx[b]),
         "cvec": np.ascontiguousarray(np.stack([c[b], c_ctx], 0))}
    m.update(W)
    return m


def kernel(x, c, ctx, c_ctx, w_mod, b_mod, norm_mix, w_in, rw_shift, dec_w0, dec_up, iclr_a0, iclr_up, k_k, k_a, r_k,
           ln_w, ln_b, g_up, conv_w, w_out, norm_ffn, w_gate, w_up, w_down, norm_final, _dbg=()):
    x = np.asarray(x)
    B, T, _ = x.shape
    CTX = ctx.shape[1]
    DEPTH = w_mod.shape[0]
    f = lambda a: np.ascontiguousarray(np.asarray(a, dtype=np.float32))
    W = dict(w_mod=f(w_mod), b_mod=f(b_mod), norm_mix=f(norm_mix), w_in=f(w_in), rw_shift=f(rw_shift), dec_w0=f(dec_w0),
             dec_up=f(dec_up), iclr_a0=f(iclr_a0), iclr_up=f(iclr_up), k_k=f(k_k), k_a=f(k_a),
             r_k=f(r_k).reshape(DEPTH, D_RWKV), ln_w=f(ln_w), ln_b=f(ln_b), g_up=f(g_up), conv_w=f(conv_w), w_out=f(w_out),
             norm_ffn=f(norm_ffn), w_gate=f(w_gate), w_up=f(w_up), w_down=f(w_down), norm_final=f(norm_final).reshape(1, D),
             ftab_x=fourier_tables(T), ftab_c=fourier_tables(CTX), c64=c64_table())
    key = (T, CTX, DEPTH, tuple(_dbg))
    nc, fw = build(T, CTX, DEPTH, _dbg)
    in_maps = [make_inputs(b, x, np.asarray(c), np.asarray(ctx), np.asarray(c_ctx), W, T, CTX) for b in range(B)]
    res = run_bass_kernel_spmd(nc, in_maps, core_ids=list(range(B)))
    if _dbg:
        return res
    return np.stack([res.results[b]["out"] for b in range(B)], 0).astype(np.float32)
```

```python
import numpy as np
import ml_dtypes
from contextlib import ExitStack
import concourse.bass as bass
import concourse.mybir as mybir
from concourse.bass_utils import run_bass_kernel_spmd

F32 = mybir.dt.float32
BF16 = mybir.dt.bfloat16
AF = mybir.ActivationFunctionType
ALU = mybir.AluOpType

D = 2048
DC = 16
HD = 64
D_RWKV = 1024
D_FF = 5632
FC = 44
R0, K0, V0, WD0, AD0, GD0, RW_COLS = 0, 1024, 2048, 3072, 3168, 3264, 3520
D_IN = 5568
PCG, PCX, PCB, PFT, PEND = 3584, 4096, 4608, 5120, 5632
RMS_EPS = 1e-6
GN_EPS = 64e-5
CH = 128
NT = 512


class Tr:
    __slots__ = ("w", "r", "sem", "dcnt", "const")

    def __init__(self, const=False):
        self.w = None
        self.r = {}
        self.sem = None
        self.dcnt = 0
        self.const = const


class Buf:
    def __init__(self, tile):
        self.t = tile
        self.tr = Tr()

    def __getitem__(self, k):
        return self.t[k]


class FW:
    ENG = ("pe", "act", "dve", "pool", "sp")

    def __init__(self, nc):
        self.nc = nc
        self.es = ExitStack()
        self.eng = {"pe": nc.tensor, "act": nc.scalar, "dve": nc.vector, "pool": nc.gpsimd, "sp": nc.sync}
        self.sem = {e: self.es.enter_context(nc.semaphore("sem_" + e)) for e in self.ENG}
        self.cnt = {e: 0 for e in self.ENG}
        self.seen = {e: {} for e in self.ENG}
        self.nsem = 0
        self.n_inst = 0
        self.dma_trs = []
        self.sem_pool = [[], []]
        self.uid = 0

    def sb(self, shape, dt, es=None, name=None):
        self.uid += 1
        return Buf((es or self.es).enter_context(self.nc.sbuf_tensor("%s_%d" % (name or "sb", self.uid), shape, dt)))

    def ps(self, shape, dt=F32, es=None, name=None):
        self.uid += 1
        return Buf((es or self.es).enter_context(self.nc.psum_tensor("%s_%d" % (name or "ps", self.uid), shape, dt)))

    def dram(self, name, shape, dt, kind="Internal"):
        return self.nc.dram_tensor(name, shape, dt, kind=kind).ap()

    def _wait(self, e, tok):
        if tok is None:
            return
        kind, key, val = tok
        if kind == "e":
            if self.seen[e].get(key, 0) >= val:
                return
            self.eng[e].wait_ge(self.sem[key], val)
            self.seen[e][key] = val
        else:
            k = id(key)
            if self.seen[e].get(k, 0) >= val:
                return
            self.eng[e].wait_ge(key, val)
            self.seen[e][k] = val

    def _deps(self, e, reads, writes):
        for t in reads:
            self._wait(e, t.w)
        for t in writes:
            self._wait(e, t.w)
            for tok in t.r.values():
                self._wait(e, tok)

    def _commit(self, tok, reads, writes):
        key = tok[1] if tok[0] == "e" else id(tok[1])
        for t in reads:
            if not t.const:
                t.r[key] = tok
        for t in writes:
            t.w = tok
            t.r = {}

    @staticmethod
    def _trs(bufs):
        return [b.tr if isinstance(b, Buf) else b for b in bufs]

    def op(self, e, fn, reads=(), writes=()):
        reads = self._trs(reads)
        writes = self._trs(writes)
        self._deps(e, reads, writes)
        ins = fn(self.eng[e])
        self.cnt[e] += 1
        ins.then_inc(self.sem[e], 1)
        self._commit(("e", e, self.cnt[e]), reads, writes)
        self.n_inst += 1

    def ops(self, e, fns, reads=(), writes=()):
        reads = self._trs(reads)
        writes = self._trs(writes)
        self._deps(e, reads, writes)
        ins = None
        for fn in fns:
            ins = fn(self.eng[e])
            self.n_inst += 1
        self.cnt[e] += 1
        ins.then_inc(self.sem[e], 1)
        self._commit(("e", e, self.cnt[e]), reads, writes)

    def dma(self, q, out, in_, own, reads=(), writes=()):
        own = own.tr if isinstance(own, Buf) else own
        reads = self._trs(reads)
        writes = self._trs(writes)
        sw = 1 if q == "pool" else 0
        if own.sem is None:
            own.sem = [None, None]
            own.dcnt = [0, 0]
            self.dma_trs.append(own)
        if own.sem[sw] is None:
            if self.sem_pool[sw]:
                own.sem[sw], own.dcnt[sw] = self.sem_pool[sw].pop()
            else:
                self.nsem += 1
                own.sem[sw] = self.es.enter_context(self.nc.semaphore("dsem%d" % self.nsem))
                own.dcnt[sw] = 0
        for i in (0, 1):
            if own.sem[i] is not None and own.dcnt[i] > 0:
                self._wait(q, ("d", own.sem[i], own.dcnt[i]))
        self._deps(q, reads, writes)
        ins = self.eng[q].dma_start(out=out, in_=in_)
        own.dcnt[sw] += 16
        ins.then_inc(own.sem[sw], 16)
        self._commit(("d", own.sem[sw], own.dcnt[sw]), reads, writes)
        self.n_inst += 1

    def load(self, q, buf, dst, src):
        self.dma(q, dst, src, buf, writes=[buf])

    def store(self, q, buf, dst, src):
        self.dma(q, dst, src, buf, reads=[buf])

    def barrier(self):
        for e in self.ENG:
            for f in self.ENG:
                if f != e and self.cnt[f] > 0:
                    self._wait(e, ("e", f, self.cnt[f]))
            for t in self.dma_trs:
                for i in (0, 1):
                    if t.sem[i] is not None and t.dcnt[i] > 0:
                        self._wait(e, ("d", t.sem[i], t.dcnt[i]))
        for t in self.dma_trs:
            for i in (0, 1):
                if t.sem[i] is not None:
                    self.sem_pool[i].append((t.sem[i], t.dcnt[i]))
            t.sem = None
            t.dcnt = 0
        self.dma_trs = []


class Ring:
    def __init__(self, bufs):
        self.bufs = bufs
        self.i = 0

    def next(self):
        b = self.bufs[self.i % len(self.bufs)]
        self.i += 1
        return b


def build(T, CTX, DEPTH, dbg=()):
    nc = bass.Bass("TRN2", target_bir_lowering=False)
    fw = FW(nc)
    TT = CTX + T
    L = DEPTH

    def din(name, shape, dt=F32):
        return nc.dram_tensor(name, shape, dt, kind="ExternalInput").ap()

    x_in = din("x", [T, D])
    ctx_in = din("ctx", [CTX, D])
    cvec_in = din("cvec", [2, D])
    w_mod = din("w_mod", [L, D, 6 * D])
    b_mod = din("b_mod", [L, 6 * D])
    norm_mix = din("norm_mix", [L, D])
    w_in = din("w_in", [L, D, D_IN])
    rw_shift = din("rw_shift", [L, 3, RW_COLS])
    dec_w0 = din("dec_w0", [L, 2, D_RWKV])
    dec_up = din("dec_up", [L, 2, 96, D_RWKV])
    iclr_a0 = din("iclr_a0", [L, 2, D_RWKV])
    iclr_up = din("iclr_up", [L, 2, 96, D_RWKV])
    k_k = din("k_k", [L, D_RWKV])
    k_a = din("k_a", [L, D_RWKV])
    r_k = din("r_k", [L, D_RWKV])
    ln_w = din("ln_w", [L, D_RWKV])
    ln_b = din("ln_b", [L, D_RWKV])
    g_up = din("g_up", [L, 256, D_RWKV])
    conv_w = din("conv_w", [L, 3, 512])
    w_out = din("w_out", [L, D, D])
    norm_ffn = din("norm_ffn", [L, D])
    w_gate = din("w_gate", [L, D, D_FF])
    w_up = din("w_up", [L, D, D_FF])
    w_down = din("w_down", [L, D_FF, D])
    norm_final = din("norm_final", [1, D])
    ftab_x = din("ftab_x", [2, T, T], BF16)
    ftab_c = din("ftab_c", [2, CTX, CTX], BF16)
    c64_in = din("c64", [128, 256], BF16)
    out_ap = nc.dram_tensor("out", [T, D], F32, kind="ExternalOutput").ap()

    def scratch(name, shape, dt):
        kind = "ExternalOutput" if name in dbg else "Internal"
        return nc.dram_tensor(name, shape, dt, kind=kind).ap()

    XT = scratch("XT", [D, TT], F32)
    PXT = scratch("PXT", [PEND, TT], F32)
    YF = scratch("YF", [D_RWKV, TT], F32)
    YT = scratch("YT", [D, TT], BF16)
    MODS = scratch("MODS", [128, 96 * 2], F32)
    WIN_b = scratch("WIN_b", [11, 128, 16, 512], BF16)
    WOUT_b = scratch("WOUT_b", [8, 128, 16, 256], BF16)
    WG_b = scratch("WG_b", [22, 128, 16, 256], BF16)
    WU_b = scratch("WU_b", [22, 128, 16, 256], BF16)
    WD_b = scratch("WD_b", [16, 128, FC, 128], BF16)

    XTv = XT.rearrange("(c p) t -> p c t", p=128)

    tiles = []
    for c0 in range(0, CTX, NT):
        n = min(NT, CTX - c0)
        tiles.append((c0, n, True, c0 == 0, c0 + n == CTX))
    for c0 in range(0, T, NT):
        n = min(NT, T - c0)
        tiles.append((CTX + c0, n, False, c0 == 0, c0 + n == T))

    with fw.es:
        ident = fw.sb([128, 128], F32, name="ident")
        identb = fw.sb([128, 128], BF16, name="identb")
        onesb = fw.sb([128, 128], BF16, name="onesb")
        blk = fw.sb([128, 128], F32, name="blk")
        blkb = fw.sb([128, 128], BF16, name="blkb")
        iot = fw.sb([128, 128], F32, name="iot")
        m_f = fw.sb([128, 256], F32, name="m_f")
        m_b = fw.sb([128, 256], F32, name="m_b")
        mt_f = fw.sb([128, 128], F32, name="mt_f")
        mt_b = fw.sb([128, 128], F32, name="mt_b")
        cmask = fw.sb([128, NT], F32, name="cmask")
        c64 = fw.sb([128, 256], BF16, name="c64")
        sc_fm = fw.sb([128, 16, 2], F32, name="sc_fm")
        nf_fm = fw.sb([128, 16], F32, name="nf_fm")
        for b in (blkb, ident, identb, onesb, blk, iot, m_f, m_b, mt_f, mt_b, cmask, c64):
            b.tr.const = True

        P = fw.eng["pool"]
        fw.op("pool", lambda e: e.iota(iot[:], pattern=[[1, 128]], base=0, channel_multiplier=-1,
                                       allow_small_or_imprecise_dtypes=True), writes=[iot])
        fw.op("dve", lambda e: e.tensor_single_scalar(out=ident[:], in_=iot[:], scalar=0.0, op=ALU.is_equal),
              reads=[iot], writes=[ident])
        fw.op("dve", lambda e: e.tensor_copy(out=identb[:], in_=ident[:]), reads=[ident], writes=[identb])
        fw.op("dve", lambda e: e.memset(onesb[:], 1.0), writes=[onesb])
        fw.op("dve", lambda e: e.memset(blk[:], 0.0), writes=[blk])
        fw.op("dve", lambda e: e.memset(blk[0:64, 0:64], 1.0), writes=[blk])
        fw.op("dve", lambda e: e.memset(blk[64:128, 64:128], 1.0), writes=[blk])
        fw.op("dve", lambda e: e.tensor_copy(out=blkb[:], in_=blk[:]), reads=[blk], writes=[blkb])
        fw.op("dve", lambda e: e.tensor_single_scalar(out=m_f[:, 0:128], in_=iot[:], scalar=0.0, op=ALU.is_gt), reads=[iot], writes=[m_f])
        fw.op("dve", lambda e: e.tensor_single_scalar(out=m_f[:, 128:256], in_=iot[:], scalar=0.0, op=ALU.is_ge), reads=[iot], writes=[m_f])
        fw.op("dve", lambda e: e.tensor_single_scalar(out=m_b[:, 0:128], in_=iot[:], scalar=0.0, op=ALU.is_lt), reads=[iot], writes=[m_b])
        fw.op("dve", lambda e: e.tensor_single_scalar(out=m_b[:, 128:256], in_=iot[:], scalar=0.0, op=ALU.is_le), reads=[iot], writes=[m_b])
        fw.op("dve", lambda e: e.tensor_single_scalar(out=mt_f[:], in_=iot[:], scalar=0.0, op=ALU.is_lt), reads=[iot], writes=[mt_f])
        fw.op("dve", lambda e: e.tensor_single_scalar(out=mt_b[:], in_=iot[:], scalar=0.0, op=ALU.is_gt), reads=[iot], writes=[mt_b])
        fw.op("dve", lambda e: e.memset(cmask[:], 1.0), writes=[cmask])
        for c in range(NT // CH):
            fw.op("dve", lambda e, c=c: e.memset(cmask[:, c * CH:c * CH + 1], 0.0), writes=[cmask])
        fw.load("sp", c64, c64[:], c64_in[:, :])

        def rows_to_fm(es, rows, R, nq, dst, dst_q0=0):
            psr = fw.ps([128, 16, R], F32, es=es, name="psr")
            for q0 in range(0, nq, 16):
                qn = min(16, nq - q0)
                fw.ops("pe", [lambda e, q=q: e.matmul(psr[:, q - q0, :], lhsT=rows[0:R, q * 128:(q + 1) * 128],
                                                      rhs=ident[0:R, 0:R], start=True, stop=True)
                              for q in range(q0, q0 + qn)], reads=[rows, ident], writes=[psr])
                fw.op("dve", lambda e: e.tensor_copy(out=dst[:, dst_q0 + q0:dst_q0 + q0 + qn, 0:R], in_=psr[:, 0:qn, :]),
                      reads=[psr], writes=[dst])

        with ExitStack() as es:
            crow = fw.sb([2, D], F32, es=es)
            fw.load("sp", crow, crow[:], cvec_in[:, :])
            fw.op("act", lambda e: e.activation(out=crow[:], in_=crow[:], func=AF.Silu), reads=[crow], writes=[crow])
            rows_to_fm(es, crow, 2, 16, sc_fm)
            nrow = fw.sb([1, D], F32, es=es)
            fw.load("sp", nrow, nrow[:], norm_final[:, :])
            nf3 = fw.sb([128, 16, 1], F32, es=es)
            rows_to_fm(es, nrow, 1, 16, nf3)
            fw.op("dve", lambda e: e.tensor_copy(out=nf_fm[:], in_=nf3[:, :, 0]), reads=[nf3], writes=[nf_fm])

            inb = [fw.sb([128, D], F32, es=es) for _ in range(2)]
            stg = [fw.sb([128, 16, 128], F32, es=es) for _ in range(2)]
            pst = [fw.ps([128, 4, 128], F32, es=es) for _ in range(4)]
            k = 0
            for (src, n0, c0) in ((ctx_in, CTX, 0), (x_in, T, CTX)):
                for tb in range(n0 // 128):
                    ib = inb[k % 2]
                    sg = stg[k % 2]
                    fw.load("sp", ib, ib[:], src[tb * 128:(tb + 1) * 128, :])
                    for g4 in range(4):
                        pt = pst[g4 % 4]
                        fw.ops("pe", [lambda e, j=j: e.transpose(pt[:, j, :], ib[:, (g4 * 4 + j) * 128:(g4 * 4 + j + 1) * 128], ident[:])
                                      for j in range(4)], reads=[ib, ident], writes=[pt])
                        fw.op("dve" if g4 % 2 == 0 else "act",
                              (lambda e: e.tensor_copy(out=sg[:, g4 * 4:(g4 + 1) * 4, :], in_=pt[:])) if g4 % 2 == 0 else
                              (lambda e: e.activation(out=sg[:, g4 * 4:(g4 + 1) * 4, :], in_=pt[:], func=AF.Copy)),
                              reads=[pt], writes=[sg])
                    fw.store("pool", sg, XTv[:, :, c0 + tb * 128:c0 + (tb + 1) * 128], sg[:])
                    k += 1
        fw.barrier()

        for li in range(L):
            last_layer = (li == L - 1)
            with ExitStack() as esl:
              with ExitStack() as es:
                st16 = Ring([fw.sb([128, 16, 512], BF16, es=es) for _ in range(3)])
                for g in range(11):
                    b = st16.next()
                    pc0 = g * 512
                    if g < 6:
                        fw.load("pool", b, b[:], w_in[li, :, pc0:pc0 + 512].rearrange("(c p) n -> p c n", p=128))
                    elif g == 6:
                        fw.load("pool", b, b[:, :, 0:448], w_in[li, :, pc0:pc0 + 448].rearrange("(c p) n -> p c n", p=128))
                    else:
                        fw.load("pool", b, b[:], w_in[li, :, pc0 - 64:pc0 - 64 + 512].rearrange("(c p) n -> p c n", p=128))
                    fw.store("sp", b, WIN_b[g], b[:])
                for (wsrc, wdst, ng) in ((w_out, WOUT_b, 4), (w_gate, WG_b, 11), (w_up, WU_b, 11)):
                    for g in range(ng):
                        b = st16.next()
                        fw.load("pool", b, b[:], wsrc[li, :, g * 512:(g + 1) * 512].rearrange("(c p) n -> p c n", p=128))
                        fw.store("sp", b, wdst[2 * g], b[:, :, 0:256])
                        fw.store("sp", b, wdst[2 * g + 1], b[:, :, 256:512])
                st44 = Ring([fw.sb([128, FC, 256], BF16, es=es) for _ in range(2)])
                for g in range(8):
                    b = st44.next()
                    fw.load("pool", b, b[:], w_down[li, :, g * 256:(g + 1) * 256].rearrange("(c p) n -> p c n", p=128))
                    fw.store("sp", b, WD_b[2 * g], b[:, :, 0:128])
                    fw.store("sp", b, WD_b[2 * g + 1], b[:, :, 128:256])
                fw.barrier()
              vfm = fw.sb([128, 96, 24], F32, name="vfm", es=esl)
              cw_fm = fw.sb([128, 4, 3], F32, name="cw_fm", es=esl)
              mods = fw.sb([128, 96, 2], F32, name="mods", es=esl)
              A1 = fw.sb([128, 16, 2], F32, name="A1", es=esl)
              A2 = fw.sb([128, 16, 2], F32, name="A2", es=esl)
              omk = fw.sb([128, 8], F32, name="omk", es=esl)
              omk2 = fw.sb([128, 8], F32, name="omk2", es=esl)
              with ExitStack() as es:
                NR = 24
                rowb = fw.sb([NR, 6 * D], F32, es=es)
                fw.op("pool", lambda e: e.memset(rowb[:], 0.0), writes=[rowb])
                rowspec = [(0, b_mod[li:li + 1, :], 6 * D), (1, norm_mix[li:li + 1, :], D), (2, norm_ffn[li:li + 1, :], D),
                           (3, rw_shift[li, :, 0:3072], 3072), (6, rw_shift[li, :, WD0:WD0 + 96], 96),
                           (9, rw_shift[li, :, AD0:AD0 + 96], 96), (12, rw_shift[li, :, GD0:GD0 + 256], 256),
                           (15, dec_w0[li], D_RWKV), (17, iclr_a0[li], D_RWKV), (19, k_k[li:li + 1, :], D_RWKV),
                           (20, k_a[li:li + 1, :], D_RWKV), (21, r_k[li:li + 1, :], D_RWKV), (22, ln_w[li:li + 1, :], D_RWKV),
                           (23, ln_b[li:li + 1, :], D_RWKV)]
                for (r0, src, ln) in rowspec:
                    nr = src.shape[0]
                    fw.dma("sp", rowb[r0:r0 + nr, 0:ln], src, rowb, writes=[rowb])
                crow3 = fw.sb([3, 512], F32, es=es)
                fw.load("sp", crow3, crow3[:], conv_w[li])
                rows_to_fm(es, rowb, NR, 96, vfm)
                rows_to_fm(es, crow3, 3, 4, cw_fm)

                wm = Ring([fw.sb([128, 16, 512], F32, es=es) for _ in range(2)])
                psm = Ring([fw.ps([128, 4, 2], F32, es=es) for _ in range(2)])
                for g in range(24):
                    b = wm.next()
                    fw.load("sp" if g % 2 == 0 else "act", b, b[:], w_mod[li, :, g * 512:(g + 1) * 512].rearrange("(c p) n -> p c n", p=128))
                    pm = psm.next()
                    fns = []
                    for j in range(4):
                        for kc in range(16):
                            fns.append(lambda e, j=j, kc=kc: e.matmul(pm[:, j, :], lhsT=b[:, kc, j * 128:(j + 1) * 128], rhs=sc_fm[:, kc, :],
                                                                      start=(kc == 0), stop=(kc == 15)))
                    fw.ops("pe", fns, reads=[b, sc_fm], writes=[pm])
                    for r in range(2):
                        fw.op("dve", lambda e, r=r: e.tensor_tensor(out=mods[:, g * 4:(g + 1) * 4, r], in0=pm[:, :, r],
                                                                    in1=vfm[:, g * 4:(g + 1) * 4, 0], op=ALU.add),
                              reads=[pm, vfm], writes=[mods])
                for r in range(2):
                    fw.op("dve", lambda e, r=r: e.scalar_tensor_tensor(out=A1[:, :, r], in0=mods[:, 16:32, r], scalar=1.0, in1=vfm[:, 0:16, 1],
                                                                       op0=ALU.add, op1=ALU.mult), reads=[mods, vfm], writes=[A1])
                    fw.op("dve", lambda e, r=r: e.scalar_tensor_tensor(out=A2[:, :, r], in0=mods[:, 64:80, r], scalar=1.0, in1=vfm[:, 0:16, 2],
                                                                       op0=ALU.add, op1=ALU.mult), reads=[mods, vfm], writes=[A2])
                if "MODS" in dbg:
                    fw.store("sp", mods, MODS, mods[:].rearrange("p q r -> p (q r)"))
                fw.op("dve", lambda e: e.tensor_scalar(out=omk[:], in0=vfm[:, 0:8, 20], scalar1=-1.0, scalar2=1.0, op0=ALU.mult, op1=ALU.add),
                      reads=[vfm], writes=[omk])
                fw.op("dve", lambda e: e.tensor_scalar(out=omk2[:], in0=omk[:], scalar1=2.0, scalar2=None, op0=ALU.mult), reads=[omk], writes=[omk2])
                fw.barrier()
              if True:
                with ExitStack() as es1:
                    xs = fw.sb([128, 16, NT], F32, es=es1)
                    sq = fw.sb([128, 16, NT], BF16, es=es1)
                    rstd = fw.sb([128, NT], F32, es=es1)
                    tmp = Ring([fw.sb([128, NT], F32, es=es1) for _ in range(2)])
                    xn2 = [fw.sb([128, 16, 2 * NT], BF16, es=es1) for _ in range(2)]
                    wr = Ring([fw.sb([128, 16, 512], BF16, es=es1) for _ in range(2)])
                    pss = fw.ps([128, NT], F32, es=es1)
                    pso = Ring([fw.ps([128, 2 * NT], F32, es=es1) for _ in range(3)])
                    ost = Ring([fw.sb([128, 2 * NT], F32, es=es1) for _ in range(3)])
                    supers = []
                    for c0 in range(0, CTX, 2 * NT):
                        supers.append((c0, min(2 * NT, CTX - c0), 1))
                    for c0 in range(0, T, 2 * NT):
                        supers.append((CTX + c0, min(2 * NT, T - c0), 0))

                    def norm_mod(col0, n, r, Amod, sh_q0, dst, dst0, src_buf=None):
                        xb = src_buf
                        if xb is None:
                            xb = xs
                            fw.load("sp", xs, xs[:, :, 0:n], XTv[:, :, col0:col0 + n])
                            yield
                        fw.op("act", lambda e: e.activation(out=sq[:, :, 0:n], in_=xb[:, :, 0:n], func=AF.Square), reads=[xb], writes=[sq])
                        yield
                        fw.ops("pe", [lambda e, dc=dc: e.matmul(pss[:, 0:n], lhsT=onesb[:], rhs=sq[:, dc, 0:n], start=(dc == 0), stop=(dc == 15))
                                      for dc in range(16)], reads=[sq, onesb], writes=[pss])
                        yield
                        fw.op("act", lambda e: e.activation(out=rstd[:, 0:n], in_=pss[:, 0:n], func=AF.Sqrt, bias=RMS_EPS, scale=1.0 / D),
                              reads=[pss], writes=[rstd])
                        yield
                        fw.op("dve", lambda e: e.reciprocal(out=rstd[:, 0:n], in_=rstd[:, 0:n]), reads=[rstd], writes=[rstd])
                        yield
                        for dc in range(16):
                            tb = tmp.next()
                            fw.op("dve", lambda e, dc=dc: e.scalar_tensor_tensor(out=tb[:, 0:n], in0=xb[:, dc, 0:n], scalar=Amod[:, dc, r:r + 1],
                                                                                 in1=rstd[:, 0:n], op0=ALU.mult, op1=ALU.mult),
                                  reads=[xb, Amod, rstd], writes=[tb])
                            yield
                            fw.op("act", lambda e, dc=dc: e.activation(out=dst[:, dc, dst0:dst0 + n], in_=tb[:, 0:n], func=AF.Identity,
                                                                       bias=mods[:, sh_q0 + dc, r:r + 1], scale=1.0),
                                  reads=[tb, mods], writes=[dst])
                            yield

                    ev = 0
                    def norm_super(si):
                        (s0_, sn_, r_) = supers[si]
                        for o in range(0, sn_, NT):
                            for _ in norm_mod(s0_ + o, min(NT, sn_ - o), r_, A1, 0, xn2[si % 2], o):
                                yield

                    for _ in norm_super(0):
                        pass
                    for si, (s0, sn, r) in enumerate(supers):
                        xn = xn2[si % 2]
                        ngen = norm_super(si + 1) if si + 1 < len(supers) else None
                        for g in range(11):
                            wb = wr.next()
                            fw.load("act", wb, wb[:], WIN_b[g])
                            for j in range(4):
                                pq = g * 4 + j
                                if pq == 27:
                                    pass
                                po = pso.next()
                                fns = []
                                for h0 in range(0, sn, NT):
                                    hn = min(NT, sn - h0)
                                    for kc in range(16):
                                        fns.append(lambda e, h0=h0, hn=hn, kc=kc, j=j: e.matmul(po[:, h0:h0 + hn], lhsT=wb[:, kc, j * 128:(j + 1) * 128],
                                                                                               rhs=xn[:, kc, h0:h0 + hn], start=(kc == 0), stop=(kc == 15)))
                                fw.ops("pe", fns, reads=[wb, xn], writes=[po])
                                ob = ost.next()
                                if ev % 2 == 0:
                                    fw.op("dve", lambda e: e.tensor_copy(out=ob[:, 0:sn], in_=po[:, 0:sn]), reads=[po], writes=[ob])
                                else:
                                    fw.op("act", lambda e: e.activation(out=ob[:, 0:sn], in_=po[:, 0:sn], func=AF.Copy), reads=[po], writes=[ob])
                                ev += 1
                                fw.store("sp", ob, PXT[pq * 128:(pq + 1) * 128, s0:s0 + sn], ob[:, 0:sn])
                                if ngen is not None:
                                    for _ in range(2):
                                        try:
                                            next(ngen)
                                        except StopIteration:
                                            ngen = None
                                            break
                        if ngen is not None:
                            for _ in ngen:
                                pass
                fw.barrier()
                if "stop_p1" in dbg:
                    break

                with ExitStack() as es2:
                    HP = 8
                    SL = 9
                    for d_ in dbg:
                        if d_.startswith('sl='):
                            SL = int(d_[3:])
                    SDT = F32 if 'scan32' in dbg else BF16
                    identS = ident if 'scan32' in dbg else identb
                    IDT = BF16 if 'inv16' in dbg else F32
                    decw = fw.sb([96, 2, D_RWKV], F32, es=es2)
                    iclw = fw.sb([96, 2, D_RWKV], F32, es=es2)
                    gupw = fw.sb([128, 2, D_RWKV], BF16, es=es2)
                    for d in range(2):
                        fw.dma("sp", decw[:, d, :], dec_up[li, d], decw, writes=[decw])
                        fw.dma("sp", iclw[:, d, :], iclr_up[li, d], iclw, writes=[iclw])
                    fw.load("pool", gupw, gupw[:], g_up[li].rearrange("(c p) n -> p c n", p=128))
                    for b in (decw, iclw, gupw):
                        pass
                    S2g = [fw.sb([128, 4, 128], F32, es=es2) for _ in range(2)]
                    S2bg = [fw.sb([128, 4, 128], SDT, es=es2) for _ in range(2)]
                    wdh = fw.sb([96, NT + 2], F32, es=es2)
                    adh = fw.sb([96, NT + 2], F32, es=es2)
                    gdh = fw.sb([128, 2, NT + 2], F32, es=es2)
                    wdt2 = [fw.sb([96, NT], F32, es=es2) for _ in range(2)]
                    ads2 = [fw.sb([96, NT], F32, es=es2) for _ in range(2)]
                    gds2 = [fw.sb([128, 2, NT], BF16, es=es2) for _ in range(2)]
                    rkv_h = Ring([[fw.sb([128, NT + 2], F32, es=es2) for _ in range(3)] for _ in range(1)])
                    r_s = Ring([fw.sb([128, NT], F32, es=es2) for _ in range(1)])
                    ks_s2 = [[fw.sb([128, NT], BF16, es=es2) for _ in range(4)] for _ in range(2)]
                    rkb = Ring([[fw.sb([128, NT + 2], BF16, es=es2) for _ in range(3)] for _ in range(1)])
                    dgr = Ring([fw.sb([128, 9, 128], BF16, es=es2) for _ in range(2)])
                    dgfr = Ring([fw.sb([128, 3, 128], F32, es=es2) for _ in range(1)])
                    vb_s2 = [[fw.sb([128, NT], SDT, es=es2) for _ in range(4)] for _ in range(2)]
                    strm2 = [[fw.sb([128, NT // CH, 4, CH], SDT, es=es2) for _ in range(4)] for _ in range(2)]
                    gC2 = [[fw.sb([128, NT // CH], F32, es=es2) for _ in range(4)] for _ in range(2)]
                    ybuf = fw.sb([128, 4, NT], F32, es=es2)
                    tw = Ring([fw.sb([128, NT], F32, es=es2) for _ in range(11)])
                    TTall = fw.sb([128, 4, 3, 128], SDT, es=es2)
                    GBK = fw.sb([128, 8, 2, 256], SDT, es=es2)
                    PkAll = [fw.sb([128, 8, 2, 128], IDT, es=es2) for _ in range(2)]
                    Zt = [fw.sb([128, 8, 64], IDT, es=es2) for _ in range(2)]
                    UP = fw.sb([128, 4, 128], SDT, es=es2)
                    stm = fw.sb([128, 4, 128], F32, es=es2)
                    psA = Ring([fw.ps([128, 512], F32, es=es2) for _ in range(3)])
                    psZ = Ring([fw.ps([128, 512], F32, es=es2) for _ in range(3)])
                    psP = Ring([fw.ps([128, 512], F32, es=es2) for _ in range(2)])

                    yio = Ring([fw.sb([128, NT], F32, es=es2) for _ in range(2)])
                    yob = Ring([fw.sb([128, NT], BF16, es=es2) for _ in range(2)])

                    for hg_ in range(2):
                        fw.op("pool", lambda e, hg_=hg_: e.memset(S2g[hg_][:], 0.0), writes=[S2g[hg_]])
                        fw.op("pool", lambda e, hg_=hg_: e.memset(S2bg[hg_][:], 0.0), writes=[S2bg[hg_]])

                    ptmp = Ring([fw.sb([128, NT], F32, es=es2) for _ in range(2)])

                    def shift3(eng, dst, src, n, wq, rows, row0, dv=None, sv_=None):
                        dv = dv or (lambda: dst[0:rows, 0:n])
                        sv_ = sv_ or (lambda a, b_: src[0:rows, a:b_])
                        w = lambda tap: vfm[0:rows, wq, row0 + tap:row0 + tap + 1]
                        fw.op(eng, lambda e: e.tensor_scalar(out=dv(), in0=sv_(1, n + 1), scalar1=w(1), scalar2=None, op0=ALU.mult), reads=[src, vfm], writes=[dst])
                        for tap, a in ((0, 0), (2, 2)):
                            if eng == "dve":
                                fw.op(eng, lambda e: e.scalar_tensor_tensor(out=dv(), in0=sv_(a, a + n), scalar=w(tap), in1=dv(), op0=ALU.mult, op1=ALU.add),
                                      reads=[src, vfm, dst], writes=[dst])
                            else:
                                pt_ = ptmp.next()
                                fw.op(eng, lambda e: e.tensor_scalar(out=pt_[0:rows, 0:n], in0=sv_(a, a + n), scalar1=w(tap), scalar2=None, op0=ALU.mult),
                                      reads=[src, vfm], writes=[pt_])
                                fw.op(eng, lambda e: e.tensor_tensor(out=dv(), in0=dv(), in1=pt_[0:rows, 0:n], op=ALU.add), reads=[dst, pt_], writes=[dst])

                    def load_halo(q, buf, dst_rows, row_a, row_b, col0, n, first, last):
                        lo = 1 if first else 0
                        hi = n + 1 if last else n + 2
                        if first:
                            fw.op("pool", lambda e: e.memset(dst_rows[:, 0:1], 0.0), writes=[buf])
                        if last:
                            fw.op("pool", lambda e: e.memset(dst_rows[:, n + 1:n + 2], 0.0), writes=[buf])
                        fw.dma(q, dst_rows[:, lo:hi], PXT[row_a:row_b, col0 - 1 + lo:col0 - 1 + hi], buf, writes=[buf])

                    for pas in (0, 1):
                        if pas == 0:
                            order = list(tiles)
                        else:
                            ctx_t = [t for t in tiles if t[2]]
                            x_t = [t for t in tiles if not t[2]]
                            order = ctx_t[::-1] + x_t[::-1]
                        msk = m_f if pas == 0 else m_b
                        mskT = mt_f if pas == 0 else mt_b
                        pump_ref = [lambda k: None]
                        pump_k = [4]

                        def unpack(tile):
                            (col0, n, is_ctx, first, last) = tile
                            return col0, n, is_ctx, first, last, n // CH, (is_ctx and last_layer)
                        def lora_shared(tile, lp):
                            col0, n, is_ctx, first, last, ncn, skip_fin = unpack(tile)
                            wdt, ads, gds = wdt2[lp], ads2[lp], gds2[lp]

                            def pe_shift(src, rows, wq, row0, evac, sf=None):
                                dgf = dgfr.next()
                                for tap in range(3):
                                    fw.op("dve", lambda e, tap=tap: e.tensor_scalar(out=dgf[0:rows, tap, 0:rows], in0=ident[0:rows, 0:rows],
                                                                                   scalar1=vfm[0:rows, wq, row0 + tap:row0 + tap + 1], scalar2=None, op0=ALU.mult),
                                          reads=[ident, vfm], writes=[dgf])
                                psh = psP.next()
                                fw.ops("pe", [lambda e, tap=tap: e.matmul(psh[0:rows, 0:n], lhsT=dgf[0:rows, tap, 0:rows], rhs=(sf(tap, tap + n) if sf else src[0:rows, tap:tap + n]),
                                                                         start=(tap == 0), stop=(tap == 2)) for tap in range(3)], reads=[dgf, src], writes=[psh])
                                evac(psh)

                            load_halo("sp", wdh, wdh[0:96, 0:n + 2], WD0, WD0 + 96, col0, n, first, last)
                            load_halo("sp", adh, adh[0:96, 0:n + 2], AD0, AD0 + 96, col0, n, first, last)
                            pe_shift(wdh, 96, 0, 6, lambda psh: fw.op("act", lambda e: e.activation(out=wdt[:, 0:n], in_=psh[0:96, 0:n], func=AF.Tanh), reads=[psh], writes=[wdt]))
                            pe_shift(adh, 96, 0, 9, lambda psh: fw.op("act", lambda e: e.activation(out=ads[:, 0:n], in_=psh[0:96, 0:n], func=AF.Copy), reads=[psh], writes=[ads]))
                            if pas == 1 and not skip_fin:
                                for c2 in range(2):
                                    load_halo("sp", gdh, gdh[:, c2, 0:n + 2], GD0 + c2 * 128, GD0 + (c2 + 1) * 128, col0, n, first, last)
                                for c2 in range(2):
                                    pe_shift(gdh, 128, c2, 12,
                                             sf=lambda a_, b_, c2=c2: gdh[:, c2, a_:b_], evac=lambda psh, c2=c2: fw.op("act", lambda e: e.activation(out=gds[:, c2, 0:n], in_=psh[:, 0:n], func=AF.Sigmoid), reads=[psh], writes=[gds]))
                        def prep_hp(tile, hp, sset, lp):
                            col0, n, is_ctx, first, last, ncn, skip_fin = unpack(tile)
                            wdt, ads, gds = wdt2[lp], ads2[lp], gds2[lp]
                            strm, vb_s, gC, ks_s = strm2[sset], vb_s2[sset], gC2[sset], ks_s2[sset]
                            hb = rkv_h.next()
                            for i3, base in enumerate((R0, K0, V0)):
                                load_halo("sp" if i3 != 1 else "act", hb[i3], hb[i3][:, 0:n + 2], base + hp * 128, base + (hp + 1) * 128, col0, n, first, last)
                                yield
                            rr = r_s.next()
                            kk_ = tw.next()
                            hbb = rkb.next()
                            dg = dgr.next()
                            for i3, q_ in enumerate((R0 // 128 + hp, K0 // 128 + hp, V0 // 128 + hp)):
                                fw.op("act", lambda e: e.activation(out=hbb[i3][:, 0:n + 2], in_=hb[i3][:, 0:n + 2], func=AF.Copy), reads=[hb[i3]], writes=[hbb[i3]])
                                yield
                                for tap in range(3):
                                    fw.op("dve", lambda e: e.tensor_scalar(out=dg[:, i3 * 3 + tap, :], in0=identb[:], scalar1=vfm[:, q_, 3 + tap:4 + tap], scalar2=None, op0=ALU.mult),
                                          reads=[identb, vfm], writes=[dg])
                                yield
                                psh = psP.next()
                                fw.ops("pe", [lambda e, tap=tap: e.matmul(psh[:, 0:n], lhsT=dg[:, i3 * 3 + tap, :], rhs=hbb[i3][:, tap:tap + n], start=(tap == 0), stop=(tap == 2))
                                              for tap in range(3)], reads=[dg, hbb[i3]], writes=[psh])
                                yield
                                dst_ = (rr, kk_, vb_s[hp % 4])[i3]
                                fw.op("act", lambda e: e.activation(out=dst_[:, 0:n], in_=psh[:, 0:n], func=AF.Copy), reads=[psh], writes=[dst_])
                                yield
                            t1 = tw.next()
                            fw.op("dve", lambda e: e.tensor_scalar(out=t1[:, 0:n], in0=kk_[:, 0:n], scalar1=vfm[:, hp, 19:20], scalar2=None, op0=ALU.mult),
                                  reads=[kk_, vfm], writes=[t1])
                            yield
                            t2 = tw.next()
                            fw.op("act", lambda e: e.activation(out=t2[:, 0:n], in_=t1[:, 0:n], func=AF.Square), reads=[t1], writes=[t2])
                            yield
                            pa = psP.next()
                            fw.op("pe", lambda e: e.matmul(pa[:, 0:n], lhsT=blk[:], rhs=t2[:, 0:n], start=True, stop=True), reads=[blk, t2], writes=[pa])
                            yield
                            fw.op("act", lambda e: e.activation(out=t2[:, 0:n], in_=pa[:, 0:n], func=AF.Sqrt), reads=[pa], writes=[t2])
                            yield
                            fw.op("dve", lambda e: e.tensor_scalar(out=t2[:, 0:n], in0=t2[:, 0:n], scalar1=1e-12, scalar2=None, op0=ALU.max), reads=[t2], writes=[t2])
                            yield
                            fw.op("dve", lambda e: e.reciprocal(out=t2[:, 0:n], in_=t2[:, 0:n]), reads=[t2], writes=[t2])
                            yield
                            kkn = t1
                            fw.op("dve", lambda e: e.tensor_tensor(out=kkn[:, 0:n], in0=t1[:, 0:n], in1=t2[:, 0:n], op=ALU.mult), reads=[t1, t2], writes=[kkn])
                            yield
                            d = pas
                            pw = psP.next()
                            fw.op("pe", lambda e: e.matmul(pw[:, 0:n], lhsT=decw[:, d, hp * 128:(hp + 1) * 128], rhs=wdt[:, 0:n], start=True, stop=True),
                                  reads=[decw, wdt], writes=[pw])
                            yield
                            lw = tw.next()
                            fw.op("act", lambda e: e.activation(out=lw[:, 0:n], in_=pw[:, 0:n], func=AF.Sigmoid, bias=vfm[:, hp, 15 + d:16 + d], scale=1.0),
                                  reads=[pw, vfm], writes=[lw])
                            yield
                            fw.op("dve", lambda e: e.tensor_scalar(out=lw[:, 0:n], in0=lw[:, 0:n], scalar1=-0.6065306597126334, scalar2=None, op0=ALU.mult),
                                  reads=[lw], writes=[lw])
                            yield
                            pa2 = psP.next()
                            fw.op("pe", lambda e: e.matmul(pa2[:, 0:n], lhsT=iclw[:, d, hp * 128:(hp + 1) * 128], rhs=ads[:, 0:n], start=True, stop=True),
                                  reads=[iclw, ads], writes=[pa2])
                            yield
                            aa = tw.next()
                            fw.op("act", lambda e: e.activation(out=aa[:, 0:n], in_=pa2[:, 0:n], func=AF.Sigmoid, bias=vfm[:, hp, 17 + d:18 + d], scale=1.0),
                                  reads=[pa2, vfm], writes=[aa])
                            yield
                            kd = tw.next()
                            fw.op("dve", lambda e: e.tensor_scalar(out=kd[:, 0:n], in0=aa[:, 0:n], scalar1=vfm[:, hp, 20:21], scalar2=omk[:, hp:hp + 1],
                                                                   op0=ALU.mult, op1=ALU.add), reads=[aa, vfm, omk], writes=[kd])
                            yield
                            fw.op("dve", lambda e: e.tensor_tensor(out=kd[:, 0:n], in0=kd[:, 0:n], in1=kk_[:, 0:n], op=ALU.mult), reads=[kd, kk_], writes=[kd])
                            yield
                            if pas == 1 and not skip_fin:
                                pa3 = psP.next()
                                fw.op("pe", lambda e: e.matmul(pa3[:, 0:n], lhsT=iclw[:, 0, hp * 128:(hp + 1) * 128], rhs=ads[:, 0:n], start=True, stop=True),
                                      reads=[iclw, ads], writes=[pa3])
                                yield
                                af = tw.next()
                                fw.op("act", lambda e: e.activation(out=af[:, 0:n], in_=pa3[:, 0:n], func=AF.Sigmoid, bias=vfm[:, hp, 17:18], scale=1.0),
                                      reads=[pa3, vfm], writes=[af])
                                yield
                                fw.op("dve", lambda e: e.tensor_tensor(out=af[:, 0:n], in0=af[:, 0:n], in1=aa[:, 0:n], op=ALU.add), reads=[af, aa], writes=[af])
                                yield
                                fw.op("dve", lambda e: e.tensor_scalar(out=af[:, 0:n], in0=af[:, 0:n], scalar1=vfm[:, hp, 20:21], scalar2=omk2[:, hp:hp + 1],
                                                                       op0=ALU.mult, op1=ALU.add), reads=[af, vfm, omk2], writes=[af])
                                yield
                                fw.op("dve", lambda e: e.tensor_tensor(out=af[:, 0:n], in0=af[:, 0:n], in1=kk_[:, 0:n], op=ALU.mult), reads=[af, kk_], writes=[af])
                                yield
                                fw.op("dve", lambda e: e.scalar_tensor_tensor(out=ks_s[hp % 4][:, 0:n], in0=af[:, 0:n], scalar=vfm[:, hp, 21:22], in1=rr[:, 0:n],
                                                                              op0=ALU.mult, op1=ALU.mult), reads=[af, vfm, rr], writes=[ks_s[hp % 4]])
                                yield
                            Lc = tw.next()
                            fw.op("dve", lambda e: e.tensor_tensor_scan(out=Lc[:, 0:n], data0=cmask[:, 0:n], data1=lw[:, 0:n], initial=0.0,
                                                                        op0=ALU.mult, op1=ALU.add), reads=[cmask, lw], writes=[Lc])
                            yield
                            Ginc, Gexc, Ginv = tw.next(), tw.next(), tw.next()
                            st = strm[hp % 4]
                            if pas == 0:
                                fw.op("act", lambda e: e.activation(out=Ginc[:, 0:n], in_=Lc[:, 0:n], func=AF.Exp), reads=[Lc], writes=[Ginc])
                                yield
                                fw.op("act", lambda e: e.activation(out=Ginv[:, 0:n], in_=Lc[:, 0:n], func=AF.Exp, scale=-1.0), reads=[Lc], writes=[Ginv])
                                yield
                                fw.op("dve", lambda e: e.tensor_tensor(out=Gexc[:, 0:n], in0=Lc[:, 0:n], in1=lw[:, 0:n], op=ALU.subtract), reads=[Lc, lw], writes=[Gexc])
                                yield
                                fw.op("act", lambda e: e.activation(out=Gexc[:, 0:n], in_=Gexc[:, 0:n], func=AF.Exp), reads=[Gexc], writes=[Gexc])
                                yield
                                for c in range(ncn):
                                    fw.op("pool", lambda e, c=c: e.tensor_copy(out=gC[hp % 4][:, c:c + 1], in_=Ginc[:, c * CH + CH - 1:c * CH + CH]),
                                          reads=[Ginc], writes=[gC[hp % 4]])
                                    yield
                            else:
                                for c in range(ncn):
                                    fw.op("dve", lambda e, c=c: e.tensor_scalar(out=Gexc[:, c * CH:(c + 1) * CH], in0=Lc[:, c * CH:(c + 1) * CH],
                                                                                scalar1=Lc[:, c * CH + CH - 1:c * CH + CH], scalar2=None, op0=ALU.subtract),
                                          reads=[Lc], writes=[Gexc])
                                    yield
                                    fw.op("act", lambda e, c=c: e.activation(out=gC[hp % 4][:, c:c + 1], in_=Lc[:, c * CH + CH - 1:c * CH + CH], func=AF.Exp),
                                          reads=[Lc], writes=[gC[hp % 4]])
                                    yield
                                fw.op("dve", lambda e: e.tensor_tensor(out=Ginc[:, 0:n], in0=lw[:, 0:n], in1=Gexc[:, 0:n], op=ALU.subtract), reads=[lw, Gexc], writes=[Ginc])
                                yield
                                fw.op("act", lambda e: e.activation(out=Ginv[:, 0:n], in_=Ginc[:, 0:n], func=AF.Exp, scale=-1.0), reads=[Ginc], writes=[Ginv])
                                yield
                                fw.op("act", lambda e: e.activation(out=Ginc[:, 0:n], in_=Ginc[:, 0:n], func=AF.Exp), reads=[Ginc], writes=[Ginc])
                                yield
                                fw.op("act", lambda e: e.activation(out=Gexc[:, 0:n], in_=Gexc[:, 0:n], func=AF.Exp, scale=-1.0), reads=[Gexc], writes=[Gexc])
                                yield
                            sv = lambda i4: st[:, 0:ncn, i4, :]
                            v3 = lambda b_: b_[:, 0:n].rearrange("p (c t) -> p c t", t=CH)
                            fw.op("dve", lambda e: e.scalar_tensor_tensor(out=sv(0), in0=v3(kkn), scalar=-1.0, in1=v3(Gexc), op0=ALU.mult, op1=ALU.mult),
                                  reads=[kkn, Gexc], writes=[st])
                            yield
                            fw.op("pool", lambda e: e.tensor_tensor(out=sv(1), in0=v3(rr), in1=v3(Ginc), op=ALU.mult), reads=[rr, Ginc], writes=[st])
                            yield
                            fw.op("dve", lambda e: e.tensor_tensor(out=aa[:, 0:n], in0=aa[:, 0:n], in1=kkn[:, 0:n], op=ALU.mult), reads=[aa, kkn], writes=[aa])
                            yield
                            fw.op("dve", lambda e: e.tensor_tensor(out=sv(2), in0=v3(aa), in1=v3(Ginv), op=ALU.mult), reads=[aa, Ginv], writes=[st])
                            yield
                            fw.op("pool", lambda e: e.tensor_tensor(out=sv(3), in0=v3(kd), in1=v3(Ginv), op=ALU.mult), reads=[kd, Ginv], writes=[st])
                            yield

                        def scan_chunk(c, hg, sset):
                            strm, vb_s, gC = strm2[sset], vb_s2[sset], gC2[sset]
                            S2_, S2b_ = S2g[hg], S2bg[hg]
                            NLEV = 7
                            if True:
                                cs = slice(c * CH, (c + 1) * CH)
                                for g in range(4):
                                    st = strm[g]
                                    pb = psA.next()
                                    fw.ops("pe", [lambda e: e.matmul(pb[:, 0:128], lhsT=vb_s[g][:, cs], rhs=identS[:], start=True, stop=True),
                                                  lambda e: e.matmul(pb[:, 128:256], lhsT=st[:, c, 2, :], rhs=identS[:], start=True, stop=True),
                                                  lambda e: e.matmul(pb[:, 256:384], lhsT=st[:, c, 3, :], rhs=identS[:], start=True, stop=True)],
                                           reads=[vb_s[g], st, identS], writes=[pb])
                                    fw.op("act", lambda e: e.activation(out=TTall[:, g, :, :], in_=pb[:, 0:384].rearrange("p (a b) -> p a b", b=128), func=AF.Copy),
                                          reads=[pb], writes=[TTall])
                                pump_ref[0](pump_k[0])
                                for k in range(8):
                                    g, h = k // 2, k % 2
                                    st = strm[g]
                                    hs = slice(h * 64, (h + 1) * 64)
                                    pg = psA.next()
                                    fw.ops("pe", [lambda e: e.matmul(pg[:, 0:256], lhsT=st[hs, c, 2, :], rhs=st[hs, c, 0:2, :], start=True, stop=True),
                                                  lambda e: e.matmul(pg[:, 256:512], lhsT=st[hs, c, 3, :], rhs=st[hs, c, 0:2, :], start=True, stop=True)],
                                           reads=[st], writes=[pg])
                                    fw.op("dve", lambda e: e.tensor_tensor(out=GBK[:, k, 0, :], in0=pg[:, 0:256], in1=msk[:], op=ALU.mult), reads=[pg, msk], writes=[GBK])
                                    fw.op("dve", lambda e: e.tensor_tensor(out=GBK[:, k, 1, :], in0=pg[:, 256:512], in1=msk[:], op=ALU.mult), reads=[pg, msk], writes=[GBK])
                                    fw.op("dve", lambda e: e.tensor_tensor(out=PkAll[0][:, k, 0, :], in0=pg[:, 0:128], in1=msk[:, 0:128], op=ALU.mult),
                                          reads=[pg, msk], writes=[PkAll[0]])
                                for m in range(2):
                                    pt = psZ.next()
                                    fns = []
                                    for kk2 in range(4):
                                        k = m * 4 + kk2
                                        g, h = k // 2, k % 2
                                        fns.append(lambda e, kk2=kk2, g=g, h=h: e.matmul(pt[:, kk2 * 128:(kk2 + 1) * 128], lhsT=strm[g][h * 64:(h + 1) * 64, c, 0, :],
                                                                                        rhs=strm[g][h * 64:(h + 1) * 64, c, 2, :], start=True, stop=True))
                                    fw.ops("pe", fns, reads=[strm[(m * 4) // 2], strm[(m * 4) // 2 + 1]], writes=[pt])
                                    for kk2 in range(4):
                                        k = m * 4 + kk2
                                        fw.op("dve", lambda e, kk2=kk2, k=k: e.tensor_tensor(out=PkAll[0][:, k, 1, :], in0=pt[:, kk2 * 128:(kk2 + 1) * 128], in1=mskT[:], op=ALU.mult),
                                              reads=[pt, mskT], writes=[PkAll[0]])
                                pump_ref[0](pump_k[0])
                                pX = psZ.next()
                                fns = []
                                for g in range(4):
                                    st = strm[g]
                                    for h in range(2):
                                        k = g * 2 + h
                                        fns.append(lambda e, g=g, h=h, k=k, st=st: e.matmul(pX[:, k * 64:(k + 1) * 64], lhsT=st[:, c, 0, :], rhs=S2b_[:, g, h * 64:(h + 1) * 64],
                                                                                          start=True, stop=False))
                                        fns.append(lambda e, g=g, h=h, k=k: e.matmul(pX[:, k * 64:(k + 1) * 64], lhsT=GBK[:, k, 1, 0:128], rhs=TTall[:, g, 0, h * 64:(h + 1) * 64],
                                                                                   start=False, stop=True))
                                fw.ops("pe", fns, reads=[strm[0], strm[1], strm[2], strm[3], S2b_, GBK, TTall], writes=[pX])
                                fw.op("act", lambda e: e.activation(out=Zt[0][:], in_=pX[:].rearrange("p (a b) -> p a b", b=64), func=AF.Copy), reads=[pX], writes=[Zt[0]])
                                pump_ref[0](pump_k[0])
                                for lv in range(NLEV):
                                    if lv > 0:
                                        pump_ref[0](pump_k[0])
                                    Pc, Pn = PkAll[lv % 2], PkAll[(lv + 1) % 2]
                                    zi, zo = Zt[lv % 2], Zt[(lv + 1) % 2]
                                    pz = psZ.next()
                                    fw.ops("pe", [lambda e, k=k: e.matmul(pz[:, k * 64:(k + 1) * 64], lhsT=Pc[:, k, 0, :], rhs=zi[:, k, :], start=True, stop=True)
                                                  for k in range(8)], reads=[Pc, zi], writes=[pz])
                                    if lv == NLEV - 1:
                                        fw.op("dve", lambda e: e.tensor_tensor(out=UP[:].rearrange("p g (h i) -> p (g h) i", i=64), in0=pz[:].rearrange("p (a b) -> p a b", b=64),
                                                                               in1=zi[:], op=ALU.add), reads=[pz, zi], writes=[UP])
                                        break
                                    fw.op("dve", lambda e: e.tensor_tensor(out=zo[:], in0=pz[:].rearrange("p (a b) -> p a b", b=64), in1=zi[:], op=ALU.add),
                                          reads=[pz, zi], writes=[zo])
                                    if lv < NLEV - 2:
                                        for m in range(4):
                                            pq = psA.next()
                                            fns = []
                                            for k in (2 * m, 2 * m + 1):
                                                o = (k % 2) * 256
                                                fns.append(lambda e, k=k, o=o: e.matmul(pq[:, o:o + 128], lhsT=Pc[:, k, 1, :], rhs=Pc[:, k, 0, :], start=True, stop=True))
                                                fns.append(lambda e, k=k, o=o: e.matmul(pq[:, o + 128:o + 256], lhsT=Pc[:, k, 0, :], rhs=Pc[:, k, 1, :], start=True, stop=True))
                                            fw.ops("pe", fns, reads=[Pc], writes=[pq])
                                            dst = Pn[:, 2 * m:2 * m + 2, :, :]
                                            src = pq[:].rearrange("p (a b c) -> p a b c", a=2, b=2)
                                            if m % 2 == 0:
                                                fw.op("act", lambda e: e.activation(out=dst, in_=src, func=AF.Copy), reads=[pq], writes=[Pn])
                                            else:
                                                fw.op("dve", lambda e: e.tensor_copy(out=dst, in_=src), reads=[pq], writes=[Pn])
                                    else:
                                        for m in range(2):
                                            pq = psA.next()
                                            fw.ops("pe", [lambda e, k=k: e.matmul(pq[:, (k % 4) * 128:(k % 4 + 1) * 128], lhsT=Pc[:, k, 1, :], rhs=Pc[:, k, 0, :], start=True, stop=True)
                                                          for k in range(4 * m, 4 * m + 4)], reads=[Pc], writes=[pq])
                                            dst = Pn[:, 4 * m:4 * m + 4, 0, :]
                                            src = pq[:].rearrange("p (a b) -> p a b", b=128)
                                            if m % 2 == 0:
                                                fw.op("act", lambda e: e.activation(out=dst, in_=src, func=AF.Copy), reads=[pq], writes=[Pn])
                                            else:
                                                fw.op("dve", lambda e: e.tensor_copy(out=dst, in_=src), reads=[pq], writes=[Pn])
                                pump_ref[0](pump_k[0])
                                pY = psZ.next()
                                fns = []
                                for g in range(4):
                                    st = strm[g]
                                    for h in range(2):
                                        k = g * 2 + h
                                        ro = slice(h * 64, (h + 1) * 64)
                                        co = slice(g * 128, (g + 1) * 128)
                                        fns.append(lambda e, g=g, ro=ro, co=co, st=st: e.matmul(pY[ro, co], lhsT=S2b_[:, g, ro], rhs=st[:, c, 1, :], start=True, stop=False))
                                        fns.append(lambda e, g=g, k=k, ro=ro, co=co: e.matmul(pY[ro, co], lhsT=UP[:, g, ro], rhs=GBK[:, k, 0, 128:256], start=False, stop=False))
                                        fns.append(lambda e, g=g, k=k, ro=ro, co=co: e.matmul(pY[ro, co], lhsT=TTall[:, g, 0, ro], rhs=GBK[:, k, 1, 128:256], start=False, stop=True))
                                fw.ops("pe", fns, reads=[strm[0], strm[1], strm[2], strm[3], S2b_, UP, GBK, TTall], writes=[pY])
                                fw.op("act", lambda e: e.activation(out=ybuf[:, :, cs], in_=pY[:].rearrange("p (a b) -> p a b", b=128), func=AF.Copy), reads=[pY], writes=[ybuf])
                                pump_ref[0](pump_k[0])
                                pS = psZ.next()
                                fns = []
                                for g in range(4):
                                    co = slice(g * 128, (g + 1) * 128)
                                    fns.append(lambda e, g=g, co=co: e.matmul(pS[:, co], lhsT=TTall[:, g, 1, :], rhs=UP[:, g, :], start=True, stop=False))
                                    fns.append(lambda e, g=g, co=co: e.matmul(pS[:, co], lhsT=TTall[:, g, 2, :], rhs=TTall[:, g, 0, :], start=False, stop=True))
                                fw.ops("pe", fns, reads=[TTall, UP], writes=[pS])
                                fw.op("dve", lambda e: e.tensor_tensor(out=stm[:], in0=pS[:].rearrange("p (a b) -> p a b", b=128), in1=S2_[:], op=ALU.add),
                                      reads=[pS, S2_], writes=[stm])
                                for g in range(4):
                                    fw.op("dve", lambda e, g=g: e.scalar_tensor_tensor(out=S2_[:, g, :], in0=stm[:, g, :], scalar=gC[g][:, c:c + 1], in1=blk[:],
                                                                                       op0=ALU.mult, op1=ALU.mult), reads=[stm, gC[g], blk], writes=[S2_])
                                fw.op("pool", lambda e: e.tensor_copy(out=S2b_[:], in_=S2_[:]), reads=[S2_], writes=[S2b_])
                                pump_ref[0](pump_k[0])

                        def outputs(tile, hg, sset, lp):
                            col0, n, is_ctx, first, last, ncn, skip_fin = unpack(tile)
                            hps = list(range(hg * 4, hg * 4 + 4))
                            gds = gds2[lp]
                            vb_s, ks_s = vb_s2[sset], ks_s2[sset]
                            for hp in hps:
                                rows = slice(hp * 128, (hp + 1) * 128)
                                if pas == 0:
                                    fw.store("sp", ybuf, YF[rows, col0:col0 + n], ybuf[:, hp % 4, 0:n])
                                    continue
                                if skip_fin or 'no_fin' in dbg:
                                    continue
                                yf = yio.next()
                                fw.load("sp", yf, yf[:, 0:n], YF[rows, col0:col0 + n])
                                y = yf
                                fw.op("dve", lambda e: e.tensor_tensor(out=y[:, 0:n], in0=yf[:, 0:n], in1=ybuf[:, hp % 4, 0:n], op=ALU.add), reads=[yf, ybuf], writes=[y])
                                pm_ = psA.next()
                                fw.op("pe", lambda e: e.matmul(pm_[:, 0:n], lhsT=blk[:], rhs=y[:, 0:n], start=True, stop=True), reads=[blk, y], writes=[pm_])
                                dd = tw.next()
                                fw.op("dve", lambda e: e.scalar_tensor_tensor(out=dd[:, 0:n], in0=pm_[:, 0:n], scalar=-1.0 / 64, in1=y[:, 0:n], op0=ALU.mult, op1=ALU.add),
                                      reads=[pm_, y], writes=[dd])
                                d2 = tw.next()
                                fw.op("act", lambda e: e.activation(out=d2[:, 0:n], in_=dd[:, 0:n], func=AF.Square), reads=[dd], writes=[d2])
                                pv_ = psA.next()
                                fw.op("pe", lambda e: e.matmul(pv_[:, 0:n], lhsT=blk[:], rhs=d2[:, 0:n], start=True, stop=True), reads=[blk, d2], writes=[pv_])
                                fw.op("act", lambda e: e.activation(out=d2[:, 0:n], in_=pv_[:, 0:n], func=AF.Sqrt, bias=GN_EPS, scale=1.0 / 64), reads=[pv_], writes=[d2])
                                fw.op("dve", lambda e: e.reciprocal(out=d2[:, 0:n], in_=d2[:, 0:n]), reads=[d2], writes=[d2])
                                fw.op("dve", lambda e: e.tensor_tensor(out=dd[:, 0:n], in0=dd[:, 0:n], in1=d2[:, 0:n], op=ALU.mult), reads=[dd, d2], writes=[dd])
                                fw.op("dve", lambda e: e.tensor_scalar(out=dd[:, 0:n], in0=dd[:, 0:n], scalar1=vfm[:, hp, 22:23], scalar2=vfm[:, hp, 23:24],
                                                                       op0=ALU.mult, op1=ALU.add), reads=[dd, vfm], writes=[dd])
                                pb_ = psA.next()
                                fw.op("pe", lambda e: e.matmul(pb_[:, 0:n], lhsT=blkb[:], rhs=ks_s[hp % 4][:, 0:n], start=True, stop=True), reads=[blkb, ks_s[hp % 4]], writes=[pb_])
                                fw.op("dve", lambda e: e.tensor_tensor(out=d2[:, 0:n], in0=pb_[:, 0:n], in1=vb_s[hp % 4][:, 0:n], op=ALU.mult), reads=[pb_, vb_s[hp % 4]], writes=[d2])
                                fw.op("dve", lambda e: e.tensor_tensor(out=dd[:, 0:n], in0=dd[:, 0:n], in1=d2[:, 0:n], op=ALU.add), reads=[dd, d2], writes=[dd])
                                pg_ = psA.next()
                                fw.ops("pe", [lambda e, c2=c2: e.matmul(pg_[:, 0:n], lhsT=gupw[:, c2, hp * 128:(hp + 1) * 128], rhs=gds[:, c2, 0:n],
                                                                        start=(c2 == 0), stop=(c2 == 1)) for c2 in range(2)],
                                       reads=[gupw, gds], writes=[pg_])
                                yo = yob.next()
                                fw.op("dve", lambda e: e.tensor_tensor(out=yo[:, 0:n], in0=pg_[:, 0:n], in1=dd[:, 0:n], op=ALU.mult), reads=[pg_, dd], writes=[yo])
                                fw.store("sp", yo, YT[rows, col0:col0 + n], yo[:, 0:n])
                        items = [(ti, hg) for ti in range(len(order)) for hg in range(2)]
                        lora_shared(order[0], 0)
                        for g in range(4):
                            for _ in prep_hp(order[0], g, 0, 0):
                                pass
                        pending = [None]

                        def pump(k):
                            gen = pending[0]
                            if gen is None:
                                return
                            for _ in range(k):
                                try:
                                    next(gen)
                                except StopIteration:
                                    pending[0] = None
                                    return

                        def chain_prep(nx, sset):
                            for g in range(4):
                                for _ in prep_hp(order[nx[0]], nx[1] * 4 + g, sset, nx[0] % 2):
                                    yield

                        pump_ref[0] = pump
                        for j, (ti, hg) in enumerate(items):
                            tile = order[ti]
                            ncn = tile[1] // CH
                            nxt = items[j + 1] if j + 1 < len(items) else None
                            if nxt is not None and nxt[0] != ti:
                                lora_shared(order[nxt[0]], nxt[0] % 2)
                            pending[0] = chain_prep(nxt, (j + 1) % 2) if nxt is not None else None
                            pump_k[0] = max(1, (4 * 62) // (ncn * 14) + 1)
                            crange = list(range(ncn)) if pas == 0 else list(range(ncn))[::-1]
                            for c in crange:
                                scan_chunk(c, hg, j % 2)
                            pump(100000)
                            outputs(tile, hg, j % 2, ti % 2)
                        fw.barrier()
                        if pas == 0:
                            for hg_ in range(2):
                                fw.op("pool", lambda e, hg_=hg_: e.memset(S2g[hg_][:], 0.0), writes=[S2g[hg_]])
                                fw.op("pool", lambda e, hg_=hg_: e.memset(S2bg[hg_][:], 0.0), writes=[S2bg[hg_]])
                fw.barrier()
                if "stop_p2" in dbg:
                    break

                with ExitStack() as es3:
                    cin = Ring([[fw.sb([128, NT], F32, es=es3) for _ in range(3)] for _ in range(2)])
                    cu = Ring([fw.sb([128, NT], F32, es=es3) for _ in range(2)])
                    co = Ring([fw.sb([128, NT], F32, es=es3) for _ in range(2)])
                    cob = Ring([fw.sb([128, NT], BF16, es=es3) for _ in range(2)])
                    for (col0, n, is_ctx, first, last) in tiles:
                        if is_ctx and last_layer:
                            continue
                        for q in range(4):
                            ib = cin.next()
                            for i3, base in enumerate((PCG, PCX, PCB)):
                                fw.load("sp", ib[i3], ib[i3][:, 0:n], PXT[base + q * 128:base + (q + 1) * 128, col0:col0 + n])
                            u = cu.next()
                            o = co.next()
                            fw.op("pool", lambda e: e.tensor_tensor(out=u[:, 0:n], in0=ib[0][:, 0:n], in1=ib[1][:, 0:n], op=ALU.mult), reads=[ib[0], ib[1]], writes=[u])
                            fw.op("dve", lambda e: e.tensor_scalar(out=o[:, 0:n], in0=u[:, 0:n], scalar1=cw_fm[:, q, 1:2], scalar2=None, op0=ALU.mult),
                                  reads=[u, cw_fm], writes=[o])
                            if is_ctx:
                                assert first and last
                                W_ = n
                            else:
                                W_ = 64
                            u3 = u[:, 0:n].rearrange("p (r w) -> p r w", w=W_)
                            o3 = o[:, 0:n].rearrange("p (r w) -> p r w", w=W_)
                            fw.op("dve", lambda e: e.scalar_tensor_tensor(out=o3[:, :, 1:W_], in0=u3[:, :, 0:W_ - 1], scalar=cw_fm[:, q, 0:1], in1=o3[:, :, 1:W_],
                                                                          op0=ALU.mult, op1=ALU.add), reads=[u, cw_fm, o], writes=[o])
                            fw.op("dve", lambda e: e.scalar_tensor_tensor(out=o3[:, :, 0:W_ - 1], in0=u3[:, :, 1:W_], scalar=cw_fm[:, q, 2:3], in1=o3[:, :, 0:W_ - 1],
                                                                          op0=ALU.mult, op1=ALU.add), reads=[u, cw_fm, o], writes=[o])
                            ob = cob.next()
                            fw.op("pool", lambda e: e.tensor_tensor(out=ob[:, 0:n], in0=o[:, 0:n], in1=ib[2][:, 0:n], op=ALU.mult), reads=[o, ib[2]], writes=[ob])
                            fw.store("sp", ob, YT[1024 + q * 128:1024 + (q + 1) * 128, col0:col0 + n], ob[:, 0:n])
                fw.barrier()

                with ExitStack() as es4:
                    LCmax = T // 128
                    UC = fw.sb([128, LCmax, 512], BF16, es=es4)
                    US = fw.sb([128, LCmax, 512], BF16, es=es4)
                    uT = Ring([fw.sb([128, NT], BF16, es=es4) for _ in range(3)])
                    pu = Ring([fw.ps([128, 512], F32, es=es4) for _ in range(2)])
                    py = [fw.ps([128, 512], F32, es=es4) for _ in range(4)]
                    LG = 8
                    tabr = Ring([[fw.sb([128, LG, 512], BF16, es=es4) for _ in range(2)] for _ in range(2)])
                    fo = Ring([fw.sb([128, 512], BF16, es=es4) for _ in range(3)])
                    for (seq0, sn, tab, is_ctx) in ((0, CTX, ftab_c, True), (CTX, T, ftab_x, False)):
                        if is_ctx and last_layer:
                            continue
                        nlc = sn // 128
                        for t0 in range(0, sn, NT):
                            tn = min(NT, sn - t0)
                            for mc in range(4):
                                ub = uT.next()
                                fw.load("pool", ub, ub[:, 0:tn], PXT[PFT + mc * 128:PFT + (mc + 1) * 128, seq0 + t0:seq0 + t0 + tn])
                                for lc in range(tn // 128):
                                    p_ = pu.next()
                                    fw.op("pe", lambda e: e.matmul(p_[:, 0:256], lhsT=ub[:, lc * 128:(lc + 1) * 128], rhs=c64[:], start=True, stop=True),
                                          reads=[ub, c64], writes=[p_])
                                    glc = t0 // 128 + lc
                                    if (lc + mc) % 2 == 0:
                                        fw.op("dve", lambda e: e.tensor_copy(out=UC[:, glc, mc * 128:(mc + 1) * 128], in_=p_[:, 0:128]), reads=[p_], writes=[UC])
                                        fw.op("dve", lambda e: e.tensor_copy(out=US[:, glc, mc * 128:(mc + 1) * 128], in_=p_[:, 128:256]), reads=[p_], writes=[US])
                                    else:
                                        fw.op("act", lambda e: e.activation(out=UC[:, glc, mc * 128:(mc + 1) * 128], in_=p_[:, 0:128], func=AF.Copy), reads=[p_], writes=[UC])
                                        fw.op("act", lambda e: e.activation(out=US[:, glc, mc * 128:(mc + 1) * 128], in_=p_[:, 128:256], func=AF.Copy), reads=[p_], writes=[US])
                        for k0 in range(0, sn, 512):
                            kn = min(512, sn - k0)
                            for lg0 in range(0, nlc, LG):
                                lgn = min(LG, nlc - lg0)
                                tb = tabr.next()
                                for i2 in range(2):
                                    fw.load("sp" if i2 == 0 else "act", tb[i2], tb[i2][:, 0:lgn, 0:kn],
                                            tab[i2, lg0 * 128:(lg0 + lgn) * 128, k0:k0 + kn].rearrange("(c p) k -> p c k", p=128))
                                for mc in range(4):
                                    fns = []
                                    for l_ in range(lgn):
                                        lc = lg0 + l_
                                        fns.append(lambda e, l_=l_, lc=lc: e.matmul(py[mc][:, 0:kn], lhsT=UC[:, lc, mc * 128:(mc + 1) * 128], rhs=tb[0][:, l_, 0:kn],
                                                                                    start=(lc == 0), stop=False))
                                        fns.append(lambda e, l_=l_, lc=lc: e.matmul(py[mc][:, 0:kn], lhsT=US[:, lc, mc * 128:(mc + 1) * 128], rhs=tb[1][:, l_, 0:kn],
                                                                                    start=False, stop=(lc == nlc - 1)))
                                    fw.ops("pe", fns, reads=[UC, US, tb[0], tb[1]], writes=[py[mc]])
                            for mc in range(4):
                                ob = fo.next()
                                fw.op("act" if mc % 2 else "dve",
                                      (lambda e: e.activation(out=ob[:, 0:kn], in_=py[mc][:, 0:kn], func=AF.Copy)) if mc % 2 else
                                      (lambda e: e.tensor_copy(out=ob[:, 0:kn], in_=py[mc][:, 0:kn])), reads=[py[mc]], writes=[ob])
                                fw.store("sp", ob, YT[1536 + mc * 128:1536 + (mc + 1) * 128, seq0 + k0:seq0 + k0 + kn], ob[:, 0:kn])
                fw.barrier()
                if "stop_p4" in dbg:
                    break

                with ExitStack() as es5:
                    xbr_ = Ring([fw.sb([128, 16, NT], F32, es=es5) for _ in range(2)])
                    yx = fw.sb([128, 16, NT], BF16, es=es5)
                    H = fw.sb([128, FC, NT], BF16, es=es5)
                    wring = Ring([fw.sb([128, 16, 256], BF16, es=es5) for _ in range(4)])
                    wdr = Ring([fw.sb([128, FC, 128], BF16, es=es5) for _ in range(2)])
                    sq = H
                    rstd = fw.sb([128, NT], F32, es=es5)
                    tmp = Ring([fw.sb([128, NT], F32, es=es5) for _ in range(2)])
                    sg = Ring([fw.sb([128, NT], F32, es=es5) for _ in range(2)])
                    pss = fw.ps([128, NT], F32, es=es5)
                    pA = Ring([fw.ps([128, NT], F32, es=es5) for _ in range(3)])
                    pG = Ring([fw.ps([128, NT], F32, es=es5) for _ in range(2)])
                    pU = Ring([fw.ps([128, NT], F32, es=es5) for _ in range(2)])
                    YTv = YT.rearrange("(c p) t -> p c t", p=128)
                    for (col0, n, is_ctx, first, last) in tiles:
                        if is_ctx and last_layer:
                            continue
                        r = 1 if is_ctx else 0
                        xb_ = xbr_.next()
                        fw.load("sp", xb_, xb_[:, :, 0:n], XTv[:, :, col0:col0 + n])
                        fw.load("act", yx, yx[:, :, 0:n], YTv[:, :, col0:col0 + n])
                        for g in range(8):
                            wb = wring.next()
                            fw.load("sp" if g % 2 else "act", wb, wb[:], WOUT_b[g])
                            for j in range(2):
                                dc = g * 2 + j
                                po = pA.next()
                                fw.ops("pe", [lambda e, kc=kc: e.matmul(po[:, 0:n], lhsT=wb[:, kc, j * 128:(j + 1) * 128], rhs=yx[:, kc, 0:n],
                                                                        start=(kc == 0), stop=(kc == 15)) for kc in range(16)], reads=[wb, yx], writes=[po])
                                fw.op("dve", lambda e: e.scalar_tensor_tensor(out=xb_[:, dc, 0:n], in0=po[:, 0:n], scalar=mods[:, 32 + dc, r:r + 1], in1=xb_[:, dc, 0:n],
                                                                              op0=ALU.mult, op1=ALU.add), reads=[po, mods, xb_], writes=[xb_])
                        fw.op("act", lambda e: e.activation(out=sq[:, 0:16, 0:n], in_=xb_[:, :, 0:n], func=AF.Square), reads=[xb_], writes=[sq])
                        fw.ops("pe", [lambda e, dc=dc: e.matmul(pss[:, 0:n], lhsT=onesb[:], rhs=sq[:, dc, 0:n], start=(dc == 0), stop=(dc == 15))
                                      for dc in range(16)], reads=[sq, onesb], writes=[pss])
                        fw.op("act", lambda e: e.activation(out=rstd[:, 0:n], in_=pss[:, 0:n], func=AF.Sqrt, bias=RMS_EPS, scale=1.0 / D), reads=[pss], writes=[rstd])
                        fw.op("dve", lambda e: e.reciprocal(out=rstd[:, 0:n], in_=rstd[:, 0:n]), reads=[rstd], writes=[rstd])
                        for dc in range(16):
                            tb = tmp.next()
                            fw.op("dve", lambda e, dc=dc: e.scalar_tensor_tensor(out=tb[:, 0:n], in0=xb_[:, dc, 0:n], scalar=A2[:, dc, r:r + 1], in1=rstd[:, 0:n],
                                                                                 op0=ALU.mult, op1=ALU.mult), reads=[xb_, A2, rstd], writes=[tb])
                            fw.op("act", lambda e, dc=dc: e.activation(out=yx[:, dc, 0:n], in_=tb[:, 0:n], func=AF.Identity, bias=mods[:, 48 + dc, r:r + 1], scale=1.0),
                                  reads=[tb, mods], writes=[yx])
                        for g in range(22):
                            wg_ = wring.next()
                            wu_ = wring.next()
                            fw.load("sp", wg_, wg_[:], WG_b[g])
                            fw.load("act", wu_, wu_[:], WU_b[g])
                            for j in range(2):
                                fc = g * 2 + j
                                pg = pG.next()
                                pu_ = pU.next()
                                fw.ops("pe", [lambda e, kc=kc: e.matmul(pg[:, 0:n], lhsT=wg_[:, kc, j * 128:(j + 1) * 128], rhs=yx[:, kc, 0:n],
                                                                        start=(kc == 0), stop=(kc == 15)) for kc in range(16)], reads=[wg_, yx], writes=[pg])
                                fw.ops("pe", [lambda e, kc=kc: e.matmul(pu_[:, 0:n], lhsT=wu_[:, kc, j * 128:(j + 1) * 128], rhs=yx[:, kc, 0:n],
                                                                        start=(kc == 0), stop=(kc == 15)) for kc in range(16)], reads=[wu_, yx], writes=[pu_])
                                s_ = sg.next()
                                fw.op("act", lambda e: e.activation(out=s_[:, 0:n], in_=pg[:, 0:n], func=AF.Silu), reads=[pg], writes=[s_])
                                fw.op("dve", lambda e: e.tensor_tensor(out=H[:, fc, 0:n], in0=pu_[:, 0:n], in1=s_[:, 0:n], op=ALU.mult), reads=[pu_, s_], writes=[H])
                        for dc in range(16):
                            wd_ = wdr.next()
                            fw.load("sp" if dc % 2 else "act", wd_, wd_[:], WD_b[dc])
                            if True:
                                po = pA.next()
                                fw.ops("pe", [lambda e, fc=fc: e.matmul(po[:, 0:n], lhsT=wd_[:, fc, :], rhs=H[:, fc, 0:n],
                                                                        start=(fc == 0), stop=(fc == FC - 1)) for fc in range(FC)], reads=[wd_, H], writes=[po])
                                fw.op("dve", lambda e: e.scalar_tensor_tensor(out=xb_[:, dc, 0:n], in0=po[:, 0:n], scalar=mods[:, 80 + dc, r:r + 1], in1=xb_[:, dc, 0:n],
                                                                              op0=ALU.mult, op1=ALU.add), reads=[po, mods, xb_], writes=[xb_])
                        fw.store("pool", xb_, XTv[:, :, col0:col0 + n], xb_[:, :, 0:n])
                fw.barrier()

        with ExitStack() as es7:
            xb_ = fw.sb([128, 16, NT], F32, es=es7)
            sq = fw.sb([128, 16, NT], BF16, es=es7)
            rstd = fw.sb([128, NT], F32, es=es7)
            xo = fw.sb([128, 16, NT], F32, es=es7)
            pss = fw.ps([128, NT], F32, es=es7)
            ptr = Ring([fw.ps([128, 512], F32, es=es7) for _ in range(4)])
            orow = Ring([fw.sb([128, D], F32, es=es7) for _ in range(2)])
            for (col0, n, is_ctx, first, last) in tiles:
                if is_ctx:
                    continue
                fw.load("sp", xb_, xb_[:, :, 0:n], XTv[:, :, col0:col0 + n])
                fw.op("act", lambda e: e.activation(out=sq[:, :, 0:n], in_=xb_[:, :, 0:n], func=AF.Square), reads=[xb_], writes=[sq])
                fw.ops("pe", [lambda e, dc=dc: e.matmul(pss[:, 0:n], lhsT=onesb[:], rhs=sq[:, dc, 0:n], start=(dc == 0), stop=(dc == 15))
                              for dc in range(16)], reads=[sq, onesb], writes=[pss])
                fw.op("act", lambda e: e.activation(out=rstd[:, 0:n], in_=pss[:, 0:n], func=AF.Sqrt, bias=RMS_EPS, scale=1.0 / D), reads=[pss], writes=[rstd])
                fw.op("dve", lambda e: e.reciprocal(out=rstd[:, 0:n], in_=rstd[:, 0:n]), reads=[rstd], writes=[rstd])
                for dc in range(16):
                    fw.op("dve", lambda e, dc=dc: e.scalar_tensor_tensor(out=xo[:, dc, 0:n], in0=xb_[:, dc, 0:n], scalar=nf_fm[:, dc:dc + 1], in1=rstd[:, 0:n],
                                                                         op0=ALU.mult, op1=ALU.mult), reads=[xb_, nf_fm, rstd], writes=[xo])
                for tb in range(n // 128):
                    ob = orow.next()
                    for g4 in range(4):
                        pt = ptr.next()
                        fw.ops("pe", [lambda e, j=j: e.transpose(pt[:, j * 128:(j + 1) * 128], xo[:, g4 * 4 + j, tb * 128:(tb + 1) * 128], ident[:])
                                      for j in range(4)], reads=[xo, ident], writes=[pt])
                        fw.op("dve" if g4 % 2 == 0 else "act",
                              (lambda e: e.tensor_copy(out=ob[:, g4 * 512:(g4 + 1) * 512], in_=pt[:])) if g4 % 2 == 0 else
                              (lambda e: e.activation(out=ob[:, g4 * 512:(g4 + 1) * 512], in_=pt[:], func=AF.Copy)), reads=[pt], writes=[ob])
                    t0 = col0 - CTX + tb * 128
                    fw.store("sp", ob, out_ap[t0:t0 + 128, :], ob[:])
        fw.barrier()
    return nc, fw


def fourier_tables(n):
    idx = np.arange(n, dtype=np.int64)
    ang = (2.0 * np.pi / n) * ((idx[:, None] * idx[None, :]) % n).astype(np.float64)
    sc = 1.0 / np.sqrt(64.0 * n)
    tab = np.stack([np.cos(ang) * sc, -np.sin(ang) * sc], 0)
    return tab.astype(np.float32).astype(ml_dtypes.bfloat16)


def c64_table():
    idx = np.arange(64)
    ang = 2.0 * np.pi * ((idx[:, None] * idx[None, :]) % 64) / 64.0
    t = np.zeros((128, 256), np.float32)
    for h in range(2):
        t[h * 64:(h + 1) * 64, h * 64:(h + 1) * 64] = np.cos(ang)
        t[h * 64:(h + 1) * 64, 128 + h * 64:128 + (h + 1) * 64] = np.sin(ang)
    return t.astype(ml_dtypes.bfloat16)


_CACHE = {}


def make_inputs(b, x, c, ctx, c_ctx, W, T, CTX):
    m = {"x": np.ascontiguousarray(x[b]), "ctx": np.ascontiguousarray(ctx[b]),
         "cvec": np.ascontiguousarray(np.stack([c[b], c_ctx], 0))}
    m.update(W)
    return m


def kernel(x, c, ctx, c_ctx, w_mod, b_mod, norm_mix, w_in, rw_shift, dec_w0, dec_up, iclr_a0, iclr_up, k_k, k_a, r_k,
           ln_w, ln_b, g_up, conv_w, w_out, norm_ffn, w_gate, w_up, w_down, norm_final, _dbg=()):
    x = np.asarray(x)
    B, T, _ = x.shape
    CTX = ctx.shape[1]
    DEPTH = w_mod.shape[0]
    f = lambda a: np.ascontiguousarray(np.asarray(a, dtype=np.float32))
    W = dict(w_mod=f(w_mod), b_mod=f(b_mod), norm_mix=f(norm_mix), w_in=f(w_in), rw_shift=f(rw_shift), dec_w0=f(dec_w0),
             dec_up=f(dec_up), iclr_a0=f(iclr_a0), iclr_up=f(iclr_up), k_k=f(k_k), k_a=f(k_a),
             r_k=f(r_k).reshape(DEPTH, D_RWKV), ln_w=f(ln_w), ln_b=f(ln_b), g_up=f(g_up), conv_w=f(conv_w), w_out=f(w_out),
             norm_ffn=f(norm_ffn), w_gate=f(w_gate), w_up=f(w_up), w_down=f(w_down), norm_final=f(norm_final).reshape(1, D),
             ftab_x=fourier_tables(T), ftab_c=fourier_tables(CTX), c64=c64_table())
    key = (T, CTX, DEPTH, tuple(_dbg))
    nc, fw = build(T, CTX, DEPTH, _dbg)
    in_maps = [make_inputs(b, x, np.asarray(c), np.asarray(ctx), np.asarray(c_ctx), W, T, CTX) for b in range(B)]
    res = run_bass_kernel_spmd(nc, in_maps, core_ids=list(range(B)))
    if _dbg:
        return res
    return np.stack([res.results[b]["out"] for b in range(B)], 0).astype(np.float32)
```
